# Optimizing a Trainium2 kernel written in Bass

```python
import jax, jax.numpy as jnp
from jax import lax
import numpy as np

D_MODEL = 1024
BATCH = 4
SEQ = 8192
DEPTH = 1

GRID_W = 64
CTX_LEN = 256
HEAD_DIM = 64
N_Q_HEADS = 8
N_KV_HEADS = 2
Q_PER_KV = N_Q_HEADS // N_KV_HEADS
ATTN_WIDTH = N_Q_HEADS * HEAD_DIM
GM_GROUPS = 8
GM_GROUP_DIM = 64
GM_WIDTH = GM_GROUPS * GM_GROUP_DIM
CHUNK = 128
Q_BLOCK = 128
D_FF = 4 * D_MODEL
ROPE_THETA = 10000.0
ROT_AXIS_DIM = HEAD_DIM // 2
EPS = 1e-6
N_MOD = 6

K_W = N_KV_HEADS * HEAD_DIM
V_W = N_KV_HEADS * HEAD_DIM
KV_COLS = K_W + V_W
Q_W = ATTN_WIDTH
U_W = GM_WIDTH
VG_W = GM_WIDTH
GA_W = D_MODEL
GB_W = D_MODEL
D_IN = KV_COLS + Q_W + U_W + VG_W + GA_W + GB_W
REST_SPLITS = tuple(int(s) for s in np.cumsum([Q_W, U_W, VG_W, GA_W]))

kernel_name = "hybrid_gqa_gmlp_dit_block"


def rmsnorm(x, g):
    xf = x.astype(jnp.float32)
    y = xf * lax.rsqrt(jnp.mean(xf * xf, axis=-1, keepdims=True) + EPS)
    return (y * g.astype(jnp.float32)).astype(x.dtype)


def modulate(x, g, shift, scale):
    return rmsnorm(x, g) * (1 + scale) + shift


def axial_rope_tables(n_tokens, dtype):
    rows = n_tokens // GRID_W
    row = jnp.repeat(jnp.arange(rows, dtype=jnp.float32), GRID_W)
    col = jnp.tile(jnp.arange(GRID_W, dtype=jnp.float32), rows)
    inv = ROPE_THETA ** (-jnp.arange(0, ROT_AXIS_DIM, 2, dtype=jnp.float32) / ROT_AXIS_DIM)
    ang = jnp.concatenate([row[:, None] * inv, col[:, None] * inv], axis=-1)
    return jnp.cos(ang).astype(dtype), jnp.sin(ang).astype(dtype)


def apply_rope(x, cos, sin):
    x1, x2 = x[..., :HEAD_DIM // 2], x[..., HEAD_DIM // 2:]
    return jnp.concatenate([x1 * cos - x2 * sin, x2 * cos + x1 * sin], axis=-1)


def q_heads(p):
    b, n, _ = p.shape
    return p.reshape(b, n, N_KV_HEADS, Q_PER_KV, HEAD_DIM).transpose(0, 2, 3, 1, 4)


def kv_heads(p):
    b, n, _ = p.shape
    return p.reshape(b, n, N_KV_HEADS, HEAD_DIM).transpose(0, 2, 1, 3)


def merge_heads(o):
    b, kv, g, n, d = o.shape
    return o.transpose(0, 3, 1, 2, 4).reshape(b, n, kv * g * d)


def attend(q, k, v):
    s = jnp.einsum('bkgqd,bknd->bkgqn', q, k, preferred_element_type=jnp.float32)
    p = jax.nn.softmax(s, axis=-1).astype(v.dtype)
    return jnp.einsum('bkgqn,bknd->bkgqd', p, v)


def attend_blocked(q, k, v):
    b, kv, g, n, d = q.shape
    nb = n // Q_BLOCK
    qb = jnp.moveaxis(q.reshape(b, kv, g, nb, Q_BLOCK, d), 3, 0)
    ob = lax.map(lambda blk: attend(blk, k, v), qb)
    return jnp.moveaxis(ob, 0, 3).reshape(b, kv, g, n, d)


def gmlp_spatial(u, v, gm_norm_g, gm_ws, gm_bs):
    b, n, _ = u.shape
    nc = n // CHUNK
    vg = v.reshape(b, nc, CHUNK, GM_GROUPS, GM_GROUP_DIM)
    vg = rmsnorm(vg, gm_norm_g)
    s = jnp.einsum('gpq,bnqgc->bnpgc', gm_ws.astype(vg.dtype), vg) + gm_bs.T[:, :, None].astype(vg.dtype)
    return u * s.reshape(b, n, GM_WIDTH)


def branch_merge(attn_o, gm_o, ga_logit, gb_logit, w_br_attn, w_br_gm, w_out):
    y = jax.nn.sigmoid(ga_logit) * (attn_o @ w_br_attn) + jax.nn.sigmoid(gb_logit) * (gm_o @ w_br_gm)
    return y @ w_out


def sq_relu_mlp(h, w1, w2):
    return jnp.square(jax.nn.relu(h @ w1)) @ w2


def setup_inputs(seed: int = 0) -> dict:
    key = jax.random.key(seed)
    ks = jax.random.split(key, 20)
    f32 = jnp.float32
    nrm = lambda k, shape, s: jax.random.normal(k, shape, f32) * s
    return {
        "x": nrm(ks[0], (BATCH, SEQ, D_MODEL), 1.0),
        "c": nrm(ks[1], (BATCH, D_MODEL), 1.0),
        "ctx": nrm(ks[2], (BATCH, CTX_LEN, D_MODEL), 1.0),
        "c_ctx": nrm(ks[3], (D_MODEL,), 1.0),
        "w_mod": nrm(ks[4], (DEPTH, D_MODEL, N_MOD * D_MODEL), 0.02),
        "b_mod": nrm(ks[5], (DEPTH, N_MOD * D_MODEL), 0.02),
        "norm1_g": 1.0 + nrm(ks[6], (DEPTH, D_MODEL), 0.02),
        "norm2_g": 1.0 + nrm(ks[7], (DEPTH, D_MODEL), 0.02),
        "w_in": nrm(ks[8], (DEPTH, D_MODEL, D_IN), D_MODEL ** -0.5),
        "q_norm_g": 1.0 + nrm(ks[9], (DEPTH, HEAD_DIM), 0.02),
        "k_norm_g": 1.0 + nrm(ks[10], (DEPTH, HEAD_DIM), 0.02),
        "gm_norm_g": 1.0 + nrm(ks[11], (DEPTH, GM_GROUPS, GM_GROUP_DIM), 0.02),
        "gm_ws": nrm(ks[12], (DEPTH, GM_GROUPS, CHUNK, CHUNK), CHUNK ** -0.5),
        "gm_bs": 1.0 + nrm(ks[13], (DEPTH, GM_GROUPS, CHUNK), 0.02),
        "w_br_attn": nrm(ks[14], (DEPTH, ATTN_WIDTH, D_MODEL), ATTN_WIDTH ** -0.5),
        "w_br_gm": nrm(ks[15], (DEPTH, GM_WIDTH, D_MODEL), GM_WIDTH ** -0.5),
        "w_out": nrm(ks[16], (DEPTH, D_MODEL, D_MODEL), D_MODEL ** -0.5),
        "w_ff1": nrm(ks[17], (DEPTH, D_MODEL, D_FF), D_MODEL ** -0.5),
        "w_ff2": nrm(ks[18], (DEPTH, D_FF, D_MODEL), D_FF ** -0.5),
    }


def reference(x, c, ctx, c_ctx, w_mod, b_mod, norm1_g, norm2_g, w_in, q_norm_g, k_norm_g,
              gm_norm_g, gm_ws, gm_bs, w_br_attn, w_br_gm, w_out, w_ff1, w_ff2):
    n_tok = x.shape[1]
    cos, sin = axial_rope_tables(n_tok, x.dtype)
    q_scale = HEAD_DIM ** -0.5
    ctx_s = ctx
    for l in range(DEPTH):
        more_layers = l + 1 < DEPTH
        mod_x = jax.nn.silu(c) @ w_mod[l] + b_mod[l]
        sh1, sc1, g1, sh2, sc2, g2 = jnp.split(mod_x[:, None, :], N_MOD, axis=-1)
        mod_c = jax.nn.silu(c_ctx) @ w_mod[l] + b_mod[l]
        sh1c, sc1c, g1c, sh2c, sc2c, g2c = jnp.split(mod_c, N_MOD, axis=-1)

        h_c = modulate(ctx_s, norm1_g[l], sh1c, sc1c)
        n_cols = D_IN if more_layers else KV_COLS
        p_c = h_c @ w_in[l][:, :n_cols]
        k_c = rmsnorm(kv_heads(p_c[..., :K_W]), k_norm_g[l])
        v_c = kv_heads(p_c[..., K_W:KV_COLS])

        h_x = modulate(x, norm1_g[l], sh1, sc1)
        p_x = h_x @ w_in[l]
        k_x = apply_rope(rmsnorm(kv_heads(p_x[..., :K_W]), k_norm_g[l]), cos, sin)
        v_x = kv_heads(p_x[..., K_W:KV_COLS])
        q_x, u_x, vg_x, ga_x, gb_x = jnp.split(p_x[..., KV_COLS:], REST_SPLITS, axis=-1)
        q_x = apply_rope(rmsnorm(q_heads(q_x), q_norm_g[l]), cos, sin) * q_scale
        k_all = jnp.concatenate([k_c, k_x], axis=2)
        v_all = jnp.concatenate([v_c, v_x], axis=2)
        attn_x = merge_heads(attend_blocked(q_x, k_all, v_all))
        gm_x = gmlp_spatial(jax.nn.gelu(u_x), jax.nn.gelu(vg_x), gm_norm_g[l], gm_ws[l], gm_bs[l])
        x = x + g1 * branch_merge(attn_x, gm_x, ga_x, gb_x, w_br_attn[l], w_br_gm[l], w_out[l])

        h2 = modulate(x, norm2_g[l], sh2, sc2)
        x = x + g2 * sq_relu_mlp(h2, w_ff1[l], w_ff2[l])

        if more_layers:
            q_c, u_c, vg_c, ga_c, gb_c = jnp.split(p_c[..., KV_COLS:], REST_SPLITS, axis=-1)
            q_c = rmsnorm(q_heads(q_c), q_norm_g[l]) * q_scale
            attn_c = merge_heads(attend(q_c, k_c, v_c))
            gm_c = gmlp_spatial(jax.nn.gelu(u_c), jax.nn.gelu(vg_c), gm_norm_g[l], gm_ws[l], gm_bs[l])
            ctx_s = ctx_s + g1c * branch_merge(attn_c, gm_c, ga_c, gb_c, w_br_attn[l], w_br_gm[l], w_out[l])
            h2c = modulate(ctx_s, norm2_g[l], sh2c, sc2c)
            ctx_s = ctx_s + g2c * sq_relu_mlp(h2c, w_ff1[l], w_ff2[l])
    return x
```

```python
import numpy as np
import concourse.bass as bass
import concourse.mybir as mybir
from concourse.bass_utils import run_bass_kernel_spmd

F32 = mybir.dt.float32
BF16 = mybir.dt.bfloat16
AF = mybir.ActivationFunctionType
ALU = mybir.AluOpType
AX = mybir.AxisListType

D = 1024
NCTX_T = 2
NOWN_T = 32
NT = 66
NQB = 8
EPS = 1e-6
NB = 5
N_UNITS_QB = 27


class Tracker:
    def __init__(self):
        self.ops = []
        self.last_w = {}
        self.readers = {}
        self.dcount = {}

    def add(self, eng, fn, reads=(), writes=(), dsem=None):
        idx = len(self.ops)
        deps = set()
        if eng in ("act", "dve"):
            writes = list(writes) + [("pslk", r[1]) for r in reads if isinstance(r, tuple) and r[0] == "ps"]
        for r in reads:
            if r in self.last_w:
                deps.add(self.last_w[r])
        for w in writes:
            if w in self.last_w:
                deps.add(self.last_w[w])
            for rd in self.readers.get(w, ()):
                deps.add(rd)
        op = dict(eng=eng, fn=fn, deps=deps, dsem=dsem, marked=False, idx=idx, val=None, desc=(tuple(reads), tuple(writes)))
        if dsem is not None:
            self.dcount[dsem] = self.dcount.get(dsem, 0) + 16
            op["dval"] = self.dcount[dsem]
        self.ops.append(op)
        for r in reads:
            self.readers.setdefault(r, []).append(idx)
        for w in writes:
            self.last_w[w] = idx
            self.readers[w] = []
        return idx

    def finalize(self):
        ops = self.ops
        for op in ops:
            red = {}
            for d in op["deps"]:
                dop = ops[d]
                if dop["dsem"] is not None:
                    key = ("d", dop["dsem"])
                    if key not in red or ops[red[key]]["dval"] < dop["dval"]:
                        red[key] = d
                else:
                    if dop["eng"] == "pe" and op["eng"] == "pe" and op["dsem"] is None:
                        continue
                    key = ("e", dop["eng"])
                    if key not in red or red[key] < d:
                        red[key] = d
            op["rdeps"] = list(red.values())
            for d in op["rdeps"]:
                if ops[d]["dsem"] is None:
                    ops[d]["marked"] = True
        cnt = {}
        for op in ops:
            if op["dsem"] is None and op["marked"]:
                cnt[op["eng"]] = cnt.get(op["eng"], 0) + 1
                op["val"] = cnt[op["eng"]]

    def trace(self, engname):
        waited = {}
        out = []
        for op in self.ops:
            if op["eng"] != engname:
                continue
            ws = []
            for d in op["rdeps"]:
                dop = self.ops[d]
                if dop["dsem"] is not None:
                    val = self.dcount[dop["dsem"]] if dop["dsem"] in ("const", "cast", "gb") else dop["dval"]
                    key = ("d", dop["dsem"])
                else:
                    val, key = dop["val"], ("e", dop["eng"])
                if waited.get(key, 0) < val:
                    ws.append((key[1], val))
                    waited[key] = val
            inc = (op["dsem"], op.get("dval")) if op["dsem"] else ((engname, op["val"]) if op["marked"] else None)
            out.append((op["idx"], ws, op["desc"], inc))
        return out

    def emit(self, engname, engobj, esems, dsems):
        waited = {}
        for op in self.ops:
            if op["eng"] != engname:
                continue
            for d in op["rdeps"]:
                dop = self.ops[d]
                if dop["dsem"] is not None:
                    val = self.dcount[dop["dsem"]] if dop["dsem"] in ("const", "cast", "gb") else dop["dval"]
                    sem, key = dsems[dop["dsem"]], ("d", dop["dsem"])
                else:
                    sem, val, key = esems[dop["eng"]], dop["val"], ("e", dop["eng"])
                if waited.get(key, 0) < val:
                    engobj.wait_ge(sem, val)
                    waited[key] = val
            ins = op["fn"](engobj)
            if op["dsem"] is not None:
                ins.then_inc(dsems[op["dsem"]], 16)
            elif op["marked"]:
                ins.then_inc(esems[engname], 1)


def build_program(stage=3, nqb=NQB, skip=()):
    nc = bass.Bass("TRN2", target_bir_lowering=False)
    T = Tracker()
    dbg = {}

    def din(name, shape, dt=F32):
        return nc.dram_tensor(name, list(shape), dt, kind="ExternalInput").ap()

    xin = din("xin", [NT * 128, D])
    rope = din("rope", [NT * 128, 128])
    cT_d = din("cT", [128, 8, 2])
    wmod_d = din("w_mod", [D, 6 * D])
    bmodT_d = din("b_modT", [128, 48])
    bmodg_d = din("b_modg", [1, 2048])
    n1g_d = din("n1g", [128, 8])
    n2g_d = din("n2g", [128, 8])
    win_d = din("w_in_p", [D, 3840])
    gq_d = din("gq_bc", [128, 512])
    gk_d = din("gk_bc", [128, 128])
    gmg_d = din("gmg_bc", [128, 512])
    wsT_d = din("wsT", [128, 8, 128])
    bsT_d = din("bsT", [128, 4, 128])
    bra_d = din("w_bra_p", [512, D])
    brg_d = din("w_brg", [512, D])
    wout_d = din("w_out", [D, D])
    ff1_d = din("w_ff1", [D, 4 * D])
    ff2_d = din("w_ff2", [4 * D, D])
    ident_d = din("ident", [128, 128])
    out_d = nc.dram_tensor("out", [NOWN_T * 128, D], F32, kind="ExternalOutput").ap()
    gscr = nc.dram_tensor("gscr", [1, 2048], F32, kind="Internal").ap()
    wsc = nc.dram_tensor("wscratch", [N_UNITS_QB, 128, 4096], BF16, kind="Internal").ap()

    import contextlib
    es = contextlib.ExitStack()

    def sb(name, shape, dt):
        return es.enter_context(nc.sbuf_tensor(name, list(shape), dt))

    with es:
        KT = sb("KT", [128, NT * 128], BF16)
        Vaug = sb("Vaug", [128, NT, 192], BF16)
        xbuf = [sb(f"xbuf{i}", [128, 4, D], F32) for i in range(2)]
        ropeb = [sb(f"ropeb{i}", [128, 4, 128], F32) for i in range(2)]
        xn = [sb(f"xn{i}", [128, D], BF16) for i in range(2)]
        hT = sb("hT", [128, 8, 512], BF16)
        big = sb("big", [128, 32, 512], BF16)
        ringT = sb("ring", [128, NB * 4096], BF16)
        ring = [ringT[:, i * 4096:(i + 1) * 4096] for i in range(NB)]
        gbc = sb("gbc", [128, 2048], F32)
        scr = [sb(f"scr{i}", [128, 512], F32) for i in range(6)]
        rc = sb("rc", [128, 1024], F32)
        sqj = [sb(f"sqj{i}", [128, D], BF16) for i in range(2)]
        Wkv = sb("Wkv", [128, 8, 256], BF16)
        identb = sb("identb", [128, 128], BF16)
        gq = sb("gq", [128, 512], F32)
        gk = sb("gk", [128, 128], F32)
        gmg = sb("gmg", [128, 512], F32)
        wsT = sb("wsTb", [128, 8, 128], BF16)
        bsT = sb("bsTs", [128, 4, 128], F32)
        cT = sb("cTs", [128, 8, 2], F32)
        scT = sb("scT", [128, 8, 2], F32)
        bmodT = sb("bmodTs", [128, 48], F32)
        n1g = sb("n1gs", [128, 8], F32)
        n2g = sb("n2gs", [128, 8], F32)
        modT = sb("modT", [128, 48, 2], F32)
        a1 = sb("a1", [128, 8, 2], F32)
        a2 = sb("a2", [128, 8, 2], F32)
        ones1 = sb("ones1", [1, 128], F32)
        ss = sb("ss", [128, 8], F32)
        rs = sb("rs", [128, 8], F32)
        rstd = sb("rstd", [128, 8], F32)
        hs = sb("hs", [128, 8], F32)
        hl = sb("hl", [128, 8], F32)
        hr = sb("hr", [128, 8], F32)
        ps = es.enter_context(nc.psum_tensor("ps", [128, 4096], F32))
        grow = big[0:1, 0:8, :].rearrange("p a b -> p (a b)").bitcast(F32)
        bmodg = big[0:1, 8:16, :].rearrange("p a b -> p (a b)").bitcast(F32)

        esems = {e: es.enter_context(nc.semaphore("sem_" + e)) for e in ["pe", "act", "dve", "pool", "sp"]}
        dnames = [f"c{i}" for i in range(6)] + ["dbg", "gb", "gb2", "const", "cast", "x0", "x1", "r0", "r1", "o0", "o1", "wm0", "wm1", "wm2"] + [f"w{i}" for i in range(NB)]
        dsems = {d: es.enter_context(nc.semaphore("ds_" + d)) for d in dnames}
        block = es.enter_context(nc.Block())

        def bank(b):
            return ps[:, b * 512:(b + 1) * 512]

        def mm(out, lhsT, rhs, start, stop, reads, writes, sgc=False):
            T.add("pe", lambda e: e.matmul(out, lhsT=lhsT, rhs=rhs, start=start, stop=stop, skip_group_check=sgc), reads, writes)

        def tr(out, in_, reads, writes):
            T.add("pe", lambda e: e.transpose(out, in_, identb[:, :]), list(reads) + ["identb"], writes)

        def act(out, in_, func, reads, writes, scale=None, bias=None, accum=None):
            kw = {}
            if scale is not None:
                kw["scale"] = scale
            if bias is not None:
                kw["bias"] = bias
            if accum is not None:
                kw["accum_out"] = accum
            T.add("act", lambda e: e.activation(out=out, in_=in_, func=func, **kw), reads, writes)

        def tt(out, in0, in1, op, reads, writes, eng="dve"):
            T.add(eng, lambda e: e.tensor_tensor(out=out, in0=in0, in1=in1, op=op), reads, writes)

        def ts(out, in0, s1, s2, op0, op1, reads, writes, eng="dve"):
            if op1 is None:
                T.add(eng, lambda e: e.tensor_scalar(out=out, in0=in0, scalar1=s1, scalar2=None, op0=op0), reads, writes)
            else:
                T.add(eng, lambda e: e.tensor_scalar(out=out, in0=in0, scalar1=s1, scalar2=s2, op0=op0, op1=op1), reads, writes)

        def stt(out, in0, scalar, in1, op0, op1, reads, writes):
            T.add("dve", lambda e: e.scalar_tensor_tensor(out=out, in0=in0, scalar=scalar, in1=in1, op0=op0, op1=op1), reads, writes)

        def recip(out, in_, reads, writes):
            T.add("dve", lambda e: e.reciprocal(out=out, in_=in_), reads, writes)

        def cp(out, in_, reads, writes, eng="dve"):
            T.add(eng, lambda e: e.tensor_copy(out=out, in_=in_), reads, writes)

        def dma(q, out, in_, reads, writes, dsem):
            T.add(q, lambda e: e.dma_start(out=out, in_=in_), reads, writes, dsem=dsem)

        def memset(ap, val, writes, eng="pool"):
            T.add(eng, lambda e: e.memset(ap, val), (), writes)

        for (dst, src, nm) in [(cT[:], cT_d, "cT"), (bmodT[:], bmodT_d, "bmodT"), (bmodg[:], bmodg_d, "bmodg"),
                               (n1g[:], n1g_d, "n1g"), (n2g[:], n2g_d, "n2g"), (gq[:], gq_d, "gq"), (gk[:], gk_d, "gk"),
                               (gmg[:], gmg_d, "gmg"), (bsT[:], bsT_d, "bsT")]:
            dma("sp", dst, src, (), [nm], "const")
        dma("pool", identb[:], ident_d, (), ["identb"], "cast")
        dma("pool", Wkv[:], win_d[:, 0:256].rearrange("(c p) n -> p c n", p=128), (), ["Wkv"], "cast")
        dma("pool", wsT[:], wsT_d, (), ["wsT"], "cast")
        memset(Vaug[:, :, 64:128], 1.0, [("Vaug", t) for t in range(NT)])
        memset(ones1[:], 1.0, ["ones1"])

        def wsrc_kn(w, c0, ncols):
            return w[:, c0:c0 + ncols].rearrange("(c p) n -> p c n", p=128)

        unit_src = []
        unit_src.append((wsrc_kn(win_d, 256, 512), 8))
        unit_src.append((wsrc_kn(win_d, 768, 512), 8))
        unit_src.append((wsrc_kn(win_d, 1280, 512), 8))
        unit_src.append((wsrc_kn(win_d, 1792, 512), 8))
        unit_src.append((wsrc_kn(win_d, 2816, 512), 8))
        unit_src.append((bra_d.rearrange("(c p) n -> p c n", p=128), 4))
        unit_src.append((brg_d.rearrange("(c p) n -> p c n", p=128), 4))
        unit_src.append((wsrc_kn(win_d, 2304, 512), 8))
        unit_src.append((wsrc_kn(win_d, 3328, 512), 8))
        unit_src.append((wsrc_kn(wout_d, 0, 512), 8))
        unit_src.append((wsrc_kn(wout_d, 512, 512), 8))
        for j in range(8):
            unit_src.append((wsrc_kn(ff1_d, j * 512, 512), 8))
        for nh in range(2):
            for kg in range(4):
                src = ff2_d[kg * 1024:(kg + 1) * 1024, nh * 512:(nh + 1) * 512].rearrange("(c p) n -> p c n", p=128)
                unit_src.append((src, 8))
        assert len(unit_src) == N_UNITS_QB
        def cast_unit(u, extra_reads=()):
            src, nch = unit_src[u]
            dst = wsc[u].rearrange("p (c n) -> p c n", c=nch)
            dma("pool", dst, src, [("cslot", u % 6)] + list(extra_reads), [("wsc", u), ("cslot", u % 6)], f"c{u % 6}")

        N_EARLY = 11 if nqb else N_UNITS_QB
        for u in range(N_EARLY):
            if "cast" in skip:
                break
            cast_unit(u)

        act(scT[:], cT[:], AF.Silu, ["cT"], ["scT"])
        for c in range(8):
            s_ = c % NB
            pa = ring[s_].bitcast(F32)
            dma("sp", pa, wmod_d[c * 128:(c + 1) * 128, 0:2048], (), [("ring", s_)], f"w{s_}")
            for jj in range(16):
                mm(ps[:, 2 * jj:2 * jj + 2], pa[:, jj * 128:(jj + 1) * 128], scT[:, c, :], c == 0 and jj == 0, c == 7 and jj == 15,
                   [("ring", s_), "scT"], [("ps", 0)], sgc=True)
        tt(modT[:, 0:16, :], ps[:, 0:32].rearrange("p (j k) -> p j k", k=2),
           bmodT[:, 0:16].unsqueeze(2).broadcast_to([128, 16, 2]), ALU.add, [("ps", 0), "bmodT"], ["modTa"])
        stt(a1[:], modT[:, 8:16, :], 1.0, n1g[:, :].unsqueeze(2).broadcast_to([128, 8, 2]), ALU.add, ALU.mult, ["modTa", "n1g"], ["a1"])

        def mod_b_buf(c):
            p_ = c % 2
            return p_, ringT[:, (2 * p_) * 4096:(2 * p_ + 2) * 4096].bitcast(F32)

        def mod_b_dma(c):
            p_, pb = mod_b_buf(c)
            dma("sp", pb, wmod_d[c * 128:(c + 1) * 128, 2048:6144], (), [("ring", 2 * p_), ("ring", 2 * p_ + 1)], f"wm{p_}")

        def mod_b_mm(c, j0, j1):
            p_, pb = mod_b_buf(c)
            for jj in range(j0, j1):
                mm(ps[:, 7 * 512 + 2 * jj:7 * 512 + 2 * jj + 2], pb[:, jj * 128:(jj + 1) * 128], scT[:, c, :], c == 0 and jj == 0, c == 7 and jj == 31,
                   [("ring", 2 * p_), ("ring", 2 * p_ + 1), "scT"], [("ps", 7)], sgc=True)

        def mod_b_piece(c):
            mod_b_dma(c)
            mod_b_mm(c, 0, 32)

        def mod_b_finish():
            tt(modT[:, 16:48, :], ps[:, 7 * 512:7 * 512 + 64].rearrange("p (j k) -> p j k", k=2),
               bmodT[:, 16:48].unsqueeze(2).broadcast_to([128, 32, 2]), ALU.add, [("ps", 7), "bmodT"], ["modTb"])
            stt(a2[:], modT[:, 32:40, :], 1.0, n2g[:, :].unsqueeze(2).broadcast_to([128, 8, 2]), ALU.add, ALU.mult, ["modTb", "n2g"], ["a2"])
            for k_, j0 in enumerate((16, 40)):
                T.add("pool", lambda e, k_=k_, j0=j0: e.dma_start(out=gscr[0, k_ * 1024:(k_ + 1) * 1024].rearrange("(c p) -> p c", p=128),
                                                             in_=modT[:, j0:j0 + 8, 0], allow_slow_non_contiguous=True),
                      ["modTb"], [("gscr", k_)], dsem="gb")
            dma("pool", gbc[:], gscr[0, :].partition_broadcast(128), [("gscr", 0), ("gscr", 1)], ["gbc"], "gb2")

        cnt = {"xn": 0, "ev": 0, "sq": 0, "par": 0}

        def nm_stats(xb, xres, nt):
            par = cnt["par"] % 2
            cnt["par"] += 1
            po = par * 4
            for t in range(nt):
                kq = cnt["sq"] % 2
                cnt["sq"] += 1
                act(sqj[kq][:], xb[:, t, :], AF.Square, [xres], [("ss", par, t), ("sqj", kq)], accum=ss[:, po + t:po + t + 1])
            act(rs[:, po:po + nt], ss[:, po:po + nt], AF.Ln, [("ss", par, t) for t in range(nt)], [("rs", par)], scale=1.0 / D, bias=EPS)
            act(rstd[:, po:po + nt], rs[:, po:po + nt], AF.Exp, [("rs", par)], [("rstd", par)], scale=-0.5)
            return dict(xb=xb, xres=xres, nt=nt, par=par, po=po)

        def xn_tile(xb, xres, t, k, rs_ap, rs_key):
            if t % 2 == 0:
                ts(xn[k][:], xb[:, t, :], rs_ap, None, ALU.mult, None, [xres, rs_key], [("xn", k)])
            else:
                act(xn[k][:], xb[:, t, :], AF.Copy, [xres, rs_key], [("xn", k)], scale=rs_ap)
            for c in range(8):
                bk = c // 2
                o = bank(bk).bitcast(BF16)[:, (c % 2) * 512 + t * 128:(c % 2) * 512 + (t + 1) * 128]
                tr(o, xn[k][:, c * 128:(c + 1) * 128], [("xn", k)], [("ps", bk)])

        def nm_xn_tr(cx):
            xb, xres, nt, par, po = cx["xb"], cx["xres"], cx["nt"], cx["par"], cx["po"]
            for t in range(nt):
                k = cnt["xn"] % 2
                cnt["xn"] += 1
                xn_tile(xb, xres, t, k, rstd[:, po + t:po + t + 1], ("rstd", par))

        def nm_evac(cx, a_t, sh_off, col, hd=None):
            nt = cx["nt"]
            hdst, hkey = hd if hd is not None else (hT, "hT")
            for c in range(8):
                bk = c // 2
                src = bank(bk).bitcast(BF16)[:, (c % 2) * 512:(c % 2) * 512 + nt * 128]
                if c % 2 == 0:
                    act(hdst[:, c, 0:nt * 128], src, AF.Identity, [("ps", bk), a_t[1], a_t[2]], [(hkey, c)],
                        scale=a_t[0][:, c, col:col + 1], bias=modT[:, sh_off + c, col:col + 1])
                else:
                    ts(hdst[:, c, 0:nt * 128], src, a_t[0][:, c, col:col + 1], modT[:, sh_off + c, col:col + 1], ALU.mult, ALU.add,
                       [("ps", bk), a_t[1], a_t[2]], [(hkey, c)])

        def norm_mod_T(xb, xres, nt, a_t, sh_off, col, hd=None):
            cx = nm_stats(xb, xres, nt)
            nm_xn_tr(cx)
            nm_evac(cx, a_t, sh_off, col, hd)

        def head_rstd(src_sq, nh, res_in):
            T.add("dve", lambda e: e.tensor_reduce(out=hs[:, 0:nh], in_=src_sq.rearrange("p (h d) -> p h d", d=64), axis=AX.X, op=ALU.add),
                  [res_in], ["hs"])
            act(hl[:, 0:nh], hs[:, 0:nh], AF.Ln, ["hs"], ["hl"], scale=1.0 / 64, bias=EPS)
            act(hr[:, 0:nh], hl[:, 0:nh], AF.Exp, ["hl"], ["hr"], scale=-0.5)

        def norm_rope(psrc, psres, nh, gain, gres, rp, rpres, outb, outres):
            W = nh * 64
            s0, s1, s2, s3 = scr[0][:, 0:W], scr[1][:, 0:W], scr[2][:, 0:W], scr[3][:, 0:W]
            act(s0, psrc, AF.Square, [psres], [("scr", 0)])
            head_rstd(s0, nh, ("scr", 0))
            tt(s1, psrc, gain, ALU.mult, [psres, gres], [("scr", 1)])
            tt(s2.rearrange("p (h d) -> p h d", d=64), s1.rearrange("p (h d) -> p h d", d=64),
               hr[:, 0:nh].unsqueeze(2).broadcast_to([128, nh, 64]), ALU.mult, [("scr", 1), "hr"], [("scr", 2)])
            v2 = s2.rearrange("p (h d) -> p h d", d=64)
            tt(s3.rearrange("p (h d) -> p h d", d=64), v2, rp[:, 0:64].unsqueeze(1).broadcast_to([128, nh, 64]), ALU.mult,
               [("scr", 2), rpres], [("scr", 3)])
            v0 = s0.rearrange("p (h d) -> p h d", d=64)
            tt(v0[:, :, 0:32], v2[:, :, 32:64], rp[:, 64:96].unsqueeze(1).broadcast_to([128, nh, 32]), ALU.mult,
               [("scr", 2), rpres], [("scr", 0)])
            tt(v0[:, :, 32:64], v2[:, :, 0:32], rp[:, 96:128].unsqueeze(1).broadcast_to([128, nh, 32]), ALU.mult,
               [("scr", 2), rpres], [("scr", 0)])
            tt(outb, s3, s0, ALU.add, [("scr", 3), ("scr", 0)], [outres])

        def dump(name, ap, res, dt=F32):
            if "nodump" in skip:
                return
            d = nc.dram_tensor("dbg_" + name, list(ap.shape), F32, kind="ExternalOutput").ap()
            dbg[name] = d
            dma("pool", d, ap, res, [("dbgout", name)], "dbg")

        ld = {"n": 0}

        def load_xo(row0, nt, s):
            dma("sp", xbuf[s][:, 0:nt, :], xin[row0:row0 + nt * 128, :].rearrange("(t p) d -> p t d", p=128), (), [("xbuf", s)], f"x{s}")

        def load_rope(row0, nt, s):
            dma("sp", ropeb[s][:, 0:nt, :], rope[row0:row0 + nt * 128, :].rearrange("(t p) d -> p t d", p=128), (), [("ropeb", s)], f"r{s}")

        def load_x(row0, nt):
            s = ld["n"] % 2
            ld["n"] += 1
            load_xo(row0, nt, s)
            load_rope(row0, nt, s)
            return s

        supers = [(0, 2, 1)] + [(256 + i * 512, 4, 0) for i in range(16)]
        if stage == 0:
            supers = []
            for c in range(8):
                mod_b_piece(c)
            mod_b_finish()
            dump("modT", modT[:], ["modTa", "modTb"])
            dump("gbc", gbc[:], ["gbc"])
            dump("a1", a1[:], ["a1"])
        if stage == 1:
            import os
            supers = supers[:int(os.environ.get("NSUP", "3"))]
        krb = big[:, 24:26, :].rearrange("p a b -> p (a b)")
        nsup = len(supers)
        hbufs = [(hT, "hT"), (big[:, 0:8, :], "big")]
        a1t = (a1, "a1", "modTa")

        def a_kv(si):
            row0, nt, col = supers[si]
            hA, hAk = hbufs[si % 2]
            for t in range(nt):
                bk = 4 + t // 2
                for c in range(8):
                    mm(ps[:, bk * 512 + (t % 2) * 256: bk * 512 + (t % 2) * 256 + 256], hA[:, c, t * 128:(t + 1) * 128], Wkv[:, c, :],
                       c == 0, c == 7, [(hAk, c), "Wkv"], [("ps", bk)])
                if 1 <= si <= 8:
                    mod_b_mm(si - 1, 8 * t, 8 * t + 8)

        def a_post(si):
            row0, nt, col = supers[si]
            slot = si % 2
            tile0 = row0 // 128
            W = nt * 128
            kvv = ps[:, 4 * 512:4 * 512 + nt * 256].rearrange("p (t n) -> p t n", n=256)
            kview = kvv[:, :, 0:128]
            kvb = [("ps", 4)] + ([("ps", 5)] if nt > 2 else [])
            s0, s1, s2, s3 = scr[0][:, 0:W], scr[1][:, 0:W], scr[2][:, 0:W], scr[3][:, 0:W]
            tv = lambda a: a.rearrange("p (t n) -> p t n", n=128)
            hv = lambda a: a.rearrange("p (h d) -> p h d", d=64)
            qv = lambda a: a.rearrange("p (t h d) -> p t h d", h=2, d=64)
            rp = ropeb[slot]
            rpk = ("ropeb", slot)
            act(tv(s0), kview, AF.Square, kvb, [("scr", 0)])
            head_rstd(s0, 2 * nt, ("scr", 0))
            tt(tv(s1), kview, gk[:, :].unsqueeze(1).broadcast_to([128, nt, 128]), ALU.mult, kvb + ["gk"], [("scr", 1)])
            tt(hv(s2), hv(s1), hr[:, 0:2 * nt].unsqueeze(2).broadcast_to([128, 2 * nt, 64]), ALU.mult, [("scr", 1), "hr"], [("scr", 2)])
            tt(qv(s3), qv(s2), rp[:, 0:nt, 0:64].unsqueeze(2).broadcast_to([128, nt, 2, 64]), ALU.mult, [("scr", 2), rpk], [("scr", 3)])
            tt(qv(s0)[:, :, :, 0:32], qv(s2)[:, :, :, 32:64], rp[:, 0:nt, 64:96].unsqueeze(2).broadcast_to([128, nt, 2, 32]), ALU.mult,
               [("scr", 2), rpk], [("scr", 0)])
            tt(qv(s0)[:, :, :, 32:64], qv(s2)[:, :, :, 0:32], rp[:, 0:nt, 96:128].unsqueeze(2).broadcast_to([128, nt, 2, 32]), ALU.mult,
               [("scr", 2), rpk], [("scr", 0)])
            tt(krb[:, 0:W], s3, s0, ALU.add, [("scr", 3), ("scr", 0)], [("big", 24)])
            for t in range(nt):
                tr(bank(6).bitcast(BF16)[:, t * 128:(t + 1) * 128], krb[:, t * 128:(t + 1) * 128], [("big", 24)], [("ps", 6)])
            act(Vaug[:, tile0:tile0 + nt, 0:64], kvv[:, :, 128:192], AF.Copy, kvb, [("Vaug", tile0 + t) for t in range(nt)])
            act(Vaug[:, tile0:tile0 + nt, 128:192], kvv[:, :, 192:256], AF.Copy, kvb, [("Vaug", tile0 + t) for t in range(nt)])
            cp(KT[:, tile0 * 128:(tile0 + nt) * 128], bank(6).bitcast(BF16)[:, 0:nt * 128], [("ps", 6)], [("KT", si)])

        actx = {}
        if nsup:
            for k_ in range(min(2, nsup)):
                load_xo(supers[k_][0], supers[k_][1], k_ % 2)
                load_rope(supers[k_][0], supers[k_][1], k_ % 2)
            ld["n"] = 0
            actx[0] = nm_stats(xbuf[0], ("xbuf", 0), supers[0][1])
            nm_xn_tr(actx[0])
            if nsup > 2:
                load_xo(supers[2][0], supers[2][1], 0)
            nm_evac(actx[0], a1t, 0, supers[0][2], hbufs[0])
        for k_ in range(nsup):
            if k_ < 8:
                mod_b_dma(k_)
            n_ = k_ + 1
            if n_ < nsup:
                actx[n_] = nm_stats(xbuf[n_ % 2], ("xbuf", n_ % 2), supers[n_][1])
            a_kv(k_)
            if n_ < nsup:
                nm_xn_tr(actx[n_])
                if k_ + 3 < nsup:
                    load_xo(supers[k_ + 3][0], supers[k_ + 3][1], (k_ + 3) % 2)
            a_post(k_)
            if k_ + 2 < nsup:
                load_rope(supers[k_ + 2][0], supers[k_ + 2][1], k_ % 2)
            if n_ < nsup:
                nm_evac(actx[n_], a1t, 0, supers[n_][2], hbufs[n_ % 2])

        ld["n"] = 1
        slot = load_x(256, 4) if (nqb and stage > 1) else None
        if stage >= 1:
            if len(supers) < 9:
                for c in range(max(len(supers) - 1, 0), 8):
                    if c >= len(supers):
                        mod_b_dma(c)
                    mod_b_mm(c, 0, 32)
            mod_b_finish()
        if stage == 1:
            dump("KT", KT[:, 0:1280], [("KT", i) for i in range(3)], BF16)
            dump("Vaug", Vaug[:, 0:10, :], [("Vaug", i) for i in range(10)], BF16)
            dump("hT", hT[:], [("hT", c) for c in range(8)], BF16)
        if stage <= 1:
            nqb = 0
        wctr = {"n": 0}

        def load_unit(u):
            s = wctr["n"] % NB
            wctr["n"] += 1
            dma("sp", ring[s][:, :], wsc[u], [("wsc", u)], [("ring", s)], f"w{s}")
            return s

        def R(s):
            return ("ring", s)

        def unit8(s):
            return ring[s][:, :].rearrange("p (c n) -> p c n", c=8)

        def unit4(s):
            return ring[s][:, :].rearrange("p (c n) -> p c n", c=4)

        yT = [big[:, c, :] for c in range(8)]
        uT = [big[:, 8 + j, :] for j in range(4)]
        gmT = [big[:, 12 + j, :] for j in range(4)]
        attnT = [big[:, 16 + j, :] for j in range(4)]
        QT = [big[:, 20 + j, :] for j in range(4)]
        vnb = [big[:, 24 + t, :] for t in range(4)]
        PT = [big[:, 28:30, :].rearrange("p a b -> p (a b)"), big[:, 30:32, :].rearrange("p a b -> p (a b)"),
              big[:, 26:28, :].rearrange("p a b -> p (a b)")]
        PTK = [[("big", 28), ("big", 29)], [("big", 30), ("big", 31)], [("big", 26), ("big", 27)]]

        for qb in range(nqb):
            row0 = 256 + qb * 512
            xb = xbuf[slot]
            xres = ("xbuf", slot)
            rpb = ropeb[slot]
            norm_mod_T(xb, xres, 4, (a1, "a1", "modTa"), 0, 0)
            nslot = None
            su = load_unit(0)
            for t in range(4):
                for c in range(8):
                    mm(bank(4 + t), hT[:, c, t * 128:(t + 1) * 128], unit8(su)[:, c, :], c == 0, c == 7, [("hT", c), R(su)], [("ps", 4 + t)])
            sv = load_unit(1)
            for t in range(4):
                for c in range(8):
                    mm(bank(t), hT[:, c, t * 128:(t + 1) * 128], unit8(sv)[:, c, :], c == 0, c == 7, [("hT", c), R(sv)], [("ps", t)])
            qrb = scr[4][:, :].bitcast(BF16)[:, 0:512]
            for t in range(4):
                norm_rope(bank(4 + t), ("ps", 4 + t), 8, gq[:, :], "gq", rpb[:, t, :], ("ropeb", slot), qrb, ("scr", 4))
                for g in range(4):
                    tr(bank(4 + t).bitcast(BF16)[:, g * 128:(g + 1) * 128], qrb[:, g * 128:(g + 1) * 128], [("scr", 4)], [("ps", 4 + t)])
                cp(big[:, 20:24, t * 128:(t + 1) * 128], bank(4 + t).bitcast(BF16)[:, 0:512].rearrange("p (g q) -> p g q", q=128),
                   [("ps", 4 + t)], [("big", 20 + g) for g in range(4)])
            for t in range(4):
                gv = scr[1][:, :]
                act(gv, bank(t), AF.Gelu_apprx_tanh, [("ps", t)], [("scr", 1)])
                act(scr[0][:, :], gv, AF.Square, [("scr", 1)], [("scr", 0)])
                head_rstd(scr[0][:, :], 8, ("scr", 0))
                tt(scr[2][:, :], gv, gmg[:, :], ALU.mult, [("scr", 1), "gmg"], [("scr", 2)])
                tt(vnb[t].rearrange("p (h d) -> p h d", d=64), scr[2][:, :].rearrange("p (h d) -> p h d", d=64),
                   hr[:, 0:8].unsqueeze(2).broadcast_to([128, 8, 64]), ALU.mult, [("scr", 2), "hr"], [("big", 24 + t)])
            for t in range(4):
                for j in range(4):
                    for gg in range(2):
                        g = 2 * j + gg
                        mm(ps[gg * 64:(gg + 1) * 64, j * 512 + t * 128: j * 512 + (t + 1) * 128], vnb[t][:, g * 64:(g + 1) * 64], wsT[:, g, :],
                           True, True, [("big", 24 + t), "wsT"], [("ps", j)])
            suu = load_unit(2)
            for j in range(4):
                for c in range(8):
                    mm(bank(4 + j), unit8(suu)[:, c, j * 128:(j + 1) * 128], hT[:, c, :], c == 0, c == 7, [("hT", c), R(suu)], [("ps", 4 + j)])
                act(uT[j], bank(4 + j), AF.Gelu_apprx_tanh, [("ps", 4 + j)], [("big", 8 + j)])
                tt(scr[5][:, :].rearrange("p (t q) -> p t q", q=128), bank(j).rearrange("p (t q) -> p t q", q=128),
                   bsT[:, j, :].unsqueeze(1).broadcast_to([128, 4, 128]), ALU.add, [("ps", j), "bsT"], [("scr", 5)])
                tt(gmT[j], scr[5][:, :], uT[j], ALU.mult, [("scr", 5), ("big", 8 + j)], [("big", 12 + j)])

            steps = [(g, kb) for g in range(4) for kb in range(NT)]

            def ksup(kb):
                return 0 if kb < 2 else 1 + (kb - 2) // 4

            def qk(i):
                g, kb = steps[i]
                sbk = (i % 2) * 2
                mm(bank(sbk), KT[0:64, kb * 128:(kb + 1) * 128], QT[g][0:64, :], True, True, [("KT", ksup(kb)), ("big", 20 + g)], [("ps", sbk)])
                mm(bank(sbk + 1), KT[64:128, kb * 128:(kb + 1) * 128], QT[g][64:128, :], True, True, [("KT", ksup(kb)), ("big", 20 + g)], [("ps", sbk + 1)])
                pace = [("pace", i)] if (qb == 0 and i % 12 == 0) else []
                act(PT[i % 3], ps[:, sbk * 512:sbk * 512 + 1024], AF.Exp, [("ps", sbk), ("ps", sbk + 1)], PTK[i % 3] + pace, scale=0.125)
                if pace and N_EARLY + i // 12 < N_UNITS_QB:
                    cast_unit(N_EARLY + i // 12, pace)

            def pv(i):
                g, kb = steps[i]
                oa = 4 + 2 * (g % 2)
                mm(bank(oa), Vaug[:, kb, 0:128], PT[i % 3][:, 0:512], kb == 0, kb == NT - 1, [("Vaug", kb)] + PTK[i % 3], [("ps", oa)])
                mm(bank(oa + 1), Vaug[:, kb, 64:192], PT[i % 3][:, 512:1024], kb == 0, kb == NT - 1, [("Vaug", kb)] + PTK[i % 3], [("ps", oa + 1)])
                if kb == NT - 1:
                    recip(rc[64:128, 0:512], bank(oa)[64:128, :], [("ps", oa)], ["rc"])
                    recip(rc[0:64, 512:1024], bank(oa + 1)[0:64, :], [("ps", oa + 1)], ["rc"])
                    tt(attnT[g][0:64, :], bank(oa)[0:64, :], rc[64:128, 0:512], ALU.mult, [("ps", oa), "rc"], [("big", 16 + g)])
                    tt(attnT[g][64:128, :], bank(oa + 1)[64:128, :], rc[0:64, 512:1024], ALU.mult, [("ps", oa + 1), "rc"], [("big", 16 + g)])

            qk(0)
            qk(1)
            for i in range(len(steps)):
                if i + 2 < len(steps):
                    qk(i + 2)
                pv(i)

            if qb + 1 < nqb:
                nslot = load_x(row0 + 512, 4)

            sga = [load_unit(3), None]
            sgb = [load_unit(4), None]
            sbra = load_unit(5)
            sbrg = load_unit(6)
            for m in range(8):
                if m == 4:
                    sga[1] = load_unit(7)
                    sgb[1] = load_unit(8)
                h = m // 4
                b0 = (m % 2) * 4
                for c in range(8):
                    mm(bank(b0), unit8(sga[h])[:, c, (m % 4) * 128:(m % 4 + 1) * 128], hT[:, c, :], c == 0, c == 7, [("hT", c), R(sga[h])], [("ps", b0)])
                for c in range(8):
                    mm(bank(b0 + 1), unit8(sgb[h])[:, c, (m % 4) * 128:(m % 4 + 1) * 128], hT[:, c, :], c == 0, c == 7, [("hT", c), R(sgb[h])], [("ps", b0 + 1)])
                for c in range(4):
                    mm(bank(b0 + 2), unit4(sbra)[:, c, m * 128:(m + 1) * 128], attnT[c], c == 0, c == 3, [("big", 16 + c), R(sbra)], [("ps", b0 + 2)])
                for c in range(4):
                    mm(bank(b0 + 3), unit4(sbrg)[:, c, m * 128:(m + 1) * 128], gmT[c], c == 0, c == 3, [("big", 12 + c), R(sbrg)], [("ps", b0 + 3)])
                act(scr[0][:, :], bank(b0), AF.Sigmoid, [("ps", b0)], [("scr", 0)])
                act(scr[1][:, :], bank(b0 + 1), AF.Sigmoid, [("ps", b0 + 1)], [("scr", 1)])
                tt(scr[2][:, :], bank(b0 + 2), scr[0][:, :], ALU.mult, [("ps", b0 + 2), ("scr", 0)], [("scr", 2)])
                tt(scr[3][:, :], bank(b0 + 3), scr[1][:, :], ALU.mult, [("ps", b0 + 3), ("scr", 1)], [("scr", 3)])
                tt(yT[m], scr[2][:, :], scr[3][:, :], ALU.add, [("scr", 2), ("scr", 3)], [("big", m)], eng="pool")

            so = [load_unit(9), load_unit(10)]
            par = cnt["par"] % 2
            cnt["par"] += 1
            po = par * 4
            k8 = 0
            for t in range(4):
                for nh in range(2):
                    bk = 4 + k8 % 4
                    k8 += 1
                    for c in range(8):
                        mm(bank(bk), yT[c][:, t * 128:(t + 1) * 128], unit8(so[nh])[:, c, :], c == 0, c == 7, [("big", c), R(so[nh])], [("ps", bk)])
                    sk = 4 + (k8 % 2)
                    tt(scr[sk][:, :], bank(bk), gbc[:, nh * 512:(nh + 1) * 512], ALU.mult, [("ps", bk), "gbc"], [("scr", sk)])
                    tt(xb[:, t, nh * 512:(nh + 1) * 512], scr[sk][:, :], xb[:, t, nh * 512:(nh + 1) * 512], ALU.add, [("scr", sk), xres], [xres, ("x1t", t)], eng="pool")
                kq = cnt["sq"] % 2
                cnt["sq"] += 1
                cs = slice(po + t, po + t + 1)
                act(sqj[kq][:], xb[:, t, :], AF.Square, [xres, ("x1t", t)], [("ss", par, t), ("sqj", kq)], accum=ss[:, cs])
                act(rs[:, cs], ss[:, cs], AF.Ln, [("ss", par, t)], [("rs", par), ("rs", par, t)], scale=1.0 / D, bias=EPS)
                act(rstd[:, cs], rs[:, cs], AF.Exp, [("rs", par, t)], [("rstd", par), ("rstd", par, t)], scale=-0.5)
                k = cnt["xn"] % 2
                cnt["xn"] += 1
                xn_tile(xb, ("x1t", t), t, k, rstd[:, cs], ("rstd", par, t))
            nm_evac(dict(nt=4), (a2, "a2", "modTb"), 24, 0)

            for j in range(32):
                if j % 4 == 0:
                    sf = load_unit(11 + j // 4)
                bk = j % 8
                for c in range(8):
                    mm(bank(bk), unit8(sf)[:, c, (j % 4) * 128:(j % 4 + 1) * 128], hT[:, c, :], c == 0, c == 7, [("hT", c), R(sf)], [("ps", bk)])
                sk = j % 4
                act(scr[sk][:, :], bank(bk), AF.Relu, [("ps", bk)], [("scr", sk)])
                tt(big[:, j, :], scr[sk][:, :], scr[sk][:, :], ALU.mult, [("scr", sk)], [("big", j)])

            for nh in range(2):
                for kg in range(4):
                    s2 = load_unit(19 + nh * 4 + kg)
                    for t in range(4):
                        bk = nh * 4 + t
                        for c in range(8):
                            mm(bank(bk), big[:, kg * 8 + c, t * 128:(t + 1) * 128], unit8(s2)[:, c, :], kg == 0 and c == 0, kg == 3 and c == 7,
                               [("big", kg * 8 + c), R(s2)], [("ps", bk)])
                for t in range(4):
                    bk = nh * 4 + t
                    sk = 4 + (t % 2)
                    tt(scr[sk][:, :], bank(bk), gbc[:, 1024 + nh * 512:1024 + (nh + 1) * 512], ALU.mult, [("ps", bk), "gbc"], [("scr", sk)])
                    tt(xb[:, t, nh * 512:(nh + 1) * 512], scr[sk][:, :], xb[:, t, nh * 512:(nh + 1) * 512], ALU.add, [("scr", sk), xres], [xres], eng="pool")
            dma("sp", out_d[qb * 512:(qb + 1) * 512, :].rearrange("(t p) d -> p t d", p=128), xb[:, :, :], [xres], [("out", qb)], f"o{slot}")
            slot = nslot

        T.add("sp", lambda e: e.nop(), [("out", q) for q in range(nqb)] + [("dbgout", n) for n in dbg], [])

        T.finalize()
        nc._tracker = T

        @block.sync
        def _(e):
            T.emit("sp", e, esems, dsems)

        @block.tensor
        def _(e):
            T.emit("pe", e, esems, dsems)

        @block.scalar
        def _(e):
            T.emit("act", e, esems, dsems)

        @block.vector
        def _(e):
            T.emit("dve", e, esems, dsems)

        @block.gpsimd
        def _(e):
            T.emit("pool", e, esems, dsems)

    return nc


_CACHE = {}


def _rope_table(tok_idx):
    n = tok_idx.shape[0]
    t = np.maximum(tok_idx, 0)
    row = (t // 64).astype(np.float32)
    colp = (t % 64).astype(np.float32)
    inv = (np.float32(10000.0) ** (-np.arange(0, 32, 2, dtype=np.float32) / np.float32(32))).astype(np.float32)
    ang = np.concatenate([row[:, None] * inv[None, :], colp[:, None] * inv[None, :]], axis=-1).astype(np.float32)
    cos = np.cos(ang).astype(np.float32)
    sin = np.sin(ang).astype(np.float32)
    ident = tok_idx < 0
    cos[ident] = 1.0
    sin[ident] = 0.0
    return np.concatenate([cos, cos, -sin, sin], axis=1).astype(np.float32)


def kernel(x, c, ctx, c_ctx, w_mod, b_mod, norm1_g, norm2_g, w_in, q_norm_g, k_norm_g,
           gm_norm_g, gm_ws, gm_bs, w_br_attn, w_br_gm, w_out, w_ff1, w_ff2):
    f = lambda a: np.ascontiguousarray(np.asarray(a, dtype=np.float32))
    x, c, ctx, c_ctx = f(x), f(c), f(ctx), f(c_ctx)
    w_mod, b_mod, w_in = f(w_mod)[0], f(b_mod)[0], f(w_in)[0]
    n1, n2 = f(norm1_g)[0], f(norm2_g)[0]
    qg, kg, gmg = f(q_norm_g)[0], f(k_norm_g)[0], f(gm_norm_g)[0]
    ws, bs = f(gm_ws)[0], f(gm_bs)[0]
    bra, brg, wo, w1, w2 = f(w_br_attn)[0], f(w_br_gm)[0], f(w_out)[0], f(w_ff1)[0], f(w_ff2)[0]

    if "nc" not in _CACHE:
        _CACHE["nc"] = build_program()
    nc = _CACHE["nc"]

    qcols = 256 + np.array([kv * 256 + g * 64 + d for g in range(4) for kv in range(2) for d in range(64)])
    order = np.concatenate([np.arange(0, 256), qcols, np.arange(1280, 1792), np.arange(768, 1280), np.arange(1792, 3840)])
    w_in_p = np.ascontiguousarray(w_in[:, order])
    rows = np.array([kv * 256 + g * 64 + d for g in range(4) for kv in range(2) for d in range(64)])
    w_bra_p = np.ascontiguousarray(bra[rows, :])
    b_modT = np.ascontiguousarray(b_mod.reshape(48, 128).T)
    b_modg = np.ascontiguousarray(np.concatenate([b_mod[2048:3072], b_mod[5120:6144]])[None, :])
    n1g = np.ascontiguousarray(n1.reshape(8, 128).T)
    n2g = np.ascontiguousarray(n2.reshape(8, 128).T)
    gq_bc = np.ascontiguousarray(np.broadcast_to(np.tile(qg, 8)[None, :], (128, 512)))
    gk_bc = np.ascontiguousarray(np.broadcast_to(np.tile(kg, 2)[None, :], (128, 128)))
    gmg_bc = np.ascontiguousarray(np.broadcast_to(gmg.reshape(512)[None, :], (128, 512)))
    wsT = np.ascontiguousarray(ws.transpose(2, 0, 1))
    bsT = np.ascontiguousarray(np.broadcast_to(bs.reshape(4, 2, 1, 128), (4, 2, 64, 128)).transpose(1, 2, 0, 3).reshape(128, 4, 128))
    ident = np.eye(128, dtype=np.float32)

    in_maps = []
    for core in range(8):
        b, hf = core // 2, core % 2
        own = np.arange(hf * 4096, (hf + 1) * 4096)
        oth = np.arange((1 - hf) * 4096, (2 - hf) * 4096)
        xin = np.concatenate([ctx[b], x[b, own], x[b, oth]], axis=0)
        tok = np.concatenate([-np.ones(256, dtype=np.int64), own, oth])
        cT = np.ascontiguousarray(np.stack([c[b], c_ctx], axis=1).reshape(8, 128, 2).transpose(1, 0, 2))
        in_maps.append({
            "xin": np.ascontiguousarray(xin), "rope": _rope_table(tok), "cT": cT, "w_mod": w_mod, "b_modT": b_modT,
            "b_modg": b_modg, "n1g": n1g, "n2g": n2g, "w_in_p": w_in_p, "gq_bc": gq_bc, "gk_bc": gk_bc, "gmg_bc": gmg_bc,
            "wsT": wsT, "bsT": bsT, "w_bra_p": w_bra_p, "w_brg": brg, "w_out": wo, "w_ff1": w1, "w_ff2": w2, "ident": ident,
        })
    res = run_bass_kernel_spmd(nc, in_maps, core_ids=list(range(8)))
    out = np.empty((4, 8192, D), dtype=np.float32)
    for core in range(8):
        b, hf = core // 2, core % 2
        out[b, hf * 4096:(hf + 1) * 4096] = res.results[core]["out"]
    return out
```

```python
import numpy as np
import concourse.bass as bass
import concourse.mybir as mybir
from concourse.bass_utils import run_bass_kernel_spmd

F32 = mybir.dt.float32
BF16 = mybir.dt.bfloat16
AF = mybir.ActivationFunctionType
ALU = mybir.AluOpType
AX = mybir.AxisListType

D = 1024
NCTX_T = 2
NOWN_T = 32
NT = 66
NQB = 8
EPS = 1e-6
NB = 5
N_UNITS_QB = 27


class Tracker:
    def __init__(self):
        self.ops = []
        self.last_w = {}
        self.readers = {}
        self.dcount = {}

    def add(self, eng, fn, reads=(), writes=(), dsem=None):
        idx = len(self.ops)
        deps = set()
        if eng in ("act", "dve"):
            writes = list(writes) + [("pslk", r[1]) for r in reads if isinstance(r, tuple) and r[0] == "ps"]
        for r in reads:
            if r in self.last_w:
                deps.add(self.last_w[r])
        for w in writes:
            if w in self.last_w:
                deps.add(self.last_w[w])
            for rd in self.readers.get(w, ()):
                deps.add(rd)
        op = dict(eng=eng, fn=fn, deps=deps, dsem=dsem, marked=False, idx=idx, val=None, desc=(tuple(reads), tuple(writes)))
        if dsem is not None:
            self.dcount[dsem] = self.dcount.get(dsem, 0) + 16
            op["dval"] = self.dcount[dsem]
        self.ops.append(op)
        for r in reads:
            self.readers.setdefault(r, []).append(idx)
        for w in writes:
            self.last_w[w] = idx
            self.readers[w] = []
        return idx

    def finalize(self):
        ops = self.ops
        for op in ops:
            red = {}
            for d in op["deps"]:
                dop = ops[d]
                if dop["dsem"] is not None:
                    key = ("d", dop["dsem"])
                    if key not in red or ops[red[key]]["dval"] < dop["dval"]:
                        red[key] = d
                else:
                    if dop["eng"] == "pe" and op["eng"] == "pe" and op["dsem"] is None:
                        continue
                    key = ("e", dop["eng"])
                    if key not in red or red[key] < d:
                        red[key] = d
            op["rdeps"] = list(red.values())
            for d in op["rdeps"]:
                if ops[d]["dsem"] is None:
                    ops[d]["marked"] = True
        cnt = {}
        for op in ops:
            if op["dsem"] is None and op["marked"]:
                cnt[op["eng"]] = cnt.get(op["eng"], 0) + 1
                op["val"] = cnt[op["eng"]]

    def trace(self, engname):
        waited = {}
        out = []
        for op in self.ops:
            if op["eng"] != engname:
                continue
            ws = []
            for d in op["rdeps"]:
                dop = self.ops[d]
                if dop["dsem"] is not None:
                    val = self.dcount[dop["dsem"]] if dop["dsem"] in ("const", "cast", "gb") else dop["dval"]
                    key = ("d", dop["dsem"])
                else:
                    val, key = dop["val"], ("e", dop["eng"])
                if waited.get(key, 0) < val:
                    ws.append((key[1], val))
                    waited[key] = val
            inc = (op["dsem"], op.get("dval")) if op["dsem"] else ((engname, op["val"]) if op["marked"] else None)
            out.append((op["idx"], ws, op["desc"], inc))
        return out

    def emit(self, engname, engobj, esems, dsems):
        waited = {}
        for op in self.ops:
            if op["eng"] != engname:
                continue
            for d in op["rdeps"]:
                dop = self.ops[d]
                if dop["dsem"] is not None:
                    val = self.dcount[dop["dsem"]] if dop["dsem"] in ("const", "cast", "gb") else dop["dval"]
                    sem, key = dsems[dop["dsem"]], ("d", dop["dsem"])
                else:
                    sem, val, key = esems[dop["eng"]], dop["val"], ("e", dop["eng"])
                if waited.get(key, 0) < val:
                    engobj.wait_ge(sem, val)
                    waited[key] = val
            ins = op["fn"](engobj)
            if op["dsem"] is not None:
                ins.then_inc(dsems[op["dsem"]], 16)
            elif op["marked"]:
                ins.then_inc(esems[engname], 1)


def build_program(stage=3, nqb=NQB, skip=()):
    nc = bass.Bass("TRN2", target_bir_lowering=False)
    T = Tracker()
    dbg = {}

    def din(name, shape, dt=F32):
        return nc.dram_tensor(name, list(shape), dt, kind="ExternalInput").ap()

    xin = din("xin", [NT * 128, D])
    rope = din("rope", [NT * 128, 128])
    cT_d = din("cT", [128, 8, 2])
    wmod_d = din("w_mod", [D, 6 * D])
    bmodT_d = din("b_modT", [128, 48])
    bmodg_d = din("b_modg", [1, 2048])
    n1g_d = din("n1g", [128, 8])
    n2g_d = din("n2g", [128, 8])
    win_d = din("w_in_p", [D, 3840])
    gq_d = din("gq_bc", [128, 512])
    gk_d = din("gk_bc", [128, 128])
    gmg_d = din("gmg_bc", [128, 512])
    wsT_d = din("wsT", [128, 8, 128])
    bsT_d = din("bsT", [128, 4, 128])
    bra_d = din("w_bra_p", [512, D])
    brg_d = din("w_brg", [512, D])
    wout_d = din("w_out", [D, D])
    ff1_d = din("w_ff1", [D, 4 * D])
    ff2_d = din("w_ff2", [4 * D, D])
    ident_d = din("ident", [128, 128])
    out_d = nc.dram_tensor("out", [NOWN_T * 128, D], F32, kind="ExternalOutput").ap()
    gscr = nc.dram_tensor("gscr", [1, 2048], F32, kind="Internal").ap()
    wsc = nc.dram_tensor("wscratch", [N_UNITS_QB, 128, 4096], BF16, kind="Internal").ap()

    import contextlib
    es = contextlib.ExitStack()

    def sb(name, shape, dt):
        return es.enter_context(nc.sbuf_tensor(name, list(shape), dt))

    with es:
        KT = sb("KT", [128, NT * 128], BF16)
        Vaug = sb("Vaug", [128, NT, 192], BF16)
        xbuf = [sb(f"xbuf{i}", [128, 4, D], F32) for i in range(2)]
        ropeb = [sb(f"ropeb{i}", [128, 4, 128], F32) for i in range(2)]
        xn = [sb(f"xn{i}", [128, D], BF16) for i in range(2)]
        hT = sb("hT", [128, 8, 512], BF16)
        big = sb("big", [128, 32, 512], BF16)
        ringT = sb("ring", [128, NB * 4096], BF16)
        ring = [ringT[:, i * 4096:(i + 1) * 4096] for i in range(NB)]
        gbc = sb("gbc", [128, 2048], F32)
        scr = [sb(f"scr{i}", [128, 512], F32) for i in range(6)]
        rc = sb("rc", [128, 1024], F32)
        sqj = [sb(f"sqj{i}", [128, D], BF16) for i in range(2)]
        Wkv = sb("Wkv", [128, 8, 256], BF16)
        identb = sb("identb", [128, 128], BF16)
        gq = sb("gq", [128, 512], F32)
        gk = sb("gk", [128, 128], F32)
        gmg = sb("gmg", [128, 512], F32)
        wsT = sb("wsTb", [128, 8, 128], BF16)
        bsT = sb("bsTs", [128, 4, 128], F32)
        cT = sb("cTs", [128, 8, 2], F32)
        scT = sb("scT", [128, 8, 2], F32)
        bmodT = sb("bmodTs", [128, 48], F32)
        n1g = sb("n1gs", [128, 8], F32)
        n2g = sb("n2gs", [128, 8], F32)
        modT = sb("modT", [128, 48, 2], F32)
        a1 = sb("a1", [128, 8, 2], F32)
        a2 = sb("a2", [128, 8, 2], F32)
        ones1 = sb("ones1", [1, 128], F32)
        ss = sb("ss", [128, 8], F32)
        rs = sb("rs", [128, 8], F32)
        rstd = sb("rstd", [128, 8], F32)
        hs = sb("hs", [128, 8], F32)
        hl = sb("hl", [128, 8], F32)
        hr = sb("hr", [128, 8], F32)
        ps = es.enter_context(nc.psum_tensor("ps", [128, 4096], F32))
        grow = big[0:1, 0:8, :].rearrange("p a b -> p (a b)").bitcast(F32)
        bmodg = big[0:1, 8:16, :].rearrange("p a b -> p (a b)").bitcast(F32)

        esems = {e: es.enter_context(nc.semaphore("sem_" + e)) for e in ["pe", "act", "dve", "pool", "sp"]}
        dnames = [f"c{i}" for i in range(6)] + ["dbg", "gb", "gb2", "const", "cast", "x0", "x1", "r0", "r1", "o0", "o1", "wm0", "wm1", "wm2"] + [f"w{i}" for i in range(NB)]
        dsems = {d: es.enter_context(nc.semaphore("ds_" + d)) for d in dnames}
        block = es.enter_context(nc.Block())

        def bank(b):
            return ps[:, b * 512:(b + 1) * 512]

        def mm(out, lhsT, rhs, start, stop, reads, writes, sgc=False):
            T.add("pe", lambda e: e.matmul(out, lhsT=lhsT, rhs=rhs, start=start, stop=stop, skip_group_check=sgc), reads, writes)

        def tr(out, in_, reads, writes):
            T.add("pe", lambda e: e.transpose(out, in_, identb[:, :]), list(reads) + ["identb"], writes)

        def act(out, in_, func, reads, writes, scale=None, bias=None, accum=None):
            kw = {}
            if scale is not None:
                kw["scale"] = scale
            if bias is not None:
                kw["bias"] = bias
            if accum is not None:
                kw["accum_out"] = accum
            T.add("act", lambda e: e.activation(out=out, in_=in_, func=func, **kw), reads, writes)

        def tt(out, in0, in1, op, reads, writes, eng="dve"):
            T.add(eng, lambda e: e.tensor_tensor(out=out, in0=in0, in1=in1, op=op), reads, writes)

        def ts(out, in0, s1, s2, op0, op1, reads, writes, eng="dve"):
            if op1 is None:
                T.add(eng, lambda e: e.tensor_scalar(out=out, in0=in0, scalar1=s1, scalar2=None, op0=op0), reads, writes)
            else:
                T.add(eng, lambda e: e.tensor_scalar(out=out, in0=in0, scalar1=s1, scalar2=s2, op0=op0, op1=op1), reads, writes)

        def stt(out, in0, scalar, in1, op0, op1, reads, writes):
            T.add("dve", lambda e: e.scalar_tensor_tensor(out=out, in0=in0, scalar=scalar, in1=in1, op0=op0, op1=op1), reads, writes)

        def recip(out, in_, reads, writes):
            T.add("dve", lambda e: e.reciprocal(out=out, in_=in_), reads, writes)

        def cp(out, in_, reads, writes, eng="dve"):
            T.add(eng, lambda e: e.tensor_copy(out=out, in_=in_), reads, writes)

        def dma(q, out, in_, reads, writes, dsem):
            T.add(q, lambda e: e.dma_start(out=out, in_=in_), reads, writes, dsem=dsem)

        def memset(ap, val, writes, eng="pool"):
            T.add(eng, lambda e: e.memset(ap, val), (), writes)

        for (dst, src, nm) in [(cT[:], cT_d, "cT"), (bmodT[:], bmodT_d, "bmodT"), (bmodg[:], bmodg_d, "bmodg"),
                               (n1g[:], n1g_d, "n1g"), (n2g[:], n2g_d, "n2g"), (gq[:], gq_d, "gq"), (gk[:], gk_d, "gk"),
                               (gmg[:], gmg_d, "gmg"), (bsT[:], bsT_d, "bsT")]:
            dma("sp", dst, src, (), [nm], "const")
        dma("pool", identb[:], ident_d, (), ["identb"], "cast")
        dma("pool", Wkv[:], win_d[:, 0:256].rearrange("(c p) n -> p c n", p=128), (), ["Wkv"], "cast")
        dma("pool", wsT[:], wsT_d, (), ["wsT"], "cast")
        memset(Vaug[:, :, 64:128], 1.0, [("Vaug", t) for t in range(NT)])
        memset(ones1[:], 1.0, ["ones1"])

        def wsrc_kn(w, c0, ncols):
            return w[:, c0:c0 + ncols].rearrange("(c p) n -> p c n", p=128)

        unit_src = []
        unit_src.append((wsrc_kn(win_d, 256, 512), 8))
        unit_src.append((wsrc_kn(win_d, 768, 512), 8))
        unit_src.append((wsrc_kn(win_d, 1280, 512), 8))
        unit_src.append((wsrc_kn(win_d, 1792, 512), 8))
        unit_src.append((wsrc_kn(win_d, 2816, 512), 8))
        unit_src.append((bra_d.rearrange("(c p) n -> p c n", p=128), 4))
        unit_src.append((brg_d.rearrange("(c p) n -> p c n", p=128), 4))
        unit_src.append((wsrc_kn(win_d, 2304, 512), 8))
        unit_src.append((wsrc_kn(win_d, 3328, 512), 8))
        unit_src.append((wsrc_kn(wout_d, 0, 512), 8))
        unit_src.append((wsrc_kn(wout_d, 512, 512), 8))
        for j in range(8):
            unit_src.append((wsrc_kn(ff1_d, j * 512, 512), 8))
        for nh in range(2):
            for kg in range(4):
                src = ff2_d[kg * 1024:(kg + 1) * 1024, nh * 512:(nh + 1) * 512].rearrange("(c p) n -> p c n", p=128)
                unit_src.append((src, 8))
        assert len(unit_src) == N_UNITS_QB
        def cast_unit(u, extra_reads=()):
            src, nch = unit_src[u]
            dst = wsc[u].rearrange("p (c n) -> p c n", c=nch)
            dma("pool", dst, src, [("cslot", u % 6)] + list(extra_reads), [("wsc", u), ("cslot", u % 6)], f"c{u % 6}")

        N_EARLY = 11 if nqb else N_UNITS_QB
        for u in range(N_EARLY):
            if "cast" in skip:
                break
            cast_unit(u)

        act(scT[:], cT[:], AF.Silu, ["cT"], ["scT"])
        for c in range(8):
            s_ = c % NB
            pa = ring[s_].bitcast(F32)
            dma("sp", pa, wmod_d[c * 128:(c + 1) * 128, 0:2048], (), [("ring", s_)], f"w{s_}")
            for jj in range(16):
                mm(ps[:, 2 * jj:2 * jj + 2], pa[:, jj * 128:(jj + 1) * 128], scT[:, c, :], c == 0 and jj == 0, c == 7 and jj == 15,
                   [("ring", s_), "scT"], [("ps", 0)], sgc=True)
        tt(modT[:, 0:16, :], ps[:, 0:32].rearrange("p (j k) -> p j k", k=2),
           bmodT[:, 0:16].unsqueeze(2).broadcast_to([128, 16, 2]), ALU.add, [("ps", 0), "bmodT"], ["modTa"])
        stt(a1[:], modT[:, 8:16, :], 1.0, n1g[:, :].unsqueeze(2).broadcast_to([128, 8, 2]), ALU.add, ALU.mult, ["modTa", "n1g"], ["a1"])

        def mod_b_buf(c):
            p_ = c % 2
            return p_, ringT[:, (2 * p_) * 4096:(2 * p_ + 2) * 4096].bitcast(F32)

        def mod_b_dma(c):
            p_, pb = mod_b_buf(c)
            dma("sp", pb, wmod_d[c * 128:(c + 1) * 128, 2048:6144], (), [("ring", 2 * p_), ("ring", 2 * p_ + 1)], f"wm{p_}")

        def mod_b_mm(c, j0, j1):
            p_, pb = mod_b_buf(c)
            for jj in range(j0, j1):
                mm(ps[:, 7 * 512 + 2 * jj:7 * 512 + 2 * jj + 2], pb[:, jj * 128:(jj + 1) * 128], scT[:, c, :], c == 0 and jj == 0, c == 7 and jj == 31,
                   [("ring", 2 * p_), ("ring", 2 * p_ + 1), "scT"], [("ps", 7)], sgc=True)

        def mod_b_piece(c):
            mod_b_dma(c)
            mod_b_mm(c, 0, 32)

        def mod_b_finish():
            tt(modT[:, 16:48, :], ps[:, 7 * 512:7 * 512 + 64].rearrange("p (j k) -> p j k", k=2),
               bmodT[:, 16:48].unsqueeze(2).broadcast_to([128, 32, 2]), ALU.add, [("ps", 7), "bmodT"], ["modTb"])
            stt(a2[:], modT[:, 32:40, :], 1.0, n2g[:, :].unsqueeze(2).broadcast_to([128, 8, 2]), ALU.add, ALU.mult, ["modTb", "n2g"], ["a2"])
            for k_, j0 in enumerate((16, 40)):
                T.add("pool", lambda e, k_=k_, j0=j0: e.dma_start(out=gscr[0, k_ * 1024:(k_ + 1) * 1024].rearrange("(c p) -> p c", p=128),
                                                             in_=modT[:, j0:j0 + 8, 0], allow_slow_non_contiguous=True),
                      ["modTb"], [("gscr", k_)], dsem="gb")
            dma("pool", gbc[:], gscr[0, :].partition_broadcast(128), [("gscr", 0), ("gscr", 1)], ["gbc"], "gb2")

        cnt = {"xn": 0, "ev": 0, "sq": 0, "par": 0}

        def nm_stats(xb, xres, nt):
            par = cnt["par"] % 2
            cnt["par"] += 1
            po = par * 4
            for t in range(nt):
                kq = cnt["sq"] % 2
                cnt["sq"] += 1
                act(sqj[kq][:], xb[:, t, :], AF.Square, [xres], [("ss", par, t), ("sqj", kq)], accum=ss[:, po + t:po + t + 1])
            act(rs[:, po:po + nt], ss[:, po:po + nt], AF.Ln, [("ss", par, t) for t in range(nt)], [("rs", par)], scale=1.0 / D, bias=EPS)
            act(rstd[:, po:po + nt], rs[:, po:po + nt], AF.Exp, [("rs", par)], [("rstd", par)], scale=-0.5)
            return dict(xb=xb, xres=xres, nt=nt, par=par, po=po)

        def xn_tile(xb, xres, t, k, rs_ap, rs_key):
            if t % 2 == 0:
                ts(xn[k][:], xb[:, t, :], rs_ap, None, ALU.mult, None, [xres, rs_key], [("xn", k)])
            else:
                act(xn[k][:], xb[:, t, :], AF.Copy, [xres, rs_key], [("xn", k)], scale=rs_ap)
            for c in range(8):
                bk = c // 2
                o = bank(bk).bitcast(BF16)[:, (c % 2) * 512 + t * 128:(c % 2) * 512 + (t + 1) * 128]
                tr(o, xn[k][:, c * 128:(c + 1) * 128], [("xn", k)], [("ps", bk)])

        def nm_xn_tr(cx):
            xb, xres, nt, par, po = cx["xb"], cx["xres"], cx["nt"], cx["par"], cx["po"]
            for t in range(nt):
                k = cnt["xn"] % 2
                cnt["xn"] += 1
                xn_tile(xb, xres, t, k, rstd[:, po + t:po + t + 1], ("rstd", par))

        def nm_evac(cx, a_t, sh_off, col, hd=None):
            nt = cx["nt"]
            hdst, hkey = hd if hd is not None else (hT, "hT")
            for c in range(8):
                bk = c // 2
                src = bank(bk).bitcast(BF16)[:, (c % 2) * 512:(c % 2) * 512 + nt * 128]
                if c % 2 == 0:
                    act(hdst[:, c, 0:nt * 128], src, AF.Identity, [("ps", bk), a_t[1], a_t[2]], [(hkey, c)],
                        scale=a_t[0][:, c, col:col + 1], bias=modT[:, sh_off + c, col:col + 1])
                else:
                    ts(hdst[:, c, 0:nt * 128], src, a_t[0][:, c, col:col + 1], modT[:, sh_off + c, col:col + 1], ALU.mult, ALU.add,
                       [("ps", bk), a_t[1], a_t[2]], [(hkey, c)])

        def norm_mod_T(xb, xres, nt, a_t, sh_off, col, hd=None):
            cx = nm_stats(xb, xres, nt)
            nm_xn_tr(cx)
            nm_evac(cx, a_t, sh_off, col, hd)

        def head_rstd(src_sq, nh, res_in):
            T.add("dve", lambda e: e.tensor_reduce(out=hs[:, 0:nh], in_=src_sq.rearrange("p (h d) -> p h d", d=64), axis=AX.X, op=ALU.add),
                  [res_in], ["hs"])
            act(hl[:, 0:nh], hs[:, 0:nh], AF.Ln, ["hs"], ["hl"], scale=1.0 / 64, bias=EPS)
            act(hr[:, 0:nh], hl[:, 0:nh], AF.Exp, ["hl"], ["hr"], scale=-0.5)

        def norm_rope(psrc, psres, nh, gain, gres, rp, rpres, outb, outres):
            W = nh * 64
            s0, s1, s2, s3 = scr[0][:, 0:W], scr[1][:, 0:W], scr[2][:, 0:W], scr[3][:, 0:W]
            act(s0, psrc, AF.Square, [psres], [("scr", 0)])
            head_rstd(s0, nh, ("scr", 0))
            tt(s1, psrc, gain, ALU.mult, [psres, gres], [("scr", 1)])
            tt(s2.rearrange("p (h d) -> p h d", d=64), s1.rearrange("p (h d) -> p h d", d=64),
               hr[:, 0:nh].unsqueeze(2).broadcast_to([128, nh, 64]), ALU.mult, [("scr", 1), "hr"], [("scr", 2)])
            v2 = s2.rearrange("p (h d) -> p h d", d=64)
            tt(s3.rearrange("p (h d) -> p h d", d=64), v2, rp[:, 0:64].unsqueeze(1).broadcast_to([128, nh, 64]), ALU.mult,
               [("scr", 2), rpres], [("scr", 3)])
            v0 = s0.rearrange("p (h d) -> p h d", d=64)
            tt(v0[:, :, 0:32], v2[:, :, 32:64], rp[:, 64:96].unsqueeze(1).broadcast_to([128, nh, 32]), ALU.mult,
               [("scr", 2), rpres], [("scr", 0)])
            tt(v0[:, :, 32:64], v2[:, :, 0:32], rp[:, 96:128].unsqueeze(1).broadcast_to([128, nh, 32]), ALU.mult,
               [("scr", 2), rpres], [("scr", 0)])
            tt(outb, s3, s0, ALU.add, [("scr", 3), ("scr", 0)], [outres])

        def dump(name, ap, res, dt=F32):
            if "nodump" in skip:
                return
            d = nc.dram_tensor("dbg_" + name, list(ap.shape), F32, kind="ExternalOutput").ap()
            dbg[name] = d
            dma("pool", d, ap, res, [("dbgout", name)], "dbg")

        ld = {"n": 0}

        def load_xo(row0, nt, s):
            dma("sp", xbuf[s][:, 0:nt, :], xin[row0:row0 + nt * 128, :].rearrange("(t p) d -> p t d", p=128), (), [("xbuf", s)], f"x{s}")

        def load_rope(row0, nt, s):
            dma("sp", ropeb[s][:, 0:nt, :], rope[row0:row0 + nt * 128, :].rearrange("(t p) d -> p t d", p=128), (), [("ropeb", s)], f"r{s}")

        def load_x(row0, nt):
            s = ld["n"] % 2
            ld["n"] += 1
            load_xo(row0, nt, s)
            load_rope(row0, nt, s)
            return s

        supers = [(0, 2, 1)] + [(256 + i * 512, 4, 0) for i in range(16)]
        if stage == 0:
            supers = []
            for c in range(8):
                mod_b_piece(c)
            mod_b_finish()
            dump("modT", modT[:], ["modTa", "modTb"])
            dump("gbc", gbc[:], ["gbc"])
            dump("a1", a1[:], ["a1"])
        if stage == 1:
            import os
            supers = supers[:int(os.environ.get("NSUP", "3"))]
        krb = big[:, 24:26, :].rearrange("p a b -> p (a b)")
        nsup = len(supers)
        hbufs = [(hT, "hT"), (big[:, 0:8, :], "big")]
        a1t = (a1, "a1", "modTa")

        def a_kv(si):
            row0, nt, col = supers[si]
            hA, hAk = hbufs[si % 2]
            for t in range(nt):
                bk = 4 + t // 2
                for c in range(8):
                    mm(ps[:, bk * 512 + (t % 2) * 256: bk * 512 + (t % 2) * 256 + 256], hA[:, c, t * 128:(t + 1) * 128], Wkv[:, c, :],
                       c == 0, c == 7, [(hAk, c), "Wkv"], [("ps", bk)])
                if 1 <= si <= 8:
                    mod_b_mm(si - 1, 8 * t, 8 * t + 8)

        def a_post(si):
            row0, nt, col = supers[si]
            slot = si % 2
            tile0 = row0 // 128
            W = nt * 128
            kvv = ps[:, 4 * 512:4 * 512 + nt * 256].rearrange("p (t n) -> p t n", n=256)
            kview = kvv[:, :, 0:128]
            kvb = [("ps", 4)] + ([("ps", 5)] if nt > 2 else [])
            s0, s1, s2, s3 = scr[0][:, 0:W], scr[1][:, 0:W], scr[2][:, 0:W], scr[3][:, 0:W]
            tv = lambda a: a.rearrange("p (t n) -> p t n", n=128)
            hv = lambda a: a.rearrange("p (h d) -> p h d", d=64)
            qv = lambda a: a.rearrange("p (t h d) -> p t h d", h=2, d=64)
            rp = ropeb[slot]
            rpk = ("ropeb", slot)
            act(tv(s0), kview, AF.Square, kvb, [("scr", 0)])
            head_rstd(s0, 2 * nt, ("scr", 0))
            tt(tv(s1), kview, gk[:, :].unsqueeze(1).broadcast_to([128, nt, 128]), ALU.mult, kvb + ["gk"], [("scr", 1)])
            tt(hv(s2), hv(s1), hr[:, 0:2 * nt].unsqueeze(2).broadcast_to([128, 2 * nt, 64]), ALU.mult, [("scr", 1), "hr"], [("scr", 2)])
            tt(qv(s3), qv(s2), rp[:, 0:nt, 0:64].unsqueeze(2).broadcast_to([128, nt, 2, 64]), ALU.mult, [("scr", 2), rpk], [("scr", 3)])
            tt(qv(s0)[:, :, :, 0:32], qv(s2)[:, :, :, 32:64], rp[:, 0:nt, 64:96].unsqueeze(2).broadcast_to([128, nt, 2, 32]), ALU.mult,
               [("scr", 2), rpk], [("scr", 0)])
            tt(qv(s0)[:, :, :, 32:64], qv(s2)[:, :, :, 0:32], rp[:, 0:nt, 96:128].unsqueeze(2).broadcast_to([128, nt, 2, 32]), ALU.mult,
               [("scr", 2), rpk], [("scr", 0)])
            tt(krb[:, 0:W], s3, s0, ALU.add, [("scr", 3), ("scr", 0)], [("big", 24)])
            for t in range(nt):
                tr(bank(6).bitcast(BF16)[:, t * 128:(t + 1) * 128], krb[:, t * 128:(t + 1) * 128], [("big", 24)], [("ps", 6)])
            act(Vaug[:, tile0:tile0 + nt, 0:64], kvv[:, :, 128:192], AF.Copy, kvb, [("Vaug", tile0 + t) for t in range(nt)])
            act(Vaug[:, tile0:tile0 + nt, 128:192], kvv[:, :, 192:256], AF.Copy, kvb, [("Vaug", tile0 + t) for t in range(nt)])
            cp(KT[:, tile0 * 128:(tile0 + nt) * 128], bank(6).bitcast(BF16)[:, 0:nt * 128], [("ps", 6)], [("KT", si)])

        actx = {}
        if nsup:
            for k_ in range(min(2, nsup)):
                load_xo(supers[k_][0], supers[k_][1], k_ % 2)
                load_rope(supers[k_][0], supers[k_][1], k_ % 2)
            ld["n"] = 0
            actx[0] = nm_stats(xbuf[0], ("xbuf", 0), supers[0][1])
            nm_xn_tr(actx[0])
            if nsup > 2:
                load_xo(supers[2][0], supers[2][1], 0)
            nm_evac(actx[0], a1t, 0, supers[0][2], hbufs[0])
        for k_ in range(nsup):
            if k_ < 8:
                mod_b_dma(k_)
            n_ = k_ + 1
            if n_ < nsup:
                actx[n_] = nm_stats(xbuf[n_ % 2], ("xbuf", n_ % 2), supers[n_][1])
            a_kv(k_)
            if n_ < nsup:
                nm_xn_tr(actx[n_])
                if k_ + 3 < nsup:
                    load_xo(supers[k_ + 3][0], supers[k_ + 3][1], (k_ + 3) % 2)
            a_post(k_)
            if k_ + 2 < nsup:
                load_rope(supers[k_ + 2][0], supers[k_ + 2][1], k_ % 2)
            if n_ < nsup:
                nm_evac(actx[n_], a1t, 0, supers[n_][2], hbufs[n_ % 2])

        ld["n"] = 1
        slot = load_x(256, 4) if (nqb and stage > 1) else None
        if stage >= 1:
            if len(supers) < 9:
                for c in range(max(len(supers) - 1, 0), 8):
                    if c >= len(supers):
                        mod_b_dma(c)
                    mod_b_mm(c, 0, 32)
            mod_b_finish()
        if stage == 1:
            dump("KT", KT[:, 0:1280], [("KT", i) for i in range(3)], BF16)
            dump("Vaug", Vaug[:, 0:10, :], [("Vaug", i) for i in range(10)], BF16)
            dump("hT", hT[:], [("hT", c) for c in range(8)], BF16)
        if stage <= 1:
            nqb = 0
        wctr = {"n": 0}

        def load_unit(u):
            s = wctr["n"] % NB
            wctr["n"] += 1
            dma("sp", ring[s][:, :], wsc[u], [("wsc", u)], [("ring", s)], f"w{s}")
            return s

        def R(s):
            return ("ring", s)

        def unit8(s):
            return ring[s][:, :].rearrange("p (c n) -> p c n", c=8)

        def unit4(s):
            return ring[s][:, :].rearrange("p (c n) -> p c n", c=4)

        yT = [big[:, c, :] for c in range(8)]
        uT = [big[:, 8 + j, :] for j in range(4)]
        gmT = [big[:, 12 + j, :] for j in range(4)]
        attnT = [big[:, 16 + j, :] for j in range(4)]
        QT = [big[:, 20 + j, :] for j in range(4)]
        vnb = [big[:, 24 + t, :] for t in range(4)]
        PT = [big[:, 28:30, :].rearrange("p a b -> p (a b)"), big[:, 30:32, :].rearrange("p a b -> p (a b)"),
              big[:, 26:28, :].rearrange("p a b -> p (a b)")]
        PTK = [[("big", 28), ("big", 29)], [("big", 30), ("big", 31)], [("big", 26), ("big", 27)]]

        for qb in range(nqb):
            row0 = 256 + qb * 512
            xb = xbuf[slot]
            xres = ("xbuf", slot)
            rpb = ropeb[slot]
            norm_mod_T(xb, xres, 4, (a1, "a1", "modTa"), 0, 0)
            nslot = None
            su = load_unit(0)
            for t in range(4):
                for c in range(8):
                    mm(bank(4 + t), hT[:, c, t * 128:(t + 1) * 128], unit8(su)[:, c, :], c == 0, c == 7, [("hT", c), R(su)], [("ps", 4 + t)])
            sv = load_unit(1)
            for t in range(4):
                for c in range(8):
                    mm(bank(t), hT[:, c, t * 128:(t + 1) * 128], unit8(sv)[:, c, :], c == 0, c == 7, [("hT", c), R(sv)], [("ps", t)])
            qrb = scr[4][:, :].bitcast(BF16)[:, 0:512]
            for t in range(4):
                norm_rope(bank(4 + t), ("ps", 4 + t), 8, gq[:, :], "gq", rpb[:, t, :], ("ropeb", slot), qrb, ("scr", 4))
                for g in range(4):
                    tr(bank(4 + t).bitcast(BF16)[:, g * 128:(g + 1) * 128], qrb[:, g * 128:(g + 1) * 128], [("scr", 4)], [("ps", 4 + t)])
                cp(big[:, 20:24, t * 128:(t + 1) * 128], bank(4 + t).bitcast(BF16)[:, 0:512].rearrange("p (g q) -> p g q", q=128),
                   [("ps", 4 + t)], [("big", 20 + g) for g in range(4)])
            for t in range(4):
                gv = scr[1][:, :]
                act(gv, bank(t), AF.Gelu_apprx_tanh, [("ps", t)], [("scr", 1)])
                act(scr[0][:, :], gv, AF.Square, [("scr", 1)], [("scr", 0)])
                head_rstd(scr[0][:, :], 8, ("scr", 0))
                tt(scr[2][:, :], gv, gmg[:, :], ALU.mult, [("scr", 1), "gmg"], [("scr", 2)])
                tt(vnb[t].rearrange("p (h d) -> p h d", d=64), scr[2][:, :].rearrange("p (h d) -> p h d", d=64),
                   hr[:, 0:8].unsqueeze(2).broadcast_to([128, 8, 64]), ALU.mult, [("scr", 2), "hr"], [("big", 24 + t)])
            for t in range(4):
                for j in range(4):
                    for gg in range(2):
                        g = 2 * j + gg
                        mm(ps[gg * 64:(gg + 1) * 64, j * 512 + t * 128: j * 512 + (t + 1) * 128], vnb[t][:, g * 64:(g + 1) * 64], wsT[:, g, :],
                           True, True, [("big", 24 + t), "wsT"], [("ps", j)])
            suu = load_unit(2)
            for j in range(4):
                for c in range(8):
                    mm(bank(4 + j), unit8(suu)[:, c, j * 128:(j + 1) * 128], hT[:, c, :], c == 0, c == 7, [("hT", c), R(suu)], [("ps", 4 + j)])
                act(uT[j], bank(4 + j), AF.Gelu_apprx_tanh, [("ps", 4 + j)], [("big", 8 + j)])
                tt(scr[5][:, :].rearrange("p (t q) -> p t q", q=128), bank(j).rearrange("p (t q) -> p t q", q=128),
                   bsT[:, j, :].unsqueeze(1).broadcast_to([128, 4, 128]), ALU.add, [("ps", j), "bsT"], [("scr", 5)])
                tt(gmT[j], scr[5][:, :], uT[j], ALU.mult, [("scr", 5), ("big", 8 + j)], [("big", 12 + j)])

            steps = [(g, kb) for g in range(4) for kb in range(NT)]

            def ksup(kb):
                return 0 if kb < 2 else 1 + (kb - 2) // 4

            def qk(i):
                g, kb = steps[i]
                sbk = (i % 2) * 2
                mm(bank(sbk), KT[0:64, kb * 128:(kb + 1) * 128], QT[g][0:64, :], True, True, [("KT", ksup(kb)), ("big", 20 + g)], [("ps", sbk)])
                mm(bank(sbk + 1), KT[64:128, kb * 128:(kb + 1) * 128], QT[g][64:128, :], True, True, [("KT", ksup(kb)), ("big", 20 + g)], [("ps", sbk + 1)])
                pace = [("pace", i)] if (qb == 0 and i % 12 == 0) else []
                act(PT[i % 3], ps[:, sbk * 512:sbk * 512 + 1024], AF.Exp, [("ps", sbk), ("ps", sbk + 1)], PTK[i % 3] + pace, scale=0.125)
                if pace and N_EARLY + i // 12 < N_UNITS_QB:
                    cast_unit(N_EARLY + i // 12, pace)

            def pv(i):
                g, kb = steps[i]
                oa = 4 + 2 * (g % 2)
                mm(bank(oa), Vaug[:, kb, 0:128], PT[i % 3][:, 0:512], kb == 0, kb == NT - 1, [("Vaug", kb)] + PTK[i % 3], [("ps", oa)])
                mm(bank(oa + 1), Vaug[:, kb, 64:192], PT[i % 3][:, 512:1024], kb == 0, kb == NT - 1, [("Vaug", kb)] + PTK[i % 3], [("ps", oa + 1)])
                if kb == NT - 1:
                    recip(rc[64:128, 0:512], bank(oa)[64:128, :], [("ps", oa)], ["rc"])
                    recip(rc[0:64, 512:1024], bank(oa + 1)[0:64, :], [("ps", oa + 1)], ["rc"])
                    tt(attnT[g][0:64, :], bank(oa)[0:64, :], rc[64:128, 0:512], ALU.mult, [("ps", oa), "rc"], [("big", 16 + g)])
                    tt(attnT[g][64:128, :], bank(oa + 1)[64:128, :], rc[0:64, 512:1024], ALU.mult, [("ps", oa + 1), "rc"], [("big", 16 + g)])

            qk(0)
            qk(1)
            for i in range(len(steps)):
                if i + 2 < len(steps):
                    qk(i + 2)
                pv(i)

            if qb + 1 < nqb:
                nslot = load_x(row0 + 512, 4)

            sga = [load_unit(3), None]
            sgb = [load_unit(4), None]
            sbra = load_unit(5)
            sbrg = load_unit(6)
            for m in range(8):
                if m == 4:
                    sga[1] = load_unit(7)
                    sgb[1] = load_unit(8)
                h = m // 4
                b0 = (m % 2) * 4
                for c in range(8):
                    mm(bank(b0), unit8(sga[h])[:, c, (m % 4) * 128:(m % 4 + 1) * 128], hT[:, c, :], c == 0, c == 7, [("hT", c), R(sga[h])], [("ps", b0)])
                for c in range(8):
                    mm(bank(b0 + 1), unit8(sgb[h])[:, c, (m % 4) * 128:(m % 4 + 1) * 128], hT[:, c, :], c == 0, c == 7, [("hT", c), R(sgb[h])], [("ps", b0 + 1)])
                for c in range(4):
                    mm(bank(b0 + 2), unit4(sbra)[:, c, m * 128:(m + 1) * 128], attnT[c], c == 0, c == 3, [("big", 16 + c), R(sbra)], [("ps", b0 + 2)])
                for c in range(4):
                    mm(bank(b0 + 3), unit4(sbrg)[:, c, m * 128:(m + 1) * 128], gmT[c], c == 0, c == 3, [("big", 12 + c), R(sbrg)], [("ps", b0 + 3)])
                act(scr[0][:, :], bank(b0), AF.Sigmoid, [("ps", b0)], [("scr", 0)])
                act(scr[1][:, :], bank(b0 + 1), AF.Sigmoid, [("ps", b0 + 1)], [("scr", 1)])
                tt(scr[2][:, :], bank(b0 + 2), scr[0][:, :], ALU.mult, [("ps", b0 + 2), ("scr", 0)], [("scr", 2)])
                tt(scr[3][:, :], bank(b0 + 3), scr[1][:, :], ALU.mult, [("ps", b0 + 3), ("scr", 1)], [("scr", 3)])
                tt(yT[m], scr[2][:, :], scr[3][:, :], ALU.add, [("scr", 2), ("scr", 3)], [("big", m)], eng="pool")

            so = [load_unit(9), load_unit(10)]
            par = cnt["par"] % 2
            cnt["par"] += 1
            po = par * 4
            k8c = {"n": 0}

            def b8_mm(t):
                for nh in range(2):
                    bk = 4 + k8c["n"] % 4
                    k8c["n"] += 1
                    for c in range(8):
                        mm(bank(bk), yT[c][:, t * 128:(t + 1) * 128], unit8(so[nh])[:, c, :], c == 0, c == 7, [("big", c), R(so[nh])], [("ps", bk)])
                    sk = 4 + (k8c["n"] % 2)
                    tt(scr[sk][:, :], bank(bk), gbc[:, nh * 512:(nh + 1) * 512], ALU.mult, [("ps", bk), "gbc"], [("scr", sk)])
                    tt(xb[:, t, nh * 512:(nh + 1) * 512], scr[sk][:, :], xb[:, t, nh * 512:(nh + 1) * 512], ALU.add, [("scr", sk), xres], [xres, ("x1t", t)], eng="pool")

            def b9_tile(t):
                kq = cnt["sq"] % 2
                cnt["sq"] += 1
                cs = slice(po + t, po + t + 1)
                act(sqj[kq][:], xb[:, t, :], AF.Square, [("x1t", t)], [("ss", par, t), ("sqj", kq)], accum=ss[:, cs])
                act(rs[:, cs], ss[:, cs], AF.Ln, [("ss", par, t)], [("rs", par), ("rs", par, t)], scale=1.0 / D, bias=EPS)
                act(rstd[:, cs], rs[:, cs], AF.Exp, [("rs", par, t)], [("rstd", par), ("rstd", par, t)], scale=-0.5)
                k = cnt["xn"] % 2
                cnt["xn"] += 1
                xn_tile(xb, ("x1t", t), t, k, rstd[:, cs], ("rstd", par, t))

            for t in range(4):
                b8_mm(t)
                if t >= 1:
                    b9_tile(t - 1)
            b9_tile(3)
            nm_evac(dict(nt=4), (a2, "a2", "modTb"), 24, 0)

            for j in range(32):
                if j % 4 == 0:
                    sf = load_unit(11 + j // 4)
                bk = j % 8
                for c in range(8):
                    mm(bank(bk), unit8(sf)[:, c, (j % 4) * 128:(j % 4 + 1) * 128], hT[:, c, :], c == 0, c == 7, [("hT", c), R(sf)], [("ps", bk)])
                sk = j % 4
                act(scr[sk][:, :], bank(bk), AF.Relu, [("ps", bk)], [("scr", sk)])
                tt(big[:, j, :], scr[sk][:, :], scr[sk][:, :], ALU.mult, [("scr", sk)], [("big", j)])

            for nh in range(2):
                for kg in range(4):
                    s2 = load_unit(19 + nh * 4 + kg)
                    for t in range(4):
                        bk = nh * 4 + t
                        for c in range(8):
                            mm(bank(bk), big[:, kg * 8 + c, t * 128:(t + 1) * 128], unit8(s2)[:, c, :], kg == 0 and c == 0, kg == 3 and c == 7,
                               [("big", kg * 8 + c), R(s2)], [("ps", bk)])
                for t in range(4):
                    bk = nh * 4 + t
                    sk = 4 + (t % 2)
                    tt(scr[sk][:, :], bank(bk), gbc[:, 1024 + nh * 512:1024 + (nh + 1) * 512], ALU.mult, [("ps", bk), "gbc"], [("scr", sk)])
                    tt(xb[:, t, nh * 512:(nh + 1) * 512], scr[sk][:, :], xb[:, t, nh * 512:(nh + 1) * 512], ALU.add, [("scr", sk), xres], [xres], eng="pool")
            dma("sp", out_d[qb * 512:(qb + 1) * 512, :].rearrange("(t p) d -> p t d", p=128), xb[:, :, :], [xres], [("out", qb)], f"o{slot}")
            slot = nslot

        T.add("sp", lambda e: e.nop(), [("out", q) for q in range(nqb)] + [("dbgout", n) for n in dbg], [])

        T.finalize()
        nc._tracker = T

        @block.sync
        def _(e):
            T.emit("sp", e, esems, dsems)

        @block.tensor
        def _(e):
            T.emit("pe", e, esems, dsems)

        @block.scalar
        def _(e):
            T.emit("act", e, esems, dsems)

        @block.vector
        def _(e):
            T.emit("dve", e, esems, dsems)

        @block.gpsimd
        def _(e):
            T.emit("pool", e, esems, dsems)

    return nc


_CACHE = {}


def _rope_table(tok_idx):
    n = tok_idx.shape[0]
    t = np.maximum(tok_idx, 0)
    row = (t // 64).astype(np.float32)
    colp = (t % 64).astype(np.float32)
    inv = (np.float32(10000.0) ** (-np.arange(0, 32, 2, dtype=np.float32) / np.float32(32))).astype(np.float32)
    ang = np.concatenate([row[:, None] * inv[None, :], colp[:, None] * inv[None, :]], axis=-1).astype(np.float32)
    cos = np.cos(ang).astype(np.float32)
    sin = np.sin(ang).astype(np.float32)
    ident = tok_idx < 0
    cos[ident] = 1.0
    sin[ident] = 0.0
    return np.concatenate([cos, cos, -sin, sin], axis=1).astype(np.float32)


def kernel(x, c, ctx, c_ctx, w_mod, b_mod, norm1_g, norm2_g, w_in, q_norm_g, k_norm_g,
           gm_norm_g, gm_ws, gm_bs, w_br_attn, w_br_gm, w_out, w_ff1, w_ff2):
    f = lambda a: np.ascontiguousarray(np.asarray(a, dtype=np.float32))
    x, c, ctx, c_ctx = f(x), f(c), f(ctx), f(c_ctx)
    w_mod, b_mod, w_in = f(w_mod)[0], f(b_mod)[0], f(w_in)[0]
    n1, n2 = f(norm1_g)[0], f(norm2_g)[0]
    qg, kg, gmg = f(q_norm_g)[0], f(k_norm_g)[0], f(gm_norm_g)[0]
    ws, bs = f(gm_ws)[0], f(gm_bs)[0]
    bra, brg, wo, w1, w2 = f(w_br_attn)[0], f(w_br_gm)[0], f(w_out)[0], f(w_ff1)[0], f(w_ff2)[0]

    if "nc" not in _CACHE:
        _CACHE["nc"] = build_program()
    nc = _CACHE["nc"]

    qcols = 256 + np.array([kv * 256 + g * 64 + d for g in range(4) for kv in range(2) for d in range(64)])
    order = np.concatenate([np.arange(0, 256), qcols, np.arange(1280, 1792), np.arange(768, 1280), np.arange(1792, 3840)])
    w_in_p = np.ascontiguousarray(w_in[:, order])
    rows = np.array([kv * 256 + g * 64 + d for g in range(4) for kv in range(2) for d in range(64)])
    w_bra_p = np.ascontiguousarray(bra[rows, :])
    b_modT = np.ascontiguousarray(b_mod.reshape(48, 128).T)
    b_modg = np.ascontiguousarray(np.concatenate([b_mod[2048:3072], b_mod[5120:6144]])[None, :])
    n1g = np.ascontiguousarray(n1.reshape(8, 128).T)
    n2g = np.ascontiguousarray(n2.reshape(8, 128).T)
    gq_bc = np.ascontiguousarray(np.broadcast_to(np.tile(qg, 8)[None, :], (128, 512)))
    gk_bc = np.ascontiguousarray(np.broadcast_to(np.tile(kg, 2)[None, :], (128, 128)))
    gmg_bc = np.ascontiguousarray(np.broadcast_to(gmg.reshape(512)[None, :], (128, 512)))
    wsT = np.ascontiguousarray(ws.transpose(2, 0, 1))
    bsT = np.ascontiguousarray(np.broadcast_to(bs.reshape(4, 2, 1, 128), (4, 2, 64, 128)).transpose(1, 2, 0, 3).reshape(128, 4, 128))
    ident = np.eye(128, dtype=np.float32)

    in_maps = []
    for core in range(8):
        b, hf = core // 2, core % 2
        own = np.arange(hf * 4096, (hf + 1) * 4096)
        oth = np.arange((1 - hf) * 4096, (2 - hf) * 4096)
        xin = np.concatenate([ctx[b], x[b, own], x[b, oth]], axis=0)
        tok = np.concatenate([-np.ones(256, dtype=np.int64), own, oth])
        cT = np.ascontiguousarray(np.stack([c[b], c_ctx], axis=1).reshape(8, 128, 2).transpose(1, 0, 2))
        in_maps.append({
            "xin": np.ascontiguousarray(xin), "rope": _rope_table(tok), "cT": cT, "w_mod": w_mod, "b_modT": b_modT,
            "b_modg": b_modg, "n1g": n1g, "n2g": n2g, "w_in_p": w_in_p, "gq_bc": gq_bc, "gk_bc": gk_bc, "gmg_bc": gmg_bc,
            "wsT": wsT, "bsT": bsT, "w_bra_p": w_bra_p, "w_brg": brg, "w_out": wo, "w_ff1": w1, "w_ff2": w2, "ident": ident,
        })
    res = run_bass_kernel_spmd(nc, in_maps, core_ids=list(range(8)))
    out = np.empty((4, 8192, D), dtype=np.float32)
    for core in range(8):
        b, hf = core // 2, core % 2
        out[b, hf * 4096:(hf + 1) * 4096] = res.results[core]["out"]
    return out
```

```python
import numpy as np
import concourse.bass as bass
import concourse.mybir as mybir
from concourse.bass_utils import run_bass_kernel_spmd

F32 = mybir.dt.float32
BF16 = mybir.dt.bfloat16
AF = mybir.ActivationFunctionType
ALU = mybir.AluOpType
AX = mybir.AxisListType

D = 1024
NCTX_T = 2
NOWN_T = 32
NT = 66
NQB = 8
EPS = 1e-6
NB = 5
N_UNITS_QB = 27


class Tracker:
    def __init__(self):
        self.ops = []
        self.last_w = {}
        self.readers = {}
        self.dcount = {}

    def add(self, eng, fn, reads=(), writes=(), dsem=None):
        idx = len(self.ops)
        deps = set()
        if eng in ("act", "dve"):
            writes = list(writes) + [("pslk", r[1]) for r in reads if isinstance(r, tuple) and r[0] == "ps"]
        for r in reads:
            if r in self.last_w:
                deps.add(self.last_w[r])
        for w in writes:
            if w in self.last_w:
                deps.add(self.last_w[w])
            for rd in self.readers.get(w, ()):
                deps.add(rd)
        op = dict(eng=eng, fn=fn, deps=deps, dsem=dsem, marked=False, idx=idx, val=None, desc=(tuple(reads), tuple(writes)))
        if dsem is not None:
            self.dcount[dsem] = self.dcount.get(dsem, 0) + 16
            op["dval"] = self.dcount[dsem]
        self.ops.append(op)
        for r in reads:
            self.readers.setdefault(r, []).append(idx)
        for w in writes:
            self.last_w[w] = idx
            self.readers[w] = []
        return idx

    def finalize(self):
        ops = self.ops
        for op in ops:
            red = {}
            for d in op["deps"]:
                dop = ops[d]
                if dop["dsem"] is not None:
                    key = ("d", dop["dsem"])
                    if key not in red or ops[red[key]]["dval"] < dop["dval"]:
                        red[key] = d
                else:
                    if dop["eng"] == "pe" and op["eng"] == "pe" and op["dsem"] is None:
                        continue
                    key = ("e", dop["eng"])
                    if key not in red or red[key] < d:
                        red[key] = d
            op["rdeps"] = list(red.values())
            for d in op["rdeps"]:
                if ops[d]["dsem"] is None:
                    ops[d]["marked"] = True
        cnt = {}
        for op in ops:
            if op["dsem"] is None and op["marked"]:
                cnt[op["eng"]] = cnt.get(op["eng"], 0) + 1
                op["val"] = cnt[op["eng"]]

    def trace(self, engname):
        waited = {}
        out = []
        for op in self.ops:
            if op["eng"] != engname:
                continue
            ws = []
            for d in op["rdeps"]:
                dop = self.ops[d]
                if dop["dsem"] is not None:
                    val = self.dcount[dop["dsem"]] if dop["dsem"] in ("const", "cast", "gb") else dop["dval"]
                    key = ("d", dop["dsem"])
                else:
                    val, key = dop["val"], ("e", dop["eng"])
                if waited.get(key, 0) < val:
                    ws.append((key[1], val))
                    waited[key] = val
            inc = (op["dsem"], op.get("dval")) if op["dsem"] else ((engname, op["val"]) if op["marked"] else None)
            out.append((op["idx"], ws, op["desc"], inc))
        return out

    def emit(self, engname, engobj, esems, dsems):
        waited = {}
        for op in self.ops:
            if op["eng"] != engname:
                continue
            for d in op["rdeps"]:
                dop = self.ops[d]
                if dop["dsem"] is not None:
                    val = self.dcount[dop["dsem"]] if dop["dsem"] in ("const", "cast", "gb") else dop["dval"]
                    sem, key = dsems[dop["dsem"]], ("d", dop["dsem"])
                else:
                    sem, val, key = esems[dop["eng"]], dop["val"], ("e", dop["eng"])
                if waited.get(key, 0) < val:
                    engobj.wait_ge(sem, val)
                    waited[key] = val
            ins = op["fn"](engobj)
            if op["dsem"] is not None:
                ins.then_inc(dsems[op["dsem"]], 16)
            elif op["marked"]:
                ins.then_inc(esems[engname], 1)


def build_program(stage=3, nqb=NQB, skip=()):
    nc = bass.Bass("TRN2", target_bir_lowering=False)
    T = Tracker()
    dbg = {}

    def din(name, shape, dt=F32):
        return nc.dram_tensor(name, list(shape), dt, kind="ExternalInput").ap()

    xin = din("xin", [NT * 128, D])
    rope = din("rope", [NT * 128, 128])
    cT_d = din("cT", [128, 8, 2])
    wmod_d = din("w_mod", [D, 6 * D])
    bmodT_d = din("b_modT", [128, 48])
    bmodg_d = din("b_modg", [1, 2048])
    n1g_d = din("n1g", [128, 8])
    n2g_d = din("n2g", [128, 8])
    win_d = din("w_in_p", [D, 3840])
    gq_d = din("gq_bc", [128, 512])
    gk_d = din("gk_bc", [128, 128])
    gmg_d = din("gmg_bc", [128, 512])
    wsT_d = din("wsT", [128, 8, 128])
    bsT_d = din("bsT", [128, 4, 128])
    bra_d = din("w_bra_p", [512, D])
    brg_d = din("w_brg", [512, D])
    wout_d = din("w_out", [D, D])
    ff1_d = din("w_ff1", [D, 4 * D])
    ff2_d = din("w_ff2", [4 * D, D])
    ident_d = din("ident", [128, 128])
    out_d = nc.dram_tensor("out", [NOWN_T * 128, D], F32, kind="ExternalOutput").ap()
    gscr = nc.dram_tensor("gscr", [1, 2048], F32, kind="Internal").ap()
    wsc = nc.dram_tensor("wscratch", [N_UNITS_QB, 128, 4096], BF16, kind="Internal").ap()

    import contextlib
    es = contextlib.ExitStack()

    def sb(name, shape, dt):
        return es.enter_context(nc.sbuf_tensor(name, list(shape), dt))

    with es:
        KT = sb("KT", [128, NT * 128], BF16)
        Vaug = sb("Vaug", [128, NT, 192], BF16)
        xbuf = [sb(f"xbuf{i}", [128, 4, D], F32) for i in range(2)]
        ropeb = [sb(f"ropeb{i}", [128, 4, 128], F32) for i in range(2)]
        xn = [sb(f"xn{i}", [128, D], BF16) for i in range(2)]
        hT = sb("hT", [128, 8, 512], BF16)
        big = sb("big", [128, 32, 512], BF16)
        ringT = sb("ring", [128, NB * 4096], BF16)
        ring = [ringT[:, i * 4096:(i + 1) * 4096] for i in range(NB)]
        gbc = sb("gbc", [128, 2048], F32)
        scr = [sb(f"scr{i}", [128, 512], F32) for i in range(6)]
        rc = sb("rc", [128, 1024], F32)
        QTb = sb("QTb", [128, 4, 512], BF16)
        Wkv = sb("Wkv", [128, 8, 256], BF16)
        identb = sb("identb", [128, 128], BF16)
        gq = sb("gq", [128, 512], F32)
        gk = sb("gk", [128, 128], F32)
        gmg = sb("gmg", [128, 512], F32)
        wsT = sb("wsTb", [128, 8, 128], BF16)
        bsT = sb("bsTs", [128, 4, 128], F32)
        cT = sb("cTs", [128, 8, 2], F32)
        scT = sb("scT", [128, 8, 2], F32)
        bmodT = sb("bmodTs", [128, 48], F32)
        n1g = sb("n1gs", [128, 8], F32)
        n2g = sb("n2gs", [128, 8], F32)
        modT = sb("modT", [128, 48, 2], F32)
        a1 = sb("a1", [128, 8, 2], F32)
        a2 = sb("a2", [128, 8, 2], F32)
        ones1 = sb("ones1", [1, 128], F32)
        ss = sb("ss", [128, 8], F32)
        rs = sb("rs", [128, 8], F32)
        rstd = sb("rstd", [128, 8], F32)
        hs = sb("hs", [128, 8], F32)
        hl = sb("hl", [128, 8], F32)
        hr = sb("hr", [128, 8], F32)
        ps = es.enter_context(nc.psum_tensor("ps", [128, 4096], F32))
        sqj = [rc[:, 0:512].bitcast(BF16), rc[:, 512:1024].bitcast(BF16)]
        grow = big[0:1, 0:8, :].rearrange("p a b -> p (a b)").bitcast(F32)
        bmodg = big[0:1, 8:16, :].rearrange("p a b -> p (a b)").bitcast(F32)

        esems = {e: es.enter_context(nc.semaphore("sem_" + e)) for e in ["pe", "act", "dve", "pool", "sp"]}
        dnames = [f"c{i}" for i in range(6)] + ["dbg", "gb", "gb2", "const", "cast", "x0", "x1", "r0", "r1", "o0", "o1", "wm0", "wm1", "wm2"] + [f"w{i}" for i in range(NB)]
        dsems = {d: es.enter_context(nc.semaphore("ds_" + d)) for d in dnames}
        block = es.enter_context(nc.Block())

        def bank(b):
            return ps[:, b * 512:(b + 1) * 512]

        def mm(out, lhsT, rhs, start, stop, reads, writes, sgc=False):
            T.add("pe", lambda e: e.matmul(out, lhsT=lhsT, rhs=rhs, start=start, stop=stop, skip_group_check=sgc), reads, writes)

        def tr(out, in_, reads, writes):
            T.add("pe", lambda e: e.transpose(out, in_, identb[:, :]), list(reads) + ["identb"], writes)

        def act(out, in_, func, reads, writes, scale=None, bias=None, accum=None):
            kw = {}
            if scale is not None:
                kw["scale"] = scale
            if bias is not None:
                kw["bias"] = bias
            if accum is not None:
                kw["accum_out"] = accum
            T.add("act", lambda e: e.activation(out=out, in_=in_, func=func, **kw), reads, writes)

        def tt(out, in0, in1, op, reads, writes, eng="dve"):
            T.add(eng, lambda e: e.tensor_tensor(out=out, in0=in0, in1=in1, op=op), reads, writes)

        def ts(out, in0, s1, s2, op0, op1, reads, writes, eng="dve"):
            if op1 is None:
                T.add(eng, lambda e: e.tensor_scalar(out=out, in0=in0, scalar1=s1, scalar2=None, op0=op0), reads, writes)
            else:
                T.add(eng, lambda e: e.tensor_scalar(out=out, in0=in0, scalar1=s1, scalar2=s2, op0=op0, op1=op1), reads, writes)

        def stt(out, in0, scalar, in1, op0, op1, reads, writes):
            T.add("dve", lambda e: e.scalar_tensor_tensor(out=out, in0=in0, scalar=scalar, in1=in1, op0=op0, op1=op1), reads, writes)

        def recip(out, in_, reads, writes):
            T.add("dve", lambda e: e.reciprocal(out=out, in_=in_), reads, writes)

        def cp(out, in_, reads, writes, eng="dve"):
            T.add(eng, lambda e: e.tensor_copy(out=out, in_=in_), reads, writes)

        def dma(q, out, in_, reads, writes, dsem):
            T.add(q, lambda e: e.dma_start(out=out, in_=in_), reads, writes, dsem=dsem)

        def memset(ap, val, writes, eng="pool"):
            T.add(eng, lambda e: e.memset(ap, val), (), writes)

        for (dst, src, nm) in [(cT[:], cT_d, "cT"), (bmodT[:], bmodT_d, "bmodT"), (bmodg[:], bmodg_d, "bmodg"),
                               (n1g[:], n1g_d, "n1g"), (n2g[:], n2g_d, "n2g"), (gq[:], gq_d, "gq"), (gk[:], gk_d, "gk"),
                               (gmg[:], gmg_d, "gmg"), (bsT[:], bsT_d, "bsT")]:
            dma("sp", dst, src, (), [nm], "const")
        dma("pool", identb[:], ident_d, (), ["identb"], "cast")
        dma("pool", Wkv[:], win_d[:, 0:256].rearrange("(c p) n -> p c n", p=128), (), ["Wkv"], "cast")
        dma("pool", wsT[:], wsT_d, (), ["wsT"], "cast")
        memset(Vaug[:, :, 64:128], 1.0, [("Vaug", t) for t in range(NT)])
        memset(ones1[:], 1.0, ["ones1"])

        def wsrc_kn(w, c0, ncols):
            return w[:, c0:c0 + ncols].rearrange("(c p) n -> p c n", p=128)

        unit_src = []
        unit_src.append((wsrc_kn(win_d, 256, 512), 8))
        unit_src.append((wsrc_kn(win_d, 768, 512), 8))
        unit_src.append((wsrc_kn(win_d, 1280, 512), 8))
        unit_src.append((wsrc_kn(win_d, 1792, 512), 8))
        unit_src.append((wsrc_kn(win_d, 2816, 512), 8))
        unit_src.append((bra_d.rearrange("(c p) n -> p c n", p=128), 4))
        unit_src.append((brg_d.rearrange("(c p) n -> p c n", p=128), 4))
        unit_src.append((wsrc_kn(win_d, 2304, 512), 8))
        unit_src.append((wsrc_kn(win_d, 3328, 512), 8))
        unit_src.append((wsrc_kn(wout_d, 0, 512), 8))
        unit_src.append((wsrc_kn(wout_d, 512, 512), 8))
        for j in range(8):
            unit_src.append((wsrc_kn(ff1_d, j * 512, 512), 8))
        for nh in range(2):
            for kg in range(4):
                src = ff2_d[kg * 1024:(kg + 1) * 1024, nh * 512:(nh + 1) * 512].rearrange("(c p) n -> p c n", p=128)
                unit_src.append((src, 8))
        assert len(unit_src) == N_UNITS_QB
        def cast_unit(u, extra_reads=()):
            src, nch = unit_src[u]
            dst = wsc[u].rearrange("p (c n) -> p c n", c=nch)
            dma("pool", dst, src, [("cslot", u % 6)] + list(extra_reads), [("wsc", u), ("cslot", u % 6)], f"c{u % 6}")

        N_EARLY = 11 if nqb else N_UNITS_QB
        for u in range(N_EARLY):
            if "cast" in skip:
                break
            cast_unit(u)

        act(scT[:], cT[:], AF.Silu, ["cT"], ["scT"])
        for c in range(8):
            s_ = c % NB
            pa = ring[s_].bitcast(F32)
            dma("sp", pa, wmod_d[c * 128:(c + 1) * 128, 0:2048], (), [("ring", s_)], f"w{s_}")
            for jj in range(16):
                mm(ps[:, 2 * jj:2 * jj + 2], pa[:, jj * 128:(jj + 1) * 128], scT[:, c, :], c == 0 and jj == 0, c == 7 and jj == 15,
                   [("ring", s_), "scT"], [("ps", 0)], sgc=True)
        tt(modT[:, 0:16, :], ps[:, 0:32].rearrange("p (j k) -> p j k", k=2),
           bmodT[:, 0:16].unsqueeze(2).broadcast_to([128, 16, 2]), ALU.add, [("ps", 0), "bmodT"], ["modTa"])
        stt(a1[:], modT[:, 8:16, :], 1.0, n1g[:, :].unsqueeze(2).broadcast_to([128, 8, 2]), ALU.add, ALU.mult, ["modTa", "n1g"], ["a1"])

        def mod_b_buf(c):
            p_ = c % 2
            return p_, ringT[:, (2 * p_) * 4096:(2 * p_ + 2) * 4096].bitcast(F32)

        def mod_b_dma(c):
            p_, pb = mod_b_buf(c)
            dma("sp", pb, wmod_d[c * 128:(c + 1) * 128, 2048:6144], (), [("ring", 2 * p_), ("ring", 2 * p_ + 1)], f"wm{p_}")

        def mod_b_mm(c, j0, j1):
            p_, pb = mod_b_buf(c)
            for jj in range(j0, j1):
                mm(ps[:, 7 * 512 + 2 * jj:7 * 512 + 2 * jj + 2], pb[:, jj * 128:(jj + 1) * 128], scT[:, c, :], c == 0 and jj == 0, c == 7 and jj == 31,
                   [("ring", 2 * p_), ("ring", 2 * p_ + 1), "scT"], [("ps", 7)], sgc=True)

        def mod_b_piece(c):
            mod_b_dma(c)
            mod_b_mm(c, 0, 32)

        def mod_b_finish():
            tt(modT[:, 16:48, :], ps[:, 7 * 512:7 * 512 + 64].rearrange("p (j k) -> p j k", k=2),
               bmodT[:, 16:48].unsqueeze(2).broadcast_to([128, 32, 2]), ALU.add, [("ps", 7), "bmodT"], ["modTb"])
            stt(a2[:], modT[:, 32:40, :], 1.0, n2g[:, :].unsqueeze(2).broadcast_to([128, 8, 2]), ALU.add, ALU.mult, ["modTb", "n2g"], ["a2"])
            for k_, j0 in enumerate((16, 40)):
                T.add("pool", lambda e, k_=k_, j0=j0: e.dma_start(out=gscr[0, k_ * 1024:(k_ + 1) * 1024].rearrange("(c p) -> p c", p=128),
                                                             in_=modT[:, j0:j0 + 8, 0], allow_slow_non_contiguous=True),
                      ["modTb"], [("gscr", k_)], dsem="gb")
            dma("pool", gbc[:], gscr[0, :].partition_broadcast(128), [("gscr", 0), ("gscr", 1)], ["gbc"], "gb2")

        cnt = {"xn": 0, "ev": 0, "sq": 0, "par": 0}

        def nm_stats(xb, xres, nt):
            par = cnt["par"] % 2
            cnt["par"] += 1
            po = par * 4
            for t in range(nt):
                kq = cnt["sq"] % 2
                cnt["sq"] += 1
                act(sqj[kq], xb[:, t, :], AF.Square, [xres], [("ss", par, t), ("rcj", kq)], accum=ss[:, po + t:po + t + 1])
            act(rs[:, po:po + nt], ss[:, po:po + nt], AF.Ln, [("ss", par, t) for t in range(nt)], [("rs", par)], scale=1.0 / D, bias=EPS)
            act(rstd[:, po:po + nt], rs[:, po:po + nt], AF.Exp, [("rs", par)], [("rstd", par)], scale=-0.5)
            return dict(xb=xb, xres=xres, nt=nt, par=par, po=po)

        def xn_tile(xb, xres, t, k, rs_ap, rs_key, pb=0):
            if t % 2 == 0:
                ts(xn[k][:], xb[:, t, :], rs_ap, None, ALU.mult, None, [xres, rs_key], [("xn", k)])
            else:
                act(xn[k][:], xb[:, t, :], AF.Copy, [xres, rs_key], [("xn", k)], scale=rs_ap)
            for c in range(8):
                bk = pb + c // 2
                o = bank(bk).bitcast(BF16)[:, (c % 2) * 512 + t * 128:(c % 2) * 512 + (t + 1) * 128]
                tr(o, xn[k][:, c * 128:(c + 1) * 128], [("xn", k)], [("ps", bk)])

        def nm_xn_tr(cx):
            xb, xres, nt, par, po = cx["xb"], cx["xres"], cx["nt"], cx["par"], cx["po"]
            for t in range(nt):
                k = cnt["xn"] % 2
                cnt["xn"] += 1
                xn_tile(xb, xres, t, k, rstd[:, po + t:po + t + 1], ("rstd", par))

        def nm_evac(cx, a_t, sh_off, col, hd=None, pb=0):
            nt = cx["nt"]
            hdst, hkey = hd if hd is not None else (hT, "hT")
            for c in range(8):
                bk = pb + c // 2
                src = bank(bk).bitcast(BF16)[:, (c % 2) * 512:(c % 2) * 512 + nt * 128]
                if c % 2 == 0:
                    act(hdst[:, c, 0:nt * 128], src, AF.Identity, [("ps", bk), a_t[1], a_t[2]], [(hkey, c)],
                        scale=a_t[0][:, c, col:col + 1], bias=modT[:, sh_off + c, col:col + 1])
                else:
                    ts(hdst[:, c, 0:nt * 128], src, a_t[0][:, c, col:col + 1], modT[:, sh_off + c, col:col + 1], ALU.mult, ALU.add,
                       [("ps", bk), a_t[1], a_t[2]], [(hkey, c)])

        def norm_mod_T(xb, xres, nt, a_t, sh_off, col, hd=None):
            cx = nm_stats(xb, xres, nt)
            nm_xn_tr(cx)
            nm_evac(cx, a_t, sh_off, col, hd)

        def head_rstd(src_sq, nh, res_in):
            T.add("dve", lambda e: e.tensor_reduce(out=hs[:, 0:nh], in_=src_sq.rearrange("p (h d) -> p h d", d=64), axis=AX.X, op=ALU.add),
                  [res_in], ["hs"])
            act(hl[:, 0:nh], hs[:, 0:nh], AF.Ln, ["hs"], ["hl"], scale=1.0 / 64, bias=EPS)
            act(hr[:, 0:nh], hl[:, 0:nh], AF.Exp, ["hl"], ["hr"], scale=-0.5)

        def norm_rope(psrc, psres, nh, gain, gres, rp, rpres, outb, outres):
            W = nh * 64
            s0, s1, s2, s3 = scr[0][:, 0:W], scr[1][:, 0:W], scr[2][:, 0:W], scr[3][:, 0:W]
            act(s0, psrc, AF.Square, [psres], [("scr", 0)])
            head_rstd(s0, nh, ("scr", 0))
            tt(s1, psrc, gain, ALU.mult, [psres, gres], [("scr", 1)])
            tt(s2.rearrange("p (h d) -> p h d", d=64), s1.rearrange("p (h d) -> p h d", d=64),
               hr[:, 0:nh].unsqueeze(2).broadcast_to([128, nh, 64]), ALU.mult, [("scr", 1), "hr"], [("scr", 2)])
            v2 = s2.rearrange("p (h d) -> p h d", d=64)
            tt(s3.rearrange("p (h d) -> p h d", d=64), v2, rp[:, 0:64].unsqueeze(1).broadcast_to([128, nh, 64]), ALU.mult,
               [("scr", 2), rpres], [("scr", 3)])
            v0 = s0.rearrange("p (h d) -> p h d", d=64)
            tt(v0[:, :, 0:32], v2[:, :, 32:64], rp[:, 64:96].unsqueeze(1).broadcast_to([128, nh, 32]), ALU.mult,
               [("scr", 2), rpres], [("scr", 0)])
            tt(v0[:, :, 32:64], v2[:, :, 0:32], rp[:, 96:128].unsqueeze(1).broadcast_to([128, nh, 32]), ALU.mult,
               [("scr", 2), rpres], [("scr", 0)])
            tt(outb, s3, s0, ALU.add, [("scr", 3), ("scr", 0)], [outres])

        def dump(name, ap, res, dt=F32):
            if "nodump" in skip:
                return
            d = nc.dram_tensor("dbg_" + name, list(ap.shape), F32, kind="ExternalOutput").ap()
            dbg[name] = d
            dma("pool", d, ap, res, [("dbgout", name)], "dbg")

        ld = {"n": 0}

        def load_xo(row0, nt, s):
            dma("sp", xbuf[s][:, 0:nt, :], xin[row0:row0 + nt * 128, :].rearrange("(t p) d -> p t d", p=128), (), [("xbuf", s)], f"x{s}")

        def load_rope(row0, nt, s):
            dma("sp", ropeb[s][:, 0:nt, :], rope[row0:row0 + nt * 128, :].rearrange("(t p) d -> p t d", p=128), (), [("ropeb", s)], f"r{s}")

        def load_x(row0, nt):
            s = ld["n"] % 2
            ld["n"] += 1
            load_xo(row0, nt, s)
            load_rope(row0, nt, s)
            return s

        supers = [(0, 2, 1)] + [(256 + i * 512, 4, 0) for i in range(16)]
        if stage == 0:
            supers = []
            for c in range(8):
                mod_b_piece(c)
            mod_b_finish()
            dump("modT", modT[:], ["modTa", "modTb"])
            dump("gbc", gbc[:], ["gbc"])
            dump("a1", a1[:], ["a1"])
        if stage == 1:
            import os
            supers = supers[:int(os.environ.get("NSUP", "3"))]
        krb = big[:, 24:26, :].rearrange("p a b -> p (a b)")
        nsup = len(supers)
        hbufs = [(hT, "hT"), (big[:, 0:8, :], "big")]
        a1t = (a1, "a1", "modTa")

        def a_kv(si):
            row0, nt, col = supers[si]
            hA, hAk = hbufs[si % 2]
            for t in range(nt):
                bk = 4 + t // 2
                for c in range(8):
                    mm(ps[:, bk * 512 + (t % 2) * 256: bk * 512 + (t % 2) * 256 + 256], hA[:, c, t * 128:(t + 1) * 128], Wkv[:, c, :],
                       c == 0, c == 7, [(hAk, c), "Wkv"], [("ps", bk)])
                if 1 <= si <= 8:
                    mod_b_mm(si - 1, 8 * t, 8 * t + 8)

        def a_post(si):
            row0, nt, col = supers[si]
            slot = si % 2
            tile0 = row0 // 128
            W = nt * 128
            kvv = ps[:, 4 * 512:4 * 512 + nt * 256].rearrange("p (t n) -> p t n", n=256)
            kview = kvv[:, :, 0:128]
            kvb = [("ps", 4)] + ([("ps", 5)] if nt > 2 else [])
            s0, s1, s2, s3 = scr[0][:, 0:W], scr[1][:, 0:W], scr[2][:, 0:W], scr[3][:, 0:W]
            tv = lambda a: a.rearrange("p (t n) -> p t n", n=128)
            hv = lambda a: a.rearrange("p (h d) -> p h d", d=64)
            qv = lambda a: a.rearrange("p (t h d) -> p t h d", h=2, d=64)
            rp = ropeb[slot]
            rpk = ("ropeb", slot)
            act(tv(s0), kview, AF.Square, kvb, [("scr", 0)])
            head_rstd(s0, 2 * nt, ("scr", 0))
            tt(tv(s1), kview, gk[:, :].unsqueeze(1).broadcast_to([128, nt, 128]), ALU.mult, kvb + ["gk"], [("scr", 1)])
            tt(hv(s2), hv(s1), hr[:, 0:2 * nt].unsqueeze(2).broadcast_to([128, 2 * nt, 64]), ALU.mult, [("scr", 1), "hr"], [("scr", 2)])
            tt(qv(s3), qv(s2), rp[:, 0:nt, 0:64].unsqueeze(2).broadcast_to([128, nt, 2, 64]), ALU.mult, [("scr", 2), rpk], [("scr", 3)])
            tt(qv(s0)[:, :, :, 0:32], qv(s2)[:, :, :, 32:64], rp[:, 0:nt, 64:96].unsqueeze(2).broadcast_to([128, nt, 2, 32]), ALU.mult,
               [("scr", 2), rpk], [("scr", 0)])
            tt(qv(s0)[:, :, :, 32:64], qv(s2)[:, :, :, 0:32], rp[:, 0:nt, 96:128].unsqueeze(2).broadcast_to([128, nt, 2, 32]), ALU.mult,
               [("scr", 2), rpk], [("scr", 0)])
            tt(krb[:, 0:W], s3, s0, ALU.add, [("scr", 3), ("scr", 0)], [("big", 24)])
            for t in range(nt):
                tr(bank(6).bitcast(BF16)[:, t * 128:(t + 1) * 128], krb[:, t * 128:(t + 1) * 128], [("big", 24)], [("ps", 6)])
            act(Vaug[:, tile0:tile0 + nt, 0:64], kvv[:, :, 128:192], AF.Copy, kvb, [("Vaug", tile0 + t) for t in range(nt)])
            act(Vaug[:, tile0:tile0 + nt, 128:192], kvv[:, :, 192:256], AF.Copy, kvb, [("Vaug", tile0 + t) for t in range(nt)])
            cp(KT[:, tile0 * 128:(tile0 + nt) * 128], bank(6).bitcast(BF16)[:, 0:nt * 128], [("ps", 6)], [("KT", si)])

        actx = {}
        if nsup:
            for k_ in range(min(2, nsup)):
                load_xo(supers[k_][0], supers[k_][1], k_ % 2)
                load_rope(supers[k_][0], supers[k_][1], k_ % 2)
            ld["n"] = 0
            actx[0] = nm_stats(xbuf[0], ("xbuf", 0), supers[0][1])
            nm_xn_tr(actx[0])
            if nsup > 2:
                load_xo(supers[2][0], supers[2][1], 0)
            nm_evac(actx[0], a1t, 0, supers[0][2], hbufs[0])
        for k_ in range(nsup):
            if k_ < 8:
                mod_b_dma(k_)
            n_ = k_ + 1
            if n_ < nsup:
                actx[n_] = nm_stats(xbuf[n_ % 2], ("xbuf", n_ % 2), supers[n_][1])
            a_kv(k_)
            if n_ < nsup:
                nm_xn_tr(actx[n_])
                if k_ + 3 < nsup:
                    load_xo(supers[k_ + 3][0], supers[k_ + 3][1], (k_ + 3) % 2)
            a_post(k_)
            if k_ + 2 < nsup:
                load_rope(supers[k_ + 2][0], supers[k_ + 2][1], k_ % 2)
            if n_ < nsup:
                nm_evac(actx[n_], a1t, 0, supers[n_][2], hbufs[n_ % 2])

        ld["n"] = 1
        slot = load_x(256, 4) if (nqb and stage > 1) else None
        if stage >= 1:
            if len(supers) < 9:
                for c in range(max(len(supers) - 1, 0), 8):
                    if c >= len(supers):
                        mod_b_dma(c)
                    mod_b_mm(c, 0, 32)
            mod_b_finish()
        if stage == 1:
            dump("KT", KT[:, 0:1280], [("KT", i) for i in range(3)], BF16)
            dump("Vaug", Vaug[:, 0:10, :], [("Vaug", i) for i in range(10)], BF16)
            dump("hT", hT[:], [("hT", c) for c in range(8)], BF16)
        if stage <= 1:
            nqb = 0
        wctr = {"n": 0}

        def load_unit(u):
            s = wctr["n"] % NB
            wctr["n"] += 1
            dma("sp", ring[s][:, :], wsc[u], [("wsc", u)], [("ring", s)], f"w{s}")
            return s

        def R(s):
            return ("ring", s)

        def unit8(s):
            return ring[s][:, :].rearrange("p (c n) -> p c n", c=8)

        def unit4(s):
            return ring[s][:, :].rearrange("p (c n) -> p c n", c=4)

        yT = [big[:, c, :] for c in range(8)]
        uT = [big[:, 8 + j, :] for j in range(4)]
        gmT = [big[:, 12 + j, :] for j in range(4)]
        attnT = [big[:, 16 + j, :] for j in range(4)]
        QT = [QTb[:, j, :] for j in range(4)]
        vnb = [big[:, 24 + t, :] for t in range(4)]
        PT = [big[:, 28:30, :].rearrange("p a b -> p (a b)"), big[:, 30:32, :].rearrange("p a b -> p (a b)"),
              big[:, 26:28, :].rearrange("p a b -> p (a b)")]
        PTK = [[("big", 28), ("big", 29)], [("big", 30), ("big", 31)], [("big", 26), ("big", 27)]]

        def pre_q_pieces(slot_):
            xb_, xres_, rpb_ = xbuf[slot_], ("xbuf", slot_), ropeb[slot_]
            st = {}
            qrb = scr[4][:, :].bitcast(BF16)[:, 0:512]

            def p_stats():
                st["cx"] = nm_stats(xb_, xres_, 4)

            def p_xn(t):
                cx = st["cx"]
                k = cnt["xn"] % 2
                cnt["xn"] += 1
                xn_tile(xb_, xres_, t, k, rstd[:, cx["po"] + t:cx["po"] + t + 1], ("rstd", cx["par"]), pb=4)

            def p_evac():
                nm_evac(st["cx"], (a1, "a1", "modTa"), 0, 0, None, pb=4)

            def p_qproj():
                su = load_unit(0)
                for t in range(4):
                    for c in range(8):
                        mm(bank(4 + t), hT[:, c, t * 128:(t + 1) * 128], unit8(su)[:, c, :], c == 0, c == 7, [("hT", c), R(su)], [("ps", 4 + t)])

            def p_qdve(t):
                norm_rope(bank(4 + t), ("ps", 4 + t), 8, gq[:, :], "gq", rpb_[:, t, :], ("ropeb", slot_), qrb, ("scr", 4))

            def p_qpe(t):
                for g in range(4):
                    tr(bank(4 + t).bitcast(BF16)[:, g * 128:(g + 1) * 128], qrb[:, g * 128:(g + 1) * 128], [("scr", 4)], [("ps", 4 + t)])
                cp(QTb[:, :, t * 128:(t + 1) * 128], bank(4 + t).bitcast(BF16)[:, 0:512].rearrange("p (g q) -> p g q", q=128),
                   [("ps", 4 + t)], [("QT", g) for g in range(4)])

            return dict(stats=p_stats, xn=p_xn, evac=p_evac, qproj=p_qproj, qdve=p_qdve, qpe=p_qpe)

        def run_pre_q_all(pq):
            pq["stats"]()
            for t in range(4):
                pq["xn"](t)
            pq["evac"]()
            pq["qproj"]()
            for t in range(4):
                pq["qdve"](t)
                pq["qpe"](t)

        if nqb:
            run_pre_q_all(pre_q_pieces(slot))

        for qb in range(nqb):
            row0 = 256 + qb * 512
            xb = xbuf[slot]
            xres = ("xbuf", slot)
            rpb = ropeb[slot]
            nslot = None
            sv = load_unit(1)
            for t in range(4):
                for c in range(8):
                    mm(bank(t), hT[:, c, t * 128:(t + 1) * 128], unit8(sv)[:, c, :], c == 0, c == 7, [("hT", c), R(sv)], [("ps", t)])
            for t in range(4):
                gv = scr[1][:, :]
                act(gv, bank(t), AF.Gelu_apprx_tanh, [("ps", t)], [("scr", 1)])
                act(scr[0][:, :], gv, AF.Square, [("scr", 1)], [("scr", 0)])
                head_rstd(scr[0][:, :], 8, ("scr", 0))
                tt(scr[2][:, :], gv, gmg[:, :], ALU.mult, [("scr", 1), "gmg"], [("scr", 2)])
                tt(vnb[t].rearrange("p (h d) -> p h d", d=64), scr[2][:, :].rearrange("p (h d) -> p h d", d=64),
                   hr[:, 0:8].unsqueeze(2).broadcast_to([128, 8, 64]), ALU.mult, [("scr", 2), "hr"], [("big", 24 + t)])
            for t in range(4):
                for j in range(4):
                    for gg in range(2):
                        g = 2 * j + gg
                        mm(ps[gg * 64:(gg + 1) * 64, j * 512 + t * 128: j * 512 + (t + 1) * 128], vnb[t][:, g * 64:(g + 1) * 64], wsT[:, g, :],
                           True, True, [("big", 24 + t), "wsT"], [("ps", j)])
            suu = load_unit(2)
            for j in range(4):
                for c in range(8):
                    mm(bank(4 + j), unit8(suu)[:, c, j * 128:(j + 1) * 128], hT[:, c, :], c == 0, c == 7, [("hT", c), R(suu)], [("ps", 4 + j)])
                act(uT[j], bank(4 + j), AF.Gelu_apprx_tanh, [("ps", 4 + j)], [("big", 8 + j)])
                tt(scr[5][:, :].rearrange("p (t q) -> p t q", q=128), bank(j).rearrange("p (t q) -> p t q", q=128),
                   bsT[:, j, :].unsqueeze(1).broadcast_to([128, 4, 128]), ALU.add, [("ps", j), "bsT"], [("scr", 5)])
                tt(gmT[j], scr[5][:, :], uT[j], ALU.mult, [("scr", 5), ("big", 8 + j)], [("big", 12 + j)])

            steps = [(g, kb) for g in range(4) for kb in range(NT)]

            def ksup(kb):
                return 0 if kb < 2 else 1 + (kb - 2) // 4

            def qk(i):
                g, kb = steps[i]
                sbk = (i % 2) * 2
                mm(bank(sbk), KT[0:64, kb * 128:(kb + 1) * 128], QT[g][0:64, :], True, True, [("KT", ksup(kb)), ("QT", g)], [("ps", sbk)])
                mm(bank(sbk + 1), KT[64:128, kb * 128:(kb + 1) * 128], QT[g][64:128, :], True, True, [("KT", ksup(kb)), ("QT", g)], [("ps", sbk + 1)])
                pace = [("pace", i)] if (qb == 0 and i % 12 == 0) else []
                act(PT[i % 3], ps[:, sbk * 512:sbk * 512 + 1024], AF.Exp, [("ps", sbk), ("ps", sbk + 1)], PTK[i % 3] + pace, scale=0.125)
                if pace and N_EARLY + i // 12 < N_UNITS_QB:
                    cast_unit(N_EARLY + i // 12, pace)

            def pv(i):
                g, kb = steps[i]
                oa = 4 + 2 * (g % 2)
                mm(bank(oa), Vaug[:, kb, 0:128], PT[i % 3][:, 0:512], kb == 0, kb == NT - 1, [("Vaug", kb)] + PTK[i % 3], [("ps", oa)])
                mm(bank(oa + 1), Vaug[:, kb, 64:192], PT[i % 3][:, 512:1024], kb == 0, kb == NT - 1, [("Vaug", kb)] + PTK[i % 3], [("ps", oa + 1)])
                if kb == NT - 1:
                    recip(rc[64:128, 0:512], bank(oa)[64:128, :], [("ps", oa)], ["rc", ("rcj", 0), ("rcj", 1)])
                    recip(rc[0:64, 512:1024], bank(oa + 1)[0:64, :], [("ps", oa + 1)], ["rc", ("rcj", 0), ("rcj", 1)])
                    tt(attnT[g][0:64, :], bank(oa)[0:64, :], rc[64:128, 0:512], ALU.mult, [("ps", oa), "rc", ("rcj", 0), ("rcj", 1)], [("big", 16 + g)])
                    tt(attnT[g][64:128, :], bank(oa + 1)[64:128, :], rc[0:64, 512:1024], ALU.mult, [("ps", oa + 1), "rc", ("rcj", 0), ("rcj", 1)], [("big", 16 + g)])

            qk(0)
            qk(1)
            for i in range(len(steps)):
                if i + 2 < len(steps):
                    qk(i + 2)
                pv(i)

            if qb + 1 < nqb:
                nslot = load_x(row0 + 512, 4)

            sga = [load_unit(3), None]
            sgb = [load_unit(4), None]
            sbra = load_unit(5)
            sbrg = load_unit(6)
            for m in range(8):
                if m == 4:
                    sga[1] = load_unit(7)
                    sgb[1] = load_unit(8)
                h = m // 4
                b0 = (m % 2) * 4
                for c in range(8):
                    mm(bank(b0), unit8(sga[h])[:, c, (m % 4) * 128:(m % 4 + 1) * 128], hT[:, c, :], c == 0, c == 7, [("hT", c), R(sga[h])], [("ps", b0)])
                for c in range(8):
                    mm(bank(b0 + 1), unit8(sgb[h])[:, c, (m % 4) * 128:(m % 4 + 1) * 128], hT[:, c, :], c == 0, c == 7, [("hT", c), R(sgb[h])], [("ps", b0 + 1)])
                for c in range(4):
                    mm(bank(b0 + 2), unit4(sbra)[:, c, m * 128:(m + 1) * 128], attnT[c], c == 0, c == 3, [("big", 16 + c), R(sbra)], [("ps", b0 + 2)])
                for c in range(4):
                    mm(bank(b0 + 3), unit4(sbrg)[:, c, m * 128:(m + 1) * 128], gmT[c], c == 0, c == 3, [("big", 12 + c), R(sbrg)], [("ps", b0 + 3)])
                act(scr[0][:, :], bank(b0), AF.Sigmoid, [("ps", b0)], [("scr", 0)])
                act(scr[1][:, :], bank(b0 + 1), AF.Sigmoid, [("ps", b0 + 1)], [("scr", 1)])
                tt(scr[2][:, :], bank(b0 + 2), scr[0][:, :], ALU.mult, [("ps", b0 + 2), ("scr", 0)], [("scr", 2)])
                tt(scr[3][:, :], bank(b0 + 3), scr[1][:, :], ALU.mult, [("ps", b0 + 3), ("scr", 1)], [("scr", 3)])
                tt(yT[m], scr[2][:, :], scr[3][:, :], ALU.add, [("scr", 2), ("scr", 3)], [("big", m)], eng="pool")

            so = [load_unit(9), load_unit(10)]
            par = cnt["par"] % 2
            cnt["par"] += 1
            po = par * 4
            k8c = {"n": 0}

            def b8_mm(t):
                for nh in range(2):
                    bk = 4 + k8c["n"] % 4
                    k8c["n"] += 1
                    for c in range(8):
                        mm(bank(bk), yT[c][:, t * 128:(t + 1) * 128], unit8(so[nh])[:, c, :], c == 0, c == 7, [("big", c), R(so[nh])], [("ps", bk)])
                    sk = 4 + (k8c["n"] % 2)
                    tt(scr[sk][:, :], bank(bk), gbc[:, nh * 512:(nh + 1) * 512], ALU.mult, [("ps", bk), "gbc"], [("scr", sk)])
                    tt(xb[:, t, nh * 512:(nh + 1) * 512], scr[sk][:, :], xb[:, t, nh * 512:(nh + 1) * 512], ALU.add, [("scr", sk), xres], [xres, ("x1t", t)], eng="pool")

            def b9_tile(t):
                kq = cnt["sq"] % 2
                cnt["sq"] += 1
                cs = slice(po + t, po + t + 1)
                act(sqj[kq], xb[:, t, :], AF.Square, [("x1t", t)], [("ss", par, t), ("rcj", kq)], accum=ss[:, cs])
                act(rs[:, cs], ss[:, cs], AF.Ln, [("ss", par, t)], [("rs", par), ("rs", par, t)], scale=1.0 / D, bias=EPS)
                act(rstd[:, cs], rs[:, cs], AF.Exp, [("rs", par, t)], [("rstd", par), ("rstd", par, t)], scale=-0.5)
                k = cnt["xn"] % 2
                cnt["xn"] += 1
                xn_tile(xb, ("x1t", t), t, k, rstd[:, cs], ("rstd", par, t))

            for t in range(4):
                b8_mm(t)
                if t >= 1:
                    b9_tile(t - 1)
            b9_tile(3)
            nm_evac(dict(nt=4), (a2, "a2", "modTb"), 24, 0)

            pq = pre_q_pieces(nslot) if nslot is not None else None
            if pq:
                pq["stats"]()
            for j in range(32):
                if j % 4 == 0:
                    sf = load_unit(11 + j // 4)
                bk = j % 8
                for c in range(8):
                    mm(bank(bk), unit8(sf)[:, c, (j % 4) * 128:(j % 4 + 1) * 128], hT[:, c, :], c == 0, c == 7, [("hT", c), R(sf)], [("ps", bk)])
                sk = j % 4
                act(scr[sk][:, :], bank(bk), AF.Relu, [("ps", bk)], [("scr", sk)])
                tt(big[:, j, :], scr[sk][:, :], scr[sk][:, :], ALU.mult, [("scr", sk)], [("big", j)])

            if pq:
                pq["xn"](0)
                pq["xn"](1)
            blk = 0
            for nh in range(2):
                for kg in range(4):
                    s2 = load_unit(19 + nh * 4 + kg)
                    for t in range(4):
                        bk = t
                        for c in range(8):
                            mm(bank(bk), big[:, kg * 8 + c, t * 128:(t + 1) * 128], unit8(s2)[:, c, :], kg == 0 and c == 0, kg == 3 and c == 7,
                               [("big", kg * 8 + c), R(s2)], [("ps", bk)])
                    if pq:
                        if blk == 0:
                            pq["xn"](2)
                            pq["xn"](3)
                            pq["evac"]()
                        elif blk == 1:
                            pq["qproj"]()
                        elif blk == 2:
                            pq["qdve"](0)
                        elif blk in (3, 4, 5):
                            pq["qpe"](blk - 3)
                            pq["qdve"](blk - 2)
                        elif blk == 6:
                            pq["qpe"](3)
                    blk += 1
                for t in range(4):
                    bk = t
                    sk = 5
                    tt(scr[sk][:, :], bank(bk), gbc[:, 1024 + nh * 512:1024 + (nh + 1) * 512], ALU.mult, [("ps", bk), "gbc"], [("scr", sk)])
                    tt(xb[:, t, nh * 512:(nh + 1) * 512], scr[sk][:, :], xb[:, t, nh * 512:(nh + 1) * 512], ALU.add, [("scr", sk), xres], [xres], eng="pool")
            dma("sp", out_d[qb * 512:(qb + 1) * 512, :].rearrange("(t p) d -> p t d", p=128), xb[:, :, :], [xres], [("out", qb)], f"o{slot}")
            slot = nslot

        T.add("sp", lambda e: e.nop(), [("out", q) for q in range(nqb)] + [("dbgout", n) for n in dbg], [])

        T.finalize()
        nc._tracker = T

        @block.sync
        def _(e):
            T.emit("sp", e, esems, dsems)

        @block.tensor
        def _(e):
            T.emit("pe", e, esems, dsems)

        @block.scalar
        def _(e):
            T.emit("act", e, esems, dsems)

        @block.vector
        def _(e):
            T.emit("dve", e, esems, dsems)

        @block.gpsimd
        def _(e):
            T.emit("pool", e, esems, dsems)

    return nc


_CACHE = {}


def _rope_table(tok_idx):
    n = tok_idx.shape[0]
    t = np.maximum(tok_idx, 0)
    row = (t // 64).astype(np.float32)
    colp = (t % 64).astype(np.float32)
    inv = (np.float32(10000.0) ** (-np.arange(0, 32, 2, dtype=np.float32) / np.float32(32))).astype(np.float32)
    ang = np.concatenate([row[:, None] * inv[None, :], colp[:, None] * inv[None, :]], axis=-1).astype(np.float32)
    cos = np.cos(ang).astype(np.float32)
    sin = np.sin(ang).astype(np.float32)
    ident = tok_idx < 0
    cos[ident] = 1.0
    sin[ident] = 0.0
    return np.concatenate([cos, cos, -sin, sin], axis=1).astype(np.float32)


def kernel(x, c, ctx, c_ctx, w_mod, b_mod, norm1_g, norm2_g, w_in, q_norm_g, k_norm_g,
           gm_norm_g, gm_ws, gm_bs, w_br_attn, w_br_gm, w_out, w_ff1, w_ff2):
    f = lambda a: np.ascontiguousarray(np.asarray(a, dtype=np.float32))
    x, c, ctx, c_ctx = f(x), f(c), f(ctx), f(c_ctx)
    w_mod, b_mod, w_in = f(w_mod)[0], f(b_mod)[0], f(w_in)[0]
    n1, n2 = f(norm1_g)[0], f(norm2_g)[0]
    qg, kg, gmg = f(q_norm_g)[0], f(k_norm_g)[0], f(gm_norm_g)[0]
    ws, bs = f(gm_ws)[0], f(gm_bs)[0]
    bra, brg, wo, w1, w2 = f(w_br_attn)[0], f(w_br_gm)[0], f(w_out)[0], f(w_ff1)[0], f(w_ff2)[0]

    if "nc" not in _CACHE:
        _CACHE["nc"] = build_program()
    nc = _CACHE["nc"]

    qcols = 256 + np.array([kv * 256 + g * 64 + d for g in range(4) for kv in range(2) for d in range(64)])
    order = np.concatenate([np.arange(0, 256), qcols, np.arange(1280, 1792), np.arange(768, 1280), np.arange(1792, 3840)])
    w_in_p = np.ascontiguousarray(w_in[:, order])
    rows = np.array([kv * 256 + g * 64 + d for g in range(4) for kv in range(2) for d in range(64)])
    w_bra_p = np.ascontiguousarray(bra[rows, :])
    b_modT = np.ascontiguousarray(b_mod.reshape(48, 128).T)
    b_modg = np.ascontiguousarray(np.concatenate([b_mod[2048:3072], b_mod[5120:6144]])[None, :])
    n1g = np.ascontiguousarray(n1.reshape(8, 128).T)
    n2g = np.ascontiguousarray(n2.reshape(8, 128).T)
    gq_bc = np.ascontiguousarray(np.broadcast_to(np.tile(qg, 8)[None, :], (128, 512)))
    gk_bc = np.ascontiguousarray(np.broadcast_to(np.tile(kg, 2)[None, :], (128, 128)))
    gmg_bc = np.ascontiguousarray(np.broadcast_to(gmg.reshape(512)[None, :], (128, 512)))
    wsT = np.ascontiguousarray(ws.transpose(2, 0, 1))
    bsT = np.ascontiguousarray(np.broadcast_to(bs.reshape(4, 2, 1, 128), (4, 2, 64, 128)).transpose(1, 2, 0, 3).reshape(128, 4, 128))
    ident = np.eye(128, dtype=np.float32)

    in_maps = []
    for core in range(8):
        b, hf = core // 2, core % 2
        own = np.arange(hf * 4096, (hf + 1) * 4096)
        oth = np.arange((1 - hf) * 4096, (2 - hf) * 4096)
        xin = np.concatenate([ctx[b], x[b, own], x[b, oth]], axis=0)
        tok = np.concatenate([-np.ones(256, dtype=np.int64), own, oth])
        cT = np.ascontiguousarray(np.stack([c[b], c_ctx], axis=1).reshape(8, 128, 2).transpose(1, 0, 2))
        in_maps.append({
            "xin": np.ascontiguousarray(xin), "rope": _rope_table(tok), "cT": cT, "w_mod": w_mod, "b_modT": b_modT,
            "b_modg": b_modg, "n1g": n1g, "n2g": n2g, "w_in_p": w_in_p, "gq_bc": gq_bc, "gk_bc": gk_bc, "gmg_bc": gmg_bc,
            "wsT": wsT, "bsT": bsT, "w_bra_p": w_bra_p, "w_brg": brg, "w_out": wo, "w_ff1": w1, "w_ff2": w2, "ident": ident,
        })
    res = run_bass_kernel_spmd(nc, in_maps, core_ids=list(range(8)))
    out = np.empty((4, 8192, D), dtype=np.float32)
    for core in range(8):
        b, hf = core // 2, core % 2
        out[b, hf * 4096:(hf + 1) * 4096] = res.results[core]["out"]
    return out
```

```python
import numpy as np
import concourse.bass as bass
import concourse.mybir as mybir
from concourse.bass_utils import run_bass_kernel_spmd

F32 = mybir.dt.float32
BF16 = mybir.dt.bfloat16
AF = mybir.ActivationFunctionType
ALU = mybir.AluOpType
AX = mybir.AxisListType

D = 1024
NCTX_T = 2
NOWN_T = 32
NT = 66
NQB = 8
EPS = 1e-6
NB = 5
N_UNITS_QB = 27


class Tracker:
    def __init__(self):
        self.ops = []
        self.last_w = {}
        self.readers = {}
        self.dcount = {}

    def add(self, eng, fn, reads=(), writes=(), dsem=None):
        idx = len(self.ops)
        deps = set()
        if eng in ("act", "dve"):
            writes = list(writes) + [("pslk", r[1]) for r in reads if isinstance(r, tuple) and r[0] == "ps"]
        for r in reads:
            if r in self.last_w:
                deps.add(self.last_w[r])
        for w in writes:
            if w in self.last_w:
                deps.add(self.last_w[w])
            for rd in self.readers.get(w, ()):
                deps.add(rd)
        op = dict(eng=eng, fn=fn, deps=deps, dsem=dsem, marked=False, idx=idx, val=None, desc=(tuple(reads), tuple(writes)))
        if dsem is not None:
            self.dcount[dsem] = self.dcount.get(dsem, 0) + 16
            op["dval"] = self.dcount[dsem]
        self.ops.append(op)
        for r in reads:
            self.readers.setdefault(r, []).append(idx)
        for w in writes:
            self.last_w[w] = idx
            self.readers[w] = []
        return idx

    def finalize(self):
        ops = self.ops
        for op in ops:
            red = {}
            for d in op["deps"]:
                dop = ops[d]
                if dop["dsem"] is not None:
                    key = ("d", dop["dsem"])
                    if key not in red or ops[red[key]]["dval"] < dop["dval"]:
                        red[key] = d
                else:
                    if dop["eng"] == "pe" and op["eng"] == "pe" and op["dsem"] is None:
                        continue
                    key = ("e", dop["eng"])
                    if key not in red or red[key] < d:
                        red[key] = d
            op["rdeps"] = list(red.values())
            for d in op["rdeps"]:
                if ops[d]["dsem"] is None:
                    ops[d]["marked"] = True
        cnt = {}
        for op in ops:
            if op["dsem"] is None and op["marked"]:
                cnt[op["eng"]] = cnt.get(op["eng"], 0) + 1
                op["val"] = cnt[op["eng"]]

    def trace(self, engname):
        waited = {}
        out = []
        for op in self.ops:
            if op["eng"] != engname:
                continue
            ws = []
            for d in op["rdeps"]:
                dop = self.ops[d]
                if dop["dsem"] is not None:
                    val = self.dcount[dop["dsem"]] if dop["dsem"] in ("const", "cast", "gb") else dop["dval"]
                    key = ("d", dop["dsem"])
                else:
                    val, key = dop["val"], ("e", dop["eng"])
                if waited.get(key, 0) < val:
                    ws.append((key[1], val))
                    waited[key] = val
            inc = (op["dsem"], op.get("dval")) if op["dsem"] else ((engname, op["val"]) if op["marked"] else None)
            out.append((op["idx"], ws, op["desc"], inc))
        return out

    def emit(self, engname, engobj, esems, dsems):
        waited = {}
        for op in self.ops:
            if op["eng"] != engname:
                continue
            for d in op["rdeps"]:
                dop = self.ops[d]
                if dop["dsem"] is not None:
                    val = self.dcount[dop["dsem"]] if dop["dsem"] in ("const", "cast", "gb") else dop["dval"]
                    sem, key = dsems[dop["dsem"]], ("d", dop["dsem"])
                else:
                    sem, val, key = esems[dop["eng"]], dop["val"], ("e", dop["eng"])
                if waited.get(key, 0) < val:
                    engobj.wait_ge(sem, val)
                    waited[key] = val
            ins = op["fn"](engobj)
            if op["dsem"] is not None:
                ins.then_inc(dsems[op["dsem"]], 16)
            elif op["marked"]:
                ins.then_inc(esems[engname], 1)


def build_program(stage=3, nqb=NQB, skip=()):
    nc = bass.Bass("TRN2", target_bir_lowering=False)
    T = Tracker()
    dbg = {}

    def din(name, shape, dt=F32):
        return nc.dram_tensor(name, list(shape), dt, kind="ExternalInput").ap()

    xin = din("xin", [NT * 128, D])
    rope = din("rope", [NT * 128, 128])
    cT_d = din("cT", [128, 8, 2])
    wmod_d = din("w_mod", [D, 6 * D])
    bmodT_d = din("b_modT", [128, 48])
    bmodg_d = din("b_modg", [1, 2048])
    n1g_d = din("n1g", [128, 8])
    n2g_d = din("n2g", [128, 8])
    win_d = din("w_in_p", [D, 3840])
    gq_d = din("gq_bc", [128, 512])
    gk_d = din("gk_bc", [128, 128])
    gmg_d = din("gmg_bc", [128, 512])
    wsT_d = din("wsT", [128, 8, 128])
    bsT_d = din("bsT", [128, 4, 128])
    bra_d = din("w_bra_p", [512, D])
    brg_d = din("w_brg", [512, D])
    wout_d = din("w_out", [D, D])
    ff1_d = din("w_ff1", [D, 4 * D])
    ff2_d = din("w_ff2", [4 * D, D])
    ident_d = din("ident", [128, 128])
    out_d = nc.dram_tensor("out", [NOWN_T * 128, D], F32, kind="ExternalOutput").ap()
    gscr = nc.dram_tensor("gscr", [1, 2048], F32, kind="Internal").ap()
    wsc = nc.dram_tensor("wscratch", [N_UNITS_QB, 128, 4096], BF16, kind="Internal").ap()

    import contextlib
    es = contextlib.ExitStack()

    def sb(name, shape, dt):
        return es.enter_context(nc.sbuf_tensor(name, list(shape), dt))

    with es:
        KT = sb("KT", [128, NT * 128], BF16)
        Vaug = sb("Vaug", [128, NT, 192], BF16)
        xbuf = [sb(f"xbuf{i}", [128, 4, D], F32) for i in range(2)]
        ropeb = [sb(f"ropeb{i}", [128, 4, 128], F32) for i in range(2)]
        xn = [sb(f"xn{i}", [128, D], BF16) for i in range(2)]
        hT = sb("hT", [128, 8, 512], BF16)
        big = sb("big", [128, 32, 512], BF16)
        ringT = sb("ring", [128, NB * 4096], BF16)
        ring = [ringT[:, i * 4096:(i + 1) * 4096] for i in range(NB)]
        gbc = sb("gbc", [128, 2048], F32)
        scr = [sb(f"scr{i}", [128, 512], F32) for i in range(6)]
        rc = sb("rc", [128, 1024], F32)
        QTb = sb("QTb", [128, 4, 512], BF16)
        Wkv = sb("Wkv", [128, 8, 256], BF16)
        identb = sb("identb", [128, 128], BF16)
        gq = sb("gq", [128, 512], F32)
        gk = sb("gk", [128, 128], F32)
        gmg = sb("gmg", [128, 512], F32)
        wsT = sb("wsTb", [128, 8, 128], BF16)
        bsT = sb("bsTs", [128, 4, 128], F32)
        cT = sb("cTs", [128, 8, 2], F32)
        scT = sb("scT", [128, 8, 2], F32)
        bmodT = sb("bmodTs", [128, 48], F32)
        n1g = sb("n1gs", [128, 8], F32)
        n2g = sb("n2gs", [128, 8], F32)
        modT = sb("modT", [128, 48, 2], F32)
        a1 = sb("a1", [128, 8, 2], F32)
        a2 = sb("a2", [128, 8, 2], F32)
        ones1 = sb("ones1", [1, 128], F32)
        ss = sb("ss", [128, 8], F32)
        rs = sb("rs", [128, 8], F32)
        rstd = sb("rstd", [128, 8], F32)
        hs = sb("hs", [128, 8], F32)
        hl = sb("hl", [128, 8], F32)
        hr = sb("hr", [128, 8], F32)
        hs32 = sb("hs32", [128, 32], F32)
        hl32 = sb("hl32", [128, 32], F32)
        hr32 = sb("hr32", [128, 32], F32)
        ps = es.enter_context(nc.psum_tensor("ps", [128, 4096], F32))
        sqj = [rc[:, 0:512].bitcast(BF16), rc[:, 512:1024].bitcast(BF16)]
        grow = big[0:1, 0:8, :].rearrange("p a b -> p (a b)").bitcast(F32)
        bmodg = big[0:1, 8:16, :].rearrange("p a b -> p (a b)").bitcast(F32)

        esems = {e: es.enter_context(nc.semaphore("sem_" + e)) for e in ["pe", "act", "dve", "pool", "sp"]}
        dnames = [f"c{i}" for i in range(6)] + ["dbg", "gb", "gb2", "const", "cast", "x0", "x1", "r0", "r1", "o0", "o1", "wm0", "wm1", "wm2"] + [f"w{i}" for i in range(NB)]
        dsems = {d: es.enter_context(nc.semaphore("ds_" + d)) for d in dnames}
        block = es.enter_context(nc.Block())

        def bank(b):
            return ps[:, b * 512:(b + 1) * 512]

        def mm(out, lhsT, rhs, start, stop, reads, writes, sgc=False):
            T.add("pe", lambda e: e.matmul(out, lhsT=lhsT, rhs=rhs, start=start, stop=stop, skip_group_check=sgc), reads, writes)

        def tr(out, in_, reads, writes):
            T.add("pe", lambda e: e.transpose(out, in_, identb[:, :]), list(reads) + ["identb"], writes)

        def act(out, in_, func, reads, writes, scale=None, bias=None, accum=None):
            kw = {}
            if scale is not None:
                kw["scale"] = scale
            if bias is not None:
                kw["bias"] = bias
            if accum is not None:
                kw["accum_out"] = accum
            T.add("act", lambda e: e.activation(out=out, in_=in_, func=func, **kw), reads, writes)

        def tt(out, in0, in1, op, reads, writes, eng="dve"):
            T.add(eng, lambda e: e.tensor_tensor(out=out, in0=in0, in1=in1, op=op), reads, writes)

        def ts(out, in0, s1, s2, op0, op1, reads, writes, eng="dve"):
            if op1 is None:
                T.add(eng, lambda e: e.tensor_scalar(out=out, in0=in0, scalar1=s1, scalar2=None, op0=op0), reads, writes)
            else:
                T.add(eng, lambda e: e.tensor_scalar(out=out, in0=in0, scalar1=s1, scalar2=s2, op0=op0, op1=op1), reads, writes)

        def stt(out, in0, scalar, in1, op0, op1, reads, writes):
            T.add("dve", lambda e: e.scalar_tensor_tensor(out=out, in0=in0, scalar=scalar, in1=in1, op0=op0, op1=op1), reads, writes)

        def recip(out, in_, reads, writes):
            T.add("dve", lambda e: e.reciprocal(out=out, in_=in_), reads, writes)

        def cp(out, in_, reads, writes, eng="dve"):
            T.add(eng, lambda e: e.tensor_copy(out=out, in_=in_), reads, writes)

        def dma(q, out, in_, reads, writes, dsem):
            T.add(q, lambda e: e.dma_start(out=out, in_=in_), reads, writes, dsem=dsem)

        def memset(ap, val, writes, eng="pool"):
            T.add(eng, lambda e: e.memset(ap, val), (), writes)

        for (dst, src, nm) in [(cT[:], cT_d, "cT"), (bmodT[:], bmodT_d, "bmodT"), (bmodg[:], bmodg_d, "bmodg"),
                               (n1g[:], n1g_d, "n1g"), (n2g[:], n2g_d, "n2g"), (gq[:], gq_d, "gq"), (gk[:], gk_d, "gk"),
                               (gmg[:], gmg_d, "gmg"), (bsT[:], bsT_d, "bsT")]:
            dma("sp", dst, src, (), [nm], "const")
        dma("pool", identb[:], ident_d, (), ["identb"], "cast")
        dma("pool", Wkv[:], win_d[:, 0:256].rearrange("(c p) n -> p c n", p=128), (), ["Wkv"], "cast")
        dma("pool", wsT[:], wsT_d, (), ["wsT"], "cast")
        memset(Vaug[:, :, 64:128], 1.0, [("Vaug", t) for t in range(NT)])
        memset(ones1[:], 1.0, ["ones1"])

        def wsrc_kn(w, c0, ncols):
            return w[:, c0:c0 + ncols].rearrange("(c p) n -> p c n", p=128)

        unit_src = []
        unit_src.append((wsrc_kn(win_d, 256, 512), 8))
        unit_src.append((wsrc_kn(win_d, 768, 512), 8))
        unit_src.append((wsrc_kn(win_d, 1280, 512), 8))
        unit_src.append((wsrc_kn(win_d, 1792, 512), 8))
        unit_src.append((wsrc_kn(win_d, 2816, 512), 8))
        unit_src.append((bra_d.rearrange("(c p) n -> p c n", p=128), 4))
        unit_src.append((brg_d.rearrange("(c p) n -> p c n", p=128), 4))
        unit_src.append((wsrc_kn(win_d, 2304, 512), 8))
        unit_src.append((wsrc_kn(win_d, 3328, 512), 8))
        unit_src.append((wsrc_kn(wout_d, 0, 512), 8))
        unit_src.append((wsrc_kn(wout_d, 512, 512), 8))
        for j in range(8):
            unit_src.append((wsrc_kn(ff1_d, j * 512, 512), 8))
        for nh in range(2):
            for kg in range(4):
                src = ff2_d[kg * 1024:(kg + 1) * 1024, nh * 512:(nh + 1) * 512].rearrange("(c p) n -> p c n", p=128)
                unit_src.append((src, 8))
        assert len(unit_src) == N_UNITS_QB
        def cast_unit(u, extra_reads=()):
            src, nch = unit_src[u]
            dst = wsc[u].rearrange("p (c n) -> p c n", c=nch)
            dma("pool", dst, src, [("cslot", u % 6)] + list(extra_reads), [("wsc", u), ("cslot", u % 6)], f"c{u % 6}")

        N_EARLY = 11 if nqb else N_UNITS_QB
        for u in range(N_EARLY):
            if "cast" in skip:
                break
            cast_unit(u)

        act(scT[:], cT[:], AF.Silu, ["cT"], ["scT"])
        for c in range(8):
            s_ = c % NB
            pa = ring[s_].bitcast(F32)
            dma("sp", pa, wmod_d[c * 128:(c + 1) * 128, 0:2048], (), [("ring", s_)], f"w{s_}")
            for jj in range(16):
                mm(ps[:, 2 * jj:2 * jj + 2], pa[:, jj * 128:(jj + 1) * 128], scT[:, c, :], c == 0 and jj == 0, c == 7 and jj == 15,
                   [("ring", s_), "scT"], [("ps", 0)], sgc=True)
        tt(modT[:, 0:16, :], ps[:, 0:32].rearrange("p (j k) -> p j k", k=2),
           bmodT[:, 0:16].unsqueeze(2).broadcast_to([128, 16, 2]), ALU.add, [("ps", 0), "bmodT"], ["modTa"])
        stt(a1[:], modT[:, 8:16, :], 1.0, n1g[:, :].unsqueeze(2).broadcast_to([128, 8, 2]), ALU.add, ALU.mult, ["modTa", "n1g"], ["a1"])

        def mod_b_buf(c):
            p_ = c % 2
            return p_, ringT[:, (2 * p_) * 4096:(2 * p_ + 2) * 4096].bitcast(F32)

        def mod_b_dma(c):
            p_, pb = mod_b_buf(c)
            dma("sp", pb, wmod_d[c * 128:(c + 1) * 128, 2048:6144], (), [("ring", 2 * p_), ("ring", 2 * p_ + 1)], f"wm{p_}")

        def mod_b_mm(c, j0, j1):
            p_, pb = mod_b_buf(c)
            for jj in range(j0, j1):
                mm(ps[:, 7 * 512 + 2 * jj:7 * 512 + 2 * jj + 2], pb[:, jj * 128:(jj + 1) * 128], scT[:, c, :], c == 0 and jj == 0, c == 7 and jj == 31,
                   [("ring", 2 * p_), ("ring", 2 * p_ + 1), "scT"], [("ps", 7)], sgc=True)

        def mod_b_piece(c):
            mod_b_dma(c)
            mod_b_mm(c, 0, 32)

        def mod_b_finish():
            tt(modT[:, 16:48, :], ps[:, 7 * 512:7 * 512 + 64].rearrange("p (j k) -> p j k", k=2),
               bmodT[:, 16:48].unsqueeze(2).broadcast_to([128, 32, 2]), ALU.add, [("ps", 7), "bmodT"], ["modTb"])
            stt(a2[:], modT[:, 32:40, :], 1.0, n2g[:, :].unsqueeze(2).broadcast_to([128, 8, 2]), ALU.add, ALU.mult, ["modTb", "n2g"], ["a2"])
            for k_, j0 in enumerate((16, 40)):
                T.add("pool", lambda e, k_=k_, j0=j0: e.dma_start(out=gscr[0, k_ * 1024:(k_ + 1) * 1024].rearrange("(c p) -> p c", p=128),
                                                             in_=modT[:, j0:j0 + 8, 0], allow_slow_non_contiguous=True),
                      ["modTb"], [("gscr", k_)], dsem="gb")
            dma("pool", gbc[:], gscr[0, :].partition_broadcast(128), [("gscr", 0), ("gscr", 1)], ["gbc"], "gb2")

        cnt = {"xn": 0, "ev": 0, "sq": 0, "par": 0}

        def nm_stats(xb, xres, nt):
            par = cnt["par"] % 2
            cnt["par"] += 1
            po = par * 4
            for t in range(nt):
                kq = cnt["sq"] % 2
                cnt["sq"] += 1
                act(sqj[kq], xb[:, t, :], AF.Square, [xres], [("ss", par, t), ("rcj", kq)], accum=ss[:, po + t:po + t + 1])
            act(rs[:, po:po + nt], ss[:, po:po + nt], AF.Ln, [("ss", par, t) for t in range(nt)], [("rs", par)], scale=1.0 / D, bias=EPS)
            act(rstd[:, po:po + nt], rs[:, po:po + nt], AF.Exp, [("rs", par)], [("rstd", par)], scale=-0.5)
            return dict(xb=xb, xres=xres, nt=nt, par=par, po=po)

        def xn_tile(xb, xres, t, k, rs_ap, rs_key, pb=0):
            if t % 2 == 0:
                ts(xn[k][:], xb[:, t, :], rs_ap, None, ALU.mult, None, [xres, rs_key], [("xn", k)])
            else:
                act(xn[k][:], xb[:, t, :], AF.Copy, [xres, rs_key], [("xn", k)], scale=rs_ap)
            for c in range(8):
                bk = pb + c // 2
                o = bank(bk).bitcast(BF16)[:, (c % 2) * 512 + t * 128:(c % 2) * 512 + (t + 1) * 128]
                tr(o, xn[k][:, c * 128:(c + 1) * 128], [("xn", k)], [("ps", bk)])

        def nm_xn_tr(cx):
            xb, xres, nt, par, po = cx["xb"], cx["xres"], cx["nt"], cx["par"], cx["po"]
            for t in range(nt):
                k = cnt["xn"] % 2
                cnt["xn"] += 1
                xn_tile(xb, xres, t, k, rstd[:, po + t:po + t + 1], ("rstd", par))

        def nm_evac(cx, a_t, sh_off, col, hd=None, pb=0):
            nt = cx["nt"]
            hdst, hkey = hd if hd is not None else (hT, "hT")
            for c in range(8):
                bk = pb + c // 2
                src = bank(bk).bitcast(BF16)[:, (c % 2) * 512:(c % 2) * 512 + nt * 128]
                if c % 2 == 0:
                    act(hdst[:, c, 0:nt * 128], src, AF.Identity, [("ps", bk), a_t[1], a_t[2]], [(hkey, c)],
                        scale=a_t[0][:, c, col:col + 1], bias=modT[:, sh_off + c, col:col + 1])
                else:
                    ts(hdst[:, c, 0:nt * 128], src, a_t[0][:, c, col:col + 1], modT[:, sh_off + c, col:col + 1], ALU.mult, ALU.add,
                       [("ps", bk), a_t[1], a_t[2]], [(hkey, c)])

        def norm_mod_T(xb, xres, nt, a_t, sh_off, col, hd=None):
            cx = nm_stats(xb, xres, nt)
            nm_xn_tr(cx)
            nm_evac(cx, a_t, sh_off, col, hd)

        def head_rstd(src_sq, nh, res_in):
            T.add("dve", lambda e: e.tensor_reduce(out=hs[:, 0:nh], in_=src_sq.rearrange("p (h d) -> p h d", d=64), axis=AX.X, op=ALU.add),
                  [res_in], ["hs"])
            act(hl[:, 0:nh], hs[:, 0:nh], AF.Ln, ["hs"], ["hl"], scale=1.0 / 64, bias=EPS)
            act(hr[:, 0:nh], hl[:, 0:nh], AF.Exp, ["hl"], ["hr"], scale=-0.5)

        def norm_rope(psrc, psres, nh, gain, gres, rp, rpres, outb, outres):
            W = nh * 64
            s0, s1, s2, s3 = scr[0][:, 0:W], scr[1][:, 0:W], scr[2][:, 0:W], scr[3][:, 0:W]
            act(s0, psrc, AF.Square, [psres], [("scr", 0)])
            head_rstd(s0, nh, ("scr", 0))
            tt(s1, psrc, gain, ALU.mult, [psres, gres], [("scr", 1)])
            tt(s2.rearrange("p (h d) -> p h d", d=64), s1.rearrange("p (h d) -> p h d", d=64),
               hr[:, 0:nh].unsqueeze(2).broadcast_to([128, nh, 64]), ALU.mult, [("scr", 1), "hr"], [("scr", 2)])
            v2 = s2.rearrange("p (h d) -> p h d", d=64)
            tt(s3.rearrange("p (h d) -> p h d", d=64), v2, rp[:, 0:64].unsqueeze(1).broadcast_to([128, nh, 64]), ALU.mult,
               [("scr", 2), rpres], [("scr", 3)])
            v0 = s0.rearrange("p (h d) -> p h d", d=64)
            tt(v0[:, :, 0:32], v2[:, :, 32:64], rp[:, 64:96].unsqueeze(1).broadcast_to([128, nh, 32]), ALU.mult,
               [("scr", 2), rpres], [("scr", 0)])
            tt(v0[:, :, 32:64], v2[:, :, 0:32], rp[:, 96:128].unsqueeze(1).broadcast_to([128, nh, 32]), ALU.mult,
               [("scr", 2), rpres], [("scr", 0)])
            tt(outb, s3, s0, ALU.add, [("scr", 3), ("scr", 0)], [outres])

        def dump(name, ap, res, dt=F32):
            if "nodump" in skip:
                return
            d = nc.dram_tensor("dbg_" + name, list(ap.shape), F32, kind="ExternalOutput").ap()
            dbg[name] = d
            dma("pool", d, ap, res, [("dbgout", name)], "dbg")

        ld = {"n": 0}

        def load_xo(row0, nt, s):
            dma("sp", xbuf[s][:, 0:nt, :], xin[row0:row0 + nt * 128, :].rearrange("(t p) d -> p t d", p=128), (), [("xbuf", s)], f"x{s}")

        def load_rope(row0, nt, s):
            dma("sp", ropeb[s][:, 0:nt, :], rope[row0:row0 + nt * 128, :].rearrange("(t p) d -> p t d", p=128), (), [("ropeb", s)], f"r{s}")

        def load_x(row0, nt):
            s = ld["n"] % 2
            ld["n"] += 1
            load_xo(row0, nt, s)
            load_rope(row0, nt, s)
            return s

        supers = [(0, 2, 1)] + [(256 + i * 512, 4, 0) for i in range(16)]
        if stage == 0:
            supers = []
            for c in range(8):
                mod_b_piece(c)
            mod_b_finish()
            dump("modT", modT[:], ["modTa", "modTb"])
            dump("gbc", gbc[:], ["gbc"])
            dump("a1", a1[:], ["a1"])
        if stage == 1:
            import os
            supers = supers[:int(os.environ.get("NSUP", "3"))]
        krb = big[:, 24:26, :].rearrange("p a b -> p (a b)")
        nsup = len(supers)
        hbufs = [(hT, "hT"), (big[:, 0:8, :], "big")]
        a1t = (a1, "a1", "modTa")

        def a_kv(si):
            row0, nt, col = supers[si]
            hA, hAk = hbufs[si % 2]
            for t in range(nt):
                bk = 4 + t // 2
                for c in range(8):
                    mm(ps[:, bk * 512 + (t % 2) * 256: bk * 512 + (t % 2) * 256 + 256], hA[:, c, t * 128:(t + 1) * 128], Wkv[:, c, :],
                       c == 0, c == 7, [(hAk, c), "Wkv"], [("ps", bk)])
                if 1 <= si <= 8:
                    mod_b_mm(si - 1, 8 * t, 8 * t + 8)

        def a_post(si):
            row0, nt, col = supers[si]
            slot = si % 2
            tile0 = row0 // 128
            W = nt * 128
            kvv = ps[:, 4 * 512:4 * 512 + nt * 256].rearrange("p (t n) -> p t n", n=256)
            kview = kvv[:, :, 0:128]
            kvb = [("ps", 4)] + ([("ps", 5)] if nt > 2 else [])
            s0, s1, s2, s3 = scr[0][:, 0:W], scr[1][:, 0:W], scr[2][:, 0:W], scr[3][:, 0:W]
            tv = lambda a: a.rearrange("p (t n) -> p t n", n=128)
            hv = lambda a: a.rearrange("p (h d) -> p h d", d=64)
            qv = lambda a: a.rearrange("p (t h d) -> p t h d", h=2, d=64)
            rp = ropeb[slot]
            rpk = ("ropeb", slot)
            act(tv(s0), kview, AF.Square, kvb, [("scr", 0)])
            head_rstd(s0, 2 * nt, ("scr", 0))
            tt(tv(s1), kview, gk[:, :].unsqueeze(1).broadcast_to([128, nt, 128]), ALU.mult, kvb + ["gk"], [("scr", 1)])
            tt(hv(s2), hv(s1), hr[:, 0:2 * nt].unsqueeze(2).broadcast_to([128, 2 * nt, 64]), ALU.mult, [("scr", 1), "hr"], [("scr", 2)])
            tt(qv(s3), qv(s2), rp[:, 0:nt, 0:64].unsqueeze(2).broadcast_to([128, nt, 2, 64]), ALU.mult, [("scr", 2), rpk], [("scr", 3)])
            tt(qv(s0)[:, :, :, 0:32], qv(s2)[:, :, :, 32:64], rp[:, 0:nt, 64:96].unsqueeze(2).broadcast_to([128, nt, 2, 32]), ALU.mult,
               [("scr", 2), rpk], [("scr", 0)])
            tt(qv(s0)[:, :, :, 32:64], qv(s2)[:, :, :, 0:32], rp[:, 0:nt, 96:128].unsqueeze(2).broadcast_to([128, nt, 2, 32]), ALU.mult,
               [("scr", 2), rpk], [("scr", 0)])
            tt(krb[:, 0:W], s3, s0, ALU.add, [("scr", 3), ("scr", 0)], [("big", 24)])
            for t in range(nt):
                tr(bank(6).bitcast(BF16)[:, t * 128:(t + 1) * 128], krb[:, t * 128:(t + 1) * 128], [("big", 24)], [("ps", 6)])
            act(Vaug[:, tile0:tile0 + nt, 0:64], kvv[:, :, 128:192], AF.Copy, kvb, [("Vaug", tile0 + t) for t in range(nt)])
            act(Vaug[:, tile0:tile0 + nt, 128:192], kvv[:, :, 192:256], AF.Copy, kvb, [("Vaug", tile0 + t) for t in range(nt)])
            cp(KT[:, tile0 * 128:(tile0 + nt) * 128], bank(6).bitcast(BF16)[:, 0:nt * 128], [("ps", 6)], [("KT", si)])

        actx = {}
        if nsup:
            for k_ in range(min(2, nsup)):
                load_xo(supers[k_][0], supers[k_][1], k_ % 2)
                load_rope(supers[k_][0], supers[k_][1], k_ % 2)
            ld["n"] = 0
            actx[0] = nm_stats(xbuf[0], ("xbuf", 0), supers[0][1])
            nm_xn_tr(actx[0])
            if nsup > 2:
                load_xo(supers[2][0], supers[2][1], 0)
            nm_evac(actx[0], a1t, 0, supers[0][2], hbufs[0])
        for k_ in range(nsup):
            if k_ < 8:
                mod_b_dma(k_)
            n_ = k_ + 1
            if n_ < nsup:
                actx[n_] = nm_stats(xbuf[n_ % 2], ("xbuf", n_ % 2), supers[n_][1])
            a_kv(k_)
            if n_ < nsup:
                nm_xn_tr(actx[n_])
                if k_ + 3 < nsup:
                    load_xo(supers[k_ + 3][0], supers[k_ + 3][1], (k_ + 3) % 2)
            a_post(k_)
            if k_ + 2 < nsup:
                load_rope(supers[k_ + 2][0], supers[k_ + 2][1], k_ % 2)
            if n_ < nsup:
                nm_evac(actx[n_], a1t, 0, supers[n_][2], hbufs[n_ % 2])

        ld["n"] = 1
        slot = load_x(256, 4) if (nqb and stage > 1) else None
        if stage >= 1:
            if len(supers) < 9:
                for c in range(max(len(supers) - 1, 0), 8):
                    if c >= len(supers):
                        mod_b_dma(c)
                    mod_b_mm(c, 0, 32)
            mod_b_finish()
        if stage == 1:
            dump("KT", KT[:, 0:1280], [("KT", i) for i in range(3)], BF16)
            dump("Vaug", Vaug[:, 0:10, :], [("Vaug", i) for i in range(10)], BF16)
            dump("hT", hT[:], [("hT", c) for c in range(8)], BF16)
        if stage <= 1:
            nqb = 0
        wctr = {"n": 0}

        def load_unit(u):
            s = wctr["n"] % NB
            wctr["n"] += 1
            dma("sp", ring[s][:, :], wsc[u], [("wsc", u)], [("ring", s)], f"w{s}")
            return s

        def R(s):
            return ("ring", s)

        def unit8(s):
            return ring[s][:, :].rearrange("p (c n) -> p c n", c=8)

        def unit4(s):
            return ring[s][:, :].rearrange("p (c n) -> p c n", c=4)

        yT = [big[:, c, :] for c in range(8)]
        uT = [big[:, 8 + j, :] for j in range(4)]
        gmT = [big[:, 12 + j, :] for j in range(4)]
        attnT = [big[:, 16 + j, :] for j in range(4)]
        QT = [QTb[:, j, :] for j in range(4)]
        vnb = [big[:, 24 + t, :] for t in range(4)]
        PT = [big[:, 28:30, :].rearrange("p a b -> p (a b)"), big[:, 30:32, :].rearrange("p a b -> p (a b)"),
              big[:, 26:28, :].rearrange("p a b -> p (a b)")]
        PTK = [[("big", 28), ("big", 29)], [("big", 30), ("big", 31)], [("big", 26), ("big", 27)]]

        def pre_q_pieces(slot_):
            xb_, xres_, rpb_ = xbuf[slot_], ("xbuf", slot_), ropeb[slot_]
            st = {}
            qrb = scr[4][:, :].bitcast(BF16)[:, 0:512]

            def p_stats():
                st["cx"] = nm_stats(xb_, xres_, 4)

            def p_xn(t):
                cx = st["cx"]
                k = cnt["xn"] % 2
                cnt["xn"] += 1
                xn_tile(xb_, xres_, t, k, rstd[:, cx["po"] + t:cx["po"] + t + 1], ("rstd", cx["par"]), pb=4)

            def p_evac():
                nm_evac(st["cx"], (a1, "a1", "modTa"), 0, 0, None, pb=4)

            def p_qproj():
                su = load_unit(0)
                for t in range(4):
                    for c in range(8):
                        mm(bank(4 + t), hT[:, c, t * 128:(t + 1) * 128], unit8(su)[:, c, :], c == 0, c == 7, [("hT", c), R(su)], [("ps", 4 + t)])

            def p_qdve(t):
                norm_rope(bank(4 + t), ("ps", 4 + t), 8, gq[:, :], "gq", rpb_[:, t, :], ("ropeb", slot_), qrb, ("scr", 4))

            def p_qpe(t):
                for g in range(4):
                    tr(bank(4 + t).bitcast(BF16)[:, g * 128:(g + 1) * 128], qrb[:, g * 128:(g + 1) * 128], [("scr", 4)], [("ps", 4 + t)])
                cp(QTb[:, :, t * 128:(t + 1) * 128], bank(4 + t).bitcast(BF16)[:, 0:512].rearrange("p (g q) -> p g q", q=128),
                   [("ps", 4 + t)], [("QT", g) for g in range(4)])

            return dict(stats=p_stats, xn=p_xn, evac=p_evac, qproj=p_qproj, qdve=p_qdve, qpe=p_qpe)

        def run_pre_q_all(pq):
            pq["stats"]()
            for t in range(4):
                pq["xn"](t)
            pq["evac"]()
            pq["qproj"]()
            for t in range(4):
                pq["qdve"](t)
                pq["qpe"](t)

        if nqb:
            run_pre_q_all(pre_q_pieces(slot))

        for qb in range(nqb):
            row0 = 256 + qb * 512
            xb = xbuf[slot]
            xres = ("xbuf", slot)
            rpb = ropeb[slot]
            nslot = None
            sv = load_unit(1)
            for t in range(4):
                for c in range(8):
                    mm(bank(t), hT[:, c, t * 128:(t + 1) * 128], unit8(sv)[:, c, :], c == 0, c == 7, [("hT", c), R(sv)], [("ps", t)])
            suu = load_unit(2)
            for j in range(4):
                for c in range(8):
                    mm(bank(4 + j), unit8(suu)[:, c, j * 128:(j + 1) * 128], hT[:, c, :], c == 0, c == 7, [("hT", c), R(suu)], [("ps", 4 + j)])
            for t in range(4):
                act(scr[t][:, :], bank(t), AF.Gelu_apprx_tanh, [("ps", t)], [("scr", t)])
            for j in range(4):
                act(uT[j], bank(4 + j), AF.Gelu_apprx_tanh, [("ps", 4 + j)], [("big", 8 + j)])
            for t in range(4):
                sq_ = scr[4 + t % 2]
                act(sq_[:, :], scr[t][:, :], AF.Square, [("scr", t)], [("scr", 4 + t % 2)])
                T.add("dve", lambda e, t=t, sq_=sq_: e.tensor_reduce(out=hs32[:, 8 * t:8 * t + 8], in_=sq_[:, :].rearrange("p (h d) -> p h d", d=64),
                                                                     axis=AX.X, op=ALU.add), [("scr", 4 + t % 2)], [("hs32", t)])
            act(hl32[:, :], hs32[:, :], AF.Ln, [("hs32", t) for t in range(4)], ["hl32"], scale=1.0 / 64, bias=EPS)
            act(hr32[:, :], hl32[:, :], AF.Exp, ["hl32"], ["hr32"], scale=-0.5)
            for t in range(4):
                tm_ = scr[4 + t % 2]
                tt(tm_[:, :], scr[t][:, :], gmg[:, :], ALU.mult, [("scr", t), "gmg"], [("scr", 4 + t % 2)])
                tt(vnb[t].rearrange("p (h d) -> p h d", d=64), tm_[:, :].rearrange("p (h d) -> p h d", d=64),
                   hr32[:, 8 * t:8 * t + 8].unsqueeze(2).broadcast_to([128, 8, 64]), ALU.mult, [("scr", 4 + t % 2), "hr32"], [("big", 24 + t)])
                for j in range(4):
                    for gg in range(2):
                        g = 2 * j + gg
                        mm(ps[gg * 64:(gg + 1) * 64, j * 512 + t * 128: j * 512 + (t + 1) * 128], vnb[t][:, g * 64:(g + 1) * 64], wsT[:, g, :],
                           True, True, [("big", 24 + t), "wsT"], [("ps", j)])
            for j in range(4):
                tt(scr[5][:, :].rearrange("p (t q) -> p t q", q=128), bank(j).rearrange("p (t q) -> p t q", q=128),
                   bsT[:, j, :].unsqueeze(1).broadcast_to([128, 4, 128]), ALU.add, [("ps", j), "bsT"], [("scr", 5)])
                tt(gmT[j], scr[5][:, :], uT[j], ALU.mult, [("scr", 5), ("big", 8 + j)], [("big", 12 + j)])

            steps = [(g, kb) for g in range(4) for kb in range(NT)]

            def ksup(kb):
                return 0 if kb < 2 else 1 + (kb - 2) // 4

            def qk(i):
                g, kb = steps[i]
                sbk = (i % 2) * 2
                mm(bank(sbk), KT[0:64, kb * 128:(kb + 1) * 128], QT[g][0:64, :], True, True, [("KT", ksup(kb)), ("QT", g)], [("ps", sbk)])
                mm(bank(sbk + 1), KT[64:128, kb * 128:(kb + 1) * 128], QT[g][64:128, :], True, True, [("KT", ksup(kb)), ("QT", g)], [("ps", sbk + 1)])
                pace = [("pace", i)] if (qb == 0 and i % 12 == 0) else []
                act(PT[i % 3], ps[:, sbk * 512:sbk * 512 + 1024], AF.Exp, [("ps", sbk), ("ps", sbk + 1)], PTK[i % 3] + pace, scale=0.125)
                if pace and N_EARLY + i // 12 < N_UNITS_QB:
                    cast_unit(N_EARLY + i // 12, pace)

            def pv(i):
                g, kb = steps[i]
                oa = 4 + 2 * (g % 2)
                mm(bank(oa), Vaug[:, kb, 0:128], PT[i % 3][:, 0:512], kb == 0, kb == NT - 1, [("Vaug", kb)] + PTK[i % 3], [("ps", oa)])
                mm(bank(oa + 1), Vaug[:, kb, 64:192], PT[i % 3][:, 512:1024], kb == 0, kb == NT - 1, [("Vaug", kb)] + PTK[i % 3], [("ps", oa + 1)])
                if kb == NT - 1:
                    recip(rc[64:128, 0:512], bank(oa)[64:128, :], [("ps", oa)], ["rc", ("rcj", 0), ("rcj", 1)])
                    recip(rc[0:64, 512:1024], bank(oa + 1)[0:64, :], [("ps", oa + 1)], ["rc", ("rcj", 0), ("rcj", 1)])
                    tt(attnT[g][0:64, :], bank(oa)[0:64, :], rc[64:128, 0:512], ALU.mult, [("ps", oa), "rc", ("rcj", 0), ("rcj", 1)], [("big", 16 + g)])
                    tt(attnT[g][64:128, :], bank(oa + 1)[64:128, :], rc[0:64, 512:1024], ALU.mult, [("ps", oa + 1), "rc", ("rcj", 0), ("rcj", 1)], [("big", 16 + g)])

            qk(0)
            qk(1)
            for i in range(len(steps)):
                if i + 2 < len(steps):
                    qk(i + 2)
                pv(i)

            if qb + 1 < nqb:
                nslot = load_x(row0 + 512, 4)

            sga = [load_unit(3), None]
            sgb = [load_unit(4), None]
            sbra = load_unit(5)
            sbrg = load_unit(6)
            for m in range(8):
                if m == 4:
                    sga[1] = load_unit(7)
                    sgb[1] = load_unit(8)
                h = m // 4
                b0 = (m % 2) * 4
                for c in range(8):
                    mm(bank(b0), unit8(sga[h])[:, c, (m % 4) * 128:(m % 4 + 1) * 128], hT[:, c, :], c == 0, c == 7, [("hT", c), R(sga[h])], [("ps", b0)])
                for c in range(8):
                    mm(bank(b0 + 1), unit8(sgb[h])[:, c, (m % 4) * 128:(m % 4 + 1) * 128], hT[:, c, :], c == 0, c == 7, [("hT", c), R(sgb[h])], [("ps", b0 + 1)])
                for c in range(4):
                    mm(bank(b0 + 2), unit4(sbra)[:, c, m * 128:(m + 1) * 128], attnT[c], c == 0, c == 3, [("big", 16 + c), R(sbra)], [("ps", b0 + 2)])
                for c in range(4):
                    mm(bank(b0 + 3), unit4(sbrg)[:, c, m * 128:(m + 1) * 128], gmT[c], c == 0, c == 3, [("big", 12 + c), R(sbrg)], [("ps", b0 + 3)])
                act(scr[0][:, :], bank(b0), AF.Sigmoid, [("ps", b0)], [("scr", 0)])
                act(scr[1][:, :], bank(b0 + 1), AF.Sigmoid, [("ps", b0 + 1)], [("scr", 1)])
                tt(scr[2][:, :], bank(b0 + 2), scr[0][:, :], ALU.mult, [("ps", b0 + 2), ("scr", 0)], [("scr", 2)])
                tt(scr[3][:, :], bank(b0 + 3), scr[1][:, :], ALU.mult, [("ps", b0 + 3), ("scr", 1)], [("scr", 3)])
                tt(yT[m], scr[2][:, :], scr[3][:, :], ALU.add, [("scr", 2), ("scr", 3)], [("big", m)], eng="pool")

            so = [load_unit(9), load_unit(10)]
            par = cnt["par"] % 2
            cnt["par"] += 1
            po = par * 4
            k8c = {"n": 0}

            def b8_mm(t):
                for nh in range(2):
                    bk = 4 + k8c["n"] % 4
                    k8c["n"] += 1
                    for c in range(8):
                        mm(bank(bk), yT[c][:, t * 128:(t + 1) * 128], unit8(so[nh])[:, c, :], c == 0, c == 7, [("big", c), R(so[nh])], [("ps", bk)])
                    sk = 4 + (k8c["n"] % 2)
                    tt(scr[sk][:, :], bank(bk), gbc[:, nh * 512:(nh + 1) * 512], ALU.mult, [("ps", bk), "gbc"], [("scr", sk)])
                    tt(xb[:, t, nh * 512:(nh + 1) * 512], scr[sk][:, :], xb[:, t, nh * 512:(nh + 1) * 512], ALU.add, [("scr", sk), xres], [xres, ("x1t", t)], eng="pool")

            def b9_tile(t):
                kq = cnt["sq"] % 2
                cnt["sq"] += 1
                cs = slice(po + t, po + t + 1)
                act(sqj[kq], xb[:, t, :], AF.Square, [("x1t", t)], [("ss", par, t), ("rcj", kq)], accum=ss[:, cs])
                act(rs[:, cs], ss[:, cs], AF.Ln, [("ss", par, t)], [("rs", par), ("rs", par, t)], scale=1.0 / D, bias=EPS)
                act(rstd[:, cs], rs[:, cs], AF.Exp, [("rs", par, t)], [("rstd", par), ("rstd", par, t)], scale=-0.5)
                k = cnt["xn"] % 2
                cnt["xn"] += 1
                xn_tile(xb, ("x1t", t), t, k, rstd[:, cs], ("rstd", par, t))

            for t in range(4):
                b8_mm(t)
                if t >= 1:
                    b9_tile(t - 1)
            b9_tile(3)
            nm_evac(dict(nt=4), (a2, "a2", "modTb"), 24, 0)

            pq = pre_q_pieces(nslot) if nslot is not None else None
            if pq:
                pq["stats"]()
            for j in range(32):
                if j % 4 == 0:
                    sf = load_unit(11 + j // 4)
                bk = j % 8
                for c in range(8):
                    mm(bank(bk), unit8(sf)[:, c, (j % 4) * 128:(j % 4 + 1) * 128], hT[:, c, :], c == 0, c == 7, [("hT", c), R(sf)], [("ps", bk)])
                sk = j % 4
                act(scr[sk][:, :], bank(bk), AF.Relu, [("ps", bk)], [("scr", sk)])
                tt(big[:, j, :], scr[sk][:, :], scr[sk][:, :], ALU.mult, [("scr", sk)], [("big", j)])

            if pq:
                pq["xn"](0)
                pq["xn"](1)
            blk = 0
            for nh in range(2):
                for kg in range(4):
                    s2 = load_unit(19 + nh * 4 + kg)
                    for t in range(4):
                        bk = t
                        for c in range(8):
                            mm(bank(bk), big[:, kg * 8 + c, t * 128:(t + 1) * 128], unit8(s2)[:, c, :], kg == 0 and c == 0, kg == 3 and c == 7,
                               [("big", kg * 8 + c), R(s2)], [("ps", bk)])
                    if pq:
                        if blk == 0:
                            pq["xn"](2)
                            pq["xn"](3)
                            pq["evac"]()
                        elif blk == 1:
                            pq["qproj"]()
                        elif blk == 2:
                            pq["qdve"](0)
                        elif blk in (3, 4, 5):
                            pq["qpe"](blk - 3)
                            pq["qdve"](blk - 2)
                        elif blk == 6:
                            pq["qpe"](3)
                    blk += 1
                for t in range(4):
                    bk = t
                    sk = 5
                    tt(scr[sk][:, :], bank(bk), gbc[:, 1024 + nh * 512:1024 + (nh + 1) * 512], ALU.mult, [("ps", bk), "gbc"], [("scr", sk)])
                    tt(xb[:, t, nh * 512:(nh + 1) * 512], scr[sk][:, :], xb[:, t, nh * 512:(nh + 1) * 512], ALU.add, [("scr", sk), xres], [xres], eng="pool")
            dma("sp", out_d[qb * 512:(qb + 1) * 512, :].rearrange("(t p) d -> p t d", p=128), xb[:, :, :], [xres], [("out", qb)], f"o{slot}")
            slot = nslot

        T.add("sp", lambda e: e.nop(), [("out", q) for q in range(nqb)] + [("dbgout", n) for n in dbg], [])

        T.finalize()
        nc._tracker = T

        @block.sync
        def _(e):
            T.emit("sp", e, esems, dsems)

        @block.tensor
        def _(e):
            T.emit("pe", e, esems, dsems)

        @block.scalar
        def _(e):
            T.emit("act", e, esems, dsems)

        @block.vector
        def _(e):
            T.emit("dve", e, esems, dsems)

        @block.gpsimd
        def _(e):
            T.emit("pool", e, esems, dsems)

    return nc


_CACHE = {}


def _rope_table(tok_idx):
    n = tok_idx.shape[0]
    t = np.maximum(tok_idx, 0)
    row = (t // 64).astype(np.float32)
    colp = (t % 64).astype(np.float32)
    inv = (np.float32(10000.0) ** (-np.arange(0, 32, 2, dtype=np.float32) / np.float32(32))).astype(np.float32)
    ang = np.concatenate([row[:, None] * inv[None, :], colp[:, None] * inv[None, :]], axis=-1).astype(np.float32)
    cos = np.cos(ang).astype(np.float32)
    sin = np.sin(ang).astype(np.float32)
    ident = tok_idx < 0
    cos[ident] = 1.0
    sin[ident] = 0.0
    return np.concatenate([cos, cos, -sin, sin], axis=1).astype(np.float32)


def kernel(x, c, ctx, c_ctx, w_mod, b_mod, norm1_g, norm2_g, w_in, q_norm_g, k_norm_g,
           gm_norm_g, gm_ws, gm_bs, w_br_attn, w_br_gm, w_out, w_ff1, w_ff2):
    f = lambda a: np.ascontiguousarray(np.asarray(a, dtype=np.float32))
    x, c, ctx, c_ctx = f(x), f(c), f(ctx), f(c_ctx)
    w_mod, b_mod, w_in = f(w_mod)[0], f(b_mod)[0], f(w_in)[0]
    n1, n2 = f(norm1_g)[0], f(norm2_g)[0]
    qg, kg, gmg = f(q_norm_g)[0], f(k_norm_g)[0], f(gm_norm_g)[0]
    ws, bs = f(gm_ws)[0], f(gm_bs)[0]
    bra, brg, wo, w1, w2 = f(w_br_attn)[0], f(w_br_gm)[0], f(w_out)[0], f(w_ff1)[0], f(w_ff2)[0]

    if "nc" not in _CACHE:
        _CACHE["nc"] = build_program()
    nc = _CACHE["nc"]

    qcols = 256 + np.array([kv * 256 + g * 64 + d for g in range(4) for kv in range(2) for d in range(64)])
    order = np.concatenate([np.arange(0, 256), qcols, np.arange(1280, 1792), np.arange(768, 1280), np.arange(1792, 3840)])
    w_in_p = np.ascontiguousarray(w_in[:, order])
    rows = np.array([kv * 256 + g * 64 + d for g in range(4) for kv in range(2) for d in range(64)])
    w_bra_p = np.ascontiguousarray(bra[rows, :])
    b_modT = np.ascontiguousarray(b_mod.reshape(48, 128).T)
    b_modg = np.ascontiguousarray(np.concatenate([b_mod[2048:3072], b_mod[5120:6144]])[None, :])
    n1g = np.ascontiguousarray(n1.reshape(8, 128).T)
    n2g = np.ascontiguousarray(n2.reshape(8, 128).T)
    gq_bc = np.ascontiguousarray(np.broadcast_to(np.tile(qg, 8)[None, :], (128, 512)))
    gk_bc = np.ascontiguousarray(np.broadcast_to(np.tile(kg, 2)[None, :], (128, 128)))
    gmg_bc = np.ascontiguousarray(np.broadcast_to(gmg.reshape(512)[None, :], (128, 512)))
    wsT = np.ascontiguousarray(ws.transpose(2, 0, 1))
    bsT = np.ascontiguousarray(np.broadcast_to(bs.reshape(4, 2, 1, 128), (4, 2, 64, 128)).transpose(1, 2, 0, 3).reshape(128, 4, 128))
    ident = np.eye(128, dtype=np.float32)

    in_maps = []
    for core in range(8):
        b, hf = core // 2, core % 2
        own = np.arange(hf * 4096, (hf + 1) * 4096)
        oth = np.arange((1 - hf) * 4096, (2 - hf) * 4096)
        xin = np.concatenate([ctx[b], x[b, own], x[b, oth]], axis=0)
        tok = np.concatenate([-np.ones(256, dtype=np.int64), own, oth])
        cT = np.ascontiguousarray(np.stack([c[b], c_ctx], axis=1).reshape(8, 128, 2).transpose(1, 0, 2))
        in_maps.append({
            "xin": np.ascontiguousarray(xin), "rope": _rope_table(tok), "cT": cT, "w_mod": w_mod, "b_modT": b_modT,
            "b_modg": b_modg, "n1g": n1g, "n2g": n2g, "w_in_p": w_in_p, "gq_bc": gq_bc, "gk_bc": gk_bc, "gmg_bc": gmg_bc,
            "wsT": wsT, "bsT": bsT, "w_bra_p": w_bra_p, "w_brg": brg, "w_out": wo, "w_ff1": w1, "w_ff2": w2, "ident": ident,
        })
    res = run_bass_kernel_spmd(nc, in_maps, core_ids=list(range(8)))
    out = np.empty((4, 8192, D), dtype=np.float32)
    for core in range(8):
        b, hf = core // 2, core % 2
        out[b, hf * 4096:(hf + 1) * 4096] = res.results[core]["out"]
    return out
```

```python
import numpy as np
import concourse.bass as bass
import concourse.mybir as mybir
from concourse.bass_utils import run_bass_kernel_spmd

F32 = mybir.dt.float32
BF16 = mybir.dt.bfloat16
AF = mybir.ActivationFunctionType
ALU = mybir.AluOpType
AX = mybir.AxisListType

D = 1024
NCTX_T = 2
NOWN_T = 32
NT = 66
NQB = 8
EPS = 1e-6
NB = 5
N_UNITS_QB = 27


class Tracker:
    def __init__(self):
        self.ops = []
        self.last_w = {}
        self.readers = {}
        self.dcount = {}

    def add(self, eng, fn, reads=(), writes=(), dsem=None):
        idx = len(self.ops)
        deps = set()
        if eng in ("act", "dve"):
            writes = list(writes) + [("pslk", r[1]) for r in reads if isinstance(r, tuple) and r[0] == "ps"]
        for r in reads:
            if r in self.last_w:
                deps.add(self.last_w[r])
        for w in writes:
            if w in self.last_w:
                deps.add(self.last_w[w])
            for rd in self.readers.get(w, ()):
                deps.add(rd)
        op = dict(eng=eng, fn=fn, deps=deps, dsem=dsem, marked=False, idx=idx, val=None, desc=(tuple(reads), tuple(writes)))
        if dsem is not None:
            self.dcount[dsem] = self.dcount.get(dsem, 0) + 16
            op["dval"] = self.dcount[dsem]
        self.ops.append(op)
        for r in reads:
            self.readers.setdefault(r, []).append(idx)
        for w in writes:
            self.last_w[w] = idx
            self.readers[w] = []
        return idx

    def finalize(self):
        ops = self.ops
        for op in ops:
            red = {}
            for d in op["deps"]:
                dop = ops[d]
                if dop["dsem"] is not None:
                    key = ("d", dop["dsem"])
                    if key not in red or ops[red[key]]["dval"] < dop["dval"]:
                        red[key] = d
                else:
                    if dop["eng"] == "pe" and op["eng"] == "pe" and op["dsem"] is None:
                        continue
                    key = ("e", dop["eng"])
                    if key not in red or red[key] < d:
                        red[key] = d
            op["rdeps"] = list(red.values())
            for d in op["rdeps"]:
                if ops[d]["dsem"] is None:
                    ops[d]["marked"] = True
        cnt = {}
        for op in ops:
            if op["dsem"] is None and op["marked"]:
                cnt[op["eng"]] = cnt.get(op["eng"], 0) + 1
                op["val"] = cnt[op["eng"]]

    def trace(self, engname):
        waited = {}
        out = []
        for op in self.ops:
            if op["eng"] != engname:
                continue
            ws = []
            for d in op["rdeps"]:
                dop = self.ops[d]
                if dop["dsem"] is not None:
                    val = self.dcount[dop["dsem"]] if dop["dsem"] in ("const", "cast", "gb") else dop["dval"]
                    key = ("d", dop["dsem"])
                else:
                    val, key = dop["val"], ("e", dop["eng"])
                if waited.get(key, 0) < val:
                    ws.append((key[1], val))
                    waited[key] = val
            inc = (op["dsem"], op.get("dval")) if op["dsem"] else ((engname, op["val"]) if op["marked"] else None)
            out.append((op["idx"], ws, op["desc"], inc))
        return out

    def emit(self, engname, engobj, esems, dsems):
        waited = {}
        for op in self.ops:
            if op["eng"] != engname:
                continue
            for d in op["rdeps"]:
                dop = self.ops[d]
                if dop["dsem"] is not None:
                    val = self.dcount[dop["dsem"]] if dop["dsem"] in ("const", "cast", "gb") else dop["dval"]
                    sem, key = dsems[dop["dsem"]], ("d", dop["dsem"])
                else:
                    sem, val, key = esems[dop["eng"]], dop["val"], ("e", dop["eng"])
                if waited.get(key, 0) < val:
                    engobj.wait_ge(sem, val)
                    waited[key] = val
            ins = op["fn"](engobj)
            if op["dsem"] is not None:
                ins.then_inc(dsems[op["dsem"]], 16)
            elif op["marked"]:
                ins.then_inc(esems[engname], 1)


def build_program(stage=3, nqb=NQB, skip=()):
    nc = bass.Bass("TRN2", target_bir_lowering=False)
    T = Tracker()
    dbg = {}

    def din(name, shape, dt=F32):
        return nc.dram_tensor(name, list(shape), dt, kind="ExternalInput").ap()

    xin = din("xin", [NT * 128, D])
    rope = din("rope", [NT * 128, 128])
    cT_d = din("cT", [128, 8, 2])
    wmod_d = din("w_mod", [D, 6 * D])
    bmodT_d = din("b_modT", [128, 48])
    bmodg_d = din("b_modg", [1, 2048])
    n1g_d = din("n1g", [128, 8])
    n2g_d = din("n2g", [128, 8])
    win_d = din("w_in_p", [D, 3840])
    gq_d = din("gq_bc", [128, 512])
    gk_d = din("gk_bc", [128, 128])
    gmg_d = din("gmg_bc", [128, 512])
    wsT_d = din("wsT", [128, 8, 128])
    bsT_d = din("bsT", [128, 4, 128])
    bra_d = din("w_bra_p", [512, D])
    brg_d = din("w_brg", [512, D])
    wout_d = din("w_out", [D, D])
    ff1_d = din("w_ff1", [D, 4 * D])
    ff2_d = din("w_ff2", [4 * D, D])
    ident_d = din("ident", [128, 128])
    out_d = nc.dram_tensor("out", [NOWN_T * 128, D], F32, kind="ExternalOutput").ap()
    gscr = nc.dram_tensor("gscr", [1, 2048], F32, kind="Internal").ap()
    wsc = nc.dram_tensor("wscratch", [N_UNITS_QB, 128, 4096], BF16, kind="Internal").ap()

    import contextlib
    es = contextlib.ExitStack()

    def sb(name, shape, dt):
        return es.enter_context(nc.sbuf_tensor(name, list(shape), dt))

    with es:
        KT = sb("KT", [128, NT * 128], BF16)
        Vaug = sb("Vaug", [128, NT, 192], BF16)
        xbuf = [sb(f"xbuf{i}", [128, 4, D], F32) for i in range(2)]
        ropeb = [sb(f"ropeb{i}", [128, 4, 128], F32) for i in range(2)]
        xn = [sb(f"xn{i}", [128, D], BF16) for i in range(2)]
        hT = sb("hT", [128, 8, 512], BF16)
        big = sb("big", [128, 32, 512], BF16)
        ringT = sb("ring", [128, NB * 4096], BF16)
        ring = [ringT[:, i * 4096:(i + 1) * 4096] for i in range(NB)]
        gbc = sb("gbc", [128, 2048], F32)
        scr = [sb(f"scr{i}", [128, 512], F32) for i in range(6)]
        rc = sb("rc", [128, 1024], F32)
        QTb = sb("QTb", [128, 4, 512], BF16)
        Wkv = sb("Wkv", [128, 8, 256], BF16)
        identb = sb("identb", [128, 128], BF16)
        gq = sb("gq", [128, 512], F32)
        gk = sb("gk", [128, 128], F32)
        gmg = sb("gmg", [128, 512], F32)
        wsT = sb("wsTb", [128, 8, 128], BF16)
        bsT = sb("bsTs", [128, 4, 128], F32)
        cT = sb("cTs", [128, 8, 2], F32)
        scT = sb("scT", [128, 8, 2], F32)
        bmodT = sb("bmodTs", [128, 48], F32)
        n1g = sb("n1gs", [128, 8], F32)
        n2g = sb("n2gs", [128, 8], F32)
        modT = sb("modT", [128, 48, 2], F32)
        a1 = sb("a1", [128, 8, 2], F32)
        a2 = sb("a2", [128, 8, 2], F32)
        ones1 = sb("ones1", [1, 128], F32)
        ss = sb("ss", [128, 8], F32)
        rs = sb("rs", [128, 8], F32)
        rstd = sb("rstd", [128, 8], F32)
        hs = sb("hs", [128, 8], F32)
        hl = sb("hl", [128, 8], F32)
        hr = sb("hr", [128, 8], F32)
        hs32 = sb("hs32", [128, 32], F32)
        hl32 = sb("hl32", [128, 32], F32)
        hr32 = sb("hr32", [128, 32], F32)
        ps = es.enter_context(nc.psum_tensor("ps", [128, 4096], F32))
        sqj = [rc[:, 0:512].bitcast(BF16), rc[:, 512:1024].bitcast(BF16)]
        grow = big[0:1, 0:8, :].rearrange("p a b -> p (a b)").bitcast(F32)
        bmodg = big[0:1, 8:16, :].rearrange("p a b -> p (a b)").bitcast(F32)

        esems = {e: es.enter_context(nc.semaphore("sem_" + e)) for e in ["pe", "act", "dve", "pool", "sp"]}
        dnames = [f"c{i}" for i in range(6)] + ["dbg", "gb", "gb2", "const", "cast", "x0", "x1", "r0", "r1", "o0", "o1", "wm0", "wm1", "wm2"] + [f"w{i}" for i in range(NB)]
        dsems = {d: es.enter_context(nc.semaphore("ds_" + d)) for d in dnames}
        block = es.enter_context(nc.Block())

        def bank(b):
            return ps[:, b * 512:(b + 1) * 512]

        def mm(out, lhsT, rhs, start, stop, reads, writes, sgc=False):
            T.add("pe", lambda e: e.matmul(out, lhsT=lhsT, rhs=rhs, start=start, stop=stop, skip_group_check=sgc), reads, writes)

        def tr(out, in_, reads, writes):
            T.add("pe", lambda e: e.transpose(out, in_, identb[:, :]), list(reads) + ["identb"], writes)

        def act(out, in_, func, reads, writes, scale=None, bias=None, accum=None):
            kw = {}
            if scale is not None:
                kw["scale"] = scale
            if bias is not None:
                kw["bias"] = bias
            if accum is not None:
                kw["accum_out"] = accum
            T.add("act", lambda e: e.activation(out=out, in_=in_, func=func, **kw), reads, writes)

        def tt(out, in0, in1, op, reads, writes, eng="dve"):
            T.add(eng, lambda e: e.tensor_tensor(out=out, in0=in0, in1=in1, op=op), reads, writes)

        def ts(out, in0, s1, s2, op0, op1, reads, writes, eng="dve"):
            if op1 is None:
                T.add(eng, lambda e: e.tensor_scalar(out=out, in0=in0, scalar1=s1, scalar2=None, op0=op0), reads, writes)
            else:
                T.add(eng, lambda e: e.tensor_scalar(out=out, in0=in0, scalar1=s1, scalar2=s2, op0=op0, op1=op1), reads, writes)

        def stt(out, in0, scalar, in1, op0, op1, reads, writes):
            T.add("dve", lambda e: e.scalar_tensor_tensor(out=out, in0=in0, scalar=scalar, in1=in1, op0=op0, op1=op1), reads, writes)

        def recip(out, in_, reads, writes):
            T.add("dve", lambda e: e.reciprocal(out=out, in_=in_), reads, writes)

        def cp(out, in_, reads, writes, eng="dve"):
            T.add(eng, lambda e: e.tensor_copy(out=out, in_=in_), reads, writes)

        def dma(q, out, in_, reads, writes, dsem):
            T.add(q, lambda e: e.dma_start(out=out, in_=in_), reads, writes, dsem=dsem)

        def memset(ap, val, writes, eng="pool"):
            T.add(eng, lambda e: e.memset(ap, val), (), writes)

        for (dst, src, nm) in [(cT[:], cT_d, "cT"), (bmodT[:], bmodT_d, "bmodT"), (bmodg[:], bmodg_d, "bmodg"),
                               (n1g[:], n1g_d, "n1g"), (n2g[:], n2g_d, "n2g"), (gq[:], gq_d, "gq"), (gk[:], gk_d, "gk"),
                               (gmg[:], gmg_d, "gmg"), (bsT[:], bsT_d, "bsT")]:
            dma("sp", dst, src, (), [nm], "const")
        dma("pool", identb[:], ident_d, (), ["identb"], "cast")
        dma("pool", Wkv[:], win_d[:, 0:256].rearrange("(c p) n -> p c n", p=128), (), ["Wkv"], "cast")
        dma("pool", wsT[:], wsT_d, (), ["wsT"], "cast")
        memset(Vaug[:, :, 64:128], 1.0, [("Vaug", t) for t in range(NT)])
        memset(ones1[:], 1.0, ["ones1"])

        def wsrc_kn(w, c0, ncols):
            return w[:, c0:c0 + ncols].rearrange("(c p) n -> p c n", p=128)

        unit_src = []
        unit_src.append((wsrc_kn(win_d, 256, 512), 8))
        unit_src.append((wsrc_kn(win_d, 768, 512), 8))
        unit_src.append((wsrc_kn(win_d, 1280, 512), 8))
        unit_src.append((wsrc_kn(win_d, 1792, 512), 8))
        unit_src.append((wsrc_kn(win_d, 2816, 512), 8))
        unit_src.append((bra_d.rearrange("(c p) n -> p c n", p=128), 4))
        unit_src.append((brg_d.rearrange("(c p) n -> p c n", p=128), 4))
        unit_src.append((wsrc_kn(win_d, 2304, 512), 8))
        unit_src.append((wsrc_kn(win_d, 3328, 512), 8))
        unit_src.append((wsrc_kn(wout_d, 0, 512), 8))
        unit_src.append((wsrc_kn(wout_d, 512, 512), 8))
        for j in range(8):
            unit_src.append((wsrc_kn(ff1_d, j * 512, 512), 8))
        for nh in range(2):
            for kg in range(4):
                src = ff2_d[kg * 1024:(kg + 1) * 1024, nh * 512:(nh + 1) * 512].rearrange("(c p) n -> p c n", p=128)
                unit_src.append((src, 8))
        assert len(unit_src) == N_UNITS_QB
        def cast_unit(u, extra_reads=()):
            src, nch = unit_src[u]
            dst = wsc[u].rearrange("p (c n) -> p c n", c=nch)
            dma("pool", dst, src, [("cslot", u % 6)] + list(extra_reads), [("wsc", u), ("cslot", u % 6)], f"c{u % 6}")

        N_EARLY = 11 if nqb else N_UNITS_QB
        for u in range(N_EARLY):
            if "cast" in skip:
                break
            cast_unit(u)

        act(scT[:], cT[:], AF.Silu, ["cT"], ["scT"])
        for c in range(8):
            s_ = c % NB
            pa = ring[s_].bitcast(F32)
            dma("sp", pa, wmod_d[c * 128:(c + 1) * 128, 0:2048], (), [("ring", s_)], f"w{s_}")
            for jj in range(16):
                mm(ps[:, 2 * jj:2 * jj + 2], pa[:, jj * 128:(jj + 1) * 128], scT[:, c, :], c == 0 and jj == 0, c == 7 and jj == 15,
                   [("ring", s_), "scT"], [("ps", 0)], sgc=True)
        tt(modT[:, 0:16, :], ps[:, 0:32].rearrange("p (j k) -> p j k", k=2),
           bmodT[:, 0:16].unsqueeze(2).broadcast_to([128, 16, 2]), ALU.add, [("ps", 0), "bmodT"], ["modTa"])
        stt(a1[:], modT[:, 8:16, :], 1.0, n1g[:, :].unsqueeze(2).broadcast_to([128, 8, 2]), ALU.add, ALU.mult, ["modTa", "n1g"], ["a1"])

        def mod_b_buf(c):
            p_ = c % 2
            return p_, ringT[:, (2 * p_) * 4096:(2 * p_ + 2) * 4096].bitcast(F32)

        def mod_b_dma(c):
            p_, pb = mod_b_buf(c)
            dma("sp", pb, wmod_d[c * 128:(c + 1) * 128, 2048:6144], (), [("ring", 2 * p_), ("ring", 2 * p_ + 1)], f"wm{p_}")

        def mod_b_mm(c, j0, j1):
            p_, pb = mod_b_buf(c)
            for jj in range(j0, j1):
                mm(ps[:, 7 * 512 + 2 * jj:7 * 512 + 2 * jj + 2], pb[:, jj * 128:(jj + 1) * 128], scT[:, c, :], c == 0 and jj == 0, c == 7 and jj == 31,
                   [("ring", 2 * p_), ("ring", 2 * p_ + 1), "scT"], [("ps", 7)], sgc=True)

        def mod_b_piece(c):
            mod_b_dma(c)
            mod_b_mm(c, 0, 32)

        def mod_b_finish():
            tt(modT[:, 16:48, :], ps[:, 7 * 512:7 * 512 + 64].rearrange("p (j k) -> p j k", k=2),
               bmodT[:, 16:48].unsqueeze(2).broadcast_to([128, 32, 2]), ALU.add, [("ps", 7), "bmodT"], ["modTb"])
            stt(a2[:], modT[:, 32:40, :], 1.0, n2g[:, :].unsqueeze(2).broadcast_to([128, 8, 2]), ALU.add, ALU.mult, ["modTb", "n2g"], ["a2"])
            for k_, j0 in enumerate((16, 40)):
                T.add("pool", lambda e, k_=k_, j0=j0: e.dma_start(out=gscr[0, k_ * 1024:(k_ + 1) * 1024].rearrange("(c p) -> p c", p=128),
                                                             in_=modT[:, j0:j0 + 8, 0], allow_slow_non_contiguous=True),
                      ["modTb"], [("gscr", k_)], dsem="gb")
            dma("pool", gbc[:], gscr[0, :].partition_broadcast(128), [("gscr", 0), ("gscr", 1)], ["gbc"], "gb2")

        cnt = {"xn": 0, "ev": 0, "sq": 0, "par": 0}

        def nm_stats(xb, xres, nt):
            par = cnt["par"] % 2
            cnt["par"] += 1
            po = par * 4
            for t in range(nt):
                kq = cnt["sq"] % 2
                cnt["sq"] += 1
                act(sqj[kq], xb[:, t, :], AF.Square, [xres], [("ss", par, t), ("rcj", kq)], accum=ss[:, po + t:po + t + 1])
            act(rs[:, po:po + nt], ss[:, po:po + nt], AF.Ln, [("ss", par, t) for t in range(nt)], [("rs", par)], scale=1.0 / D, bias=EPS)
            act(rstd[:, po:po + nt], rs[:, po:po + nt], AF.Exp, [("rs", par)], [("rstd", par)], scale=-0.5)
            return dict(xb=xb, xres=xres, nt=nt, par=par, po=po)

        def xn_tile(xb, xres, t, k, rs_ap, rs_key, pb=0):
            if t % 2 == 0:
                ts(xn[k][:], xb[:, t, :], rs_ap, None, ALU.mult, None, [xres, rs_key], [("xn", k)])
            else:
                act(xn[k][:], xb[:, t, :], AF.Copy, [xres, rs_key], [("xn", k)], scale=rs_ap)
            for c in range(8):
                bk = pb + c // 2
                o = bank(bk).bitcast(BF16)[:, (c % 2) * 512 + t * 128:(c % 2) * 512 + (t + 1) * 128]
                tr(o, xn[k][:, c * 128:(c + 1) * 128], [("xn", k)], [("ps", bk)])

        def nm_xn_tr(cx):
            xb, xres, nt, par, po = cx["xb"], cx["xres"], cx["nt"], cx["par"], cx["po"]
            for t in range(nt):
                k = cnt["xn"] % 2
                cnt["xn"] += 1
                xn_tile(xb, xres, t, k, rstd[:, po + t:po + t + 1], ("rstd", par))

        def nm_evac(cx, a_t, sh_off, col, hd=None, pb=0):
            nt = cx["nt"]
            hdst, hkey = hd if hd is not None else (hT, "hT")
            for c in range(8):
                bk = pb + c // 2
                src = bank(bk).bitcast(BF16)[:, (c % 2) * 512:(c % 2) * 512 + nt * 128]
                if c % 2 == 0:
                    act(hdst[:, c, 0:nt * 128], src, AF.Identity, [("ps", bk), a_t[1], a_t[2]], [(hkey, c)],
                        scale=a_t[0][:, c, col:col + 1], bias=modT[:, sh_off + c, col:col + 1])
                else:
                    ts(hdst[:, c, 0:nt * 128], src, a_t[0][:, c, col:col + 1], modT[:, sh_off + c, col:col + 1], ALU.mult, ALU.add,
                       [("ps", bk), a_t[1], a_t[2]], [(hkey, c)])

        def norm_mod_T(xb, xres, nt, a_t, sh_off, col, hd=None):
            cx = nm_stats(xb, xres, nt)
            nm_xn_tr(cx)
            nm_evac(cx, a_t, sh_off, col, hd)

        def head_rstd(src_sq, nh, res_in):
            T.add("dve", lambda e: e.tensor_reduce(out=hs[:, 0:nh], in_=src_sq.rearrange("p (h d) -> p h d", d=64), axis=AX.X, op=ALU.add),
                  [res_in], ["hs"])
            act(hl[:, 0:nh], hs[:, 0:nh], AF.Ln, ["hs"], ["hl"], scale=1.0 / 64, bias=EPS)
            act(hr[:, 0:nh], hl[:, 0:nh], AF.Exp, ["hl"], ["hr"], scale=-0.5)

        def norm_rope(psrc, psres, nh, gain, gres, rp, rpres, outb, outres):
            W = nh * 64
            s0, s1, s2, s3 = scr[0][:, 0:W], scr[1][:, 0:W], scr[2][:, 0:W], scr[3][:, 0:W]
            act(s0, psrc, AF.Square, [psres], [("scr", 0)])
            head_rstd(s0, nh, ("scr", 0))
            tt(s1, psrc, gain, ALU.mult, [psres, gres], [("scr", 1)])
            tt(s2.rearrange("p (h d) -> p h d", d=64), s1.rearrange("p (h d) -> p h d", d=64),
               hr[:, 0:nh].unsqueeze(2).broadcast_to([128, nh, 64]), ALU.mult, [("scr", 1), "hr"], [("scr", 2)])
            v2 = s2.rearrange("p (h d) -> p h d", d=64)
            tt(s3.rearrange("p (h d) -> p h d", d=64), v2, rp[:, 0:64].unsqueeze(1).broadcast_to([128, nh, 64]), ALU.mult,
               [("scr", 2), rpres], [("scr", 3)])
            v0 = s0.rearrange("p (h d) -> p h d", d=64)
            tt(v0[:, :, 0:32], v2[:, :, 32:64], rp[:, 64:96].unsqueeze(1).broadcast_to([128, nh, 32]), ALU.mult,
               [("scr", 2), rpres], [("scr", 0)])
            tt(v0[:, :, 32:64], v2[:, :, 0:32], rp[:, 96:128].unsqueeze(1).broadcast_to([128, nh, 32]), ALU.mult,
               [("scr", 2), rpres], [("scr", 0)])
            tt(outb, s3, s0, ALU.add, [("scr", 3), ("scr", 0)], [outres])

        def dump(name, ap, res, dt=F32):
            if "nodump" in skip:
                return
            d = nc.dram_tensor("dbg_" + name, list(ap.shape), F32, kind="ExternalOutput").ap()
            dbg[name] = d
            dma("pool", d, ap, res, [("dbgout", name)], "dbg")

        ld = {"n": 0}

        def load_xo(row0, nt, s):
            dma("sp", xbuf[s][:, 0:nt, :], xin[row0:row0 + nt * 128, :].rearrange("(t p) d -> p t d", p=128), (), [("xbuf", s)], f"x{s}")

        def load_rope(row0, nt, s):
            dma("sp", ropeb[s][:, 0:nt, :], rope[row0:row0 + nt * 128, :].rearrange("(t p) d -> p t d", p=128), (), [("ropeb", s)], f"r{s}")

        def load_x(row0, nt):
            s = ld["n"] % 2
            ld["n"] += 1
            load_xo(row0, nt, s)
            load_rope(row0, nt, s)
            return s

        supers = [(0, 2, 1)] + [(256 + i * 512, 4, 0) for i in range(16)]
        if stage == 0:
            supers = []
            for c in range(8):
                mod_b_piece(c)
            mod_b_finish()
            dump("modT", modT[:], ["modTa", "modTb"])
            dump("gbc", gbc[:], ["gbc"])
            dump("a1", a1[:], ["a1"])
        if stage == 1:
            import os
            supers = supers[:int(os.environ.get("NSUP", "3"))]
        krb = big[:, 24:26, :].rearrange("p a b -> p (a b)")
        nsup = len(supers)
        hbufs = [(hT, "hT"), (big[:, 0:8, :], "big")]
        a1t = (a1, "a1", "modTa")

        def a_kv(si):
            row0, nt, col = supers[si]
            hA, hAk = hbufs[si % 2]
            for t in range(nt):
                bk = 4 + t // 2
                for c in range(8):
                    mm(ps[:, bk * 512 + (t % 2) * 256: bk * 512 + (t % 2) * 256 + 256], hA[:, c, t * 128:(t + 1) * 128], Wkv[:, c, :],
                       c == 0, c == 7, [(hAk, c), "Wkv"], [("ps", bk)])
                if 1 <= si <= 8:
                    mod_b_mm(si - 1, 8 * t, 8 * t + 8)

        def a_post(si):
            row0, nt, col = supers[si]
            slot = si % 2
            tile0 = row0 // 128
            W = nt * 128
            kvv = ps[:, 4 * 512:4 * 512 + nt * 256].rearrange("p (t n) -> p t n", n=256)
            kview = kvv[:, :, 0:128]
            kvb = [("ps", 4)] + ([("ps", 5)] if nt > 2 else [])
            s0, s1, s2, s3 = scr[0][:, 0:W], scr[1][:, 0:W], scr[2][:, 0:W], scr[3][:, 0:W]
            tv = lambda a: a.rearrange("p (t n) -> p t n", n=128)
            hv = lambda a: a.rearrange("p (h d) -> p h d", d=64)
            qv = lambda a: a.rearrange("p (t h d) -> p t h d", h=2, d=64)
            rp = ropeb[slot]
            rpk = ("ropeb", slot)
            act(tv(s0), kview, AF.Square, kvb, [("scr", 0)])
            head_rstd(s0, 2 * nt, ("scr", 0))
            tt(tv(s1), kview, gk[:, :].unsqueeze(1).broadcast_to([128, nt, 128]), ALU.mult, kvb + ["gk"], [("scr", 1)])
            tt(hv(s2), hv(s1), hr[:, 0:2 * nt].unsqueeze(2).broadcast_to([128, 2 * nt, 64]), ALU.mult, [("scr", 1), "hr"], [("scr", 2)])
            tt(qv(s3), qv(s2), rp[:, 0:nt, 0:64].unsqueeze(2).broadcast_to([128, nt, 2, 64]), ALU.mult, [("scr", 2), rpk], [("scr", 3)])
            tt(qv(s0)[:, :, :, 0:32], qv(s2)[:, :, :, 32:64], rp[:, 0:nt, 64:96].unsqueeze(2).broadcast_to([128, nt, 2, 32]), ALU.mult,
               [("scr", 2), rpk], [("scr", 0)])
            tt(qv(s0)[:, :, :, 32:64], qv(s2)[:, :, :, 0:32], rp[:, 0:nt, 96:128].unsqueeze(2).broadcast_to([128, nt, 2, 32]), ALU.mult,
               [("scr", 2), rpk], [("scr", 0)])
            tt(krb[:, 0:W], s3, s0, ALU.add, [("scr", 3), ("scr", 0)], [("big", 24)])
            for t in range(nt):
                tr(bank(6).bitcast(BF16)[:, t * 128:(t + 1) * 128], krb[:, t * 128:(t + 1) * 128], [("big", 24)], [("ps", 6)])
            act(Vaug[:, tile0:tile0 + nt, 0:64], kvv[:, :, 128:192], AF.Copy, kvb, [("Vaug", tile0 + t) for t in range(nt)])
            act(Vaug[:, tile0:tile0 + nt, 128:192], kvv[:, :, 192:256], AF.Copy, kvb, [("Vaug", tile0 + t) for t in range(nt)])
            cp(KT[:, tile0 * 128:(tile0 + nt) * 128], bank(6).bitcast(BF16)[:, 0:nt * 128], [("ps", 6)], [("KT", si)])

        actx = {}
        if nsup:
            for k_ in range(min(2, nsup)):
                load_xo(supers[k_][0], supers[k_][1], k_ % 2)
                load_rope(supers[k_][0], supers[k_][1], k_ % 2)
            ld["n"] = 0
            actx[0] = nm_stats(xbuf[0], ("xbuf", 0), supers[0][1])
            nm_xn_tr(actx[0])
            if nsup > 2:
                load_xo(supers[2][0], supers[2][1], 0)
            nm_evac(actx[0], a1t, 0, supers[0][2], hbufs[0])
        for k_ in range(nsup):
            if k_ < 8:
                mod_b_dma(k_)
            n_ = k_ + 1
            if n_ < nsup:
                actx[n_] = nm_stats(xbuf[n_ % 2], ("xbuf", n_ % 2), supers[n_][1])
            a_kv(k_)
            if n_ < nsup:
                nm_xn_tr(actx[n_])
                if k_ + 3 < nsup:
                    load_xo(supers[k_ + 3][0], supers[k_ + 3][1], (k_ + 3) % 2)
            a_post(k_)
            if k_ + 2 < nsup:
                load_rope(supers[k_ + 2][0], supers[k_ + 2][1], k_ % 2)
            if n_ < nsup:
                nm_evac(actx[n_], a1t, 0, supers[n_][2], hbufs[n_ % 2])

        ld["n"] = 1
        slot = load_x(256, 4) if (nqb and stage > 1) else None
        if stage >= 1:
            if len(supers) < 9:
                for c in range(max(len(supers) - 1, 0), 8):
                    if c >= len(supers):
                        mod_b_dma(c)
                    mod_b_mm(c, 0, 32)
            mod_b_finish()
        if stage == 1:
            dump("KT", KT[:, 0:1280], [("KT", i) for i in range(3)], BF16)
            dump("Vaug", Vaug[:, 0:10, :], [("Vaug", i) for i in range(10)], BF16)
            dump("hT", hT[:], [("hT", c) for c in range(8)], BF16)
        if stage <= 1:
            nqb = 0
        wctr = {"n": 0}

        def load_unit(u):
            s = wctr["n"] % NB
            wctr["n"] += 1
            dma("sp", ring[s][:, :], wsc[u], [("wsc", u)], [("ring", s)], f"w{s}")
            return s

        def R(s):
            return ("ring", s)

        def unit8(s):
            return ring[s][:, :].rearrange("p (c n) -> p c n", c=8)

        def unit4(s):
            return ring[s][:, :].rearrange("p (c n) -> p c n", c=4)

        yT = [big[:, c, :] for c in range(8)]
        uT = [big[:, 8 + j, :] for j in range(4)]
        gmT = [big[:, 12 + j, :] for j in range(4)]
        attnT = [big[:, 16 + j, :] for j in range(4)]
        QT = [QTb[:, j, :] for j in range(4)]
        vnb = [big[:, 24 + t, :] for t in range(4)]
        PT = [big[:, 28:30, :].rearrange("p a b -> p (a b)"), big[:, 30:32, :].rearrange("p a b -> p (a b)"),
              big[:, 26:28, :].rearrange("p a b -> p (a b)")]
        PTK = [[("big", 28), ("big", 29)], [("big", 30), ("big", 31)], [("big", 26), ("big", 27)]]

        def pre_q_pieces(slot_):
            xb_, xres_, rpb_ = xbuf[slot_], ("xbuf", slot_), ropeb[slot_]
            st = {}
            qrb = scr[4][:, :].bitcast(BF16)[:, 0:512]

            def p_stats():
                st["cx"] = nm_stats(xb_, xres_, 4)

            def p_xn(t):
                cx = st["cx"]
                k = cnt["xn"] % 2
                cnt["xn"] += 1
                xn_tile(xb_, xres_, t, k, rstd[:, cx["po"] + t:cx["po"] + t + 1], ("rstd", cx["par"]), pb=4)

            def p_evac():
                nm_evac(st["cx"], (a1, "a1", "modTa"), 0, 0, None, pb=4)

            def p_qproj():
                su = load_unit(0)
                for t in range(4):
                    for c in range(8):
                        mm(bank(4 + t), hT[:, c, t * 128:(t + 1) * 128], unit8(su)[:, c, :], c == 0, c == 7, [("hT", c), R(su)], [("ps", 4 + t)])

            def p_qdve(t):
                norm_rope(bank(4 + t), ("ps", 4 + t), 8, gq[:, :], "gq", rpb_[:, t, :], ("ropeb", slot_), qrb, ("scr", 4))

            def p_qpe(t):
                for g in range(4):
                    tr(bank(4 + t).bitcast(BF16)[:, g * 128:(g + 1) * 128], qrb[:, g * 128:(g + 1) * 128], [("scr", 4)], [("ps", 4 + t)])
                cp(QTb[:, :, t * 128:(t + 1) * 128], bank(4 + t).bitcast(BF16)[:, 0:512].rearrange("p (g q) -> p g q", q=128),
                   [("ps", 4 + t)], [("QT", g) for g in range(4)])

            return dict(stats=p_stats, xn=p_xn, evac=p_evac, qproj=p_qproj, qdve=p_qdve, qpe=p_qpe)

        def run_pre_q_all(pq):
            pq["stats"]()
            for t in range(4):
                pq["xn"](t)
            pq["evac"]()
            pq["qproj"]()
            for t in range(4):
                pq["qdve"](t)
                pq["qpe"](t)

        if nqb:
            run_pre_q_all(pre_q_pieces(slot))
            pre_units = (load_unit(1), load_unit(2))

        for qb in range(nqb):
            row0 = 256 + qb * 512
            xb = xbuf[slot]
            xres = ("xbuf", slot)
            rpb = ropeb[slot]
            nslot = None
            sv, suu = pre_units
            for t in range(4):
                for c in range(8):
                    mm(bank(t), hT[:, c, t * 128:(t + 1) * 128], unit8(sv)[:, c, :], c == 0, c == 7, [("hT", c), R(sv)], [("ps", t)])
            for j in range(4):
                for c in range(8):
                    mm(bank(4 + j), unit8(suu)[:, c, j * 128:(j + 1) * 128], hT[:, c, :], c == 0, c == 7, [("hT", c), R(suu)], [("ps", 4 + j)])
            for t in range(4):
                act(scr[t][:, :], bank(t), AF.Gelu_apprx_tanh, [("ps", t)], [("scr", t)])
            for j in range(4):
                act(uT[j], bank(4 + j), AF.Gelu_apprx_tanh, [("ps", 4 + j)], [("big", 8 + j)])
            for t in range(4):
                sq_ = scr[4 + t % 2]
                act(sq_[:, :], scr[t][:, :], AF.Square, [("scr", t)], [("scr", 4 + t % 2)])
                T.add("dve", lambda e, t=t, sq_=sq_: e.tensor_reduce(out=hs32[:, 8 * t:8 * t + 8], in_=sq_[:, :].rearrange("p (h d) -> p h d", d=64),
                                                                     axis=AX.X, op=ALU.add), [("scr", 4 + t % 2)], [("hs32", t)])
            act(hl32[:, :], hs32[:, :], AF.Ln, [("hs32", t) for t in range(4)], ["hl32"], scale=1.0 / 64, bias=EPS)
            act(hr32[:, :], hl32[:, :], AF.Exp, ["hl32"], ["hr32"], scale=-0.5)
            for t in range(4):
                tm_ = scr[4 + t % 2]
                tt(tm_[:, :], scr[t][:, :], gmg[:, :], ALU.mult, [("scr", t), "gmg"], [("scr", 4 + t % 2)])
                tt(vnb[t].rearrange("p (h d) -> p h d", d=64), tm_[:, :].rearrange("p (h d) -> p h d", d=64),
                   hr32[:, 8 * t:8 * t + 8].unsqueeze(2).broadcast_to([128, 8, 64]), ALU.mult, [("scr", 4 + t % 2), "hr32"], [("big", 24 + t)])
                for j in range(4):
                    for gg in range(2):
                        g = 2 * j + gg
                        mm(ps[gg * 64:(gg + 1) * 64, j * 512 + t * 128: j * 512 + (t + 1) * 128], vnb[t][:, g * 64:(g + 1) * 64], wsT[:, g, :],
                           True, True, [("big", 24 + t), "wsT"], [("ps", j)])
            for j in range(4):
                tt(scr[5][:, :].rearrange("p (t q) -> p t q", q=128), bank(j).rearrange("p (t q) -> p t q", q=128),
                   bsT[:, j, :].unsqueeze(1).broadcast_to([128, 4, 128]), ALU.add, [("ps", j), "bsT"], [("scr", 5)])
                tt(gmT[j], scr[5][:, :], uT[j], ALU.mult, [("scr", 5), ("big", 8 + j)], [("big", 12 + j)])

            steps = [(g, kb) for g in range(4) for kb in range(NT)]

            def ksup(kb):
                return 0 if kb < 2 else 1 + (kb - 2) // 4

            def qk(i):
                g, kb = steps[i]
                sbk = (i % 2) * 2
                mm(bank(sbk), KT[0:64, kb * 128:(kb + 1) * 128], QT[g][0:64, :], True, True, [("KT", ksup(kb)), ("QT", g)], [("ps", sbk)])
                mm(bank(sbk + 1), KT[64:128, kb * 128:(kb + 1) * 128], QT[g][64:128, :], True, True, [("KT", ksup(kb)), ("QT", g)], [("ps", sbk + 1)])
                pace = [("pace", i)] if (qb == 0 and i % 12 == 0) else []
                act(PT[i % 3], ps[:, sbk * 512:sbk * 512 + 1024], AF.Exp, [("ps", sbk), ("ps", sbk + 1)], PTK[i % 3] + pace, scale=0.125)
                if pace and N_EARLY + i // 12 < N_UNITS_QB:
                    cast_unit(N_EARLY + i // 12, pace)

            def pv(i):
                g, kb = steps[i]
                oa = 4 + 2 * (g % 2)
                mm(bank(oa), Vaug[:, kb, 0:128], PT[i % 3][:, 0:512], kb == 0, kb == NT - 1, [("Vaug", kb)] + PTK[i % 3], [("ps", oa)])
                mm(bank(oa + 1), Vaug[:, kb, 64:192], PT[i % 3][:, 512:1024], kb == 0, kb == NT - 1, [("Vaug", kb)] + PTK[i % 3], [("ps", oa + 1)])
                if kb == NT - 1:
                    recip(rc[64:128, 0:512], bank(oa)[64:128, :], [("ps", oa)], ["rc", ("rcj", 0), ("rcj", 1)])
                    recip(rc[0:64, 512:1024], bank(oa + 1)[0:64, :], [("ps", oa + 1)], ["rc", ("rcj", 0), ("rcj", 1)])
                    tt(attnT[g][0:64, :], bank(oa)[0:64, :], rc[64:128, 0:512], ALU.mult, [("ps", oa), "rc", ("rcj", 0), ("rcj", 1)], [("big", 16 + g)])
                    tt(attnT[g][64:128, :], bank(oa + 1)[64:128, :], rc[0:64, 512:1024], ALU.mult, [("ps", oa + 1), "rc", ("rcj", 0), ("rcj", 1)], [("big", 16 + g)])

            qk(0)
            qk(1)
            for i in range(len(steps)):
                if i + 2 < len(steps):
                    qk(i + 2)
                pv(i)

            if qb + 1 < nqb:
                nslot = load_x(row0 + 512, 4)

            sga = [load_unit(3), None]
            sgb = [load_unit(4), None]
            sbra = load_unit(5)
            sbrg = load_unit(6)
            for m in range(8):
                if m == 4:
                    sga[1] = load_unit(7)
                    sgb[1] = load_unit(8)
                h = m // 4
                b0 = (m % 2) * 4
                for c in range(8):
                    mm(bank(b0), unit8(sga[h])[:, c, (m % 4) * 128:(m % 4 + 1) * 128], hT[:, c, :], c == 0, c == 7, [("hT", c), R(sga[h])], [("ps", b0)])
                for c in range(8):
                    mm(bank(b0 + 1), unit8(sgb[h])[:, c, (m % 4) * 128:(m % 4 + 1) * 128], hT[:, c, :], c == 0, c == 7, [("hT", c), R(sgb[h])], [("ps", b0 + 1)])
                for c in range(4):
                    mm(bank(b0 + 2), unit4(sbra)[:, c, m * 128:(m + 1) * 128], attnT[c], c == 0, c == 3, [("big", 16 + c), R(sbra)], [("ps", b0 + 2)])
                for c in range(4):
                    mm(bank(b0 + 3), unit4(sbrg)[:, c, m * 128:(m + 1) * 128], gmT[c], c == 0, c == 3, [("big", 12 + c), R(sbrg)], [("ps", b0 + 3)])
                act(scr[0][:, :], bank(b0), AF.Sigmoid, [("ps", b0)], [("scr", 0)])
                act(scr[1][:, :], bank(b0 + 1), AF.Sigmoid, [("ps", b0 + 1)], [("scr", 1)])
                tt(scr[2][:, :], bank(b0 + 2), scr[0][:, :], ALU.mult, [("ps", b0 + 2), ("scr", 0)], [("scr", 2)])
                tt(scr[3][:, :], bank(b0 + 3), scr[1][:, :], ALU.mult, [("ps", b0 + 3), ("scr", 1)], [("scr", 3)])
                tt(yT[m], scr[2][:, :], scr[3][:, :], ALU.add, [("scr", 2), ("scr", 3)], [("big", m)], eng="pool")

            so = [load_unit(9), load_unit(10)]
            par = cnt["par"] % 2
            cnt["par"] += 1
            po = par * 4
            k8c = {"n": 0}

            def b8_mm(t):
                for nh in range(2):
                    bk = 4 + k8c["n"] % 4
                    k8c["n"] += 1
                    for c in range(8):
                        mm(bank(bk), yT[c][:, t * 128:(t + 1) * 128], unit8(so[nh])[:, c, :], c == 0, c == 7, [("big", c), R(so[nh])], [("ps", bk)])
                    sk = 4 + (k8c["n"] % 2)
                    tt(scr[sk][:, :], bank(bk), gbc[:, nh * 512:(nh + 1) * 512], ALU.mult, [("ps", bk), "gbc"], [("scr", sk)])
                    tt(xb[:, t, nh * 512:(nh + 1) * 512], scr[sk][:, :], xb[:, t, nh * 512:(nh + 1) * 512], ALU.add, [("scr", sk), xres], [xres, ("x1t", t)], eng="pool")

            def b9_tile(t):
                kq = cnt["sq"] % 2
                cnt["sq"] += 1
                cs = slice(po + t, po + t + 1)
                act(sqj[kq], xb[:, t, :], AF.Square, [("x1t", t)], [("ss", par, t), ("rcj", kq)], accum=ss[:, cs])
                act(rs[:, cs], ss[:, cs], AF.Ln, [("ss", par, t)], [("rs", par), ("rs", par, t)], scale=1.0 / D, bias=EPS)
                act(rstd[:, cs], rs[:, cs], AF.Exp, [("rs", par, t)], [("rstd", par), ("rstd", par, t)], scale=-0.5)
                k = cnt["xn"] % 2
                cnt["xn"] += 1
                xn_tile(xb, ("x1t", t), t, k, rstd[:, cs], ("rstd", par, t))

            for t in range(4):
                b8_mm(t)
                if t >= 1:
                    b9_tile(t - 1)
            b9_tile(3)
            nm_evac(dict(nt=4), (a2, "a2", "modTb"), 24, 0)

            pq = pre_q_pieces(nslot) if nslot is not None else None
            if pq:
                pq["stats"]()
            for j in range(32):
                if j % 4 == 0:
                    sf = load_unit(11 + j // 4)
                bk = j % 8
                for c in range(8):
                    mm(bank(bk), unit8(sf)[:, c, (j % 4) * 128:(j % 4 + 1) * 128], hT[:, c, :], c == 0, c == 7, [("hT", c), R(sf)], [("ps", bk)])
                sk = j % 4
                act(scr[sk][:, :], bank(bk), AF.Relu, [("ps", bk)], [("scr", sk)])
                tt(big[:, j, :], scr[sk][:, :], scr[sk][:, :], ALU.mult, [("scr", sk)], [("big", j)])

            if pq:
                pq["xn"](0)
                pq["xn"](1)
            blk = 0
            for nh in range(2):
                for kg in range(4):
                    s2 = load_unit(19 + nh * 4 + kg)
                    for t in range(4):
                        bk = t
                        for c in range(8):
                            mm(bank(bk), big[:, kg * 8 + c, t * 128:(t + 1) * 128], unit8(s2)[:, c, :], kg == 0 and c == 0, kg == 3 and c == 7,
                               [("big", kg * 8 + c), R(s2)], [("ps", bk)])
                    if pq:
                        if blk == 0:
                            pq["xn"](2)
                            pq["xn"](3)
                            pq["evac"]()
                        elif blk == 1:
                            pq["qproj"]()
                        elif blk == 2:
                            pq["qdve"](0)
                        elif blk in (3, 4, 5):
                            pq["qpe"](blk - 3)
                            pq["qdve"](blk - 2)
                        elif blk == 6:
                            pq["qpe"](3)
                    blk += 1
                for t in range(4):
                    bk = t
                    sk = 5
                    tt(scr[sk][:, :], bank(bk), gbc[:, 1024 + nh * 512:1024 + (nh + 1) * 512], ALU.mult, [("ps", bk), "gbc"], [("scr", sk)])
                    tt(xb[:, t, nh * 512:(nh + 1) * 512], scr[sk][:, :], xb[:, t, nh * 512:(nh + 1) * 512], ALU.add, [("scr", sk), xres], [xres], eng="pool")
            if qb + 1 < nqb:
                pre_units = (load_unit(1), load_unit(2))
            dma("sp", out_d[qb * 512:(qb + 1) * 512, :].rearrange("(t p) d -> p t d", p=128), xb[:, :, :], [xres], [("out", qb)], f"o{slot}")
            slot = nslot

        T.add("sp", lambda e: e.nop(), [("out", q) for q in range(nqb)] + [("dbgout", n) for n in dbg], [])

        T.finalize()
        nc._tracker = T

        @block.sync
        def _(e):
            T.emit("sp", e, esems, dsems)

        @block.tensor
        def _(e):
            T.emit("pe", e, esems, dsems)

        @block.scalar
        def _(e):
            T.emit("act", e, esems, dsems)

        @block.vector
        def _(e):
            T.emit("dve", e, esems, dsems)

        @block.gpsimd
        def _(e):
            T.emit("pool", e, esems, dsems)

    return nc


_CACHE = {}


def _rope_table(tok_idx):
    n = tok_idx.shape[0]
    t = np.maximum(tok_idx, 0)
    row = (t // 64).astype(np.float32)
    colp = (t % 64).astype(np.float32)
    inv = (np.float32(10000.0) ** (-np.arange(0, 32, 2, dtype=np.float32) / np.float32(32))).astype(np.float32)
    ang = np.concatenate([row[:, None] * inv[None, :], colp[:, None] * inv[None, :]], axis=-1).astype(np.float32)
    cos = np.cos(ang).astype(np.float32)
    sin = np.sin(ang).astype(np.float32)
    ident = tok_idx < 0
    cos[ident] = 1.0
    sin[ident] = 0.0
    return np.concatenate([cos, cos, -sin, sin], axis=1).astype(np.float32)


def kernel(x, c, ctx, c_ctx, w_mod, b_mod, norm1_g, norm2_g, w_in, q_norm_g, k_norm_g,
           gm_norm_g, gm_ws, gm_bs, w_br_attn, w_br_gm, w_out, w_ff1, w_ff2):
    f = lambda a: np.ascontiguousarray(np.asarray(a, dtype=np.float32))
    x, c, ctx, c_ctx = f(x), f(c), f(ctx), f(c_ctx)
    w_mod, b_mod, w_in = f(w_mod)[0], f(b_mod)[0], f(w_in)[0]
    n1, n2 = f(norm1_g)[0], f(norm2_g)[0]
    qg, kg, gmg = f(q_norm_g)[0], f(k_norm_g)[0], f(gm_norm_g)[0]
    ws, bs = f(gm_ws)[0], f(gm_bs)[0]
    bra, brg, wo, w1, w2 = f(w_br_attn)[0], f(w_br_gm)[0], f(w_out)[0], f(w_ff1)[0], f(w_ff2)[0]

    if "nc" not in _CACHE:
        _CACHE["nc"] = build_program()
    nc = _CACHE["nc"]

    qcols = 256 + np.array([kv * 256 + g * 64 + d for g in range(4) for kv in range(2) for d in range(64)])
    order = np.concatenate([np.arange(0, 256), qcols, np.arange(1280, 1792), np.arange(768, 1280), np.arange(1792, 3840)])
    w_in_p = np.ascontiguousarray(w_in[:, order])
    rows = np.array([kv * 256 + g * 64 + d for g in range(4) for kv in range(2) for d in range(64)])
    w_bra_p = np.ascontiguousarray(bra[rows, :])
    b_modT = np.ascontiguousarray(b_mod.reshape(48, 128).T)
    b_modg = np.ascontiguousarray(np.concatenate([b_mod[2048:3072], b_mod[5120:6144]])[None, :])
    n1g = np.ascontiguousarray(n1.reshape(8, 128).T)
    n2g = np.ascontiguousarray(n2.reshape(8, 128).T)
    gq_bc = np.ascontiguousarray(np.broadcast_to(np.tile(qg, 8)[None, :], (128, 512)))
    gk_bc = np.ascontiguousarray(np.broadcast_to(np.tile(kg, 2)[None, :], (128, 128)))
    gmg_bc = np.ascontiguousarray(np.broadcast_to(gmg.reshape(512)[None, :], (128, 512)))
    wsT = np.ascontiguousarray(ws.transpose(2, 0, 1))
    bsT = np.ascontiguousarray(np.broadcast_to(bs.reshape(4, 2, 1, 128), (4, 2, 64, 128)).transpose(1, 2, 0, 3).reshape(128, 4, 128))
    ident = np.eye(128, dtype=np.float32)

    in_maps = []
    for core in range(8):
        b, hf = core // 2, core % 2
        own = np.arange(hf * 4096, (hf + 1) * 4096)
        oth = np.arange((1 - hf) * 4096, (2 - hf) * 4096)
        xin = np.concatenate([ctx[b], x[b, own], x[b, oth]], axis=0)
        tok = np.concatenate([-np.ones(256, dtype=np.int64), own, oth])
        cT = np.ascontiguousarray(np.stack([c[b], c_ctx], axis=1).reshape(8, 128, 2).transpose(1, 0, 2))
        in_maps.append({
            "xin": np.ascontiguousarray(xin), "rope": _rope_table(tok), "cT": cT, "w_mod": w_mod, "b_modT": b_modT,
            "b_modg": b_modg, "n1g": n1g, "n2g": n2g, "w_in_p": w_in_p, "gq_bc": gq_bc, "gk_bc": gk_bc, "gmg_bc": gmg_bc,
            "wsT": wsT, "bsT": bsT, "w_bra_p": w_bra_p, "w_brg": brg, "w_out": wo, "w_ff1": w1, "w_ff2": w2, "ident": ident,
        })
    res = run_bass_kernel_spmd(nc, in_maps, core_ids=list(range(8)))
    out = np.empty((4, 8192, D), dtype=np.float32)
    for core in range(8):
        b, hf = core // 2, core % 2
        out[b, hf * 4096:(hf + 1) * 4096] = res.results[core]["out"]
    return out
```

```python
import numpy as np
import concourse.bass as bass
import concourse.mybir as mybir
from concourse.bass_utils import run_bass_kernel_spmd

F32 = mybir.dt.float32
BF16 = mybir.dt.bfloat16
AF = mybir.ActivationFunctionType
ALU = mybir.AluOpType
AX = mybir.AxisListType

D = 1024
NCTX_T = 2
NOWN_T = 32
NT = 66
NQB = 8
EPS = 1e-6
NB = 5
N_UNITS_QB = 27


class Tracker:
    def __init__(self):
        self.ops = []
        self.last_w = {}
        self.readers = {}
        self.dcount = {}

    def add(self, eng, fn, reads=(), writes=(), dsem=None):
        idx = len(self.ops)
        deps = set()
        if eng in ("act", "dve"):
            writes = list(writes) + [("pslk", r[1]) for r in reads if isinstance(r, tuple) and r[0] == "ps"]
        for r in reads:
            if r in self.last_w:
                deps.add(self.last_w[r])
        for w in writes:
            if w in self.last_w:
                deps.add(self.last_w[w])
            for rd in self.readers.get(w, ()):
                deps.add(rd)
        op = dict(eng=eng, fn=fn, deps=deps, dsem=dsem, marked=False, idx=idx, val=None, desc=(tuple(reads), tuple(writes)))
        if dsem is not None:
            self.dcount[dsem] = self.dcount.get(dsem, 0) + 16
            op["dval"] = self.dcount[dsem]
        self.ops.append(op)
        for r in reads:
            self.readers.setdefault(r, []).append(idx)
        for w in writes:
            self.last_w[w] = idx
            self.readers[w] = []
        return idx

    def finalize(self):
        ops = self.ops
        for op in ops:
            red = {}
            for d in op["deps"]:
                dop = ops[d]
                if dop["dsem"] is not None:
                    key = ("d", dop["dsem"])
                    if key not in red or ops[red[key]]["dval"] < dop["dval"]:
                        red[key] = d
                else:
                    if dop["eng"] == "pe" and op["eng"] == "pe" and op["dsem"] is None:
                        continue
                    key = ("e", dop["eng"])
                    if key not in red or red[key] < d:
                        red[key] = d
            op["rdeps"] = list(red.values())
            for d in op["rdeps"]:
                if ops[d]["dsem"] is None:
                    ops[d]["marked"] = True
        cnt = {}
        for op in ops:
            if op["dsem"] is None and op["marked"]:
                cnt[op["eng"]] = cnt.get(op["eng"], 0) + 1
                op["val"] = cnt[op["eng"]]

    def trace(self, engname):
        waited = {}
        out = []
        for op in self.ops:
            if op["eng"] != engname:
                continue
            ws = []
            for d in op["rdeps"]:
                dop = self.ops[d]
                if dop["dsem"] is not None:
                    val = self.dcount[dop["dsem"]] if dop["dsem"] in ("const", "cast", "gb") else dop["dval"]
                    key = ("d", dop["dsem"])
                else:
                    val, key = dop["val"], ("e", dop["eng"])
                if waited.get(key, 0) < val:
                    ws.append((key[1], val))
                    waited[key] = val
            inc = (op["dsem"], op.get("dval")) if op["dsem"] else ((engname, op["val"]) if op["marked"] else None)
            out.append((op["idx"], ws, op["desc"], inc))
        return out

    def emit(self, engname, engobj, esems, dsems):
        waited = {}
        for op in self.ops:
            if op["eng"] != engname:
                continue
            for d in op["rdeps"]:
                dop = self.ops[d]
                if dop["dsem"] is not None:
                    val = self.dcount[dop["dsem"]] if dop["dsem"] in ("const", "cast", "gb") else dop["dval"]
                    sem, key = dsems[dop["dsem"]], ("d", dop["dsem"])
                else:
                    sem, val, key = esems[dop["eng"]], dop["val"], ("e", dop["eng"])
                if waited.get(key, 0) < val:
                    engobj.wait_ge(sem, val)
                    waited[key] = val
            ins = op["fn"](engobj)
            if op["dsem"] is not None:
                ins.then_inc(dsems[op["dsem"]], 16)
            elif op["marked"]:
                ins.then_inc(esems[engname], 1)


def build_program(stage=3, nqb=NQB, skip=()):
    nc = bass.Bass("TRN2", target_bir_lowering=False)
    T = Tracker()
    dbg = {}

    def din(name, shape, dt=F32):
        return nc.dram_tensor(name, list(shape), dt, kind="ExternalInput").ap()

    xin = din("xin", [NT * 128, D])
    rope = din("rope", [NT * 128, 128])
    cT_d = din("cT", [128, 8, 2])
    wmod_d = din("w_mod", [D, 6 * D])
    bmodT_d = din("b_modT", [128, 48])
    bmodg_d = din("b_modg", [1, 2048])
    n1g_d = din("n1g", [128, 8])
    n2g_d = din("n2g", [128, 8])
    win_d = din("w_in_p", [D, 3840])
    gq_d = din("gq_bc", [128, 512])
    gk_d = din("gk_bc", [128, 128])
    gmg_d = din("gmg_bc", [128, 512])
    wsT_d = din("wsT", [128, 8, 128])
    bsT_d = din("bsT", [128, 4, 128])
    bra_d = din("w_bra_p", [512, D])
    brg_d = din("w_brg", [512, D])
    wout_d = din("w_out", [D, D])
    ff1_d = din("w_ff1", [D, 4 * D])
    ff2_d = din("w_ff2", [4 * D, D])
    ident_d = din("ident", [128, 128])
    out_d = nc.dram_tensor("out", [NOWN_T * 128, D], F32, kind="ExternalOutput").ap()
    gscr = nc.dram_tensor("gscr", [1, 2048], F32, kind="Internal").ap()
    wsc = nc.dram_tensor("wscratch", [N_UNITS_QB, 128, 4096], BF16, kind="Internal").ap()

    import contextlib
    es = contextlib.ExitStack()

    def sb(name, shape, dt):
        return es.enter_context(nc.sbuf_tensor(name, list(shape), dt))

    with es:
        KT = sb("KT", [128, NT * 128], BF16)
        Vaug = sb("Vaug", [128, NT, 192], BF16)
        xbuf = [sb(f"xbuf{i}", [128, 4, D], F32) for i in range(2)]
        ropeb = [sb(f"ropeb{i}", [128, 4, 128], F32) for i in range(2)]
        xn = [sb(f"xn{i}", [128, D], BF16) for i in range(2)]
        hT = sb("hT", [128, 8, 512], BF16)
        big = sb("big", [128, 32, 512], BF16)
        ringT = sb("ring", [128, NB * 4096], BF16)
        ring = [ringT[:, i * 4096:(i + 1) * 4096] for i in range(NB)]
        gbc = sb("gbc", [128, 2048], F32)
        scr = [sb(f"scr{i}", [128, 512], F32) for i in range(6)]
        rc = sb("rc", [128, 1024], F32)
        QTb = sb("QTb", [128, 4, 512], BF16)
        Wkv = sb("Wkv", [128, 8, 256], BF16)
        identb = sb("identb", [128, 128], BF16)
        gq = sb("gq", [128, 512], F32)
        gk = sb("gk", [128, 128], F32)
        gmg = sb("gmg", [128, 512], F32)
        wsT = sb("wsTb", [128, 8, 128], BF16)
        bsT = sb("bsTs", [128, 4, 128], F32)
        cT = sb("cTs", [128, 8, 2], F32)
        scT = sb("scT", [128, 8, 2], F32)
        bmodT = sb("bmodTs", [128, 48], F32)
        n1g = sb("n1gs", [128, 8], F32)
        n2g = sb("n2gs", [128, 8], F32)
        modT = sb("modT", [128, 48, 2], F32)
        a1 = sb("a1", [128, 8, 2], F32)
        a2 = sb("a2", [128, 8, 2], F32)
        ones1 = sb("ones1", [1, 128], F32)
        ss = sb("ss", [128, 8], F32)
        rs = sb("rs", [128, 8], F32)
        rstd = sb("rstd", [128, 8], F32)
        hs = sb("hs", [128, 8], F32)
        hl = sb("hl", [128, 8], F32)
        hr = sb("hr", [128, 8], F32)
        hs32 = sb("hs32", [128, 32], F32)
        hl32 = sb("hl32", [128, 32], F32)
        hr32 = sb("hr32", [128, 32], F32)
        ps = es.enter_context(nc.psum_tensor("ps", [128, 4096], F32))
        sqj = [rc[:, 0:512].bitcast(BF16), rc[:, 512:1024].bitcast(BF16)]
        grow = big[0:1, 0:8, :].rearrange("p a b -> p (a b)").bitcast(F32)
        bmodg = big[0:1, 8:16, :].rearrange("p a b -> p (a b)").bitcast(F32)

        esems = {e: es.enter_context(nc.semaphore("sem_" + e)) for e in ["pe", "act", "dve", "pool", "sp"]}
        dnames = [f"c{i}" for i in range(6)] + ["dbg", "gb", "gb2", "const", "cast", "x0", "x1", "r0", "r1", "o0", "o1", "wm0", "wm1", "wm2"] + [f"w{i}" for i in range(NB)]
        dsems = {d: es.enter_context(nc.semaphore("ds_" + d)) for d in dnames}
        block = es.enter_context(nc.Block())

        def bank(b):
            return ps[:, b * 512:(b + 1) * 512]

        def mm(out, lhsT, rhs, start, stop, reads, writes, sgc=False):
            T.add("pe", lambda e: e.matmul(out, lhsT=lhsT, rhs=rhs, start=start, stop=stop, skip_group_check=sgc), reads, writes)

        def tr(out, in_, reads, writes):
            T.add("pe", lambda e: e.transpose(out, in_, identb[:, :]), list(reads) + ["identb"], writes)

        def act(out, in_, func, reads, writes, scale=None, bias=None, accum=None):
            kw = {}
            if scale is not None:
                kw["scale"] = scale
            if bias is not None:
                kw["bias"] = bias
            if accum is not None:
                kw["accum_out"] = accum
            T.add("act", lambda e: e.activation(out=out, in_=in_, func=func, **kw), reads, writes)

        def tt(out, in0, in1, op, reads, writes, eng="dve"):
            T.add(eng, lambda e: e.tensor_tensor(out=out, in0=in0, in1=in1, op=op), reads, writes)

        def ts(out, in0, s1, s2, op0, op1, reads, writes, eng="dve"):
            if op1 is None:
                T.add(eng, lambda e: e.tensor_scalar(out=out, in0=in0, scalar1=s1, scalar2=None, op0=op0), reads, writes)
            else:
                T.add(eng, lambda e: e.tensor_scalar(out=out, in0=in0, scalar1=s1, scalar2=s2, op0=op0, op1=op1), reads, writes)

        def stt(out, in0, scalar, in1, op0, op1, reads, writes):
            T.add("dve", lambda e: e.scalar_tensor_tensor(out=out, in0=in0, scalar=scalar, in1=in1, op0=op0, op1=op1), reads, writes)

        def recip(out, in_, reads, writes):
            T.add("dve", lambda e: e.reciprocal(out=out, in_=in_), reads, writes)

        def cp(out, in_, reads, writes, eng="dve"):
            T.add(eng, lambda e: e.tensor_copy(out=out, in_=in_), reads, writes)

        def dma(q, out, in_, reads, writes, dsem):
            T.add(q, lambda e: e.dma_start(out=out, in_=in_), reads, writes, dsem=dsem)

        def memset(ap, val, writes, eng="pool"):
            T.add(eng, lambda e: e.memset(ap, val), (), writes)

        for (dst, src, nm) in [(cT[:], cT_d, "cT"), (bmodT[:], bmodT_d, "bmodT"), (bmodg[:], bmodg_d, "bmodg"),
                               (n1g[:], n1g_d, "n1g"), (n2g[:], n2g_d, "n2g"), (gq[:], gq_d, "gq"), (gk[:], gk_d, "gk"),
                               (gmg[:], gmg_d, "gmg"), (bsT[:], bsT_d, "bsT")]:
            dma("sp", dst, src, (), [nm], "const")
        dma("pool", identb[:], ident_d, (), ["identb"], "cast")
        dma("pool", Wkv[:], win_d[:, 0:256].rearrange("(c p) n -> p c n", p=128), (), ["Wkv"], "cast")
        dma("pool", wsT[:], wsT_d, (), ["wsT"], "cast")
        memset(Vaug[:, :, 64:128], 1.0, [("Vaug", t) for t in range(NT)])
        memset(ones1[:], 1.0, ["ones1"])

        def wsrc_kn(w, c0, ncols):
            return w[:, c0:c0 + ncols].rearrange("(c p) n -> p c n", p=128)

        unit_src = []
        unit_src.append((wsrc_kn(win_d, 256, 512), 8))
        unit_src.append((wsrc_kn(win_d, 768, 512), 8))
        unit_src.append((wsrc_kn(win_d, 1280, 512), 8))
        unit_src.append((wsrc_kn(win_d, 1792, 512), 8))
        unit_src.append((wsrc_kn(win_d, 2816, 512), 8))
        unit_src.append((bra_d.rearrange("(c p) n -> p c n", p=128), 4))
        unit_src.append((brg_d.rearrange("(c p) n -> p c n", p=128), 4))
        unit_src.append((wsrc_kn(win_d, 2304, 512), 8))
        unit_src.append((wsrc_kn(win_d, 3328, 512), 8))
        unit_src.append((wsrc_kn(wout_d, 0, 512), 8))
        unit_src.append((wsrc_kn(wout_d, 512, 512), 8))
        for j in range(8):
            unit_src.append((wsrc_kn(ff1_d, j * 512, 512), 8))
        for nh in range(2):
            for kg in range(4):
                src = ff2_d[kg * 1024:(kg + 1) * 1024, nh * 512:(nh + 1) * 512].rearrange("(c p) n -> p c n", p=128)
                unit_src.append((src, 8))
        assert len(unit_src) == N_UNITS_QB
        def cast_unit(u, extra_reads=()):
            src, nch = unit_src[u]
            dst = wsc[u].rearrange("p (c n) -> p c n", c=nch)
            dma("pool", dst, src, [("cslot", u % 6)] + list(extra_reads), [("wsc", u), ("cslot", u % 6)], f"c{u % 6}")

        N_EARLY = 11 if nqb else N_UNITS_QB
        for u in range(N_EARLY):
            if "cast" in skip:
                break
            cast_unit(u)

        act(scT[:], cT[:], AF.Silu, ["cT"], ["scT"])
        for c in range(8):
            s_ = c % NB
            pa = ring[s_].bitcast(F32)
            dma("sp" if c % 2 == 0 else "act", pa, wmod_d[c * 128:(c + 1) * 128, 0:2048], (), [("ring", s_)], f"w{s_}")
            for jj in range(16):
                mm(ps[:, 2 * jj:2 * jj + 2], pa[:, jj * 128:(jj + 1) * 128], scT[:, c, :], c == 0 and jj == 0, c == 7 and jj == 15,
                   [("ring", s_), "scT"], [("ps", 0)], sgc=True)
        tt(modT[:, 0:16, :], ps[:, 0:32].rearrange("p (j k) -> p j k", k=2),
           bmodT[:, 0:16].unsqueeze(2).broadcast_to([128, 16, 2]), ALU.add, [("ps", 0), "bmodT"], ["modTa"])
        stt(a1[:], modT[:, 8:16, :], 1.0, n1g[:, :].unsqueeze(2).broadcast_to([128, 8, 2]), ALU.add, ALU.mult, ["modTa", "n1g"], ["a1"])

        def mod_b_buf(c):
            p_ = c % 2
            return p_, ringT[:, (2 * p_) * 4096:(2 * p_ + 2) * 4096].bitcast(F32)

        def mod_b_dma(c):
            p_, pb = mod_b_buf(c)
            dma("sp", pb, wmod_d[c * 128:(c + 1) * 128, 2048:6144], (), [("ring", 2 * p_), ("ring", 2 * p_ + 1)], f"wm{p_}")

        def mod_b_mm(c, j0, j1):
            p_, pb = mod_b_buf(c)
            for jj in range(j0, j1):
                mm(ps[:, 7 * 512 + 2 * jj:7 * 512 + 2 * jj + 2], pb[:, jj * 128:(jj + 1) * 128], scT[:, c, :], c == 0 and jj == 0, c == 7 and jj == 31,
                   [("ring", 2 * p_), ("ring", 2 * p_ + 1), "scT"], [("ps", 7)], sgc=True)

        def mod_b_piece(c):
            mod_b_dma(c)
            mod_b_mm(c, 0, 32)

        def mod_b_finish():
            tt(modT[:, 16:48, :], ps[:, 7 * 512:7 * 512 + 64].rearrange("p (j k) -> p j k", k=2),
               bmodT[:, 16:48].unsqueeze(2).broadcast_to([128, 32, 2]), ALU.add, [("ps", 7), "bmodT"], ["modTb"])
            stt(a2[:], modT[:, 32:40, :], 1.0, n2g[:, :].unsqueeze(2).broadcast_to([128, 8, 2]), ALU.add, ALU.mult, ["modTb", "n2g"], ["a2"])
            for k_, j0 in enumerate((16, 40)):
                T.add("pool", lambda e, k_=k_, j0=j0: e.dma_start(out=gscr[0, k_ * 1024:(k_ + 1) * 1024].rearrange("(c p) -> p c", p=128),
                                                             in_=modT[:, j0:j0 + 8, 0], allow_slow_non_contiguous=True),
                      ["modTb"], [("gscr", k_)], dsem="gb")
            dma("pool", gbc[:], gscr[0, :].partition_broadcast(128), [("gscr", 0), ("gscr", 1)], ["gbc"], "gb2")

        cnt = {"xn": 0, "ev": 0, "sq": 0, "par": 0}

        def nm_stats(xb, xres, nt):
            par = cnt["par"] % 2
            cnt["par"] += 1
            po = par * 4
            for t in range(nt):
                kq = cnt["sq"] % 2
                cnt["sq"] += 1
                act(sqj[kq], xb[:, t, :], AF.Square, [xres], [("ss", par, t), ("rcj", kq)], accum=ss[:, po + t:po + t + 1])
            act(rs[:, po:po + nt], ss[:, po:po + nt], AF.Ln, [("ss", par, t) for t in range(nt)], [("rs", par)], scale=1.0 / D, bias=EPS)
            act(rstd[:, po:po + nt], rs[:, po:po + nt], AF.Exp, [("rs", par)], [("rstd", par)], scale=-0.5)
            return dict(xb=xb, xres=xres, nt=nt, par=par, po=po)

        def xn_tile(xb, xres, t, k, rs_ap, rs_key, pb=0):
            if t % 2 == 0:
                ts(xn[k][:], xb[:, t, :], rs_ap, None, ALU.mult, None, [xres, rs_key], [("xn", k)])
            else:
                act(xn[k][:], xb[:, t, :], AF.Copy, [xres, rs_key], [("xn", k)], scale=rs_ap)
            for c in range(8):
                bk = pb + c // 2
                o = bank(bk).bitcast(BF16)[:, (c % 2) * 512 + t * 128:(c % 2) * 512 + (t + 1) * 128]
                tr(o, xn[k][:, c * 128:(c + 1) * 128], [("xn", k)], [("ps", bk)])

        def nm_xn_tr(cx):
            xb, xres, nt, par, po = cx["xb"], cx["xres"], cx["nt"], cx["par"], cx["po"]
            for t in range(nt):
                k = cnt["xn"] % 2
                cnt["xn"] += 1
                xn_tile(xb, xres, t, k, rstd[:, po + t:po + t + 1], ("rstd", par))

        def nm_evac(cx, a_t, sh_off, col, hd=None, pb=0):
            nt = cx["nt"]
            hdst, hkey = hd if hd is not None else (hT, "hT")
            for c in range(8):
                bk = pb + c // 2
                src = bank(bk).bitcast(BF16)[:, (c % 2) * 512:(c % 2) * 512 + nt * 128]
                if c % 2 == 0:
                    act(hdst[:, c, 0:nt * 128], src, AF.Identity, [("ps", bk), a_t[1], a_t[2]], [(hkey, c)],
                        scale=a_t[0][:, c, col:col + 1], bias=modT[:, sh_off + c, col:col + 1])
                else:
                    ts(hdst[:, c, 0:nt * 128], src, a_t[0][:, c, col:col + 1], modT[:, sh_off + c, col:col + 1], ALU.mult, ALU.add,
                       [("ps", bk), a_t[1], a_t[2]], [(hkey, c)])

        def norm_mod_T(xb, xres, nt, a_t, sh_off, col, hd=None):
            cx = nm_stats(xb, xres, nt)
            nm_xn_tr(cx)
            nm_evac(cx, a_t, sh_off, col, hd)

        def head_rstd(src_sq, nh, res_in):
            T.add("dve", lambda e: e.tensor_reduce(out=hs[:, 0:nh], in_=src_sq.rearrange("p (h d) -> p h d", d=64), axis=AX.X, op=ALU.add),
                  [res_in], ["hs"])
            act(hl[:, 0:nh], hs[:, 0:nh], AF.Ln, ["hs"], ["hl"], scale=1.0 / 64, bias=EPS)
            act(hr[:, 0:nh], hl[:, 0:nh], AF.Exp, ["hl"], ["hr"], scale=-0.5)

        def norm_rope(psrc, psres, nh, gain, gres, rp, rpres, outb, outres):
            W = nh * 64
            s0, s1, s2, s3 = scr[0][:, 0:W], scr[1][:, 0:W], scr[2][:, 0:W], scr[3][:, 0:W]
            act(s0, psrc, AF.Square, [psres], [("scr", 0)])
            head_rstd(s0, nh, ("scr", 0))
            tt(s1, psrc, gain, ALU.mult, [psres, gres], [("scr", 1)])
            tt(s2.rearrange("p (h d) -> p h d", d=64), s1.rearrange("p (h d) -> p h d", d=64),
               hr[:, 0:nh].unsqueeze(2).broadcast_to([128, nh, 64]), ALU.mult, [("scr", 1), "hr"], [("scr", 2)])
            v2 = s2.rearrange("p (h d) -> p h d", d=64)
            tt(s3.rearrange("p (h d) -> p h d", d=64), v2, rp[:, 0:64].unsqueeze(1).broadcast_to([128, nh, 64]), ALU.mult,
               [("scr", 2), rpres], [("scr", 3)])
            v0 = s0.rearrange("p (h d) -> p h d", d=64)
            tt(v0[:, :, 0:32], v2[:, :, 32:64], rp[:, 64:96].unsqueeze(1).broadcast_to([128, nh, 32]), ALU.mult,
               [("scr", 2), rpres], [("scr", 0)])
            tt(v0[:, :, 32:64], v2[:, :, 0:32], rp[:, 96:128].unsqueeze(1).broadcast_to([128, nh, 32]), ALU.mult,
               [("scr", 2), rpres], [("scr", 0)])
            tt(outb, s3, s0, ALU.add, [("scr", 3), ("scr", 0)], [outres])

        def dump(name, ap, res, dt=F32):
            if "nodump" in skip:
                return
            d = nc.dram_tensor("dbg_" + name, list(ap.shape), F32, kind="ExternalOutput").ap()
            dbg[name] = d
            dma("pool", d, ap, res, [("dbgout", name)], "dbg")

        ld = {"n": 0}

        def load_xo(row0, nt, s):
            dma("sp", xbuf[s][:, 0:nt, :], xin[row0:row0 + nt * 128, :].rearrange("(t p) d -> p t d", p=128), (), [("xbuf", s)], f"x{s}")

        def load_rope(row0, nt, s):
            dma("sp", ropeb[s][:, 0:nt, :], rope[row0:row0 + nt * 128, :].rearrange("(t p) d -> p t d", p=128), (), [("ropeb", s)], f"r{s}")

        def load_x(row0, nt):
            s = ld["n"] % 2
            ld["n"] += 1
            load_xo(row0, nt, s)
            load_rope(row0, nt, s)
            return s

        supers = [(0, 2, 1)] + [(256 + i * 512, 4, 0) for i in range(16)]
        if stage == 0:
            supers = []
            for c in range(8):
                mod_b_piece(c)
            mod_b_finish()
            dump("modT", modT[:], ["modTa", "modTb"])
            dump("gbc", gbc[:], ["gbc"])
            dump("a1", a1[:], ["a1"])
        if stage == 1:
            import os
            supers = supers[:int(os.environ.get("NSUP", "3"))]
        krb = big[:, 24:26, :].rearrange("p a b -> p (a b)")
        nsup = len(supers)
        hbufs = [(hT, "hT"), (big[:, 0:8, :], "big")]
        a1t = (a1, "a1", "modTa")

        def a_kv(si):
            row0, nt, col = supers[si]
            hA, hAk = hbufs[si % 2]
            for t in range(nt):
                bk = 4 + t // 2
                for c in range(8):
                    mm(ps[:, bk * 512 + (t % 2) * 256: bk * 512 + (t % 2) * 256 + 256], hA[:, c, t * 128:(t + 1) * 128], Wkv[:, c, :],
                       c == 0, c == 7, [(hAk, c), "Wkv"], [("ps", bk)])
                if 1 <= si <= 8:
                    mod_b_mm(si - 1, 8 * t, 8 * t + 8)

        def a_post(si):
            row0, nt, col = supers[si]
            slot = si % 2
            tile0 = row0 // 128
            W = nt * 128
            kvv = ps[:, 4 * 512:4 * 512 + nt * 256].rearrange("p (t n) -> p t n", n=256)
            kview = kvv[:, :, 0:128]
            kvb = [("ps", 4)] + ([("ps", 5)] if nt > 2 else [])
            s0, s1, s2, s3 = scr[0][:, 0:W], scr[1][:, 0:W], scr[2][:, 0:W], scr[3][:, 0:W]
            tv = lambda a: a.rearrange("p (t n) -> p t n", n=128)
            hv = lambda a: a.rearrange("p (h d) -> p h d", d=64)
            qv = lambda a: a.rearrange("p (t h d) -> p t h d", h=2, d=64)
            rp = ropeb[slot]
            rpk = ("ropeb", slot)
            act(tv(s0), kview, AF.Square, kvb, [("scr", 0)])
            head_rstd(s0, 2 * nt, ("scr", 0))
            tt(tv(s1), kview, gk[:, :].unsqueeze(1).broadcast_to([128, nt, 128]), ALU.mult, kvb + ["gk"], [("scr", 1)])
            tt(hv(s2), hv(s1), hr[:, 0:2 * nt].unsqueeze(2).broadcast_to([128, 2 * nt, 64]), ALU.mult, [("scr", 1), "hr"], [("scr", 2)])
            tt(qv(s3), qv(s2), rp[:, 0:nt, 0:64].unsqueeze(2).broadcast_to([128, nt, 2, 64]), ALU.mult, [("scr", 2), rpk], [("scr", 3)])
            tt(qv(s0)[:, :, :, 0:32], qv(s2)[:, :, :, 32:64], rp[:, 0:nt, 64:96].unsqueeze(2).broadcast_to([128, nt, 2, 32]), ALU.mult,
               [("scr", 2), rpk], [("scr", 0)])
            tt(qv(s0)[:, :, :, 32:64], qv(s2)[:, :, :, 0:32], rp[:, 0:nt, 96:128].unsqueeze(2).broadcast_to([128, nt, 2, 32]), ALU.mult,
               [("scr", 2), rpk], [("scr", 0)])
            tt(krb[:, 0:W], s3, s0, ALU.add, [("scr", 3), ("scr", 0)], [("big", 24)])
            for t in range(nt):
                tr(bank(6).bitcast(BF16)[:, t * 128:(t + 1) * 128], krb[:, t * 128:(t + 1) * 128], [("big", 24)], [("ps", 6)])
            act(Vaug[:, tile0:tile0 + nt, 0:64], kvv[:, :, 128:192], AF.Copy, kvb, [("Vaug", tile0 + t) for t in range(nt)])
            act(Vaug[:, tile0:tile0 + nt, 128:192], kvv[:, :, 192:256], AF.Copy, kvb, [("Vaug", tile0 + t) for t in range(nt)])
            cp(KT[:, tile0 * 128:(tile0 + nt) * 128], bank(6).bitcast(BF16)[:, 0:nt * 128], [("ps", 6)], [("KT", si)])

        actx = {}
        if nsup:
            for k_ in range(min(2, nsup)):
                load_xo(supers[k_][0], supers[k_][1], k_ % 2)
                load_rope(supers[k_][0], supers[k_][1], k_ % 2)
            ld["n"] = 0
            actx[0] = nm_stats(xbuf[0], ("xbuf", 0), supers[0][1])
            nm_xn_tr(actx[0])
            if nsup > 2:
                load_xo(supers[2][0], supers[2][1], 0)
            nm_evac(actx[0], a1t, 0, supers[0][2], hbufs[0])
        for k_ in range(nsup):
            if k_ < 8:
                mod_b_dma(k_)
            n_ = k_ + 1
            if n_ < nsup:
                actx[n_] = nm_stats(xbuf[n_ % 2], ("xbuf", n_ % 2), supers[n_][1])
            a_kv(k_)
            if n_ < nsup:
                nm_xn_tr(actx[n_])
                if k_ + 3 < nsup:
                    load_xo(supers[k_ + 3][0], supers[k_ + 3][1], (k_ + 3) % 2)
            a_post(k_)
            if k_ + 2 < nsup:
                load_rope(supers[k_ + 2][0], supers[k_ + 2][1], k_ % 2)
            if n_ < nsup:
                nm_evac(actx[n_], a1t, 0, supers[n_][2], hbufs[n_ % 2])

        ld["n"] = 1
        slot = load_x(256, 4) if (nqb and stage > 1) else None
        if stage >= 1:
            if len(supers) < 9:
                for c in range(max(len(supers) - 1, 0), 8):
                    if c >= len(supers):
                        mod_b_dma(c)
                    mod_b_mm(c, 0, 32)
            mod_b_finish()
        if stage == 1:
            dump("KT", KT[:, 0:1280], [("KT", i) for i in range(3)], BF16)
            dump("Vaug", Vaug[:, 0:10, :], [("Vaug", i) for i in range(10)], BF16)
            dump("hT", hT[:], [("hT", c) for c in range(8)], BF16)
        if stage <= 1:
            nqb = 0
        wctr = {"n": 0}

        def load_unit(u):
            s = wctr["n"] % NB
            wctr["n"] += 1
            dma("sp", ring[s][:, :], wsc[u], [("wsc", u)], [("ring", s)], f"w{s}")
            return s

        def R(s):
            return ("ring", s)

        def unit8(s):
            return ring[s][:, :].rearrange("p (c n) -> p c n", c=8)

        def unit4(s):
            return ring[s][:, :].rearrange("p (c n) -> p c n", c=4)

        yT = [big[:, c, :] for c in range(8)]
        uT = [big[:, 8 + j, :] for j in range(4)]
        gmT = [big[:, 12 + j, :] for j in range(4)]
        attnT = [big[:, 16 + j, :] for j in range(4)]
        QT = [QTb[:, j, :] for j in range(4)]
        vnb = [big[:, 24 + t, :] for t in range(4)]
        PT = [big[:, 28:30, :].rearrange("p a b -> p (a b)"), big[:, 30:32, :].rearrange("p a b -> p (a b)"),
              big[:, 26:28, :].rearrange("p a b -> p (a b)")]
        PTK = [[("big", 28), ("big", 29)], [("big", 30), ("big", 31)], [("big", 26), ("big", 27)]]

        def pre_q_pieces(slot_):
            xb_, xres_, rpb_ = xbuf[slot_], ("xbuf", slot_), ropeb[slot_]
            st = {}
            qrb = scr[4][:, :].bitcast(BF16)[:, 0:512]

            def p_stats():
                st["cx"] = nm_stats(xb_, xres_, 4)

            def p_xn(t):
                cx = st["cx"]
                k = cnt["xn"] % 2
                cnt["xn"] += 1
                xn_tile(xb_, xres_, t, k, rstd[:, cx["po"] + t:cx["po"] + t + 1], ("rstd", cx["par"]), pb=4)

            def p_evac():
                nm_evac(st["cx"], (a1, "a1", "modTa"), 0, 0, None, pb=4)

            def p_qproj():
                su = load_unit(0)
                for t in range(4):
                    for c in range(8):
                        mm(bank(4 + t), hT[:, c, t * 128:(t + 1) * 128], unit8(su)[:, c, :], c == 0, c == 7, [("hT", c), R(su)], [("ps", 4 + t)])

            def p_qdve(t):
                norm_rope(bank(4 + t), ("ps", 4 + t), 8, gq[:, :], "gq", rpb_[:, t, :], ("ropeb", slot_), qrb, ("scr", 4))

            def p_qpe(t):
                for g in range(4):
                    tr(bank(4 + t).bitcast(BF16)[:, g * 128:(g + 1) * 128], qrb[:, g * 128:(g + 1) * 128], [("scr", 4)], [("ps", 4 + t)])
                cp(QTb[:, :, t * 128:(t + 1) * 128], bank(4 + t).bitcast(BF16)[:, 0:512].rearrange("p (g q) -> p g q", q=128),
                   [("ps", 4 + t)], [("QT", g) for g in range(4)])

            return dict(stats=p_stats, xn=p_xn, evac=p_evac, qproj=p_qproj, qdve=p_qdve, qpe=p_qpe)

        def run_pre_q_all(pq):
            pq["stats"]()
            for t in range(4):
                pq["xn"](t)
            pq["evac"]()
            pq["qproj"]()
            for t in range(4):
                pq["qdve"](t)
                pq["qpe"](t)

        if nqb:
            run_pre_q_all(pre_q_pieces(slot))
            pre_units = (load_unit(1), load_unit(2))

        for qb in range(nqb):
            row0 = 256 + qb * 512
            xb = xbuf[slot]
            xres = ("xbuf", slot)
            rpb = ropeb[slot]
            nslot = None
            sv, suu = pre_units
            for t in range(4):
                for c in range(8):
                    mm(bank(t), hT[:, c, t * 128:(t + 1) * 128], unit8(sv)[:, c, :], c == 0, c == 7, [("hT", c), R(sv)], [("ps", t)])
            for j in range(4):
                for c in range(8):
                    mm(bank(4 + j), unit8(suu)[:, c, j * 128:(j + 1) * 128], hT[:, c, :], c == 0, c == 7, [("hT", c), R(suu)], [("ps", 4 + j)])
            for t in range(4):
                act(scr[t][:, :], bank(t), AF.Gelu_apprx_tanh, [("ps", t)], [("scr", t)])
            for j in range(4):
                act(uT[j], bank(4 + j), AF.Gelu_apprx_tanh, [("ps", 4 + j)], [("big", 8 + j)])
            for t in range(4):
                sq_ = scr[4 + t % 2]
                act(sq_[:, :], scr[t][:, :], AF.Square, [("scr", t)], [("scr", 4 + t % 2)])
                T.add("dve", lambda e, t=t, sq_=sq_: e.tensor_reduce(out=hs32[:, 8 * t:8 * t + 8], in_=sq_[:, :].rearrange("p (h d) -> p h d", d=64),
                                                                     axis=AX.X, op=ALU.add), [("scr", 4 + t % 2)], [("hs32", t)])
            act(hl32[:, :], hs32[:, :], AF.Ln, [("hs32", t) for t in range(4)], ["hl32"], scale=1.0 / 64, bias=EPS)
            act(hr32[:, :], hl32[:, :], AF.Exp, ["hl32"], ["hr32"], scale=-0.5)
            for t in range(4):
                tm_ = scr[4 + t % 2]
                tt(tm_[:, :], scr[t][:, :], gmg[:, :], ALU.mult, [("scr", t), "gmg"], [("scr", 4 + t % 2)])
                tt(vnb[t].rearrange("p (h d) -> p h d", d=64), tm_[:, :].rearrange("p (h d) -> p h d", d=64),
                   hr32[:, 8 * t:8 * t + 8].unsqueeze(2).broadcast_to([128, 8, 64]), ALU.mult, [("scr", 4 + t % 2), "hr32"], [("big", 24 + t)])
                for j in range(4):
                    for gg in range(2):
                        g = 2 * j + gg
                        mm(ps[gg * 64:(gg + 1) * 64, j * 512 + t * 128: j * 512 + (t + 1) * 128], vnb[t][:, g * 64:(g + 1) * 64], wsT[:, g, :],
                           True, True, [("big", 24 + t), "wsT"], [("ps", j)])
            for j in range(4):
                tt(scr[5][:, :].rearrange("p (t q) -> p t q", q=128), bank(j).rearrange("p (t q) -> p t q", q=128),
                   bsT[:, j, :].unsqueeze(1).broadcast_to([128, 4, 128]), ALU.add, [("ps", j), "bsT"], [("scr", 5)])
                tt(gmT[j], scr[5][:, :], uT[j], ALU.mult, [("scr", 5), ("big", 8 + j)], [("big", 12 + j)])

            steps = [(g, kb) for g in range(4) for kb in range(NT)]

            def ksup(kb):
                return 0 if kb < 2 else 1 + (kb - 2) // 4

            def qk(i):
                g, kb = steps[i]
                sbk = (i % 2) * 2
                mm(bank(sbk), KT[0:64, kb * 128:(kb + 1) * 128], QT[g][0:64, :], True, True, [("KT", ksup(kb)), ("QT", g)], [("ps", sbk)])
                mm(bank(sbk + 1), KT[64:128, kb * 128:(kb + 1) * 128], QT[g][64:128, :], True, True, [("KT", ksup(kb)), ("QT", g)], [("ps", sbk + 1)])
                pace = [("pace", i)] if (qb == 0 and i % 12 == 0) else []
                act(PT[i % 3], ps[:, sbk * 512:sbk * 512 + 1024], AF.Exp, [("ps", sbk), ("ps", sbk + 1)], PTK[i % 3] + pace, scale=0.125)
                if pace and N_EARLY + i // 12 < N_UNITS_QB:
                    cast_unit(N_EARLY + i // 12, pace)

            def pv(i):
                g, kb = steps[i]
                oa = 4 + 2 * (g % 2)
                mm(bank(oa), Vaug[:, kb, 0:128], PT[i % 3][:, 0:512], kb == 0, kb == NT - 1, [("Vaug", kb)] + PTK[i % 3], [("ps", oa)])
                mm(bank(oa + 1), Vaug[:, kb, 64:192], PT[i % 3][:, 512:1024], kb == 0, kb == NT - 1, [("Vaug", kb)] + PTK[i % 3], [("ps", oa + 1)])
                if kb == NT - 1:
                    if g == 3:
                        act(rc[64:128, 0:512], bank(oa)[64:128, :], AF.Ln, [("ps", oa)], ["rc", ("rcj", 0), ("rcj", 1)])
                        act(rc[64:128, 0:512], rc[64:128, 0:512], AF.Exp, ["rc"], ["rc"], scale=-1.0)
                        act(rc[0:64, 512:1024], bank(oa + 1)[0:64, :], AF.Ln, [("ps", oa + 1)], ["rc", ("rcj", 0), ("rcj", 1)])
                        act(rc[0:64, 512:1024], rc[0:64, 512:1024], AF.Exp, ["rc"], ["rc"], scale=-1.0)
                    else:
                        recip(rc[64:128, 0:512], bank(oa)[64:128, :], [("ps", oa)], ["rc", ("rcj", 0), ("rcj", 1)])
                        recip(rc[0:64, 512:1024], bank(oa + 1)[0:64, :], [("ps", oa + 1)], ["rc", ("rcj", 0), ("rcj", 1)])
                    tt(attnT[g][0:64, :], bank(oa)[0:64, :], rc[64:128, 0:512], ALU.mult, [("ps", oa), "rc", ("rcj", 0), ("rcj", 1)], [("big", 16 + g)])
                    tt(attnT[g][64:128, :], bank(oa + 1)[64:128, :], rc[0:64, 512:1024], ALU.mult, [("ps", oa + 1), "rc", ("rcj", 0), ("rcj", 1)], [("big", 16 + g)])

            qk(0)
            qk(1)
            for i in range(len(steps)):
                if i + 2 < len(steps):
                    qk(i + 2)
                pv(i)

            if qb + 1 < nqb:
                nslot = load_x(row0 + 512, 4)

            sga = [load_unit(3), None]
            sgb = [load_unit(4), None]
            sbra = load_unit(5)
            sbrg = load_unit(6)
            for m in range(8):
                if m == 4:
                    sga[1] = load_unit(7)
                    sgb[1] = load_unit(8)
                h = m // 4
                b0 = (m % 2) * 4
                for c in range(8):
                    mm(bank(b0), unit8(sga[h])[:, c, (m % 4) * 128:(m % 4 + 1) * 128], hT[:, c, :], c == 0, c == 7, [("hT", c), R(sga[h])], [("ps", b0)])
                for c in range(8):
                    mm(bank(b0 + 1), unit8(sgb[h])[:, c, (m % 4) * 128:(m % 4 + 1) * 128], hT[:, c, :], c == 0, c == 7, [("hT", c), R(sgb[h])], [("ps", b0 + 1)])
                for c in range(4):
                    mm(bank(b0 + 2), unit4(sbra)[:, c, m * 128:(m + 1) * 128], attnT[c], c == 0, c == 3, [("big", 16 + c), R(sbra)], [("ps", b0 + 2)])
                for c in range(4):
                    mm(bank(b0 + 3), unit4(sbrg)[:, c, m * 128:(m + 1) * 128], gmT[c], c == 0, c == 3, [("big", 12 + c), R(sbrg)], [("ps", b0 + 3)])
                act(scr[0][:, :], bank(b0), AF.Sigmoid, [("ps", b0)], [("scr", 0)])
                act(scr[1][:, :], bank(b0 + 1), AF.Sigmoid, [("ps", b0 + 1)], [("scr", 1)])
                tt(scr[2][:, :], bank(b0 + 2), scr[0][:, :], ALU.mult, [("ps", b0 + 2), ("scr", 0)], [("scr", 2)])
                tt(scr[3][:, :], bank(b0 + 3), scr[1][:, :], ALU.mult, [("ps", b0 + 3), ("scr", 1)], [("scr", 3)])
                tt(yT[m], scr[2][:, :], scr[3][:, :], ALU.add, [("scr", 2), ("scr", 3)], [("big", m)], eng="pool")

            so = [load_unit(9), load_unit(10)]
            par = cnt["par"] % 2
            cnt["par"] += 1
            po = par * 4
            k8c = {"n": 0}

            def b8_mm(t):
                for nh in range(2):
                    bk = 4 + k8c["n"] % 4
                    k8c["n"] += 1
                    for c in range(8):
                        mm(bank(bk), yT[c][:, t * 128:(t + 1) * 128], unit8(so[nh])[:, c, :], c == 0, c == 7, [("big", c), R(so[nh])], [("ps", bk)])
                    sk = 4 + (k8c["n"] % 2)
                    tt(scr[sk][:, :], bank(bk), gbc[:, nh * 512:(nh + 1) * 512], ALU.mult, [("ps", bk), "gbc"], [("scr", sk)])
                    tt(xb[:, t, nh * 512:(nh + 1) * 512], scr[sk][:, :], xb[:, t, nh * 512:(nh + 1) * 512], ALU.add, [("scr", sk), xres], [xres, ("x1t", t)], eng="pool")

            def b9_tile(t):
                kq = cnt["sq"] % 2
                cnt["sq"] += 1
                cs = slice(po + t, po + t + 1)
                act(sqj[kq], xb[:, t, :], AF.Square, [("x1t", t)], [("ss", par, t), ("rcj", kq)], accum=ss[:, cs])
                act(rs[:, cs], ss[:, cs], AF.Ln, [("ss", par, t)], [("rs", par), ("rs", par, t)], scale=1.0 / D, bias=EPS)
                act(rstd[:, cs], rs[:, cs], AF.Exp, [("rs", par, t)], [("rstd", par), ("rstd", par, t)], scale=-0.5)
                k = cnt["xn"] % 2
                cnt["xn"] += 1
                xn_tile(xb, ("x1t", t), t, k, rstd[:, cs], ("rstd", par, t))

            for t in range(4):
                b8_mm(t)
                if t >= 1:
                    b9_tile(t - 1)
            b9_tile(3)
            nm_evac(dict(nt=4), (a2, "a2", "modTb"), 24, 0)

            pq = pre_q_pieces(nslot) if nslot is not None else None
            if pq:
                pq["stats"]()
            for j in range(32):
                if j % 4 == 0:
                    sf = load_unit(11 + j // 4)
                bk = j % 8
                for c in range(8):
                    mm(bank(bk), unit8(sf)[:, c, (j % 4) * 128:(j % 4 + 1) * 128], hT[:, c, :], c == 0, c == 7, [("hT", c), R(sf)], [("ps", bk)])
                sk = j % 4
                act(scr[sk][:, :], bank(bk), AF.Relu, [("ps", bk)], [("scr", sk)])
                tt(big[:, j, :], scr[sk][:, :], scr[sk][:, :], ALU.mult, [("scr", sk)], [("big", j)])

            if pq:
                pq["xn"](0)
                pq["xn"](1)
            blk = 0
            for nh in range(2):
                for kg in range(4):
                    s2 = load_unit(19 + nh * 4 + kg)
                    for t in range(4):
                        bk = t
                        for c in range(8):
                            mm(bank(bk), big[:, kg * 8 + c, t * 128:(t + 1) * 128], unit8(s2)[:, c, :], kg == 0 and c == 0, kg == 3 and c == 7,
                               [("big", kg * 8 + c), R(s2)], [("ps", bk)])
                    if pq:
                        if blk == 0:
                            pq["xn"](2)
                            pq["xn"](3)
                            pq["evac"]()
                        elif blk == 1:
                            pq["qproj"]()
                        elif blk == 2:
                            pq["qdve"](0)
                        elif blk in (3, 4, 5):
                            pq["qpe"](blk - 3)
                            pq["qdve"](blk - 2)
                        elif blk == 6:
                            pq["qpe"](3)
                    blk += 1
                for t in range(4):
                    bk = t
                    sk = 5
                    tt(scr[sk][:, :], bank(bk), gbc[:, 1024 + nh * 512:1024 + (nh + 1) * 512], ALU.mult, [("ps", bk), "gbc"], [("scr", sk)])
                    tt(xb[:, t, nh * 512:(nh + 1) * 512], scr[sk][:, :], xb[:, t, nh * 512:(nh + 1) * 512], ALU.add, [("scr", sk), xres], [xres], eng="pool")
            if qb + 1 < nqb:
                pre_units = (load_unit(1), load_unit(2))
            dma("sp", out_d[qb * 512:(qb + 1) * 512, :].rearrange("(t p) d -> p t d", p=128), xb[:, :, :], [xres], [("out", qb)], f"o{slot}")
            slot = nslot

        T.add("sp", lambda e: e.nop(), [("out", q) for q in range(nqb)] + [("dbgout", n) for n in dbg], [])

        T.finalize()
        nc._tracker = T

        @block.sync
        def _(e):
            T.emit("sp", e, esems, dsems)

        @block.tensor
        def _(e):
            T.emit("pe", e, esems, dsems)

        @block.scalar
        def _(e):
            T.emit("act", e, esems, dsems)

        @block.vector
        def _(e):
            T.emit("dve", e, esems, dsems)

        @block.gpsimd
        def _(e):
            T.emit("pool", e, esems, dsems)

    return nc


_CACHE = {}


def _rope_table(tok_idx):
    n = tok_idx.shape[0]
    t = np.maximum(tok_idx, 0)
    row = (t // 64).astype(np.float32)
    colp = (t % 64).astype(np.float32)
    inv = (np.float32(10000.0) ** (-np.arange(0, 32, 2, dtype=np.float32) / np.float32(32))).astype(np.float32)
    ang = np.concatenate([row[:, None] * inv[None, :], colp[:, None] * inv[None, :]], axis=-1).astype(np.float32)
    cos = np.cos(ang).astype(np.float32)
    sin = np.sin(ang).astype(np.float32)
    ident = tok_idx < 0
    cos[ident] = 1.0
    sin[ident] = 0.0
    return np.concatenate([cos, cos, -sin, sin], axis=1).astype(np.float32)


def kernel(x, c, ctx, c_ctx, w_mod, b_mod, norm1_g, norm2_g, w_in, q_norm_g, k_norm_g,
           gm_norm_g, gm_ws, gm_bs, w_br_attn, w_br_gm, w_out, w_ff1, w_ff2):
    f = lambda a: np.ascontiguousarray(np.asarray(a, dtype=np.float32))
    x, c, ctx, c_ctx = f(x), f(c), f(ctx), f(c_ctx)
    w_mod, b_mod, w_in = f(w_mod)[0], f(b_mod)[0], f(w_in)[0]
    n1, n2 = f(norm1_g)[0], f(norm2_g)[0]
    qg, kg, gmg = f(q_norm_g)[0], f(k_norm_g)[0], f(gm_norm_g)[0]
    ws, bs = f(gm_ws)[0], f(gm_bs)[0]
    bra, brg, wo, w1, w2 = f(w_br_attn)[0], f(w_br_gm)[0], f(w_out)[0], f(w_ff1)[0], f(w_ff2)[0]

    if "nc" not in _CACHE:
        _CACHE["nc"] = build_program()
    nc = _CACHE["nc"]

    qcols = 256 + np.array([kv * 256 + g * 64 + d for g in range(4) for kv in range(2) for d in range(64)])
    order = np.concatenate([np.arange(0, 256), qcols, np.arange(1280, 1792), np.arange(768, 1280), np.arange(1792, 3840)])
    w_in_p = np.ascontiguousarray(w_in[:, order])
    rows = np.array([kv * 256 + g * 64 + d for g in range(4) for kv in range(2) for d in range(64)])
    w_bra_p = np.ascontiguousarray(bra[rows, :])
    b_modT = np.ascontiguousarray(b_mod.reshape(48, 128).T)
    b_modg = np.ascontiguousarray(np.concatenate([b_mod[2048:3072], b_mod[5120:6144]])[None, :])
    n1g = np.ascontiguousarray(n1.reshape(8, 128).T)
    n2g = np.ascontiguousarray(n2.reshape(8, 128).T)
    gq_bc = np.ascontiguousarray(np.broadcast_to(np.tile(qg, 8)[None, :], (128, 512)))
    gk_bc = np.ascontiguousarray(np.broadcast_to(np.tile(kg, 2)[None, :], (128, 128)))
    gmg_bc = np.ascontiguousarray(np.broadcast_to(gmg.reshape(512)[None, :], (128, 512)))
    wsT = np.ascontiguousarray(ws.transpose(2, 0, 1))
    bsT = np.ascontiguousarray(np.broadcast_to(bs.reshape(4, 2, 1, 128), (4, 2, 64, 128)).transpose(1, 2, 0, 3).reshape(128, 4, 128))
    ident = np.eye(128, dtype=np.float32)

    in_maps = []
    for core in range(8):
        b, hf = core // 2, core % 2
        own = np.arange(hf * 4096, (hf + 1) * 4096)
        oth = np.arange((1 - hf) * 4096, (2 - hf) * 4096)
        xin = np.concatenate([ctx[b], x[b, own], x[b, oth]], axis=0)
        tok = np.concatenate([-np.ones(256, dtype=np.int64), own, oth])
        cT = np.ascontiguousarray(np.stack([c[b], c_ctx], axis=1).reshape(8, 128, 2).transpose(1, 0, 2))
        in_maps.append({
            "xin": np.ascontiguousarray(xin), "rope": _rope_table(tok), "cT": cT, "w_mod": w_mod, "b_modT": b_modT,
            "b_modg": b_modg, "n1g": n1g, "n2g": n2g, "w_in_p": w_in_p, "gq_bc": gq_bc, "gk_bc": gk_bc, "gmg_bc": gmg_bc,
            "wsT": wsT, "bsT": bsT, "w_bra_p": w_bra_p, "w_brg": brg, "w_out": wo, "w_ff1": w1, "w_ff2": w2, "ident": ident,
        })
    res = run_bass_kernel_spmd(nc, in_maps, core_ids=list(range(8)))
    out = np.empty((4, 8192, D), dtype=np.float32)
    for core in range(8):
        b, hf = core // 2, core % 2
        out[b, hf * 4096:(hf + 1) * 4096] = res.results[core]["out"]
    return out
```

```python
import numpy as np
import concourse.bass as bass
import concourse.mybir as mybir
from concourse.bass_utils import run_bass_kernel_spmd

F32 = mybir.dt.float32
BF16 = mybir.dt.bfloat16
AF = mybir.ActivationFunctionType
ALU = mybir.AluOpType
AX = mybir.AxisListType

D = 1024
NCTX_T = 2
NOWN_T = 32
NT = 66
NQB = 8
EPS = 1e-6
NB = 5
N_UNITS_QB = 27


class Tracker:
    def __init__(self):
        self.ops = []
        self.last_w = {}
        self.readers = {}
        self.dcount = {}

    def add(self, eng, fn, reads=(), writes=(), dsem=None):
        idx = len(self.ops)
        deps = set()
        if eng in ("act", "dve"):
            writes = list(writes) + [("pslk", r[1]) for r in reads if isinstance(r, tuple) and r[0] == "ps"]
        for r in reads:
            if r in self.last_w:
                deps.add(self.last_w[r])
        for w in writes:
            if w in self.last_w:
                deps.add(self.last_w[w])
            for rd in self.readers.get(w, ()):
                deps.add(rd)
        op = dict(eng=eng, fn=fn, deps=deps, dsem=dsem, marked=False, idx=idx, val=None, desc=(tuple(reads), tuple(writes)))
        if dsem is not None:
            self.dcount[dsem] = self.dcount.get(dsem, 0) + 16
            op["dval"] = self.dcount[dsem]
        self.ops.append(op)
        for r in reads:
            self.readers.setdefault(r, []).append(idx)
        for w in writes:
            self.last_w[w] = idx
            self.readers[w] = []
        return idx

    def finalize(self):
        ops = self.ops
        for op in ops:
            red = {}
            for d in op["deps"]:
                dop = ops[d]
                if dop["dsem"] is not None:
                    key = ("d", dop["dsem"])
                    if key not in red or ops[red[key]]["dval"] < dop["dval"]:
                        red[key] = d
                else:
                    if dop["eng"] == "pe" and op["eng"] == "pe" and op["dsem"] is None:
                        continue
                    key = ("e", dop["eng"])
                    if key not in red or red[key] < d:
                        red[key] = d
            op["rdeps"] = list(red.values())
            for d in op["rdeps"]:
                if ops[d]["dsem"] is None:
                    ops[d]["marked"] = True
        cnt = {}
        for op in ops:
            if op["dsem"] is None and op["marked"]:
                cnt[op["eng"]] = cnt.get(op["eng"], 0) + 1
                op["val"] = cnt[op["eng"]]

    def trace(self, engname):
        waited = {}
        out = []
        for op in self.ops:
            if op["eng"] != engname:
                continue
            ws = []
            for d in op["rdeps"]:
                dop = self.ops[d]
                if dop["dsem"] is not None:
                    val = self.dcount[dop["dsem"]] if dop["dsem"] in ("const", "cast", "gb") else dop["dval"]
                    key = ("d", dop["dsem"])
                else:
                    val, key = dop["val"], ("e", dop["eng"])
                if waited.get(key, 0) < val:
                    ws.append((key[1], val))
                    waited[key] = val
            inc = (op["dsem"], op.get("dval")) if op["dsem"] else ((engname, op["val"]) if op["marked"] else None)
            out.append((op["idx"], ws, op["desc"], inc))
        return out

    def emit(self, engname, engobj, esems, dsems):
        waited = {}
        for op in self.ops:
            if op["eng"] != engname:
                continue
            for d in op["rdeps"]:
                dop = self.ops[d]
                if dop["dsem"] is not None:
                    val = self.dcount[dop["dsem"]] if dop["dsem"] in ("const", "cast", "gb") else dop["dval"]
                    sem, key = dsems[dop["dsem"]], ("d", dop["dsem"])
                else:
                    sem, val, key = esems[dop["eng"]], dop["val"], ("e", dop["eng"])
                if waited.get(key, 0) < val:
                    engobj.wait_ge(sem, val)
                    waited[key] = val
            ins = op["fn"](engobj)
            if op["dsem"] is not None:
                ins.then_inc(dsems[op["dsem"]], 16)
            elif op["marked"]:
                ins.then_inc(esems[engname], 1)


def build_program(stage=3, nqb=NQB, skip=()):
    nc = bass.Bass("TRN2", target_bir_lowering=False)
    T = Tracker()
    dbg = {}

    def din(name, shape, dt=F32):
        return nc.dram_tensor(name, list(shape), dt, kind="ExternalInput").ap()

    xin = din("xin", [NT * 128, D])
    rope = din("rope", [NT * 128, 128])
    cT_d = din("cT", [128, 8, 2])
    wmod_d = din("w_mod", [D, 6 * D])
    bmodT_d = din("b_modT", [128, 48])
    bmodg_d = din("b_modg", [1, 2048])
    n1g_d = din("n1g", [128, 8])
    n2g_d = din("n2g", [128, 8])
    win_d = din("w_in_p", [D, 3840])
    gq_d = din("gq_bc", [128, 512])
    gk_d = din("gk_bc", [128, 128])
    gmg_d = din("gmg_bc", [128, 512])
    wsT_d = din("wsT", [128, 8, 128])
    bsT_d = din("bsT", [128, 4, 128])
    bra_d = din("w_bra_p", [512, D])
    brg_d = din("w_brg", [512, D])
    wout_d = din("w_out", [D, D])
    ff1_d = din("w_ff1", [D, 4 * D])
    ff2_d = din("w_ff2", [4 * D, D])
    ident_d = din("ident", [128, 128])
    out_d = nc.dram_tensor("out", [NOWN_T * 128, D], F32, kind="ExternalOutput").ap()
    gscr = nc.dram_tensor("gscr", [1, 2048], F32, kind="Internal").ap()
    wsc = nc.dram_tensor("wscratch", [N_UNITS_QB, 128, 4096], BF16, kind="Internal").ap()

    import contextlib
    es = contextlib.ExitStack()

    def sb(name, shape, dt):
        return es.enter_context(nc.sbuf_tensor(name, list(shape), dt))

    with es:
        KT = sb("KT", [128, NT * 128], BF16)
        Vaug = sb("Vaug", [128, NT, 192], BF16)
        xbuf = [sb(f"xbuf{i}", [128, 4, D], F32) for i in range(2)]
        ropeb = [sb(f"ropeb{i}", [128, 4, 128], F32) for i in range(2)]
        xn = [sb(f"xn{i}", [128, D], BF16) for i in range(2)]
        hT = sb("hT", [128, 8, 512], BF16)
        big = sb("big", [128, 32, 512], BF16)
        ringT = sb("ring", [128, NB * 4096], BF16)
        ring = [ringT[:, i * 4096:(i + 1) * 4096] for i in range(NB)]
        gbc = sb("gbc", [128, 2048], F32)
        scr = [sb(f"scr{i}", [128, 512], F32) for i in range(6)]
        rc = sb("rc", [128, 1024], F32)
        QTb = sb("QTb", [128, 4, 512], BF16)
        Wkv = sb("Wkv", [128, 8, 256], BF16)
        identb = sb("identb", [128, 128], BF16)
        gq = sb("gq", [128, 512], F32)
        gk = sb("gk", [128, 128], F32)
        gmg = sb("gmg", [128, 512], F32)
        wsT = sb("wsTb", [128, 8, 128], BF16)
        bsT = sb("bsTs", [128, 4, 128], F32)
        cT = sb("cTs", [128, 8, 2], F32)
        scT = sb("scT", [128, 8, 2], F32)
        bmodT = sb("bmodTs", [128, 48], F32)
        n1g = sb("n1gs", [128, 8], F32)
        n2g = sb("n2gs", [128, 8], F32)
        modT = sb("modT", [128, 48, 2], F32)
        a1 = sb("a1", [128, 8, 2], F32)
        a2 = sb("a2", [128, 8, 2], F32)
        ones1 = sb("ones1", [1, 128], F32)
        ss = sb("ss", [128, 8], F32)
        rs = sb("rs", [128, 8], F32)
        rstd = sb("rstd", [128, 8], F32)
        hs = sb("hs", [128, 8], F32)
        hl = sb("hl", [128, 8], F32)
        hr = sb("hr", [128, 8], F32)
        hs32 = sb("hs32", [128, 32], F32)
        hl32 = sb("hl32", [128, 32], F32)
        hr32 = sb("hr32", [128, 32], F32)
        ps = es.enter_context(nc.psum_tensor("ps", [128, 4096], F32))
        sqj = [rc[:, 0:512].bitcast(BF16), rc[:, 512:1024].bitcast(BF16)]
        grow = big[0:1, 0:8, :].rearrange("p a b -> p (a b)").bitcast(F32)
        bmodg = big[0:1, 8:16, :].rearrange("p a b -> p (a b)").bitcast(F32)

        esems = {e: es.enter_context(nc.semaphore("sem_" + e)) for e in ["pe", "act", "dve", "pool", "sp"]}
        dnames = [f"c{i}" for i in range(6)] + ["dbg", "gb", "gb2", "const", "cast", "x0", "x1", "r0", "r1", "o0", "o1", "wm0", "wm1", "wm2"] + [f"w{i}" for i in range(NB)]
        dsems = {d: es.enter_context(nc.semaphore("ds_" + d)) for d in dnames}
        block = es.enter_context(nc.Block())

        def bank(b):
            return ps[:, b * 512:(b + 1) * 512]

        def mm(out, lhsT, rhs, start, stop, reads, writes, sgc=False):
            T.add("pe", lambda e: e.matmul(out, lhsT=lhsT, rhs=rhs, start=start, stop=stop, skip_group_check=sgc), reads, writes)

        def tr(out, in_, reads, writes):
            T.add("pe", lambda e: e.transpose(out, in_, identb[:, :]), list(reads) + ["identb"], writes)

        def act(out, in_, func, reads, writes, scale=None, bias=None, accum=None):
            kw = {}
            if scale is not None:
                kw["scale"] = scale
            if bias is not None:
                kw["bias"] = bias
            if accum is not None:
                kw["accum_out"] = accum
            T.add("act", lambda e: e.activation(out=out, in_=in_, func=func, **kw), reads, writes)

        def tt(out, in0, in1, op, reads, writes, eng="dve"):
            T.add(eng, lambda e: e.tensor_tensor(out=out, in0=in0, in1=in1, op=op), reads, writes)

        def ts(out, in0, s1, s2, op0, op1, reads, writes, eng="dve"):
            if op1 is None:
                T.add(eng, lambda e: e.tensor_scalar(out=out, in0=in0, scalar1=s1, scalar2=None, op0=op0), reads, writes)
            else:
                T.add(eng, lambda e: e.tensor_scalar(out=out, in0=in0, scalar1=s1, scalar2=s2, op0=op0, op1=op1), reads, writes)

        def stt(out, in0, scalar, in1, op0, op1, reads, writes):
            T.add("dve", lambda e: e.scalar_tensor_tensor(out=out, in0=in0, scalar=scalar, in1=in1, op0=op0, op1=op1), reads, writes)

        def recip(out, in_, reads, writes):
            T.add("dve", lambda e: e.reciprocal(out=out, in_=in_), reads, writes)

        def cp(out, in_, reads, writes, eng="dve"):
            T.add(eng, lambda e: e.tensor_copy(out=out, in_=in_), reads, writes)

        def dma(q, out, in_, reads, writes, dsem):
            T.add(q, lambda e: e.dma_start(out=out, in_=in_), reads, writes, dsem=dsem)

        def memset(ap, val, writes, eng="pool"):
            T.add(eng, lambda e: e.memset(ap, val), (), writes)

        for (dst, src, nm) in [(cT[:], cT_d, "cT"), (bmodT[:], bmodT_d, "bmodT"), (bmodg[:], bmodg_d, "bmodg"),
                               (n1g[:], n1g_d, "n1g"), (n2g[:], n2g_d, "n2g"), (gq[:], gq_d, "gq"), (gk[:], gk_d, "gk"),
                               (gmg[:], gmg_d, "gmg"), (bsT[:], bsT_d, "bsT")]:
            dma("sp", dst, src, (), [nm], "const")
        dma("pool", identb[:], ident_d, (), ["identb"], "cast")
        dma("pool", Wkv[:], win_d[:, 0:256].rearrange("(c p) n -> p c n", p=128), (), ["Wkv"], "cast")
        dma("pool", wsT[:], wsT_d, (), ["wsT"], "cast")
        memset(Vaug[:, :, 64:128], 1.0, [("Vaug", t) for t in range(NT)])
        memset(ones1[:], 1.0, ["ones1"])

        def wsrc_kn(w, c0, ncols):
            return w[:, c0:c0 + ncols].rearrange("(c p) n -> p c n", p=128)

        unit_src = []
        unit_src.append((wsrc_kn(win_d, 256, 512), 8))
        unit_src.append((wsrc_kn(win_d, 768, 512), 8))
        unit_src.append((wsrc_kn(win_d, 1280, 512), 8))
        unit_src.append((wsrc_kn(win_d, 1792, 512), 8))
        unit_src.append((wsrc_kn(win_d, 2816, 512), 8))
        unit_src.append((bra_d.rearrange("(c p) n -> p c n", p=128), 4))
        unit_src.append((brg_d.rearrange("(c p) n -> p c n", p=128), 4))
        unit_src.append((wsrc_kn(win_d, 2304, 512), 8))
        unit_src.append((wsrc_kn(win_d, 3328, 512), 8))
        unit_src.append((wsrc_kn(wout_d, 0, 512), 8))
        unit_src.append((wsrc_kn(wout_d, 512, 512), 8))
        for j in range(8):
            unit_src.append((wsrc_kn(ff1_d, j * 512, 512), 8))
        for nh in range(2):
            for kg in range(4):
                src = ff2_d[kg * 1024:(kg + 1) * 1024, nh * 512:(nh + 1) * 512].rearrange("(c p) n -> p c n", p=128)
                unit_src.append((src, 8))
        assert len(unit_src) == N_UNITS_QB
        def cast_unit(u, extra_reads=()):
            src, nch = unit_src[u]
            dst = wsc[u].rearrange("p (c n) -> p c n", c=nch)
            dma("pool", dst, src, [("cslot", u % 6)] + list(extra_reads), [("wsc", u), ("cslot", u % 6)], f"c{u % 6}")

        N_EARLY = 11 if nqb else N_UNITS_QB
        for u in range(N_EARLY):
            if "cast" in skip:
                break
            cast_unit(u)

        act(scT[:], cT[:], AF.Silu, ["cT"], ["scT"])
        for c in range(8):
            s_ = c % NB
            pa = ring[s_].bitcast(F32)
            dma("sp" if c % 2 == 0 else "act", pa, wmod_d[c * 128:(c + 1) * 128, 0:2048], (), [("ring", s_)], f"w{s_}")
            for jj in range(16):
                mm(ps[:, 2 * jj:2 * jj + 2], pa[:, jj * 128:(jj + 1) * 128], scT[:, c, :], c == 0 and jj == 0, c == 7 and jj == 15,
                   [("ring", s_), "scT"], [("ps", 0)], sgc=True)
        tt(modT[:, 0:16, :], ps[:, 0:32].rearrange("p (j k) -> p j k", k=2),
           bmodT[:, 0:16].unsqueeze(2).broadcast_to([128, 16, 2]), ALU.add, [("ps", 0), "bmodT"], ["modTa"])
        stt(a1[:], modT[:, 8:16, :], 1.0, n1g[:, :].unsqueeze(2).broadcast_to([128, 8, 2]), ALU.add, ALU.mult, ["modTa", "n1g"], ["a1"])

        def mod_b_buf(c):
            p_ = c % 2
            return p_, ringT[:, (2 * p_) * 4096:(2 * p_ + 2) * 4096].bitcast(F32)

        def mod_b_dma(c):
            p_, pb = mod_b_buf(c)
            dma("sp", pb, wmod_d[c * 128:(c + 1) * 128, 2048:6144], (), [("ring", 2 * p_), ("ring", 2 * p_ + 1)], f"wm{p_}")

        def mod_b_mm(c, j0, j1):
            p_, pb = mod_b_buf(c)
            for jj in range(j0, j1):
                mm(ps[:, 7 * 512 + 2 * jj:7 * 512 + 2 * jj + 2], pb[:, jj * 128:(jj + 1) * 128], scT[:, c, :], c == 0 and jj == 0, c == 7 and jj == 31,
                   [("ring", 2 * p_), ("ring", 2 * p_ + 1), "scT"], [("ps", 7)], sgc=True)

        def mod_b_piece(c):
            mod_b_dma(c)
            mod_b_mm(c, 0, 32)

        def mod_b_finish():
            tt(modT[:, 16:48, :], ps[:, 7 * 512:7 * 512 + 64].rearrange("p (j k) -> p j k", k=2),
               bmodT[:, 16:48].unsqueeze(2).broadcast_to([128, 32, 2]), ALU.add, [("ps", 7), "bmodT"], ["modTb"])
            stt(a2[:], modT[:, 32:40, :], 1.0, n2g[:, :].unsqueeze(2).broadcast_to([128, 8, 2]), ALU.add, ALU.mult, ["modTb", "n2g"], ["a2"])
            for k_, j0 in enumerate((16, 40)):
                T.add("pool", lambda e, k_=k_, j0=j0: e.dma_start(out=gscr[0, k_ * 1024:(k_ + 1) * 1024].rearrange("(c p) -> p c", p=128),
                                                             in_=modT[:, j0:j0 + 8, 0], allow_slow_non_contiguous=True),
                      ["modTb"], [("gscr", k_)], dsem="gb")
            dma("pool", gbc[:], gscr[0, :].partition_broadcast(128), [("gscr", 0), ("gscr", 1)], ["gbc"], "gb2")

        cnt = {"xn": 0, "ev": 0, "sq": 0, "par": 0}

        def nm_stats(xb, xres, nt):
            par = cnt["par"] % 2
            cnt["par"] += 1
            po = par * 4
            for t in range(nt):
                kq = cnt["sq"] % 2
                cnt["sq"] += 1
                act(sqj[kq], xb[:, t, :], AF.Square, [xres], [("ss", par, t), ("rcj", kq)], accum=ss[:, po + t:po + t + 1])
            act(rs[:, po:po + nt], ss[:, po:po + nt], AF.Ln, [("ss", par, t) for t in range(nt)], [("rs", par)], scale=1.0 / D, bias=EPS)
            act(rstd[:, po:po + nt], rs[:, po:po + nt], AF.Exp, [("rs", par)], [("rstd", par)], scale=-0.5)
            return dict(xb=xb, xres=xres, nt=nt, par=par, po=po, use_act=not cnt.get("phaseA", False))

        def xn_tile(xb, xres, t, k, rs_ap, rs_key, pb=0, use_act=True):
            if t % 2 == 0 or not use_act:
                ts(xn[k][:], xb[:, t, :], rs_ap, None, ALU.mult, None, [xres, rs_key], [("xn", k)])
            else:
                act(xn[k][:], xb[:, t, :], AF.Copy, [xres, rs_key], [("xn", k)], scale=rs_ap)
            for c in range(8):
                bk = pb + c // 2
                o = bank(bk).bitcast(BF16)[:, (c % 2) * 512 + t * 128:(c % 2) * 512 + (t + 1) * 128]
                tr(o, xn[k][:, c * 128:(c + 1) * 128], [("xn", k)], [("ps", bk)])

        def nm_xn_tr(cx):
            xb, xres, nt, par, po = cx["xb"], cx["xres"], cx["nt"], cx["par"], cx["po"]
            for t in range(nt):
                k = cnt["xn"] % 2
                cnt["xn"] += 1
                xn_tile(xb, xres, t, k, rstd[:, po + t:po + t + 1], ("rstd", par), use_act=cx.get("use_act", True))

        def nm_evac(cx, a_t, sh_off, col, hd=None, pb=0):
            nt = cx["nt"]
            hdst, hkey = hd if hd is not None else (hT, "hT")
            for c in range(8):
                bk = pb + c // 2
                src = bank(bk).bitcast(BF16)[:, (c % 2) * 512:(c % 2) * 512 + nt * 128]
                if c % 2 == 0:
                    act(hdst[:, c, 0:nt * 128], src, AF.Identity, [("ps", bk), a_t[1], a_t[2]], [(hkey, c)],
                        scale=a_t[0][:, c, col:col + 1], bias=modT[:, sh_off + c, col:col + 1])
                else:
                    ts(hdst[:, c, 0:nt * 128], src, a_t[0][:, c, col:col + 1], modT[:, sh_off + c, col:col + 1], ALU.mult, ALU.add,
                       [("ps", bk), a_t[1], a_t[2]], [(hkey, c)])

        def norm_mod_T(xb, xres, nt, a_t, sh_off, col, hd=None):
            cx = nm_stats(xb, xres, nt)
            nm_xn_tr(cx)
            nm_evac(cx, a_t, sh_off, col, hd)

        def head_rstd(src_sq, nh, res_in):
            T.add("dve", lambda e: e.tensor_reduce(out=hs[:, 0:nh], in_=src_sq.rearrange("p (h d) -> p h d", d=64), axis=AX.X, op=ALU.add),
                  [res_in], ["hs"])
            act(hl[:, 0:nh], hs[:, 0:nh], AF.Ln, ["hs"], ["hl"], scale=1.0 / 64, bias=EPS)
            act(hr[:, 0:nh], hl[:, 0:nh], AF.Exp, ["hl"], ["hr"], scale=-0.5)

        def norm_rope(psrc, psres, nh, gain, gres, rp, rpres, outb, outres):
            W = nh * 64
            s0, s1, s2, s3 = scr[0][:, 0:W], scr[1][:, 0:W], scr[2][:, 0:W], scr[3][:, 0:W]
            act(s0, psrc, AF.Square, [psres], [("scr", 0)])
            head_rstd(s0, nh, ("scr", 0))
            tt(s1, psrc, gain, ALU.mult, [psres, gres], [("scr", 1)])
            tt(s2.rearrange("p (h d) -> p h d", d=64), s1.rearrange("p (h d) -> p h d", d=64),
               hr[:, 0:nh].unsqueeze(2).broadcast_to([128, nh, 64]), ALU.mult, [("scr", 1), "hr"], [("scr", 2)])
            v2 = s2.rearrange("p (h d) -> p h d", d=64)
            tt(s3.rearrange("p (h d) -> p h d", d=64), v2, rp[:, 0:64].unsqueeze(1).broadcast_to([128, nh, 64]), ALU.mult,
               [("scr", 2), rpres], [("scr", 3)])
            v0 = s0.rearrange("p (h d) -> p h d", d=64)
            tt(v0[:, :, 0:32], v2[:, :, 32:64], rp[:, 64:96].unsqueeze(1).broadcast_to([128, nh, 32]), ALU.mult,
               [("scr", 2), rpres], [("scr", 0)])
            tt(v0[:, :, 32:64], v2[:, :, 0:32], rp[:, 96:128].unsqueeze(1).broadcast_to([128, nh, 32]), ALU.mult,
               [("scr", 2), rpres], [("scr", 0)])
            tt(outb, s3, s0, ALU.add, [("scr", 3), ("scr", 0)], [outres])

        def dump(name, ap, res, dt=F32):
            if "nodump" in skip:
                return
            d = nc.dram_tensor("dbg_" + name, list(ap.shape), F32, kind="ExternalOutput").ap()
            dbg[name] = d
            dma("pool", d, ap, res, [("dbgout", name)], "dbg")

        ld = {"n": 0}

        def load_xo(row0, nt, s):
            dma("sp", xbuf[s][:, 0:nt, :], xin[row0:row0 + nt * 128, :].rearrange("(t p) d -> p t d", p=128), (), [("xbuf", s)], f"x{s}")

        def load_rope(row0, nt, s):
            dma("sp", ropeb[s][:, 0:nt, :], rope[row0:row0 + nt * 128, :].rearrange("(t p) d -> p t d", p=128), (), [("ropeb", s)], f"r{s}")

        def load_x(row0, nt):
            s = ld["n"] % 2
            ld["n"] += 1
            load_xo(row0, nt, s)
            load_rope(row0, nt, s)
            return s

        supers = [(0, 2, 1)] + [(256 + i * 512, 4, 0) for i in range(16)]
        if stage == 0:
            supers = []
            for c in range(8):
                mod_b_piece(c)
            mod_b_finish()
            dump("modT", modT[:], ["modTa", "modTb"])
            dump("gbc", gbc[:], ["gbc"])
            dump("a1", a1[:], ["a1"])
        if stage == 1:
            import os
            supers = supers[:int(os.environ.get("NSUP", "3"))]
        krb = big[:, 24:26, :].rearrange("p a b -> p (a b)")
        nsup = len(supers)
        hbufs = [(hT, "hT"), (big[:, 0:8, :], "big")]
        a1t = (a1, "a1", "modTa")

        def a_kv(si):
            row0, nt, col = supers[si]
            hA, hAk = hbufs[si % 2]
            for t in range(nt):
                bk = 4 + t // 2
                for c in range(8):
                    mm(ps[:, bk * 512 + (t % 2) * 256: bk * 512 + (t % 2) * 256 + 256], hA[:, c, t * 128:(t + 1) * 128], Wkv[:, c, :],
                       c == 0, c == 7, [(hAk, c), "Wkv"], [("ps", bk)])
                if 1 <= si <= 16:
                    h_ = (si - 1) % 2
                    mod_b_mm((si - 1) // 2, 16 * h_ + 4 * t, 16 * h_ + 4 * t + 4)

        def a_post(si):
            row0, nt, col = supers[si]
            slot = si % 2
            tile0 = row0 // 128
            W = nt * 128
            kvv = ps[:, 4 * 512:4 * 512 + nt * 256].rearrange("p (t n) -> p t n", n=256)
            kview = kvv[:, :, 0:128]
            kvb = [("ps", 4)] + ([("ps", 5)] if nt > 2 else [])
            s0, s1, s2, s3 = scr[0][:, 0:W], scr[1][:, 0:W], scr[2][:, 0:W], scr[3][:, 0:W]
            tv = lambda a: a.rearrange("p (t n) -> p t n", n=128)
            hv = lambda a: a.rearrange("p (h d) -> p h d", d=64)
            qv = lambda a: a.rearrange("p (t h d) -> p t h d", h=2, d=64)
            rp = ropeb[slot]
            rpk = ("ropeb", slot)
            act(tv(s0), kview, AF.Square, kvb, [("scr", 0)])
            head_rstd(s0, 2 * nt, ("scr", 0))
            tt(tv(s1), kview, gk[:, :].unsqueeze(1).broadcast_to([128, nt, 128]), ALU.mult, kvb + ["gk"], [("scr", 1)])
            tt(hv(s2), hv(s1), hr[:, 0:2 * nt].unsqueeze(2).broadcast_to([128, 2 * nt, 64]), ALU.mult, [("scr", 1), "hr"], [("scr", 2)])
            tt(qv(s3), qv(s2), rp[:, 0:nt, 0:64].unsqueeze(2).broadcast_to([128, nt, 2, 64]), ALU.mult, [("scr", 2), rpk], [("scr", 3)])
            tt(qv(s0)[:, :, :, 0:32], qv(s2)[:, :, :, 32:64], rp[:, 0:nt, 64:96].unsqueeze(2).broadcast_to([128, nt, 2, 32]), ALU.mult,
               [("scr", 2), rpk], [("scr", 0)])
            tt(qv(s0)[:, :, :, 32:64], qv(s2)[:, :, :, 0:32], rp[:, 0:nt, 96:128].unsqueeze(2).broadcast_to([128, nt, 2, 32]), ALU.mult,
               [("scr", 2), rpk], [("scr", 0)])
            tt(krb[:, 0:W], s3, s0, ALU.add, [("scr", 3), ("scr", 0)], [("big", 24)])
            for t in range(nt):
                tr(bank(6).bitcast(BF16)[:, t * 128:(t + 1) * 128], krb[:, t * 128:(t + 1) * 128], [("big", 24)], [("ps", 6)])
            act(Vaug[:, tile0:tile0 + nt, 0:64], kvv[:, :, 128:192], AF.Copy, kvb, [("Vaug", tile0 + t) for t in range(nt)])
            act(Vaug[:, tile0:tile0 + nt, 128:192], kvv[:, :, 192:256], AF.Copy, kvb, [("Vaug", tile0 + t) for t in range(nt)])
            cp(KT[:, tile0 * 128:(tile0 + nt) * 128], bank(6).bitcast(BF16)[:, 0:nt * 128], [("ps", 6)], [("KT", si)])

        cnt["phaseA"] = True
        actx = {}
        if nsup:
            for k_ in range(min(2, nsup)):
                load_xo(supers[k_][0], supers[k_][1], k_ % 2)
                load_rope(supers[k_][0], supers[k_][1], k_ % 2)
            ld["n"] = 0
            actx[0] = nm_stats(xbuf[0], ("xbuf", 0), supers[0][1])
            nm_xn_tr(actx[0])
            if nsup > 2:
                load_xo(supers[2][0], supers[2][1], 0)
            nm_evac(actx[0], a1t, 0, supers[0][2], hbufs[0])
        for k_ in range(nsup):
            if k_ % 2 == 0 and k_ // 2 < 8:
                mod_b_dma(k_ // 2)
            n_ = k_ + 1
            if n_ < nsup:
                actx[n_] = nm_stats(xbuf[n_ % 2], ("xbuf", n_ % 2), supers[n_][1])
            a_kv(k_)
            if n_ < nsup:
                nm_xn_tr(actx[n_])
                if k_ + 3 < nsup:
                    load_xo(supers[k_ + 3][0], supers[k_ + 3][1], (k_ + 3) % 2)
            a_post(k_)
            if k_ + 2 < nsup:
                load_rope(supers[k_ + 2][0], supers[k_ + 2][1], k_ % 2)
            if n_ < nsup:
                nm_evac(actx[n_], a1t, 0, supers[n_][2], hbufs[n_ % 2])

        cnt["phaseA"] = False
        ld["n"] = 1
        slot = load_x(256, 4) if (nqb and stage > 1) else None
        if stage >= 1:
            if len(supers) < 17:
                raise NotImplementedError("debug stage with truncated phase A not supported any more")
            mod_b_finish()
        if stage == 1:
            dump("KT", KT[:, 0:1280], [("KT", i) for i in range(3)], BF16)
            dump("Vaug", Vaug[:, 0:10, :], [("Vaug", i) for i in range(10)], BF16)
            dump("hT", hT[:], [("hT", c) for c in range(8)], BF16)
        if stage <= 1:
            nqb = 0
        wctr = {"n": 0}

        def load_unit(u):
            s = wctr["n"] % NB
            wctr["n"] += 1
            dma("sp", ring[s][:, :], wsc[u], [("wsc", u)], [("ring", s)], f"w{s}")
            return s

        def R(s):
            return ("ring", s)

        def unit8(s):
            return ring[s][:, :].rearrange("p (c n) -> p c n", c=8)

        def unit4(s):
            return ring[s][:, :].rearrange("p (c n) -> p c n", c=4)

        yT = [big[:, c, :] for c in range(8)]
        uT = [big[:, 8 + j, :] for j in range(4)]
        gmT = [big[:, 12 + j, :] for j in range(4)]
        attnT = [big[:, 16 + j, :] for j in range(4)]
        QT = [QTb[:, j, :] for j in range(4)]
        vnb = [big[:, 24 + t, :] for t in range(4)]
        PT = [big[:, 28:30, :].rearrange("p a b -> p (a b)"), big[:, 30:32, :].rearrange("p a b -> p (a b)"),
              big[:, 26:28, :].rearrange("p a b -> p (a b)")]
        PTK = [[("big", 28), ("big", 29)], [("big", 30), ("big", 31)], [("big", 26), ("big", 27)]]

        def pre_q_pieces(slot_):
            xb_, xres_, rpb_ = xbuf[slot_], ("xbuf", slot_), ropeb[slot_]
            st = {}
            qrb = scr[4][:, :].bitcast(BF16)[:, 0:512]

            def p_stats():
                st["cx"] = nm_stats(xb_, xres_, 4)

            def p_xn(t):
                cx = st["cx"]
                k = cnt["xn"] % 2
                cnt["xn"] += 1
                xn_tile(xb_, xres_, t, k, rstd[:, cx["po"] + t:cx["po"] + t + 1], ("rstd", cx["par"]), pb=4)

            def p_evac():
                nm_evac(st["cx"], (a1, "a1", "modTa"), 0, 0, None, pb=4)

            def p_qproj():
                su = load_unit(0)
                for t in range(4):
                    for c in range(8):
                        mm(bank(4 + t), hT[:, c, t * 128:(t + 1) * 128], unit8(su)[:, c, :], c == 0, c == 7, [("hT", c), R(su)], [("ps", 4 + t)])

            def p_qdve(t):
                norm_rope(bank(4 + t), ("ps", 4 + t), 8, gq[:, :], "gq", rpb_[:, t, :], ("ropeb", slot_), qrb, ("scr", 4))

            def p_qpe(t):
                for g in range(4):
                    tr(bank(4 + t).bitcast(BF16)[:, g * 128:(g + 1) * 128], qrb[:, g * 128:(g + 1) * 128], [("scr", 4)], [("ps", 4 + t)])
                cp(QTb[:, :, t * 128:(t + 1) * 128], bank(4 + t).bitcast(BF16)[:, 0:512].rearrange("p (g q) -> p g q", q=128),
                   [("ps", 4 + t)], [("QT", g) for g in range(4)])

            return dict(stats=p_stats, xn=p_xn, evac=p_evac, qproj=p_qproj, qdve=p_qdve, qpe=p_qpe)

        def run_pre_q_all(pq):
            pq["stats"]()
            for t in range(4):
                pq["xn"](t)
            pq["evac"]()
            pq["qproj"]()
            for t in range(4):
                pq["qdve"](t)
                pq["qpe"](t)

        if nqb:
            run_pre_q_all(pre_q_pieces(slot))
            pre_units = (load_unit(1), load_unit(2))

        for qb in range(nqb):
            row0 = 256 + qb * 512
            xb = xbuf[slot]
            xres = ("xbuf", slot)
            rpb = ropeb[slot]
            nslot = None
            sv, suu = pre_units
            for t in range(4):
                for c in range(8):
                    mm(bank(t), hT[:, c, t * 128:(t + 1) * 128], unit8(sv)[:, c, :], c == 0, c == 7, [("hT", c), R(sv)], [("ps", t)])
            for j in range(4):
                for c in range(8):
                    mm(bank(4 + j), unit8(suu)[:, c, j * 128:(j + 1) * 128], hT[:, c, :], c == 0, c == 7, [("hT", c), R(suu)], [("ps", 4 + j)])
            for t in range(4):
                act(scr[t][:, :], bank(t), AF.Gelu_apprx_tanh, [("ps", t)], [("scr", t)])
            for j in range(4):
                act(uT[j], bank(4 + j), AF.Gelu_apprx_tanh, [("ps", 4 + j)], [("big", 8 + j)])
            for t in range(4):
                sq_ = scr[4 + t % 2]
                act(sq_[:, :], scr[t][:, :], AF.Square, [("scr", t)], [("scr", 4 + t % 2)])
                T.add("dve", lambda e, t=t, sq_=sq_: e.tensor_reduce(out=hs32[:, 8 * t:8 * t + 8], in_=sq_[:, :].rearrange("p (h d) -> p h d", d=64),
                                                                     axis=AX.X, op=ALU.add), [("scr", 4 + t % 2)], [("hs32", t)])
            act(hl32[:, :], hs32[:, :], AF.Ln, [("hs32", t) for t in range(4)], ["hl32"], scale=1.0 / 64, bias=EPS)
            act(hr32[:, :], hl32[:, :], AF.Exp, ["hl32"], ["hr32"], scale=-0.5)
            for t in range(4):
                tm_ = scr[4 + t % 2]
                tt(tm_[:, :], scr[t][:, :], gmg[:, :], ALU.mult, [("scr", t), "gmg"], [("scr", 4 + t % 2)])
                tt(vnb[t].rearrange("p (h d) -> p h d", d=64), tm_[:, :].rearrange("p (h d) -> p h d", d=64),
                   hr32[:, 8 * t:8 * t + 8].unsqueeze(2).broadcast_to([128, 8, 64]), ALU.mult, [("scr", 4 + t % 2), "hr32"], [("big", 24 + t)])
                for j in range(4):
                    for gg in range(2):
                        g = 2 * j + gg
                        mm(ps[gg * 64:(gg + 1) * 64, j * 512 + t * 128: j * 512 + (t + 1) * 128], vnb[t][:, g * 64:(g + 1) * 64], wsT[:, g, :],
                           True, True, [("big", 24 + t), "wsT"], [("ps", j)])
            for j in range(4):
                tt(scr[5][:, :].rearrange("p (t q) -> p t q", q=128), bank(j).rearrange("p (t q) -> p t q", q=128),
                   bsT[:, j, :].unsqueeze(1).broadcast_to([128, 4, 128]), ALU.add, [("ps", j), "bsT"], [("scr", 5)])
                tt(gmT[j], scr[5][:, :], uT[j], ALU.mult, [("scr", 5), ("big", 8 + j)], [("big", 12 + j)])

            steps = [(g, kb) for g in range(4) for kb in range(NT)]

            def ksup(kb):
                return 0 if kb < 2 else 1 + (kb - 2) // 4

            def qk(i):
                g, kb = steps[i]
                sbk = (i % 2) * 2
                mm(bank(sbk), KT[0:64, kb * 128:(kb + 1) * 128], QT[g][0:64, :], True, True, [("KT", ksup(kb)), ("QT", g)], [("ps", sbk)])
                mm(bank(sbk + 1), KT[64:128, kb * 128:(kb + 1) * 128], QT[g][64:128, :], True, True, [("KT", ksup(kb)), ("QT", g)], [("ps", sbk + 1)])
                pace = [("pace", i)] if (qb == 0 and i % 12 == 0) else []
                act(PT[i % 3], ps[:, sbk * 512:sbk * 512 + 1024], AF.Exp, [("ps", sbk), ("ps", sbk + 1)], PTK[i % 3] + pace, scale=0.125)
                if pace and N_EARLY + i // 12 < N_UNITS_QB:
                    cast_unit(N_EARLY + i // 12, pace)

            def pv(i):
                g, kb = steps[i]
                oa = 4 + 2 * (g % 2)
                mm(bank(oa), Vaug[:, kb, 0:128], PT[i % 3][:, 0:512], kb == 0, kb == NT - 1, [("Vaug", kb)] + PTK[i % 3], [("ps", oa)])
                mm(bank(oa + 1), Vaug[:, kb, 64:192], PT[i % 3][:, 512:1024], kb == 0, kb == NT - 1, [("Vaug", kb)] + PTK[i % 3], [("ps", oa + 1)])
                if kb == NT - 1:
                    if g == 3:
                        act(rc[64:128, 0:512], bank(oa)[64:128, :], AF.Ln, [("ps", oa)], ["rc", ("rcj", 0), ("rcj", 1)])
                        act(rc[64:128, 0:512], rc[64:128, 0:512], AF.Exp, ["rc"], ["rc"], scale=-1.0)
                        act(rc[0:64, 512:1024], bank(oa + 1)[0:64, :], AF.Ln, [("ps", oa + 1)], ["rc", ("rcj", 0), ("rcj", 1)])
                        act(rc[0:64, 512:1024], rc[0:64, 512:1024], AF.Exp, ["rc"], ["rc"], scale=-1.0)
                    else:
                        recip(rc[64:128, 0:512], bank(oa)[64:128, :], [("ps", oa)], ["rc", ("rcj", 0), ("rcj", 1)])
                        recip(rc[0:64, 512:1024], bank(oa + 1)[0:64, :], [("ps", oa + 1)], ["rc", ("rcj", 0), ("rcj", 1)])
                    tt(attnT[g][0:64, :], bank(oa)[0:64, :], rc[64:128, 0:512], ALU.mult, [("ps", oa), "rc", ("rcj", 0), ("rcj", 1)], [("big", 16 + g)])
                    tt(attnT[g][64:128, :], bank(oa + 1)[64:128, :], rc[0:64, 512:1024], ALU.mult, [("ps", oa + 1), "rc", ("rcj", 0), ("rcj", 1)], [("big", 16 + g)])

            qk(0)
            qk(1)
            for i in range(len(steps)):
                if i + 2 < len(steps):
                    qk(i + 2)
                pv(i)

            if qb + 1 < nqb:
                nslot = load_x(row0 + 512, 4)

            sga = [load_unit(3), None]
            sgb = [load_unit(4), None]
            sbra = load_unit(5)
            sbrg = load_unit(6)
            for m in range(8):
                if m == 4:
                    sga[1] = load_unit(7)
                    sgb[1] = load_unit(8)
                h = m // 4
                b0 = (m % 2) * 4
                for c in range(8):
                    mm(bank(b0), unit8(sga[h])[:, c, (m % 4) * 128:(m % 4 + 1) * 128], hT[:, c, :], c == 0, c == 7, [("hT", c), R(sga[h])], [("ps", b0)])
                for c in range(8):
                    mm(bank(b0 + 1), unit8(sgb[h])[:, c, (m % 4) * 128:(m % 4 + 1) * 128], hT[:, c, :], c == 0, c == 7, [("hT", c), R(sgb[h])], [("ps", b0 + 1)])
                for c in range(4):
                    mm(bank(b0 + 2), unit4(sbra)[:, c, m * 128:(m + 1) * 128], attnT[c], c == 0, c == 3, [("big", 16 + c), R(sbra)], [("ps", b0 + 2)])
                for c in range(4):
                    mm(bank(b0 + 3), unit4(sbrg)[:, c, m * 128:(m + 1) * 128], gmT[c], c == 0, c == 3, [("big", 12 + c), R(sbrg)], [("ps", b0 + 3)])
                act(scr[0][:, :], bank(b0), AF.Sigmoid, [("ps", b0)], [("scr", 0)])
                act(scr[1][:, :], bank(b0 + 1), AF.Sigmoid, [("ps", b0 + 1)], [("scr", 1)])
                tt(scr[2][:, :], bank(b0 + 2), scr[0][:, :], ALU.mult, [("ps", b0 + 2), ("scr", 0)], [("scr", 2)])
                tt(scr[3][:, :], bank(b0 + 3), scr[1][:, :], ALU.mult, [("ps", b0 + 3), ("scr", 1)], [("scr", 3)])
                tt(yT[m], scr[2][:, :], scr[3][:, :], ALU.add, [("scr", 2), ("scr", 3)], [("big", m)], eng="pool")

            so = [load_unit(9), load_unit(10)]
            par = cnt["par"] % 2
            cnt["par"] += 1
            po = par * 4
            k8c = {"n": 0}

            def b8_mm(t):
                for nh in range(2):
                    bk = 4 + k8c["n"] % 4
                    k8c["n"] += 1
                    for c in range(8):
                        mm(bank(bk), yT[c][:, t * 128:(t + 1) * 128], unit8(so[nh])[:, c, :], c == 0, c == 7, [("big", c), R(so[nh])], [("ps", bk)])
                    sk = 4 + (k8c["n"] % 2)
                    tt(scr[sk][:, :], bank(bk), gbc[:, nh * 512:(nh + 1) * 512], ALU.mult, [("ps", bk), "gbc"], [("scr", sk)])
                    tt(xb[:, t, nh * 512:(nh + 1) * 512], scr[sk][:, :], xb[:, t, nh * 512:(nh + 1) * 512], ALU.add, [("scr", sk), xres], [xres, ("x1t", t)], eng="pool")

            def b9_tile(t):
                kq = cnt["sq"] % 2
                cnt["sq"] += 1
                cs = slice(po + t, po + t + 1)
                act(sqj[kq], xb[:, t, :], AF.Square, [("x1t", t)], [("ss", par, t), ("rcj", kq)], accum=ss[:, cs])
                act(rs[:, cs], ss[:, cs], AF.Ln, [("ss", par, t)], [("rs", par), ("rs", par, t)], scale=1.0 / D, bias=EPS)
                act(rstd[:, cs], rs[:, cs], AF.Exp, [("rs", par, t)], [("rstd", par), ("rstd", par, t)], scale=-0.5)
                k = cnt["xn"] % 2
                cnt["xn"] += 1
                xn_tile(xb, ("x1t", t), t, k, rstd[:, cs], ("rstd", par, t))

            for t in range(4):
                b8_mm(t)
                if t >= 1:
                    b9_tile(t - 1)
            b9_tile(3)
            nm_evac(dict(nt=4), (a2, "a2", "modTb"), 24, 0)

            pq = pre_q_pieces(nslot) if nslot is not None else None
            if pq:
                pq["stats"]()
            for j in range(32):
                if j % 4 == 0:
                    sf = load_unit(11 + j // 4)
                bk = j % 8
                for c in range(8):
                    mm(bank(bk), unit8(sf)[:, c, (j % 4) * 128:(j % 4 + 1) * 128], hT[:, c, :], c == 0, c == 7, [("hT", c), R(sf)], [("ps", bk)])
                sk = j % 4
                act(scr[sk][:, :], bank(bk), AF.Relu, [("ps", bk)], [("scr", sk)])
                tt(big[:, j, :], scr[sk][:, :], scr[sk][:, :], ALU.mult, [("scr", sk)], [("big", j)])

            if pq:
                pq["xn"](0)
                pq["xn"](1)
            blk = 0
            for nh in range(2):
                for kg in range(4):
                    s2 = load_unit(19 + nh * 4 + kg)
                    for t in range(4):
                        bk = t
                        for c in range(8):
                            mm(bank(bk), big[:, kg * 8 + c, t * 128:(t + 1) * 128], unit8(s2)[:, c, :], kg == 0 and c == 0, kg == 3 and c == 7,
                               [("big", kg * 8 + c), R(s2)], [("ps", bk)])
                    if pq:
                        if blk == 0:
                            pq["xn"](2)
                            pq["xn"](3)
                            pq["evac"]()
                        elif blk == 1:
                            pq["qproj"]()
                        elif blk == 2:
                            pq["qdve"](0)
                        elif blk in (3, 4, 5):
                            pq["qpe"](blk - 3)
                            pq["qdve"](blk - 2)
                        elif blk == 6:
                            pq["qpe"](3)
                    blk += 1
                for t in range(4):
                    bk = t
                    sk = 5
                    tt(scr[sk][:, :], bank(bk), gbc[:, 1024 + nh * 512:1024 + (nh + 1) * 512], ALU.mult, [("ps", bk), "gbc"], [("scr", sk)])
                    tt(xb[:, t, nh * 512:(nh + 1) * 512], scr[sk][:, :], xb[:, t, nh * 512:(nh + 1) * 512], ALU.add, [("scr", sk), xres], [xres], eng="pool")
            if qb + 1 < nqb:
                pre_units = (load_unit(1), load_unit(2))
            dma("sp", out_d[qb * 512:(qb + 1) * 512, :].rearrange("(t p) d -> p t d", p=128), xb[:, :, :], [xres], [("out", qb)], f"o{slot}")
            slot = nslot

        T.add("sp", lambda e: e.nop(), [("out", q) for q in range(nqb)] + [("dbgout", n) for n in dbg], [])

        T.finalize()
        nc._tracker = T

        @block.sync
        def _(e):
            T.emit("sp", e, esems, dsems)

        @block.tensor
        def _(e):
            T.emit("pe", e, esems, dsems)

        @block.scalar
        def _(e):
            T.emit("act", e, esems, dsems)

        @block.vector
        def _(e):
            T.emit("dve", e, esems, dsems)

        @block.gpsimd
        def _(e):
            T.emit("pool", e, esems, dsems)

    return nc


_CACHE = {}


def _rope_table(tok_idx):
    n = tok_idx.shape[0]
    t = np.maximum(tok_idx, 0)
    row = (t // 64).astype(np.float32)
    colp = (t % 64).astype(np.float32)
    inv = (np.float32(10000.0) ** (-np.arange(0, 32, 2, dtype=np.float32) / np.float32(32))).astype(np.float32)
    ang = np.concatenate([row[:, None] * inv[None, :], colp[:, None] * inv[None, :]], axis=-1).astype(np.float32)
    cos = np.cos(ang).astype(np.float32)
    sin = np.sin(ang).astype(np.float32)
    ident = tok_idx < 0
    cos[ident] = 1.0
    sin[ident] = 0.0
    return np.concatenate([cos, cos, -sin, sin], axis=1).astype(np.float32)


def kernel(x, c, ctx, c_ctx, w_mod, b_mod, norm1_g, norm2_g, w_in, q_norm_g, k_norm_g,
           gm_norm_g, gm_ws, gm_bs, w_br_attn, w_br_gm, w_out, w_ff1, w_ff2):
    f = lambda a: np.ascontiguousarray(np.asarray(a, dtype=np.float32))
    x, c, ctx, c_ctx = f(x), f(c), f(ctx), f(c_ctx)
    w_mod, b_mod, w_in = f(w_mod)[0], f(b_mod)[0], f(w_in)[0]
    n1, n2 = f(norm1_g)[0], f(norm2_g)[0]
    qg, kg, gmg = f(q_norm_g)[0], f(k_norm_g)[0], f(gm_norm_g)[0]
    ws, bs = f(gm_ws)[0], f(gm_bs)[0]
    bra, brg, wo, w1, w2 = f(w_br_attn)[0], f(w_br_gm)[0], f(w_out)[0], f(w_ff1)[0], f(w_ff2)[0]

    if "nc" not in _CACHE:
        _CACHE["nc"] = build_program()
    nc = _CACHE["nc"]

    qcols = 256 + np.array([kv * 256 + g * 64 + d for g in range(4) for kv in range(2) for d in range(64)])
    order = np.concatenate([np.arange(0, 256), qcols, np.arange(1280, 1792), np.arange(768, 1280), np.arange(1792, 3840)])
    w_in_p = np.ascontiguousarray(w_in[:, order])
    rows = np.array([kv * 256 + g * 64 + d for g in range(4) for kv in range(2) for d in range(64)])
    w_bra_p = np.ascontiguousarray(bra[rows, :])
    b_modT = np.ascontiguousarray(b_mod.reshape(48, 128).T)
    b_modg = np.ascontiguousarray(np.concatenate([b_mod[2048:3072], b_mod[5120:6144]])[None, :])
    n1g = np.ascontiguousarray(n1.reshape(8, 128).T)
    n2g = np.ascontiguousarray(n2.reshape(8, 128).T)
    gq_bc = np.ascontiguousarray(np.broadcast_to(np.tile(qg, 8)[None, :], (128, 512)))
    gk_bc = np.ascontiguousarray(np.broadcast_to(np.tile(kg, 2)[None, :], (128, 128)))
    gmg_bc = np.ascontiguousarray(np.broadcast_to(gmg.reshape(512)[None, :], (128, 512)))
    wsT = np.ascontiguousarray(ws.transpose(2, 0, 1))
    bsT = np.ascontiguousarray(np.broadcast_to(bs.reshape(4, 2, 1, 128), (4, 2, 64, 128)).transpose(1, 2, 0, 3).reshape(128, 4, 128))
    ident = np.eye(128, dtype=np.float32)

    in_maps = []
    for core in range(8):
        b, hf = core // 2, core % 2
        own = np.arange(hf * 4096, (hf + 1) * 4096)
        oth = np.arange((1 - hf) * 4096, (2 - hf) * 4096)
        xin = np.concatenate([ctx[b], x[b, own], x[b, oth]], axis=0)
        tok = np.concatenate([-np.ones(256, dtype=np.int64), own, oth])
        cT = np.ascontiguousarray(np.stack([c[b], c_ctx], axis=1).reshape(8, 128, 2).transpose(1, 0, 2))
        in_maps.append({
            "xin": np.ascontiguousarray(xin), "rope": _rope_table(tok), "cT": cT, "w_mod": w_mod, "b_modT": b_modT,
            "b_modg": b_modg, "n1g": n1g, "n2g": n2g, "w_in_p": w_in_p, "gq_bc": gq_bc, "gk_bc": gk_bc, "gmg_bc": gmg_bc,
            "wsT": wsT, "bsT": bsT, "w_bra_p": w_bra_p, "w_brg": brg, "w_out": wo, "w_ff1": w1, "w_ff2": w2, "ident": ident,
        })
    res = run_bass_kernel_spmd(nc, in_maps, core_ids=list(range(8)))
    out = np.empty((4, 8192, D), dtype=np.float32)
    for core in range(8):
        b, hf = core // 2, core % 2
        out[b, hf * 4096:(hf + 1) * 4096] = res.results[core]["out"]
    return out
```

```python
import numpy as np
import concourse.bass as bass
import concourse.mybir as mybir
from concourse.bass_utils import run_bass_kernel_spmd

F32 = mybir.dt.float32
BF16 = mybir.dt.bfloat16
AF = mybir.ActivationFunctionType
ALU = mybir.AluOpType
AX = mybir.AxisListType

D = 1024
NCTX_T = 2
NOWN_T = 32
NT = 66
NQB = 8
EPS = 1e-6
NB = 5
N_UNITS_QB = 27


class Tracker:
    def __init__(self):
        self.ops = []
        self.last_w = {}
        self.readers = {}
        self.dcount = {}

    def add(self, eng, fn, reads=(), writes=(), dsem=None):
        idx = len(self.ops)
        deps = set()
        if eng in ("act", "dve"):
            writes = list(writes) + [("pslk", r[1]) for r in reads if isinstance(r, tuple) and r[0] == "ps"]
        for r in reads:
            if r in self.last_w:
                deps.add(self.last_w[r])
        for w in writes:
            if w in self.last_w:
                deps.add(self.last_w[w])
            for rd in self.readers.get(w, ()):
                deps.add(rd)
        op = dict(eng=eng, fn=fn, deps=deps, dsem=dsem, marked=False, idx=idx, val=None, desc=(tuple(reads), tuple(writes)))
        if dsem is not None:
            self.dcount[dsem] = self.dcount.get(dsem, 0) + 16
            op["dval"] = self.dcount[dsem]
        self.ops.append(op)
        for r in reads:
            self.readers.setdefault(r, []).append(idx)
        for w in writes:
            self.last_w[w] = idx
            self.readers[w] = []
        return idx

    def finalize(self):
        ops = self.ops
        for op in ops:
            red = {}
            for d in op["deps"]:
                dop = ops[d]
                if dop["dsem"] is not None:
                    key = ("d", dop["dsem"])
                    if key not in red or ops[red[key]]["dval"] < dop["dval"]:
                        red[key] = d
                else:
                    if dop["eng"] == "pe" and op["eng"] == "pe" and op["dsem"] is None:
                        continue
                    key = ("e", dop["eng"])
                    if key not in red or red[key] < d:
                        red[key] = d
            op["rdeps"] = list(red.values())
            for d in op["rdeps"]:
                if ops[d]["dsem"] is None:
                    ops[d]["marked"] = True
        cnt = {}
        for op in ops:
            if op["dsem"] is None and op["marked"]:
                cnt[op["eng"]] = cnt.get(op["eng"], 0) + 1
                op["val"] = cnt[op["eng"]]

    def trace(self, engname):
        waited = {}
        out = []
        for op in self.ops:
            if op["eng"] != engname:
                continue
            ws = []
            for d in op["rdeps"]:
                dop = self.ops[d]
                if dop["dsem"] is not None:
                    val = self.dcount[dop["dsem"]] if dop["dsem"] in ("const", "cast", "gb") else dop["dval"]
                    key = ("d", dop["dsem"])
                else:
                    val, key = dop["val"], ("e", dop["eng"])
                if waited.get(key, 0) < val:
                    ws.append((key[1], val))
                    waited[key] = val
            inc = (op["dsem"], op.get("dval")) if op["dsem"] else ((engname, op["val"]) if op["marked"] else None)
            out.append((op["idx"], ws, op["desc"], inc))
        return out

    def emit(self, engname, engobj, esems, dsems):
        waited = {}
        for op in self.ops:
            if op["eng"] != engname:
                continue
            for d in op["rdeps"]:
                dop = self.ops[d]
                if dop["dsem"] is not None:
                    val = self.dcount[dop["dsem"]] if dop["dsem"] in ("const", "cast", "gb") else dop["dval"]
                    sem, key = dsems[dop["dsem"]], ("d", dop["dsem"])
                else:
                    sem, val, key = esems[dop["eng"]], dop["val"], ("e", dop["eng"])
                if waited.get(key, 0) < val:
                    engobj.wait_ge(sem, val)
                    waited[key] = val
            ins = op["fn"](engobj)
            if op["dsem"] is not None:
                ins.then_inc(dsems[op["dsem"]], 16)
            elif op["marked"]:
                ins.then_inc(esems[engname], 1)


def build_program(stage=3, nqb=NQB, skip=()):
    nc = bass.Bass("TRN2", target_bir_lowering=False)
    T = Tracker()
    dbg = {}

    def din(name, shape, dt=F32):
        return nc.dram_tensor(name, list(shape), dt, kind="ExternalInput").ap()

    xin = din("xin", [NT * 128, D])
    rope = din("rope", [NT * 128, 128])
    cT_d = din("cT", [128, 8, 2])
    wmod_d = din("w_mod", [D, 6 * D])
    bmodT_d = din("b_modT", [128, 48])
    bmodg_d = din("b_modg", [1, 2048])
    n1g_d = din("n1g", [128, 8])
    n2g_d = din("n2g", [128, 8])
    win_d = din("w_in_p", [D, 3840])
    gq_d = din("gq_bc", [128, 512])
    gk_d = din("gk_bc", [128, 128])
    gmg_d = din("gmg_bc", [128, 512])
    wsT_d = din("wsT", [128, 8, 128])
    bsT_d = din("bsT", [128, 4, 128])
    bra_d = din("w_bra_p", [512, D])
    brg_d = din("w_brg", [512, D])
    wout_d = din("w_out", [D, D])
    ff1_d = din("w_ff1", [D, 4 * D])
    ff2_d = din("w_ff2", [4 * D, D])
    ident_d = din("ident", [128, 128])
    out_d = nc.dram_tensor("out", [NOWN_T * 128, D], F32, kind="ExternalOutput").ap()
    gscr = nc.dram_tensor("gscr", [1, 2048], F32, kind="Internal").ap()
    wsc = nc.dram_tensor("wscratch", [N_UNITS_QB, 128, 4096], BF16, kind="Internal").ap()

    import contextlib
    es = contextlib.ExitStack()

    def sb(name, shape, dt):
        return es.enter_context(nc.sbuf_tensor(name, list(shape), dt))

    with es:
        KT = sb("KT", [128, NT * 128], BF16)
        Vaug = sb("Vaug", [128, NT, 192], BF16)
        xbuf = [sb(f"xbuf{i}", [128, 4, D], F32) for i in range(2)]
        ropeb = [sb(f"ropeb{i}", [128, 4, 128], F32) for i in range(2)]
        xn = [sb(f"xn{i}", [128, D], BF16) for i in range(2)]
        hT = sb("hT", [128, 8, 512], BF16)
        big = sb("big", [128, 32, 512], BF16)
        ringT = sb("ring", [128, NB * 4096], BF16)
        ring = [ringT[:, i * 4096:(i + 1) * 4096] for i in range(NB)]
        gbc = sb("gbc", [128, 2048], F32)
        scr = [sb(f"scr{i}", [128, 512], F32) for i in range(6)]
        rc = sb("rc", [128, 1024], F32)
        QTb = sb("QTb", [128, 4, 512], BF16)
        Wkv = sb("Wkv", [128, 8, 256], BF16)
        identb = sb("identb", [128, 128], BF16)
        gq = sb("gq", [128, 512], F32)
        gk = sb("gk", [128, 128], F32)
        gmg = sb("gmg", [128, 512], F32)
        wsT = sb("wsTb", [128, 8, 128], BF16)
        bsT = sb("bsTs", [128, 4, 128], F32)
        cT = sb("cTs", [128, 8, 2], F32)
        scT = sb("scT", [128, 8, 2], F32)
        bmodT = sb("bmodTs", [128, 48], F32)
        n1g = sb("n1gs", [128, 8], F32)
        n2g = sb("n2gs", [128, 8], F32)
        modT = sb("modT", [128, 48, 2], F32)
        a1 = sb("a1", [128, 8, 2], F32)
        a2 = sb("a2", [128, 8, 2], F32)
        ones1 = sb("ones1", [1, 128], F32)
        ss = sb("ss", [128, 8], F32)
        rs = sb("rs", [128, 8], F32)
        rstd = sb("rstd", [128, 8], F32)
        hs = sb("hs", [128, 8], F32)
        hl = sb("hl", [128, 8], F32)
        hr = sb("hr", [128, 8], F32)
        hs32 = sb("hs32", [128, 32], F32)
        hl32 = sb("hl32", [128, 32], F32)
        hr32 = sb("hr32", [128, 32], F32)
        ps = es.enter_context(nc.psum_tensor("ps", [128, 4096], F32))
        sqj = [rc[:, 0:512].bitcast(BF16), rc[:, 512:1024].bitcast(BF16)]
        grow = big[0:1, 0:8, :].rearrange("p a b -> p (a b)").bitcast(F32)
        bmodg = big[0:1, 8:16, :].rearrange("p a b -> p (a b)").bitcast(F32)

        esems = {e: es.enter_context(nc.semaphore("sem_" + e)) for e in ["pe", "act", "dve", "pool", "sp"]}
        dnames = [f"c{i}" for i in range(6)] + ["dbg", "gb", "gb2", "const", "cast", "x0", "x1", "r0", "r1", "o0", "o1", "wm0", "wm1", "wm2"] + [f"w{i}" for i in range(NB)]
        dsems = {d: es.enter_context(nc.semaphore("ds_" + d)) for d in dnames}
        block = es.enter_context(nc.Block())

        def bank(b):
            return ps[:, b * 512:(b + 1) * 512]

        def mm(out, lhsT, rhs, start, stop, reads, writes, sgc=False):
            T.add("pe", lambda e: e.matmul(out, lhsT=lhsT, rhs=rhs, start=start, stop=stop, skip_group_check=sgc), reads, writes)

        def tr(out, in_, reads, writes):
            T.add("pe", lambda e: e.transpose(out, in_, identb[:, :]), list(reads) + ["identb"], writes)

        def act(out, in_, func, reads, writes, scale=None, bias=None, accum=None):
            kw = {}
            if scale is not None:
                kw["scale"] = scale
            if bias is not None:
                kw["bias"] = bias
            if accum is not None:
                kw["accum_out"] = accum
            T.add("act", lambda e: e.activation(out=out, in_=in_, func=func, **kw), reads, writes)

        def tt(out, in0, in1, op, reads, writes, eng="dve"):
            T.add(eng, lambda e: e.tensor_tensor(out=out, in0=in0, in1=in1, op=op), reads, writes)

        def ts(out, in0, s1, s2, op0, op1, reads, writes, eng="dve"):
            if op1 is None:
                T.add(eng, lambda e: e.tensor_scalar(out=out, in0=in0, scalar1=s1, scalar2=None, op0=op0), reads, writes)
            else:
                T.add(eng, lambda e: e.tensor_scalar(out=out, in0=in0, scalar1=s1, scalar2=s2, op0=op0, op1=op1), reads, writes)

        def stt(out, in0, scalar, in1, op0, op1, reads, writes):
            T.add("dve", lambda e: e.scalar_tensor_tensor(out=out, in0=in0, scalar=scalar, in1=in1, op0=op0, op1=op1), reads, writes)

        def recip(out, in_, reads, writes):
            T.add("dve", lambda e: e.reciprocal(out=out, in_=in_), reads, writes)

        def cp(out, in_, reads, writes, eng="dve"):
            T.add(eng, lambda e: e.tensor_copy(out=out, in_=in_), reads, writes)

        def dma(q, out, in_, reads, writes, dsem):
            T.add(q, lambda e: e.dma_start(out=out, in_=in_), reads, writes, dsem=dsem)

        def memset(ap, val, writes, eng="pool"):
            T.add(eng, lambda e: e.memset(ap, val), (), writes)

        for (dst, src, nm) in [(cT[:], cT_d, "cT"), (bmodT[:], bmodT_d, "bmodT"), (bmodg[:], bmodg_d, "bmodg"),
                               (n1g[:], n1g_d, "n1g"), (n2g[:], n2g_d, "n2g"), (gq[:], gq_d, "gq"), (gk[:], gk_d, "gk"),
                               (gmg[:], gmg_d, "gmg"), (bsT[:], bsT_d, "bsT")]:
            dma("sp", dst, src, (), [nm], "const")
        dma("pool", identb[:], ident_d, (), ["identb"], "cast")
        dma("pool", Wkv[:], win_d[:, 0:256].rearrange("(c p) n -> p c n", p=128), (), ["Wkv"], "cast")
        dma("pool", wsT[:], wsT_d, (), ["wsT"], "cast")
        memset(Vaug[:, :, 64:128], 1.0, [("Vaug", t) for t in range(NT)])
        memset(ones1[:], 1.0, ["ones1"])

        def wsrc_kn(w, c0, ncols):
            return w[:, c0:c0 + ncols].rearrange("(c p) n -> p c n", p=128)

        unit_src = []
        unit_src.append((wsrc_kn(win_d, 256, 512), 8))
        unit_src.append((wsrc_kn(win_d, 768, 512), 8))
        unit_src.append((wsrc_kn(win_d, 1280, 512), 8))
        unit_src.append((wsrc_kn(win_d, 1792, 512), 8))
        unit_src.append((wsrc_kn(win_d, 2816, 512), 8))
        unit_src.append((bra_d.rearrange("(c p) n -> p c n", p=128), 4))
        unit_src.append((brg_d.rearrange("(c p) n -> p c n", p=128), 4))
        unit_src.append((wsrc_kn(win_d, 2304, 512), 8))
        unit_src.append((wsrc_kn(win_d, 3328, 512), 8))
        unit_src.append((wsrc_kn(wout_d, 0, 512), 8))
        unit_src.append((wsrc_kn(wout_d, 512, 512), 8))
        for j in range(8):
            unit_src.append((wsrc_kn(ff1_d, j * 512, 512), 8))
        for nh in range(2):
            for kg in range(4):
                src = ff2_d[kg * 1024:(kg + 1) * 1024, nh * 512:(nh + 1) * 512].rearrange("(c p) n -> p c n", p=128)
                unit_src.append((src, 8))
        assert len(unit_src) == N_UNITS_QB
        def cast_unit(u, extra_reads=()):
            src, nch = unit_src[u]
            dst = wsc[u].rearrange("p (c n) -> p c n", c=nch)
            dma("pool", dst, src, [("cslot", u % 6)] + list(extra_reads), [("wsc", u), ("cslot", u % 6)], f"c{u % 6}")

        N_EARLY = 11 if nqb else N_UNITS_QB
        for u in range(N_EARLY):
            if "cast" in skip:
                break
            cast_unit(u)

        act(scT[:], cT[:], AF.Silu, ["cT"], ["scT"])
        for c in range(8):
            s_ = c % NB
            pa = ring[s_].bitcast(F32)
            dma("sp" if c % 2 == 0 else "act", pa, wmod_d[c * 128:(c + 1) * 128, 0:2048], (), [("ring", s_)], f"w{s_}")
            for jj in range(16):
                mm(ps[:, 2 * jj:2 * jj + 2], pa[:, jj * 128:(jj + 1) * 128], scT[:, c, :], c == 0 and jj == 0, c == 7 and jj == 15,
                   [("ring", s_), "scT"], [("ps", 0)], sgc=True)
        tt(modT[:, 0:16, :], ps[:, 0:32].rearrange("p (j k) -> p j k", k=2),
           bmodT[:, 0:16].unsqueeze(2).broadcast_to([128, 16, 2]), ALU.add, [("ps", 0), "bmodT"], ["modTa"])
        stt(a1[:], modT[:, 8:16, :], 1.0, n1g[:, :].unsqueeze(2).broadcast_to([128, 8, 2]), ALU.add, ALU.mult, ["modTa", "n1g"], ["a1"])

        def mod_b_buf(c):
            p_ = c % 2
            return p_, ringT[:, (2 * p_) * 4096:(2 * p_ + 2) * 4096].bitcast(F32)

        def mod_b_dma(c):
            p_, pb = mod_b_buf(c)
            dma("sp", pb, wmod_d[c * 128:(c + 1) * 128, 2048:6144], (), [("ring", 2 * p_), ("ring", 2 * p_ + 1)], f"wm{p_}")

        def mod_b_mm(c, j0, j1):
            p_, pb = mod_b_buf(c)
            for jj in range(j0, j1):
                mm(ps[:, 7 * 512 + 2 * jj:7 * 512 + 2 * jj + 2], pb[:, jj * 128:(jj + 1) * 128], scT[:, c, :], c == 0 and jj == 0, c == 7 and jj == 31,
                   [("ring", 2 * p_), ("ring", 2 * p_ + 1), "scT"], [("ps", 7)], sgc=True)

        def mod_b_piece(c):
            mod_b_dma(c)
            mod_b_mm(c, 0, 32)

        def mod_b_finish():
            tt(modT[:, 16:48, :], ps[:, 7 * 512:7 * 512 + 64].rearrange("p (j k) -> p j k", k=2),
               bmodT[:, 16:48].unsqueeze(2).broadcast_to([128, 32, 2]), ALU.add, [("ps", 7), "bmodT"], ["modTb"])
            stt(a2[:], modT[:, 32:40, :], 1.0, n2g[:, :].unsqueeze(2).broadcast_to([128, 8, 2]), ALU.add, ALU.mult, ["modTb", "n2g"], ["a2"])
            for k_, j0 in enumerate((16, 40)):
                T.add("pool", lambda e, k_=k_, j0=j0: e.dma_start(out=gscr[0, k_ * 1024:(k_ + 1) * 1024].rearrange("(c p) -> p c", p=128),
                                                             in_=modT[:, j0:j0 + 8, 0], allow_slow_non_contiguous=True),
                      ["modTb"], [("gscr", k_)], dsem="gb")
            dma("pool", gbc[:], gscr[0, :].partition_broadcast(128), [("gscr", 0), ("gscr", 1)], ["gbc"], "gb2")

        cnt = {"xn": 0, "ev": 0, "sq": 0, "par": 0}

        def nm_stats(xb, xres, nt):
            par = cnt["par"] % 2
            cnt["par"] += 1
            po = par * 4
            for t in range(nt):
                kq = cnt["sq"] % 2
                cnt["sq"] += 1
                act(sqj[kq], xb[:, t, :], AF.Square, [xres], [("ss", par, t), ("rcj", kq)], accum=ss[:, po + t:po + t + 1])
            act(rs[:, po:po + nt], ss[:, po:po + nt], AF.Ln, [("ss", par, t) for t in range(nt)], [("rs", par)], scale=1.0 / D, bias=EPS)
            act(rstd[:, po:po + nt], rs[:, po:po + nt], AF.Exp, [("rs", par)], [("rstd", par)], scale=-0.5)
            return dict(xb=xb, xres=xres, nt=nt, par=par, po=po, use_act=not cnt.get("phaseA", False))

        def xn_tile(xb, xres, t, k, rs_ap, rs_key, pb=0, use_act=True):
            if t % 2 == 0 or not use_act:
                ts(xn[k][:], xb[:, t, :], rs_ap, None, ALU.mult, None, [xres, rs_key], [("xn", k)])
            else:
                act(xn[k][:], xb[:, t, :], AF.Copy, [xres, rs_key], [("xn", k)], scale=rs_ap)
            for c in range(8):
                bk = pb + c // 2
                o = bank(bk).bitcast(BF16)[:, (c % 2) * 512 + t * 128:(c % 2) * 512 + (t + 1) * 128]
                tr(o, xn[k][:, c * 128:(c + 1) * 128], [("xn", k)], [("ps", bk)])

        def nm_xn_tr(cx):
            xb, xres, nt, par, po = cx["xb"], cx["xres"], cx["nt"], cx["par"], cx["po"]
            for t in range(nt):
                k = cnt["xn"] % 2
                cnt["xn"] += 1
                xn_tile(xb, xres, t, k, rstd[:, po + t:po + t + 1], ("rstd", par), use_act=cx.get("use_act", True))

        def nm_evac(cx, a_t, sh_off, col, hd=None, pb=0):
            nt = cx["nt"]
            hdst, hkey = hd if hd is not None else (hT, "hT")
            for c in range(8):
                bk = pb + c // 2
                src = bank(bk).bitcast(BF16)[:, (c % 2) * 512:(c % 2) * 512 + nt * 128]
                if c % 2 == 0:
                    act(hdst[:, c, 0:nt * 128], src, AF.Identity, [("ps", bk), a_t[1], a_t[2]], [(hkey, c)],
                        scale=a_t[0][:, c, col:col + 1], bias=modT[:, sh_off + c, col:col + 1])
                else:
                    ts(hdst[:, c, 0:nt * 128], src, a_t[0][:, c, col:col + 1], modT[:, sh_off + c, col:col + 1], ALU.mult, ALU.add,
                       [("ps", bk), a_t[1], a_t[2]], [(hkey, c)])

        def norm_mod_T(xb, xres, nt, a_t, sh_off, col, hd=None):
            cx = nm_stats(xb, xres, nt)
            nm_xn_tr(cx)
            nm_evac(cx, a_t, sh_off, col, hd)

        def head_rstd(src_sq, nh, res_in):
            T.add("dve", lambda e: e.tensor_reduce(out=hs[:, 0:nh], in_=src_sq.rearrange("p (h d) -> p h d", d=64), axis=AX.X, op=ALU.add),
                  [res_in], ["hs"])
            act(hl[:, 0:nh], hs[:, 0:nh], AF.Ln, ["hs"], ["hl"], scale=1.0 / 64, bias=EPS)
            act(hr[:, 0:nh], hl[:, 0:nh], AF.Exp, ["hl"], ["hr"], scale=-0.5)

        def norm_rope(psrc, psres, nh, gain, gres, rp, rpres, outb, outres):
            W = nh * 64
            s0, s1, s2, s3 = scr[0][:, 0:W], scr[1][:, 0:W], scr[2][:, 0:W], scr[3][:, 0:W]
            act(s0, psrc, AF.Square, [psres], [("scr", 0)])
            head_rstd(s0, nh, ("scr", 0))
            tt(s1, psrc, gain, ALU.mult, [psres, gres], [("scr", 1)])
            tt(s2.rearrange("p (h d) -> p h d", d=64), s1.rearrange("p (h d) -> p h d", d=64),
               hr[:, 0:nh].unsqueeze(2).broadcast_to([128, nh, 64]), ALU.mult, [("scr", 1), "hr"], [("scr", 2)])
            v2 = s2.rearrange("p (h d) -> p h d", d=64)
            tt(s3.rearrange("p (h d) -> p h d", d=64), v2, rp[:, 0:64].unsqueeze(1).broadcast_to([128, nh, 64]), ALU.mult,
               [("scr", 2), rpres], [("scr", 3)])
            v0 = s0.rearrange("p (h d) -> p h d", d=64)
            tt(v0[:, :, 0:32], v2[:, :, 32:64], rp[:, 64:96].unsqueeze(1).broadcast_to([128, nh, 32]), ALU.mult,
               [("scr", 2), rpres], [("scr", 0)])
            tt(v0[:, :, 32:64], v2[:, :, 0:32], rp[:, 96:128].unsqueeze(1).broadcast_to([128, nh, 32]), ALU.mult,
               [("scr", 2), rpres], [("scr", 0)])
            tt(outb, s3, s0, ALU.add, [("scr", 3), ("scr", 0)], [outres])

        def dump(name, ap, res, dt=F32):
            if "nodump" in skip:
                return
            d = nc.dram_tensor("dbg_" + name, list(ap.shape), F32, kind="ExternalOutput").ap()
            dbg[name] = d
            dma("pool", d, ap, res, [("dbgout", name)], "dbg")

        ld = {"n": 0}

        def load_xo(row0, nt, s):
            dma("sp", xbuf[s][:, 0:nt, :], xin[row0:row0 + nt * 128, :].rearrange("(t p) d -> p t d", p=128), (), [("xbuf", s)], f"x{s}")

        def load_rope(row0, nt, s):
            dma("sp", ropeb[s][:, 0:nt, :], rope[row0:row0 + nt * 128, :].rearrange("(t p) d -> p t d", p=128), (), [("ropeb", s)], f"r{s}")

        def load_x(row0, nt):
            s = ld["n"] % 2
            ld["n"] += 1
            load_xo(row0, nt, s)
            load_rope(row0, nt, s)
            return s

        supers = [(0, 2, 1)] + [(256 + i * 512, 4, 0) for i in range(16)]
        if stage == 0:
            supers = []
            for c in range(8):
                mod_b_piece(c)
            mod_b_finish()
            dump("modT", modT[:], ["modTa", "modTb"])
            dump("gbc", gbc[:], ["gbc"])
            dump("a1", a1[:], ["a1"])
        if stage == 1:
            import os
            supers = supers[:int(os.environ.get("NSUP", "3"))]
        krb = big[:, 24:26, :].rearrange("p a b -> p (a b)")
        nsup = len(supers)
        hbufs = [(hT, "hT"), (big[:, 0:8, :], "big")]
        a1t = (a1, "a1", "modTa")

        def a_kv(si):
            row0, nt, col = supers[si]
            hA, hAk = hbufs[si % 2]
            for t in range(nt):
                bk = 4 + t // 2
                for c in range(8):
                    mm(ps[:, bk * 512 + (t % 2) * 256: bk * 512 + (t % 2) * 256 + 256], hA[:, c, t * 128:(t + 1) * 128], Wkv[:, c, :],
                       c == 0, c == 7, [(hAk, c), "Wkv"], [("ps", bk)])
                if 1 <= si <= 16:
                    h_ = (si - 1) % 2
                    mod_b_mm((si - 1) // 2, 16 * h_ + 4 * t, 16 * h_ + 4 * t + 4)

        def a_post(si):
            row0, nt, col = supers[si]
            slot = si % 2
            tile0 = row0 // 128
            W = nt * 128
            kvv = ps[:, 4 * 512:4 * 512 + nt * 256].rearrange("p (t n) -> p t n", n=256)
            kview = kvv[:, :, 0:128]
            kvb = [("ps", 4)] + ([("ps", 5)] if nt > 2 else [])
            s0, s1, s2, s3 = scr[0][:, 0:W], scr[1][:, 0:W], scr[2][:, 0:W], scr[3][:, 0:W]
            tv = lambda a: a.rearrange("p (t n) -> p t n", n=128)
            hv = lambda a: a.rearrange("p (h d) -> p h d", d=64)
            qv = lambda a: a.rearrange("p (t h d) -> p t h d", h=2, d=64)
            rp = ropeb[slot]
            rpk = ("ropeb", slot)
            act(tv(s0), kview, AF.Square, kvb, [("scr", 0)])
            head_rstd(s0, 2 * nt, ("scr", 0))
            tt(tv(s1), kview, gk[:, :].unsqueeze(1).broadcast_to([128, nt, 128]), ALU.mult, kvb + ["gk"], [("scr", 1)])
            tt(hv(s2), hv(s1), hr[:, 0:2 * nt].unsqueeze(2).broadcast_to([128, 2 * nt, 64]), ALU.mult, [("scr", 1), "hr"], [("scr", 2)])
            tt(qv(s3), qv(s2), rp[:, 0:nt, 0:64].unsqueeze(2).broadcast_to([128, nt, 2, 64]), ALU.mult, [("scr", 2), rpk], [("scr", 3)])
            tt(qv(s0)[:, :, :, 0:32], qv(s2)[:, :, :, 32:64], rp[:, 0:nt, 64:96].unsqueeze(2).broadcast_to([128, nt, 2, 32]), ALU.mult,
               [("scr", 2), rpk], [("scr", 0)])
            tt(qv(s0)[:, :, :, 32:64], qv(s2)[:, :, :, 0:32], rp[:, 0:nt, 96:128].unsqueeze(2).broadcast_to([128, nt, 2, 32]), ALU.mult,
               [("scr", 2), rpk], [("scr", 0)])
            tt(krb[:, 0:W], s3, s0, ALU.add, [("scr", 3), ("scr", 0)], [("big", 24)])
            for t in range(nt):
                tr(bank(6).bitcast(BF16)[:, t * 128:(t + 1) * 128], krb[:, t * 128:(t + 1) * 128], [("big", 24)], [("ps", 6)])
            act(Vaug[:, tile0:tile0 + nt, 0:64], kvv[:, :, 128:192], AF.Copy, kvb, [("Vaug", tile0 + t) for t in range(nt)])
            act(Vaug[:, tile0:tile0 + nt, 128:192], kvv[:, :, 192:256], AF.Copy, kvb, [("Vaug", tile0 + t) for t in range(nt)])
            cp(KT[:, tile0 * 128:(tile0 + nt) * 128], bank(6).bitcast(BF16)[:, 0:nt * 128], [("ps", 6)], [("KT", si)])

        cnt["phaseA"] = True
        actx = {}
        if nsup:
            for k_ in range(min(2, nsup)):
                load_xo(supers[k_][0], supers[k_][1], k_ % 2)
                load_rope(supers[k_][0], supers[k_][1], k_ % 2)
            ld["n"] = 0
            actx[0] = nm_stats(xbuf[0], ("xbuf", 0), supers[0][1])
            nm_xn_tr(actx[0])
            if nsup > 2:
                load_xo(supers[2][0], supers[2][1], 0)
            nm_evac(actx[0], a1t, 0, supers[0][2], hbufs[0])
        for k_ in range(nsup):
            if k_ % 2 == 0 and k_ // 2 < 8:
                mod_b_dma(k_ // 2)
            n_ = k_ + 1
            if n_ < nsup:
                actx[n_] = nm_stats(xbuf[n_ % 2], ("xbuf", n_ % 2), supers[n_][1])
            a_kv(k_)
            if n_ < nsup:
                nm_xn_tr(actx[n_])
                if k_ + 3 < nsup:
                    load_xo(supers[k_ + 3][0], supers[k_ + 3][1], (k_ + 3) % 2)
            a_post(k_)
            if k_ + 2 < nsup:
                load_rope(supers[k_ + 2][0], supers[k_ + 2][1], k_ % 2)
            if n_ < nsup:
                nm_evac(actx[n_], a1t, 0, supers[n_][2], hbufs[n_ % 2])

        cnt["phaseA"] = False
        ld["n"] = 1
        slot = load_x(256, 4) if (nqb and stage > 1) else None
        if stage >= 1:
            if len(supers) < 17:
                raise NotImplementedError("debug stage with truncated phase A not supported any more")
            mod_b_finish()
        if stage == 1:
            dump("KT", KT[:, 0:1280], [("KT", i) for i in range(3)], BF16)
            dump("Vaug", Vaug[:, 0:10, :], [("Vaug", i) for i in range(10)], BF16)
            dump("hT", hT[:], [("hT", c) for c in range(8)], BF16)
        if stage <= 1:
            nqb = 0
        wctr = {"n": 0}

        def load_unit(u):
            s = wctr["n"] % NB
            wctr["n"] += 1
            dma("sp", ring[s][:, :], wsc[u], [("wsc", u)], [("ring", s)], f"w{s}")
            return s

        def R(s):
            return ("ring", s)

        def unit8(s):
            return ring[s][:, :].rearrange("p (c n) -> p c n", c=8)

        def unit4(s):
            return ring[s][:, :].rearrange("p (c n) -> p c n", c=4)

        yT = [big[:, c, :] for c in range(8)]
        uT = [big[:, 8 + j, :] for j in range(4)]
        gmT = [big[:, 12 + j, :] for j in range(4)]
        attnT = [big[:, 16 + j, :] for j in range(4)]
        QT = [QTb[:, j, :] for j in range(4)]
        vnb = [big[:, 16 + t, :] for t in range(4)]
        PT = [big[:, 28:30, :].rearrange("p a b -> p (a b)"), big[:, 30:32, :].rearrange("p a b -> p (a b)"),
              big[:, 26:28, :].rearrange("p a b -> p (a b)")]
        PTK = [[("big", 28), ("big", 29)], [("big", 30), ("big", 31)], [("big", 26), ("big", 27)]]

        def pre_q_pieces(slot_):
            xb_, xres_, rpb_ = xbuf[slot_], ("xbuf", slot_), ropeb[slot_]
            st = {}
            qrb = scr[4][:, :].bitcast(BF16)[:, 0:512]

            def p_stats():
                st["cx"] = nm_stats(xb_, xres_, 4)

            def p_xn(t):
                cx = st["cx"]
                k = cnt["xn"] % 2
                cnt["xn"] += 1
                xn_tile(xb_, xres_, t, k, rstd[:, cx["po"] + t:cx["po"] + t + 1], ("rstd", cx["par"]), pb=4)

            def p_evac():
                nm_evac(st["cx"], (a1, "a1", "modTa"), 0, 0, None, pb=4)

            def p_qproj():
                su = load_unit(0)
                for t in range(4):
                    for c in range(8):
                        mm(bank(4 + t), hT[:, c, t * 128:(t + 1) * 128], unit8(su)[:, c, :], c == 0, c == 7, [("hT", c), R(su)], [("ps", 4 + t)])

            def p_qdve(t):
                norm_rope(bank(4 + t), ("ps", 4 + t), 8, gq[:, :], "gq", rpb_[:, t, :], ("ropeb", slot_), qrb, ("scr", 4))

            def p_qpe(t):
                for g in range(4):
                    tr(bank(4 + t).bitcast(BF16)[:, g * 128:(g + 1) * 128], qrb[:, g * 128:(g + 1) * 128], [("scr", 4)], [("ps", 4 + t)])
                cp(QTb[:, :, t * 128:(t + 1) * 128], bank(4 + t).bitcast(BF16)[:, 0:512].rearrange("p (g q) -> p g q", q=128),
                   [("ps", 4 + t)], [("QT", g) for g in range(4)])

            return dict(stats=p_stats, xn=p_xn, evac=p_evac, qproj=p_qproj, qdve=p_qdve, qpe=p_qpe)

        def run_pre_q_all(pq):
            pq["stats"]()
            for t in range(4):
                pq["xn"](t)
            pq["evac"]()
            pq["qproj"]()
            for t in range(4):
                pq["qdve"](t)
                pq["qpe"](t)

        if nqb:
            run_pre_q_all(pre_q_pieces(slot))
            pre_units = (load_unit(1), load_unit(2))

        for qb in range(nqb):
            row0 = 256 + qb * 512
            xb = xbuf[slot]
            xres = ("xbuf", slot)
            rpb = ropeb[slot]
            nslot = None
            sv, suu = pre_units
            for t in range(4):
                for c in range(8):
                    mm(bank(t), hT[:, c, t * 128:(t + 1) * 128], unit8(sv)[:, c, :], c == 0, c == 7, [("hT", c), R(sv)], [("ps", t)])
            for j in range(4):
                for c in range(8):
                    mm(bank(4 + j), unit8(suu)[:, c, j * 128:(j + 1) * 128], hT[:, c, :], c == 0, c == 7, [("hT", c), R(suu)], [("ps", 4 + j)])
            for t in range(4):
                act(scr[t][:, :], bank(t), AF.Gelu_apprx_tanh, [("ps", t)], [("scr", t)])
            for j in range(4):
                act(uT[j], bank(4 + j), AF.Gelu_apprx_tanh, [("ps", 4 + j)], [("big", 8 + j)])
            for t in range(4):
                sq_ = scr[4 + t % 2]
                act(sq_[:, :], scr[t][:, :], AF.Square, [("scr", t)], [("scr", 4 + t % 2)])
                T.add("dve", lambda e, t=t, sq_=sq_: e.tensor_reduce(out=hs32[:, 8 * t:8 * t + 8], in_=sq_[:, :].rearrange("p (h d) -> p h d", d=64),
                                                                     axis=AX.X, op=ALU.add), [("scr", 4 + t % 2)], [("hs32", t)])
            act(hl32[:, :], hs32[:, :], AF.Ln, [("hs32", t) for t in range(4)], ["hl32"], scale=1.0 / 64, bias=EPS)
            act(hr32[:, :], hl32[:, :], AF.Exp, ["hl32"], ["hr32"], scale=-0.5)
            for t in range(4):
                tm_ = scr[4 + t % 2]
                tt(tm_[:, :], scr[t][:, :], gmg[:, :], ALU.mult, [("scr", t), "gmg"], [("scr", 4 + t % 2)])
                tt(vnb[t].rearrange("p (h d) -> p h d", d=64), tm_[:, :].rearrange("p (h d) -> p h d", d=64),
                   hr32[:, 8 * t:8 * t + 8].unsqueeze(2).broadcast_to([128, 8, 64]), ALU.mult, [("scr", 4 + t % 2), "hr32"], [("big", 16 + t)])

            def spatial_gm(t):
                bk = 6 + t % 2
                for j in range(4):
                    for gg in range(2):
                        g = 2 * j + gg
                        mm(ps[gg * 64:(gg + 1) * 64, bk * 512 + j * 128: bk * 512 + (j + 1) * 128], vnb[t][:, g * 64:(g + 1) * 64], wsT[:, g, :],
                           True, True, [("big", 16 + t), "wsT"], [("ps", bk)])
                tmp = scr[t % 2]
                tt(tmp[:, :].rearrange("p (j q) -> p j q", q=128), bank(bk).rearrange("p (j q) -> p j q", q=128), bsT[:, :, :], ALU.add,
                   [("ps", bk), "bsT"], [("scr", t % 2)])
                tt(big[:, 12:16, t * 128:(t + 1) * 128], tmp[:, :].rearrange("p (j q) -> p j q", q=128), big[:, 8:12, t * 128:(t + 1) * 128], ALU.mult,
                   [("scr", t % 2)] + [("big", 8 + j) for j in range(4)], [("big", 12 + j) for j in range(4)])

            steps = [(g, kb) for g in range(4) for kb in range(NT)]

            def ksup(kb):
                return 0 if kb < 2 else 1 + (kb - 2) // 4

            def qk(i):
                g, kb = steps[i]
                sbk = (i % 2) * 2
                mm(bank(sbk), KT[0:64, kb * 128:(kb + 1) * 128], QT[g][0:64, :], True, True, [("KT", ksup(kb)), ("QT", g)], [("ps", sbk)])
                mm(bank(sbk + 1), KT[64:128, kb * 128:(kb + 1) * 128], QT[g][64:128, :], True, True, [("KT", ksup(kb)), ("QT", g)], [("ps", sbk + 1)])
                pace = [("pace", i)] if (qb == 0 and i % 12 == 0) else []
                act(PT[i % 3], ps[:, sbk * 512:sbk * 512 + 1024], AF.Exp, [("ps", sbk), ("ps", sbk + 1)], PTK[i % 3] + pace, scale=0.125)
                if pace and N_EARLY + i // 12 < N_UNITS_QB:
                    cast_unit(N_EARLY + i // 12, pace)

            def pv(i):
                g, kb = steps[i]
                oa = 4 + 2 * (g % 2)
                mm(bank(oa), Vaug[:, kb, 0:128], PT[i % 3][:, 0:512], kb == 0, kb == NT - 1, [("Vaug", kb)] + PTK[i % 3], [("ps", oa)])
                mm(bank(oa + 1), Vaug[:, kb, 64:192], PT[i % 3][:, 512:1024], kb == 0, kb == NT - 1, [("Vaug", kb)] + PTK[i % 3], [("ps", oa + 1)])
                if kb == NT - 1:
                    if g == 3:
                        act(rc[64:128, 0:512], bank(oa)[64:128, :], AF.Ln, [("ps", oa)], ["rc", ("rcj", 0), ("rcj", 1)])
                        act(rc[64:128, 0:512], rc[64:128, 0:512], AF.Exp, ["rc"], ["rc"], scale=-1.0)
                        act(rc[0:64, 512:1024], bank(oa + 1)[0:64, :], AF.Ln, [("ps", oa + 1)], ["rc", ("rcj", 0), ("rcj", 1)])
                        act(rc[0:64, 512:1024], rc[0:64, 512:1024], AF.Exp, ["rc"], ["rc"], scale=-1.0)
                    else:
                        recip(rc[64:128, 0:512], bank(oa)[64:128, :], [("ps", oa)], ["rc", ("rcj", 0), ("rcj", 1)])
                        recip(rc[0:64, 512:1024], bank(oa + 1)[0:64, :], [("ps", oa + 1)], ["rc", ("rcj", 0), ("rcj", 1)])
                    tt(attnT[g][0:64, :], bank(oa)[0:64, :], rc[64:128, 0:512], ALU.mult, [("ps", oa), "rc", ("rcj", 0), ("rcj", 1)], [("big", 16 + g)])
                    tt(attnT[g][64:128, :], bank(oa + 1)[64:128, :], rc[0:64, 512:1024], ALU.mult, [("ps", oa + 1), "rc", ("rcj", 0), ("rcj", 1)], [("big", 16 + g)])

            qk(0)
            qk(1)
            for i in range(len(steps)):
                if i + 2 < len(steps):
                    qk(i + 2)
                pv(i)
                if i in (1, 3, 5, 7):
                    spatial_gm((i - 1) // 2)

            if qb + 1 < nqb:
                nslot = load_x(row0 + 512, 4)

            sga = [load_unit(3), None]
            sgb = [load_unit(4), None]
            sbra = load_unit(5)
            sbrg = load_unit(6)
            for m in range(8):
                if m == 4:
                    sga[1] = load_unit(7)
                    sgb[1] = load_unit(8)
                h = m // 4
                b0 = (m % 2) * 4
                for c in range(8):
                    mm(bank(b0), unit8(sga[h])[:, c, (m % 4) * 128:(m % 4 + 1) * 128], hT[:, c, :], c == 0, c == 7, [("hT", c), R(sga[h])], [("ps", b0)])
                for c in range(8):
                    mm(bank(b0 + 1), unit8(sgb[h])[:, c, (m % 4) * 128:(m % 4 + 1) * 128], hT[:, c, :], c == 0, c == 7, [("hT", c), R(sgb[h])], [("ps", b0 + 1)])
                for c in range(4):
                    mm(bank(b0 + 2), unit4(sbra)[:, c, m * 128:(m + 1) * 128], attnT[c], c == 0, c == 3, [("big", 16 + c), R(sbra)], [("ps", b0 + 2)])
                for c in range(4):
                    mm(bank(b0 + 3), unit4(sbrg)[:, c, m * 128:(m + 1) * 128], gmT[c], c == 0, c == 3, [("big", 12 + c), R(sbrg)], [("ps", b0 + 3)])
                act(scr[0][:, :], bank(b0), AF.Sigmoid, [("ps", b0)], [("scr", 0)])
                act(scr[1][:, :], bank(b0 + 1), AF.Sigmoid, [("ps", b0 + 1)], [("scr", 1)])
                tt(scr[2][:, :], bank(b0 + 2), scr[0][:, :], ALU.mult, [("ps", b0 + 2), ("scr", 0)], [("scr", 2)])
                tt(scr[3][:, :], bank(b0 + 3), scr[1][:, :], ALU.mult, [("ps", b0 + 3), ("scr", 1)], [("scr", 3)])
                tt(yT[m], scr[2][:, :], scr[3][:, :], ALU.add, [("scr", 2), ("scr", 3)], [("big", m)], eng="pool")

            so = [load_unit(9), load_unit(10)]
            par = cnt["par"] % 2
            cnt["par"] += 1
            po = par * 4
            k8c = {"n": 0}

            def b8_mm(t):
                for nh in range(2):
                    bk = 4 + k8c["n"] % 4
                    k8c["n"] += 1
                    for c in range(8):
                        mm(bank(bk), yT[c][:, t * 128:(t + 1) * 128], unit8(so[nh])[:, c, :], c == 0, c == 7, [("big", c), R(so[nh])], [("ps", bk)])
                    sk = 4 + (k8c["n"] % 2)
                    tt(scr[sk][:, :], bank(bk), gbc[:, nh * 512:(nh + 1) * 512], ALU.mult, [("ps", bk), "gbc"], [("scr", sk)])
                    tt(xb[:, t, nh * 512:(nh + 1) * 512], scr[sk][:, :], xb[:, t, nh * 512:(nh + 1) * 512], ALU.add, [("scr", sk), xres], [xres, ("x1t", t)], eng="pool")

            def b9_tile(t):
                kq = cnt["sq"] % 2
                cnt["sq"] += 1
                cs = slice(po + t, po + t + 1)
                act(sqj[kq], xb[:, t, :], AF.Square, [("x1t", t)], [("ss", par, t), ("rcj", kq)], accum=ss[:, cs])
                act(rs[:, cs], ss[:, cs], AF.Ln, [("ss", par, t)], [("rs", par), ("rs", par, t)], scale=1.0 / D, bias=EPS)
                act(rstd[:, cs], rs[:, cs], AF.Exp, [("rs", par, t)], [("rstd", par), ("rstd", par, t)], scale=-0.5)
                k = cnt["xn"] % 2
                cnt["xn"] += 1
                xn_tile(xb, ("x1t", t), t, k, rstd[:, cs], ("rstd", par, t))

            for t in range(4):
                b8_mm(t)
                if t >= 1:
                    b9_tile(t - 1)
            b9_tile(3)
            nm_evac(dict(nt=4), (a2, "a2", "modTb"), 24, 0)

            pq = pre_q_pieces(nslot) if nslot is not None else None
            if pq:
                pq["stats"]()
            for j in range(32):
                if j % 4 == 0:
                    sf = load_unit(11 + j // 4)
                bk = j % 8
                for c in range(8):
                    mm(bank(bk), unit8(sf)[:, c, (j % 4) * 128:(j % 4 + 1) * 128], hT[:, c, :], c == 0, c == 7, [("hT", c), R(sf)], [("ps", bk)])
                sk = j % 4
                act(scr[sk][:, :], bank(bk), AF.Relu, [("ps", bk)], [("scr", sk)])
                tt(big[:, j, :], scr[sk][:, :], scr[sk][:, :], ALU.mult, [("scr", sk)], [("big", j)])

            if pq:
                pq["xn"](0)
                pq["xn"](1)
            blk = 0
            for nh in range(2):
                for kg in range(4):
                    s2 = load_unit(19 + nh * 4 + kg)
                    for t in range(4):
                        bk = t
                        for c in range(8):
                            mm(bank(bk), big[:, kg * 8 + c, t * 128:(t + 1) * 128], unit8(s2)[:, c, :], kg == 0 and c == 0, kg == 3 and c == 7,
                               [("big", kg * 8 + c), R(s2)], [("ps", bk)])
                    if pq:
                        if blk == 0:
                            pq["xn"](2)
                            pq["xn"](3)
                            pq["evac"]()
                        elif blk == 1:
                            pq["qproj"]()
                        elif blk == 2:
                            pq["qdve"](0)
                        elif blk in (3, 4, 5):
                            pq["qpe"](blk - 3)
                            pq["qdve"](blk - 2)
                        elif blk == 6:
                            pq["qpe"](3)
                    blk += 1
                for t in range(4):
                    bk = t
                    sk = 5
                    tt(scr[sk][:, :], bank(bk), gbc[:, 1024 + nh * 512:1024 + (nh + 1) * 512], ALU.mult, [("ps", bk), "gbc"], [("scr", sk)])
                    tt(xb[:, t, nh * 512:(nh + 1) * 512], scr[sk][:, :], xb[:, t, nh * 512:(nh + 1) * 512], ALU.add, [("scr", sk), xres], [xres], eng="pool")
            if qb + 1 < nqb:
                pre_units = (load_unit(1), load_unit(2))
            dma("sp", out_d[qb * 512:(qb + 1) * 512, :].rearrange("(t p) d -> p t d", p=128), xb[:, :, :], [xres], [("out", qb)], f"o{slot}")
            slot = nslot

        T.add("sp", lambda e: e.nop(), [("out", q) for q in range(nqb)] + [("dbgout", n) for n in dbg], [])

        T.finalize()
        nc._tracker = T

        @block.sync
        def _(e):
            T.emit("sp", e, esems, dsems)

        @block.tensor
        def _(e):
            T.emit("pe", e, esems, dsems)

        @block.scalar
        def _(e):
            T.emit("act", e, esems, dsems)

        @block.vector
        def _(e):
            T.emit("dve", e, esems, dsems)

        @block.gpsimd
        def _(e):
            T.emit("pool", e, esems, dsems)

    return nc


_CACHE = {}


def _rope_table(tok_idx):
    n = tok_idx.shape[0]
    t = np.maximum(tok_idx, 0)
    row = (t // 64).astype(np.float32)
    colp = (t % 64).astype(np.float32)
    inv = (np.float32(10000.0) ** (-np.arange(0, 32, 2, dtype=np.float32) / np.float32(32))).astype(np.float32)
    ang = np.concatenate([row[:, None] * inv[None, :], colp[:, None] * inv[None, :]], axis=-1).astype(np.float32)
    cos = np.cos(ang).astype(np.float32)
    sin = np.sin(ang).astype(np.float32)
    ident = tok_idx < 0
    cos[ident] = 1.0
    sin[ident] = 0.0
    return np.concatenate([cos, cos, -sin, sin], axis=1).astype(np.float32)


def kernel(x, c, ctx, c_ctx, w_mod, b_mod, norm1_g, norm2_g, w_in, q_norm_g, k_norm_g,
           gm_norm_g, gm_ws, gm_bs, w_br_attn, w_br_gm, w_out, w_ff1, w_ff2):
    f = lambda a: np.ascontiguousarray(np.asarray(a, dtype=np.float32))
    x, c, ctx, c_ctx = f(x), f(c), f(ctx), f(c_ctx)
    w_mod, b_mod, w_in = f(w_mod)[0], f(b_mod)[0], f(w_in)[0]
    n1, n2 = f(norm1_g)[0], f(norm2_g)[0]
    qg, kg, gmg = f(q_norm_g)[0], f(k_norm_g)[0], f(gm_norm_g)[0]
    ws, bs = f(gm_ws)[0], f(gm_bs)[0]
    bra, brg, wo, w1, w2 = f(w_br_attn)[0], f(w_br_gm)[0], f(w_out)[0], f(w_ff1)[0], f(w_ff2)[0]

    if "nc" not in _CACHE:
        _CACHE["nc"] = build_program()
    nc = _CACHE["nc"]

    qcols = 256 + np.array([kv * 256 + g * 64 + d for g in range(4) for kv in range(2) for d in range(64)])
    order = np.concatenate([np.arange(0, 256), qcols, np.arange(1280, 1792), np.arange(768, 1280), np.arange(1792, 3840)])
    w_in_p = np.ascontiguousarray(w_in[:, order])
    rows = np.array([kv * 256 + g * 64 + d for g in range(4) for kv in range(2) for d in range(64)])
    w_bra_p = np.ascontiguousarray(bra[rows, :])
    b_modT = np.ascontiguousarray(b_mod.reshape(48, 128).T)
    b_modg = np.ascontiguousarray(np.concatenate([b_mod[2048:3072], b_mod[5120:6144]])[None, :])
    n1g = np.ascontiguousarray(n1.reshape(8, 128).T)
    n2g = np.ascontiguousarray(n2.reshape(8, 128).T)
    gq_bc = np.ascontiguousarray(np.broadcast_to(np.tile(qg, 8)[None, :], (128, 512)))
    gk_bc = np.ascontiguousarray(np.broadcast_to(np.tile(kg, 2)[None, :], (128, 128)))
    gmg_bc = np.ascontiguousarray(np.broadcast_to(gmg.reshape(512)[None, :], (128, 512)))
    wsT = np.ascontiguousarray(ws.transpose(2, 0, 1))
    bsT = np.ascontiguousarray(np.broadcast_to(bs.reshape(4, 2, 1, 128), (4, 2, 64, 128)).transpose(1, 2, 0, 3).reshape(128, 4, 128))
    ident = np.eye(128, dtype=np.float32)

    in_maps = []
    for core in range(8):
        b, hf = core // 2, core % 2
        own = np.arange(hf * 4096, (hf + 1) * 4096)
        oth = np.arange((1 - hf) * 4096, (2 - hf) * 4096)
        xin = np.concatenate([ctx[b], x[b, own], x[b, oth]], axis=0)
        tok = np.concatenate([-np.ones(256, dtype=np.int64), own, oth])
        cT = np.ascontiguousarray(np.stack([c[b], c_ctx], axis=1).reshape(8, 128, 2).transpose(1, 0, 2))
        in_maps.append({
            "xin": np.ascontiguousarray(xin), "rope": _rope_table(tok), "cT": cT, "w_mod": w_mod, "b_modT": b_modT,
            "b_modg": b_modg, "n1g": n1g, "n2g": n2g, "w_in_p": w_in_p, "gq_bc": gq_bc, "gk_bc": gk_bc, "gmg_bc": gmg_bc,
            "wsT": wsT, "bsT": bsT, "w_bra_p": w_bra_p, "w_brg": brg, "w_out": wo, "w_ff1": w1, "w_ff2": w2, "ident": ident,
        })
    res = run_bass_kernel_spmd(nc, in_maps, core_ids=list(range(8)))
    out = np.empty((4, 8192, D), dtype=np.float32)
    for core in range(8):
        b, hf = core // 2, core % 2
        out[b, hf * 4096:(hf + 1) * 4096] = res.results[core]["out"]
    return out
```

```python
import numpy as np
import concourse.bass as bass
import concourse.mybir as mybir
from concourse.bass_utils import run_bass_kernel_spmd

F32 = mybir.dt.float32
BF16 = mybir.dt.bfloat16
AF = mybir.ActivationFunctionType
ALU = mybir.AluOpType
AX = mybir.AxisListType

D = 1024
NCTX_T = 2
NOWN_T = 32
NT = 66
NQB = 8
EPS = 1e-6
NB = 5
N_UNITS_QB = 27


class Tracker:
    def __init__(self):
        self.ops = []
        self.last_w = {}
        self.readers = {}
        self.dcount = {}

    def add(self, eng, fn, reads=(), writes=(), dsem=None):
        idx = len(self.ops)
        deps = set()
        if eng in ("act", "dve"):
            writes = list(writes) + [("pslk", r[1]) for r in reads if isinstance(r, tuple) and r[0] == "ps"]
        for r in reads:
            if r in self.last_w:
                deps.add(self.last_w[r])
        for w in writes:
            if w in self.last_w:
                deps.add(self.last_w[w])
            for rd in self.readers.get(w, ()):
                deps.add(rd)
        op = dict(eng=eng, fn=fn, deps=deps, dsem=dsem, marked=False, idx=idx, val=None, desc=(tuple(reads), tuple(writes)))
        if dsem is not None:
            self.dcount[dsem] = self.dcount.get(dsem, 0) + 16
            op["dval"] = self.dcount[dsem]
        self.ops.append(op)
        for r in reads:
            self.readers.setdefault(r, []).append(idx)
        for w in writes:
            self.last_w[w] = idx
            self.readers[w] = []
        return idx

    def finalize(self):
        ops = self.ops
        for op in ops:
            red = {}
            for d in op["deps"]:
                dop = ops[d]
                if dop["dsem"] is not None:
                    key = ("d", dop["dsem"])
                    if key not in red or ops[red[key]]["dval"] < dop["dval"]:
                        red[key] = d
                else:
                    if dop["eng"] == "pe" and op["eng"] == "pe" and op["dsem"] is None:
                        continue
                    key = ("e", dop["eng"])
                    if key not in red or red[key] < d:
                        red[key] = d
            op["rdeps"] = list(red.values())
            for d in op["rdeps"]:
                if ops[d]["dsem"] is None:
                    ops[d]["marked"] = True
        cnt = {}
        for op in ops:
            if op["dsem"] is None and op["marked"]:
                cnt[op["eng"]] = cnt.get(op["eng"], 0) + 1
                op["val"] = cnt[op["eng"]]

    def trace(self, engname):
        waited = {}
        out = []
        for op in self.ops:
            if op["eng"] != engname:
                continue
            ws = []
            for d in op["rdeps"]:
                dop = self.ops[d]
                if dop["dsem"] is not None:
                    val = self.dcount[dop["dsem"]] if dop["dsem"] in ("const", "cast", "gb") else dop["dval"]
                    key = ("d", dop["dsem"])
                else:
                    val, key = dop["val"], ("e", dop["eng"])
                if waited.get(key, 0) < val:
                    ws.append((key[1], val))
                    waited[key] = val
            inc = (op["dsem"], op.get("dval")) if op["dsem"] else ((engname, op["val"]) if op["marked"] else None)
            out.append((op["idx"], ws, op["desc"], inc))
        return out

    def emit(self, engname, engobj, esems, dsems):
        waited = {}
        for op in self.ops:
            if op["eng"] != engname:
                continue
            for d in op["rdeps"]:
                dop = self.ops[d]
                if dop["dsem"] is not None:
                    val = self.dcount[dop["dsem"]] if dop["dsem"] in ("const", "cast", "gb") else dop["dval"]
                    sem, key = dsems[dop["dsem"]], ("d", dop["dsem"])
                else:
                    sem, val, key = esems[dop["eng"]], dop["val"], ("e", dop["eng"])
                if waited.get(key, 0) < val:
                    engobj.wait_ge(sem, val)
                    waited[key] = val
            ins = op["fn"](engobj)
            if op["dsem"] is not None:
                ins.then_inc(dsems[op["dsem"]], 16)
            elif op["marked"]:
                ins.then_inc(esems[engname], 1)


def build_program(stage=3, nqb=NQB, skip=()):
    nc = bass.Bass("TRN2", target_bir_lowering=False)
    T = Tracker()
    dbg = {}

    def din(name, shape, dt=F32):
        return nc.dram_tensor(name, list(shape), dt, kind="ExternalInput").ap()

    xin = din("xin", [NT * 128, D])
    rope = din("rope", [NT * 128, 128])
    cT_d = din("cT", [128, 8, 2])
    wmod_d = din("w_mod", [D, 6 * D])
    bmodT_d = din("b_modT", [128, 48])
    bmodg_d = din("b_modg", [1, 2048])
    n1g_d = din("n1g", [128, 8])
    n2g_d = din("n2g", [128, 8])
    win_d = din("w_in_p", [D, 3840])
    gq_d = din("gq_bc", [128, 512])
    gk_d = din("gk_bc", [128, 128])
    gmg_d = din("gmg_bc", [128, 512])
    wsT_d = din("wsT", [128, 8, 128])
    bsT_d = din("bsT", [128, 4, 128])
    bra_d = din("w_bra_p", [512, D])
    brg_d = din("w_brg", [512, D])
    wout_d = din("w_out", [D, D])
    ff1_d = din("w_ff1", [D, 4 * D])
    ff2_d = din("w_ff2", [4 * D, D])
    ident_d = din("ident", [128, 128])
    out_d = nc.dram_tensor("out", [NOWN_T * 128, D], F32, kind="ExternalOutput").ap()
    gscr = nc.dram_tensor("gscr", [1, 2048], F32, kind="Internal").ap()
    wsc = nc.dram_tensor("wscratch", [N_UNITS_QB, 128, 4096], BF16, kind="Internal").ap()

    import contextlib
    es = contextlib.ExitStack()

    def sb(name, shape, dt):
        return es.enter_context(nc.sbuf_tensor(name, list(shape), dt))

    with es:
        KT = sb("KT", [128, NT * 128], BF16)
        Vaug = sb("Vaug", [128, NT, 192], BF16)
        xbuf = [sb(f"xbuf{i}", [128, 4, D], F32) for i in range(2)]
        ropeb = [sb(f"ropeb{i}", [128, 4, 128], F32) for i in range(2)]
        xn = [sb(f"xn{i}", [128, D], BF16) for i in range(2)]
        hT = sb("hT", [128, 8, 512], BF16)
        big = sb("big", [128, 32, 512], BF16)
        ringT = sb("ring", [128, NB * 4096], BF16)
        ring = [ringT[:, i * 4096:(i + 1) * 4096] for i in range(NB)]
        gbc = sb("gbc", [128, 2048], F32)
        scr = [sb(f"scr{i}", [128, 512], F32) for i in range(6)]
        rc = sb("rc", [128, 1024], F32)
        QTb = sb("QTb", [128, 4, 512], BF16)
        Wkv = sb("Wkv", [128, 8, 256], BF16)
        identb = sb("identb", [128, 128], BF16)
        gq = sb("gq", [128, 512], F32)
        gk = sb("gk", [128, 128], F32)
        gmg = sb("gmg", [128, 512], F32)
        wsT = sb("wsTb", [128, 8, 128], BF16)
        bsT = sb("bsTs", [128, 4, 128], F32)
        cT = sb("cTs", [128, 8, 2], F32)
        scT = sb("scT", [128, 8, 2], F32)
        bmodT = sb("bmodTs", [128, 48], F32)
        n1g = sb("n1gs", [128, 8], F32)
        n2g = sb("n2gs", [128, 8], F32)
        modT = sb("modT", [128, 48, 2], F32)
        a1 = sb("a1", [128, 8, 2], F32)
        a2 = sb("a2", [128, 8, 2], F32)
        ones1 = sb("ones1", [1, 128], F32)
        ss = sb("ss", [128, 8], F32)
        rs = sb("rs", [128, 8], F32)
        rstd = sb("rstd", [128, 8], F32)
        hs = sb("hs", [128, 8], F32)
        hl = sb("hl", [128, 8], F32)
        hr = sb("hr", [128, 8], F32)
        hs32 = sb("hs32", [128, 32], F32)
        hl32 = sb("hl32", [128, 32], F32)
        hr32 = sb("hr32", [128, 32], F32)
        ps = es.enter_context(nc.psum_tensor("ps", [128, 4096], F32))
        sqj = [rc[:, 0:512].bitcast(BF16), rc[:, 512:1024].bitcast(BF16)]
        grow = big[0:1, 0:8, :].rearrange("p a b -> p (a b)").bitcast(F32)
        bmodg = big[0:1, 8:16, :].rearrange("p a b -> p (a b)").bitcast(F32)

        esems = {e: es.enter_context(nc.semaphore("sem_" + e)) for e in ["pe", "act", "dve", "pool", "sp"]}
        dnames = [f"c{i}" for i in range(6)] + ["dbg", "gb", "gb2", "const", "cast", "x0", "x1", "r0", "r1", "o0", "o1", "wm0", "wm1", "wm2"] + [f"w{i}" for i in range(NB)]
        dsems = {d: es.enter_context(nc.semaphore("ds_" + d)) for d in dnames}
        block = es.enter_context(nc.Block())

        def bank(b):
            return ps[:, b * 512:(b + 1) * 512]

        def mm(out, lhsT, rhs, start, stop, reads, writes, sgc=False):
            T.add("pe", lambda e: e.matmul(out, lhsT=lhsT, rhs=rhs, start=start, stop=stop, skip_group_check=sgc), reads, writes)

        def tr(out, in_, reads, writes):
            T.add("pe", lambda e: e.transpose(out, in_, identb[:, :]), list(reads) + ["identb"], writes)

        def act(out, in_, func, reads, writes, scale=None, bias=None, accum=None):
            kw = {}
            if scale is not None:
                kw["scale"] = scale
            if bias is not None:
                kw["bias"] = bias
            if accum is not None:
                kw["accum_out"] = accum
            T.add("act", lambda e: e.activation(out=out, in_=in_, func=func, **kw), reads, writes)

        def tt(out, in0, in1, op, reads, writes, eng="dve"):
            T.add(eng, lambda e: e.tensor_tensor(out=out, in0=in0, in1=in1, op=op), reads, writes)

        def ts(out, in0, s1, s2, op0, op1, reads, writes, eng="dve"):
            if op1 is None:
                T.add(eng, lambda e: e.tensor_scalar(out=out, in0=in0, scalar1=s1, scalar2=None, op0=op0), reads, writes)
            else:
                T.add(eng, lambda e: e.tensor_scalar(out=out, in0=in0, scalar1=s1, scalar2=s2, op0=op0, op1=op1), reads, writes)

        def stt(out, in0, scalar, in1, op0, op1, reads, writes):
            T.add("dve", lambda e: e.scalar_tensor_tensor(out=out, in0=in0, scalar=scalar, in1=in1, op0=op0, op1=op1), reads, writes)

        def recip(out, in_, reads, writes):
            T.add("dve", lambda e: e.reciprocal(out=out, in_=in_), reads, writes)

        def cp(out, in_, reads, writes, eng="dve"):
            T.add(eng, lambda e: e.tensor_copy(out=out, in_=in_), reads, writes)

        def dma(q, out, in_, reads, writes, dsem):
            T.add(q, lambda e: e.dma_start(out=out, in_=in_), reads, writes, dsem=dsem)

        def memset(ap, val, writes, eng="pool"):
            T.add(eng, lambda e: e.memset(ap, val), (), writes)

        for (dst, src, nm) in [(cT[:], cT_d, "cT"), (bmodT[:], bmodT_d, "bmodT"), (bmodg[:], bmodg_d, "bmodg"),
                               (n1g[:], n1g_d, "n1g"), (n2g[:], n2g_d, "n2g"), (gq[:], gq_d, "gq"), (gk[:], gk_d, "gk"),
                               (gmg[:], gmg_d, "gmg"), (bsT[:], bsT_d, "bsT")]:
            dma("sp", dst, src, (), [nm], "const")
        dma("pool", identb[:], ident_d, (), ["identb"], "cast")
        dma("pool", Wkv[:], win_d[:, 0:256].rearrange("(c p) n -> p c n", p=128), (), ["Wkv"], "cast")
        dma("pool", wsT[:], wsT_d, (), ["wsT"], "cast")
        memset(Vaug[:, :, 64:128], 1.0, [("Vaug", t) for t in range(NT)])
        memset(ones1[:], 1.0, ["ones1"])

        def wsrc_kn(w, c0, ncols):
            return w[:, c0:c0 + ncols].rearrange("(c p) n -> p c n", p=128)

        unit_src = []
        unit_src.append((wsrc_kn(win_d, 256, 512), 8))
        unit_src.append((wsrc_kn(win_d, 768, 512), 8))
        unit_src.append((wsrc_kn(win_d, 1280, 512), 8))
        unit_src.append((wsrc_kn(win_d, 1792, 512), 8))
        unit_src.append((wsrc_kn(win_d, 2816, 512), 8))
        unit_src.append((bra_d.rearrange("(c p) n -> p c n", p=128), 4))
        unit_src.append((brg_d.rearrange("(c p) n -> p c n", p=128), 4))
        unit_src.append((wsrc_kn(win_d, 2304, 512), 8))
        unit_src.append((wsrc_kn(win_d, 3328, 512), 8))
        unit_src.append((wsrc_kn(wout_d, 0, 512), 8))
        unit_src.append((wsrc_kn(wout_d, 512, 512), 8))
        for j in range(8):
            unit_src.append((wsrc_kn(ff1_d, j * 512, 512), 8))
        for nh in range(2):
            for kg in range(4):
                src = ff2_d[kg * 1024:(kg + 1) * 1024, nh * 512:(nh + 1) * 512].rearrange("(c p) n -> p c n", p=128)
                unit_src.append((src, 8))
        assert len(unit_src) == N_UNITS_QB
        def cast_unit(u, extra_reads=()):
            src, nch = unit_src[u]
            dst = wsc[u].rearrange("p (c n) -> p c n", c=nch)
            dma("pool", dst, src, [("cslot", u % 6)] + list(extra_reads), [("wsc", u), ("cslot", u % 6)], f"c{u % 6}")

        N_EARLY = 11 if nqb else N_UNITS_QB
        for u in range(N_EARLY):
            if "cast" in skip:
                break
            cast_unit(u)

        act(scT[:], cT[:], AF.Silu, ["cT"], ["scT"])
        for c in range(8):
            s_ = c % NB
            pa = ring[s_].bitcast(F32)
            dma("sp" if c % 2 == 0 else "act", pa, wmod_d[c * 128:(c + 1) * 128, 0:2048], (), [("ring", s_)], f"w{s_}")
            for jj in range(16):
                mm(ps[:, 2 * jj:2 * jj + 2], pa[:, jj * 128:(jj + 1) * 128], scT[:, c, :], c == 0 and jj == 0, c == 7 and jj == 15,
                   [("ring", s_), "scT"], [("ps", 0)], sgc=True)
        tt(modT[:, 0:16, :], ps[:, 0:32].rearrange("p (j k) -> p j k", k=2),
           bmodT[:, 0:16].unsqueeze(2).broadcast_to([128, 16, 2]), ALU.add, [("ps", 0), "bmodT"], ["modTa"])
        stt(a1[:], modT[:, 8:16, :], 1.0, n1g[:, :].unsqueeze(2).broadcast_to([128, 8, 2]), ALU.add, ALU.mult, ["modTa", "n1g"], ["a1"])

        def mod_b_buf(c):
            p_ = c % 2
            return p_, ringT[:, (2 * p_) * 4096:(2 * p_ + 2) * 4096].bitcast(F32)

        def mod_b_dma(c):
            p_, pb = mod_b_buf(c)
            dma("sp", pb, wmod_d[c * 128:(c + 1) * 128, 2048:6144], (), [("ring", 2 * p_), ("ring", 2 * p_ + 1)], f"wm{p_}")

        def mod_b_mm(c, j0, j1):
            p_, pb = mod_b_buf(c)
            for jj in range(j0, j1):
                mm(ps[:, 7 * 512 + 2 * jj:7 * 512 + 2 * jj + 2], pb[:, jj * 128:(jj + 1) * 128], scT[:, c, :], c == 0 and jj == 0, c == 7 and jj == 31,
                   [("ring", 2 * p_), ("ring", 2 * p_ + 1), "scT"], [("ps", 7)], sgc=True)

        def mod_b_piece(c):
            mod_b_dma(c)
            mod_b_mm(c, 0, 32)

        def mod_b_finish():
            tt(modT[:, 16:48, :], ps[:, 7 * 512:7 * 512 + 64].rearrange("p (j k) -> p j k", k=2),
               bmodT[:, 16:48].unsqueeze(2).broadcast_to([128, 32, 2]), ALU.add, [("ps", 7), "bmodT"], ["modTb"])
            stt(a2[:], modT[:, 32:40, :], 1.0, n2g[:, :].unsqueeze(2).broadcast_to([128, 8, 2]), ALU.add, ALU.mult, ["modTb", "n2g"], ["a2"])
            for k_, j0 in enumerate((16, 40)):
                T.add("pool", lambda e, k_=k_, j0=j0: e.dma_start(out=gscr[0, k_ * 1024:(k_ + 1) * 1024].rearrange("(c p) -> p c", p=128),
                                                             in_=modT[:, j0:j0 + 8, 0], allow_slow_non_contiguous=True),
                      ["modTb"], [("gscr", k_)], dsem="gb")
            dma("pool", gbc[:], gscr[0, :].partition_broadcast(128), [("gscr", 0), ("gscr", 1)], ["gbc"], "gb2")

        cnt = {"xn": 0, "ev": 0, "sq": 0, "par": 0}

        def nm_stats(xb, xres, nt):
            par = cnt["par"] % 2
            cnt["par"] += 1
            po = par * 4
            for t in range(nt):
                kq = cnt["sq"] % 2
                cnt["sq"] += 1
                act(sqj[kq], xb[:, t, :], AF.Square, [xres], [("ss", par, t), ("rcj", kq)], accum=ss[:, po + t:po + t + 1])
            act(rs[:, po:po + nt], ss[:, po:po + nt], AF.Ln, [("ss", par, t) for t in range(nt)], [("rs", par)], scale=1.0 / D, bias=EPS)
            act(rstd[:, po:po + nt], rs[:, po:po + nt], AF.Exp, [("rs", par)], [("rstd", par)], scale=-0.5)
            return dict(xb=xb, xres=xres, nt=nt, par=par, po=po, use_act=not cnt.get("phaseA", False))

        def xn_scale(xb, xres, t, k, rs_ap, rs_key, use_act=True):
            if t % 2 == 0 or not use_act:
                ts(xn[k][:], xb[:, t, :], rs_ap, None, ALU.mult, None, [xres, rs_key], [("xn", k)])
            else:
                act(xn[k][:], xb[:, t, :], AF.Copy, [xres, rs_key], [("xn", k)], scale=rs_ap)

        def xn_tr(t, k, pb=0):
            for c in range(8):
                bk = pb + c // 2
                o = bank(bk).bitcast(BF16)[:, (c % 2) * 512 + t * 128:(c % 2) * 512 + (t + 1) * 128]
                tr(o, xn[k][:, c * 128:(c + 1) * 128], [("xn", k)], [("ps", bk)])

        def xn_tile(xb, xres, t, k, rs_ap, rs_key, pb=0, use_act=True):
            xn_scale(xb, xres, t, k, rs_ap, rs_key, use_act)
            xn_tr(t, k, pb)

        def _unused_xn_tile(xb, xres, t, k, rs_ap, rs_key, pb=0, use_act=True):
            for c in range(8):
                bk = pb + c // 2
                o = bank(bk).bitcast(BF16)[:, (c % 2) * 512 + t * 128:(c % 2) * 512 + (t + 1) * 128]
                tr(o, xn[k][:, c * 128:(c + 1) * 128], [("xn", k)], [("ps", bk)])

        def nm_xn_tr(cx):
            xb, xres, nt, par, po = cx["xb"], cx["xres"], cx["nt"], cx["par"], cx["po"]
            for t in range(nt):
                k = cnt["xn"] % 2
                cnt["xn"] += 1
                xn_tile(xb, xres, t, k, rstd[:, po + t:po + t + 1], ("rstd", par), use_act=cx.get("use_act", True))

        def nm_evac(cx, a_t, sh_off, col, hd=None, pb=0):
            nt = cx["nt"]
            hdst, hkey = hd if hd is not None else (hT, "hT")
            for c in range(8):
                bk = pb + c // 2
                src = bank(bk).bitcast(BF16)[:, (c % 2) * 512:(c % 2) * 512 + nt * 128]
                if c % 2 == 0:
                    act(hdst[:, c, 0:nt * 128], src, AF.Identity, [("ps", bk), a_t[1], a_t[2]], [(hkey, c)],
                        scale=a_t[0][:, c, col:col + 1], bias=modT[:, sh_off + c, col:col + 1])
                else:
                    ts(hdst[:, c, 0:nt * 128], src, a_t[0][:, c, col:col + 1], modT[:, sh_off + c, col:col + 1], ALU.mult, ALU.add,
                       [("ps", bk), a_t[1], a_t[2]], [(hkey, c)])

        def norm_mod_T(xb, xres, nt, a_t, sh_off, col, hd=None):
            cx = nm_stats(xb, xres, nt)
            nm_xn_tr(cx)
            nm_evac(cx, a_t, sh_off, col, hd)

        def head_rstd(src_sq, nh, res_in):
            T.add("dve", lambda e: e.tensor_reduce(out=hs[:, 0:nh], in_=src_sq.rearrange("p (h d) -> p h d", d=64), axis=AX.X, op=ALU.add),
                  [res_in], ["hs"])
            act(hl[:, 0:nh], hs[:, 0:nh], AF.Ln, ["hs"], ["hl"], scale=1.0 / 64, bias=EPS)
            act(hr[:, 0:nh], hl[:, 0:nh], AF.Exp, ["hl"], ["hr"], scale=-0.5)

        def norm_rope(psrc, psres, nh, gain, gres, rp, rpres, outb, outres):
            W = nh * 64
            s0, s1, s2, s3 = scr[0][:, 0:W], scr[1][:, 0:W], scr[2][:, 0:W], scr[3][:, 0:W]
            act(s0, psrc, AF.Square, [psres], [("scr", 0)])
            head_rstd(s0, nh, ("scr", 0))
            tt(s1, psrc, gain, ALU.mult, [psres, gres], [("scr", 1)])
            tt(s2.rearrange("p (h d) -> p h d", d=64), s1.rearrange("p (h d) -> p h d", d=64),
               hr[:, 0:nh].unsqueeze(2).broadcast_to([128, nh, 64]), ALU.mult, [("scr", 1), "hr"], [("scr", 2)])
            v2 = s2.rearrange("p (h d) -> p h d", d=64)
            tt(s3.rearrange("p (h d) -> p h d", d=64), v2, rp[:, 0:64].unsqueeze(1).broadcast_to([128, nh, 64]), ALU.mult,
               [("scr", 2), rpres], [("scr", 3)])
            v0 = s0.rearrange("p (h d) -> p h d", d=64)
            tt(v0[:, :, 0:32], v2[:, :, 32:64], rp[:, 64:96].unsqueeze(1).broadcast_to([128, nh, 32]), ALU.mult,
               [("scr", 2), rpres], [("scr", 0)])
            tt(v0[:, :, 32:64], v2[:, :, 0:32], rp[:, 96:128].unsqueeze(1).broadcast_to([128, nh, 32]), ALU.mult,
               [("scr", 2), rpres], [("scr", 0)])
            tt(outb, s3, s0, ALU.add, [("scr", 3), ("scr", 0)], [outres])

        def dump(name, ap, res, dt=F32):
            if "nodump" in skip:
                return
            d = nc.dram_tensor("dbg_" + name, list(ap.shape), F32, kind="ExternalOutput").ap()
            dbg[name] = d
            dma("pool", d, ap, res, [("dbgout", name)], "dbg")

        ld = {"n": 0}

        def load_xo(row0, nt, s):
            dma("sp", xbuf[s][:, 0:nt, :], xin[row0:row0 + nt * 128, :].rearrange("(t p) d -> p t d", p=128), (), [("xbuf", s)], f"x{s}")

        def load_rope(row0, nt, s):
            dma("sp", ropeb[s][:, 0:nt, :], rope[row0:row0 + nt * 128, :].rearrange("(t p) d -> p t d", p=128), (), [("ropeb", s)], f"r{s}")

        def load_x(row0, nt):
            s = ld["n"] % 2
            ld["n"] += 1
            load_xo(row0, nt, s)
            load_rope(row0, nt, s)
            return s

        supers = [(0, 2, 1)] + [(256 + i * 512, 4, 0) for i in range(16)]
        if stage == 0:
            supers = []
            for c in range(8):
                mod_b_piece(c)
            mod_b_finish()
            dump("modT", modT[:], ["modTa", "modTb"])
            dump("gbc", gbc[:], ["gbc"])
            dump("a1", a1[:], ["a1"])
        if stage == 1:
            import os
            supers = supers[:int(os.environ.get("NSUP", "3"))]
        krb = big[:, 24:26, :].rearrange("p a b -> p (a b)")
        nsup = len(supers)
        hbufs = [(hT, "hT"), (big[:, 0:8, :], "big")]
        a1t = (a1, "a1", "modTa")

        def a_kv(si):
            row0, nt, col = supers[si]
            hA, hAk = hbufs[si % 2]
            for t in range(nt):
                bk = 4 + t // 2
                for c in range(8):
                    mm(ps[:, bk * 512 + (t % 2) * 256: bk * 512 + (t % 2) * 256 + 256], hA[:, c, t * 128:(t + 1) * 128], Wkv[:, c, :],
                       c == 0, c == 7, [(hAk, c), "Wkv"], [("ps", bk)])
                if 1 <= si <= 16:
                    h_ = (si - 1) % 2
                    mod_b_mm((si - 1) // 2, 16 * h_ + 4 * t, 16 * h_ + 4 * t + 4)

        def a_post(si):
            row0, nt, col = supers[si]
            slot = si % 2
            tile0 = row0 // 128
            W = nt * 128
            kvv = ps[:, 4 * 512:4 * 512 + nt * 256].rearrange("p (t n) -> p t n", n=256)
            kview = kvv[:, :, 0:128]
            kvb = [("ps", 4)] + ([("ps", 5)] if nt > 2 else [])
            s0, s1, s2, s3 = scr[0][:, 0:W], scr[1][:, 0:W], scr[2][:, 0:W], scr[3][:, 0:W]
            tv = lambda a: a.rearrange("p (t n) -> p t n", n=128)
            hv = lambda a: a.rearrange("p (h d) -> p h d", d=64)
            qv = lambda a: a.rearrange("p (t h d) -> p t h d", h=2, d=64)
            rp = ropeb[slot]
            rpk = ("ropeb", slot)
            act(tv(s0), kview, AF.Square, kvb, [("scr", 0)])
            head_rstd(s0, 2 * nt, ("scr", 0))
            tt(tv(s1), kview, gk[:, :].unsqueeze(1).broadcast_to([128, nt, 128]), ALU.mult, kvb + ["gk"], [("scr", 1)])
            tt(hv(s2), hv(s1), hr[:, 0:2 * nt].unsqueeze(2).broadcast_to([128, 2 * nt, 64]), ALU.mult, [("scr", 1), "hr"], [("scr", 2)])
            tt(qv(s3), qv(s2), rp[:, 0:nt, 0:64].unsqueeze(2).broadcast_to([128, nt, 2, 64]), ALU.mult, [("scr", 2), rpk], [("scr", 3)])
            tt(qv(s0)[:, :, :, 0:32], qv(s2)[:, :, :, 32:64], rp[:, 0:nt, 64:96].unsqueeze(2).broadcast_to([128, nt, 2, 32]), ALU.mult,
               [("scr", 2), rpk], [("scr", 0)])
            tt(qv(s0)[:, :, :, 32:64], qv(s2)[:, :, :, 0:32], rp[:, 0:nt, 96:128].unsqueeze(2).broadcast_to([128, nt, 2, 32]), ALU.mult,
               [("scr", 2), rpk], [("scr", 0)])
            tt(krb[:, 0:W], s3, s0, ALU.add, [("scr", 3), ("scr", 0)], [("big", 24)])
            for t in range(nt):
                tr(bank(6).bitcast(BF16)[:, t * 128:(t + 1) * 128], krb[:, t * 128:(t + 1) * 128], [("big", 24)], [("ps", 6)])
            act(Vaug[:, tile0:tile0 + nt, 0:64], kvv[:, :, 128:192], AF.Copy, kvb, [("Vaug", tile0 + t) for t in range(nt)])
            act(Vaug[:, tile0:tile0 + nt, 128:192], kvv[:, :, 192:256], AF.Copy, kvb, [("Vaug", tile0 + t) for t in range(nt)])
            cp(KT[:, tile0 * 128:(tile0 + nt) * 128], bank(6).bitcast(BF16)[:, 0:nt * 128], [("ps", 6)], [("KT", si)])

        cnt["phaseA"] = True
        actx = {}
        if nsup:
            for k_ in range(min(2, nsup)):
                load_xo(supers[k_][0], supers[k_][1], k_ % 2)
                load_rope(supers[k_][0], supers[k_][1], k_ % 2)
            ld["n"] = 0
            actx[0] = nm_stats(xbuf[0], ("xbuf", 0), supers[0][1])
            nm_xn_tr(actx[0])
            if nsup > 2:
                load_xo(supers[2][0], supers[2][1], 0)
            nm_evac(actx[0], a1t, 0, supers[0][2], hbufs[0])
        for k_ in range(nsup):
            if k_ % 2 == 0 and k_ // 2 < 8:
                mod_b_dma(k_ // 2)
            n_ = k_ + 1
            if n_ < nsup:
                actx[n_] = nm_stats(xbuf[n_ % 2], ("xbuf", n_ % 2), supers[n_][1])
            a_kv(k_)
            if n_ < nsup:
                nm_xn_tr(actx[n_])
                if k_ + 3 < nsup:
                    load_xo(supers[k_ + 3][0], supers[k_ + 3][1], (k_ + 3) % 2)
            a_post(k_)
            if k_ + 2 < nsup:
                load_rope(supers[k_ + 2][0], supers[k_ + 2][1], k_ % 2)
            if n_ < nsup:
                nm_evac(actx[n_], a1t, 0, supers[n_][2], hbufs[n_ % 2])

        cnt["phaseA"] = False
        ld["n"] = 1
        slot = load_x(256, 4) if (nqb and stage > 1) else None
        if stage >= 1:
            if len(supers) < 17:
                raise NotImplementedError("debug stage with truncated phase A not supported any more")
            mod_b_finish()
        if stage == 1:
            dump("KT", KT[:, 0:1280], [("KT", i) for i in range(3)], BF16)
            dump("Vaug", Vaug[:, 0:10, :], [("Vaug", i) for i in range(10)], BF16)
            dump("hT", hT[:], [("hT", c) for c in range(8)], BF16)
        if stage <= 1:
            nqb = 0
        wctr = {"n": 0}

        def load_unit(u):
            s = wctr["n"] % NB
            wctr["n"] += 1
            dma("sp", ring[s][:, :], wsc[u], [("wsc", u)], [("ring", s)], f"w{s}")
            return s

        def R(s):
            return ("ring", s)

        def unit8(s):
            return ring[s][:, :].rearrange("p (c n) -> p c n", c=8)

        def unit4(s):
            return ring[s][:, :].rearrange("p (c n) -> p c n", c=4)

        yT = [big[:, c, :] for c in range(8)]
        uT = [big[:, 8 + j, :] for j in range(4)]
        gmT = [big[:, 12 + j, :] for j in range(4)]
        attnT = [big[:, 16 + j, :] for j in range(4)]
        QT = [QTb[:, j, :] for j in range(4)]
        vnb = [big[:, 16 + t, :] for t in range(4)]
        PT = [big[:, 28:30, :].rearrange("p a b -> p (a b)"), big[:, 30:32, :].rearrange("p a b -> p (a b)"),
              big[:, 26:28, :].rearrange("p a b -> p (a b)")]
        PTK = [[("big", 28), ("big", 29)], [("big", 30), ("big", 31)], [("big", 26), ("big", 27)]]

        def pre_q_pieces(slot_):
            xb_, xres_, rpb_ = xbuf[slot_], ("xbuf", slot_), ropeb[slot_]
            st = {}
            qrb = scr[4][:, :].bitcast(BF16)[:, 0:512]

            def p_stats():
                st["cx"] = nm_stats(xb_, xres_, 4)

            def p_xs(t):
                cx = st["cx"]
                k = cnt["xn"] % 2
                cnt["xn"] += 1
                st[("k", t)] = k
                xn_scale(xb_, xres_, t, k, rstd[:, cx["po"] + t:cx["po"] + t + 1], ("rstd", cx["par"]))

            def p_xt(t):
                xn_tr(t, st[("k", t)], pb=4)

            def p_xn(t):
                p_xs(t)
                p_xt(t)

            def p_evac():
                nm_evac(st["cx"], (a1, "a1", "modTa"), 0, 0, None, pb=4)

            def p_qproj():
                su = load_unit(0)
                for t in range(4):
                    for c in range(8):
                        mm(bank(4 + t), hT[:, c, t * 128:(t + 1) * 128], unit8(su)[:, c, :], c == 0, c == 7, [("hT", c), R(su)], [("ps", 4 + t)])

            def p_qdve(t):
                norm_rope(bank(4 + t), ("ps", 4 + t), 8, gq[:, :], "gq", rpb_[:, t, :], ("ropeb", slot_), qrb, ("scr", 4))

            def p_qpe(t):
                for g in range(4):
                    tr(bank(4 + t).bitcast(BF16)[:, g * 128:(g + 1) * 128], qrb[:, g * 128:(g + 1) * 128], [("scr", 4)], [("ps", 4 + t)])
                cp(QTb[:, :, t * 128:(t + 1) * 128], bank(4 + t).bitcast(BF16)[:, 0:512].rearrange("p (g q) -> p g q", q=128),
                   [("ps", 4 + t)], [("QT", g) for g in range(4)])

            return dict(stats=p_stats, xn=p_xn, xs=p_xs, xt=p_xt, evac=p_evac, qproj=p_qproj, qdve=p_qdve, qpe=p_qpe)

        def run_pre_q_all(pq):
            pq["stats"]()
            for t in range(4):
                pq["xn"](t)
            pq["evac"]()
            pq["qproj"]()
            for t in range(4):
                pq["qdve"](t)
                pq["qpe"](t)

        if nqb:
            run_pre_q_all(pre_q_pieces(slot))
            pre_units = (load_unit(1), load_unit(2))

        for qb in range(nqb):
            row0 = 256 + qb * 512
            xb = xbuf[slot]
            xres = ("xbuf", slot)
            rpb = ropeb[slot]
            nslot = None
            sv, suu = pre_units
            for t in range(4):
                for c in range(8):
                    mm(bank(t), hT[:, c, t * 128:(t + 1) * 128], unit8(sv)[:, c, :], c == 0, c == 7, [("hT", c), R(sv)], [("ps", t)])
            for j in range(4):
                for c in range(8):
                    mm(bank(4 + j), unit8(suu)[:, c, j * 128:(j + 1) * 128], hT[:, c, :], c == 0, c == 7, [("hT", c), R(suu)], [("ps", 4 + j)])
            for t in range(4):
                act(scr[t][:, :], bank(t), AF.Gelu_apprx_tanh, [("ps", t)], [("scr", t)])
            for j in range(4):
                act(uT[j], bank(4 + j), AF.Gelu_apprx_tanh, [("ps", 4 + j)], [("big", 8 + j)])
            for t in range(4):
                sq_ = scr[4 + t % 2]
                act(sq_[:, :], scr[t][:, :], AF.Square, [("scr", t)], [("scr", 4 + t % 2)])
                T.add("dve", lambda e, t=t, sq_=sq_: e.tensor_reduce(out=hs32[:, 8 * t:8 * t + 8], in_=sq_[:, :].rearrange("p (h d) -> p h d", d=64),
                                                                     axis=AX.X, op=ALU.add), [("scr", 4 + t % 2)], [("hs32", t)])
            act(hl32[:, :], hs32[:, :], AF.Ln, [("hs32", t) for t in range(4)], ["hl32"], scale=1.0 / 64, bias=EPS)
            act(hr32[:, :], hl32[:, :], AF.Exp, ["hl32"], ["hr32"], scale=-0.5)
            for t in range(4):
                tm_ = scr[4 + t % 2]
                tt(tm_[:, :], scr[t][:, :], gmg[:, :], ALU.mult, [("scr", t), "gmg"], [("scr", 4 + t % 2)])
                tt(vnb[t].rearrange("p (h d) -> p h d", d=64), tm_[:, :].rearrange("p (h d) -> p h d", d=64),
                   hr32[:, 8 * t:8 * t + 8].unsqueeze(2).broadcast_to([128, 8, 64]), ALU.mult, [("scr", 4 + t % 2), "hr32"], [("big", 16 + t)])

            def spatial_gm(t):
                bk = 6 + t % 2
                for j in range(4):
                    for gg in range(2):
                        g = 2 * j + gg
                        mm(ps[gg * 64:(gg + 1) * 64, bk * 512 + j * 128: bk * 512 + (j + 1) * 128], vnb[t][:, g * 64:(g + 1) * 64], wsT[:, g, :],
                           True, True, [("big", 16 + t), "wsT"], [("ps", bk)])
                tmp = scr[t % 2]
                tt(tmp[:, :].rearrange("p (j q) -> p j q", q=128), bank(bk).rearrange("p (j q) -> p j q", q=128), bsT[:, :, :], ALU.add,
                   [("ps", bk), "bsT"], [("scr", t % 2)])
                tt(big[:, 12:16, t * 128:(t + 1) * 128], tmp[:, :].rearrange("p (j q) -> p j q", q=128), big[:, 8:12, t * 128:(t + 1) * 128], ALU.mult,
                   [("scr", t % 2)] + [("big", 8 + j) for j in range(4)], [("big", 12 + j) for j in range(4)])

            steps = [(g, kb) for g in range(4) for kb in range(NT)]

            def ksup(kb):
                return 0 if kb < 2 else 1 + (kb - 2) // 4

            def qk(i):
                g, kb = steps[i]
                sbk = (i % 2) * 2
                mm(bank(sbk), KT[0:64, kb * 128:(kb + 1) * 128], QT[g][0:64, :], True, True, [("KT", ksup(kb)), ("QT", g)], [("ps", sbk)])
                mm(bank(sbk + 1), KT[64:128, kb * 128:(kb + 1) * 128], QT[g][64:128, :], True, True, [("KT", ksup(kb)), ("QT", g)], [("ps", sbk + 1)])
                pace = [("pace", i)] if (qb == 0 and i % 12 == 0) else []
                act(PT[i % 3], ps[:, sbk * 512:sbk * 512 + 1024], AF.Exp, [("ps", sbk), ("ps", sbk + 1)], PTK[i % 3] + pace, scale=0.125)
                if pace and N_EARLY + i // 12 < N_UNITS_QB:
                    cast_unit(N_EARLY + i // 12, pace)

            def pv(i):
                g, kb = steps[i]
                oa = 4 + 2 * (g % 2)
                mm(bank(oa), Vaug[:, kb, 0:128], PT[i % 3][:, 0:512], kb == 0, kb == NT - 1, [("Vaug", kb)] + PTK[i % 3], [("ps", oa)])
                mm(bank(oa + 1), Vaug[:, kb, 64:192], PT[i % 3][:, 512:1024], kb == 0, kb == NT - 1, [("Vaug", kb)] + PTK[i % 3], [("ps", oa + 1)])
                if kb == NT - 1:
                    if g == 3:
                        act(rc[64:128, 0:512], bank(oa)[64:128, :], AF.Ln, [("ps", oa)], ["rc", ("rcj", 0), ("rcj", 1)])
                        act(rc[64:128, 0:512], rc[64:128, 0:512], AF.Exp, ["rc"], ["rc"], scale=-1.0)
                        act(rc[0:64, 512:1024], bank(oa + 1)[0:64, :], AF.Ln, [("ps", oa + 1)], ["rc", ("rcj", 0), ("rcj", 1)])
                        act(rc[0:64, 512:1024], rc[0:64, 512:1024], AF.Exp, ["rc"], ["rc"], scale=-1.0)
                    else:
                        recip(rc[64:128, 0:512], bank(oa)[64:128, :], [("ps", oa)], ["rc", ("rcj", 0), ("rcj", 1)])
                        recip(rc[0:64, 512:1024], bank(oa + 1)[0:64, :], [("ps", oa + 1)], ["rc", ("rcj", 0), ("rcj", 1)])
                    tt(attnT[g][0:64, :], bank(oa)[0:64, :], rc[64:128, 0:512], ALU.mult, [("ps", oa), "rc", ("rcj", 0), ("rcj", 1)], [("big", 16 + g)])
                    tt(attnT[g][64:128, :], bank(oa + 1)[64:128, :], rc[0:64, 512:1024], ALU.mult, [("ps", oa + 1), "rc", ("rcj", 0), ("rcj", 1)], [("big", 16 + g)])

            qk(0)
            qk(1)
            for i in range(len(steps)):
                if i + 2 < len(steps):
                    qk(i + 2)
                pv(i)
                if i in (1, 3, 5, 7):
                    spatial_gm((i - 1) // 2)

            if qb + 1 < nqb:
                nslot = load_x(row0 + 512, 4)

            sga = [load_unit(3), None]
            sgb = [load_unit(4), None]
            sbra = load_unit(5)
            sbrg = load_unit(6)
            for m in range(8):
                if m == 4:
                    sga[1] = load_unit(7)
                    sgb[1] = load_unit(8)
                h = m // 4
                b0 = (m % 2) * 4
                for c in range(8):
                    mm(bank(b0), unit8(sga[h])[:, c, (m % 4) * 128:(m % 4 + 1) * 128], hT[:, c, :], c == 0, c == 7, [("hT", c), R(sga[h])], [("ps", b0)])
                for c in range(8):
                    mm(bank(b0 + 1), unit8(sgb[h])[:, c, (m % 4) * 128:(m % 4 + 1) * 128], hT[:, c, :], c == 0, c == 7, [("hT", c), R(sgb[h])], [("ps", b0 + 1)])
                for c in range(4):
                    mm(bank(b0 + 2), unit4(sbra)[:, c, m * 128:(m + 1) * 128], attnT[c], c == 0, c == 3, [("big", 16 + c), R(sbra)], [("ps", b0 + 2)])
                for c in range(4):
                    mm(bank(b0 + 3), unit4(sbrg)[:, c, m * 128:(m + 1) * 128], gmT[c], c == 0, c == 3, [("big", 12 + c), R(sbrg)], [("ps", b0 + 3)])
                act(scr[0][:, :], bank(b0), AF.Sigmoid, [("ps", b0)], [("scr", 0)])
                act(scr[1][:, :], bank(b0 + 1), AF.Sigmoid, [("ps", b0 + 1)], [("scr", 1)])
                tt(scr[2][:, :], bank(b0 + 2), scr[0][:, :], ALU.mult, [("ps", b0 + 2), ("scr", 0)], [("scr", 2)])
                tt(scr[3][:, :], bank(b0 + 3), scr[1][:, :], ALU.mult, [("ps", b0 + 3), ("scr", 1)], [("scr", 3)])
                tt(yT[m], scr[2][:, :], scr[3][:, :], ALU.add, [("scr", 2), ("scr", 3)], [("big", m)], eng="pool")

            so = [load_unit(9), load_unit(10)]
            par = cnt["par"] % 2
            cnt["par"] += 1
            po = par * 4
            k8c = {"n": 0}

            def b8_mm(t):
                for nh in range(2):
                    bk = 4 + k8c["n"] % 4
                    k8c["n"] += 1
                    for c in range(8):
                        mm(bank(bk), yT[c][:, t * 128:(t + 1) * 128], unit8(so[nh])[:, c, :], c == 0, c == 7, [("big", c), R(so[nh])], [("ps", bk)])
                    sk = 4 + (k8c["n"] % 2)
                    tt(scr[sk][:, :], bank(bk), gbc[:, nh * 512:(nh + 1) * 512], ALU.mult, [("ps", bk), "gbc"], [("scr", sk)])
                    tt(xb[:, t, nh * 512:(nh + 1) * 512], scr[sk][:, :], xb[:, t, nh * 512:(nh + 1) * 512], ALU.add, [("scr", sk), xres], [xres, ("x1t", t)], eng="pool")

            def b9_tile(t):
                kq = cnt["sq"] % 2
                cnt["sq"] += 1
                cs = slice(po + t, po + t + 1)
                act(sqj[kq], xb[:, t, :], AF.Square, [("x1t", t)], [("ss", par, t), ("rcj", kq)], accum=ss[:, cs])
                act(rs[:, cs], ss[:, cs], AF.Ln, [("ss", par, t)], [("rs", par), ("rs", par, t)], scale=1.0 / D, bias=EPS)
                act(rstd[:, cs], rs[:, cs], AF.Exp, [("rs", par, t)], [("rstd", par), ("rstd", par, t)], scale=-0.5)
                k = cnt["xn"] % 2
                cnt["xn"] += 1
                xn_tile(xb, ("x1t", t), t, k, rstd[:, cs], ("rstd", par, t))

            for t in range(4):
                b8_mm(t)
                if t >= 1:
                    b9_tile(t - 1)
            b9_tile(3)
            nm_evac(dict(nt=4), (a2, "a2", "modTb"), 24, 0)

            pq = pre_q_pieces(nslot) if nslot is not None else None
            if pq:
                pq["stats"]()
            for j in range(32):
                if j % 4 == 0:
                    sf = load_unit(11 + j // 4)
                bk = j % 8
                for c in range(8):
                    mm(bank(bk), unit8(sf)[:, c, (j % 4) * 128:(j % 4 + 1) * 128], hT[:, c, :], c == 0, c == 7, [("hT", c), R(sf)], [("ps", bk)])
                sk = j % 4
                act(scr[sk][:, :], bank(bk), AF.Relu, [("ps", bk)], [("scr", sk)])
                tt(big[:, j, :], scr[sk][:, :], scr[sk][:, :], ALU.mult, [("scr", sk)], [("big", j)])
                if pq and j == 27:
                    pq["xs"](0)
                    pq["xs"](1)

            if pq:
                pq["xt"](0)
                pq["xt"](1)
                pq["xs"](2)
                pq["xs"](3)
            blk = 0
            for nh in range(2):
                for kg in range(4):
                    s2 = load_unit(19 + nh * 4 + kg)
                    for t in range(4):
                        bk = t
                        for c in range(8):
                            mm(bank(bk), big[:, kg * 8 + c, t * 128:(t + 1) * 128], unit8(s2)[:, c, :], kg == 0 and c == 0, kg == 3 and c == 7,
                               [("big", kg * 8 + c), R(s2)], [("ps", bk)])
                    if pq:
                        if blk == 0:
                            pq["xt"](2)
                            pq["xt"](3)
                            pq["evac"]()
                        elif blk == 1:
                            pq["qproj"]()
                        elif blk == 2:
                            pq["qdve"](0)
                        elif blk in (3, 4, 5):
                            pq["qpe"](blk - 3)
                            pq["qdve"](blk - 2)
                        elif blk == 6:
                            pq["qpe"](3)
                    blk += 1
                for t in range(4):
                    bk = t
                    sk = 5
                    tt(scr[sk][:, :], bank(bk), gbc[:, 1024 + nh * 512:1024 + (nh + 1) * 512], ALU.mult, [("ps", bk), "gbc"], [("scr", sk)])
                    tt(xb[:, t, nh * 512:(nh + 1) * 512], scr[sk][:, :], xb[:, t, nh * 512:(nh + 1) * 512], ALU.add, [("scr", sk), xres], [xres], eng="pool")
            if qb + 1 < nqb:
                pre_units = (load_unit(1), load_unit(2))
            dma("sp", out_d[qb * 512:(qb + 1) * 512, :].rearrange("(t p) d -> p t d", p=128), xb[:, :, :], [xres], [("out", qb)], f"o{slot}")
            slot = nslot

        T.add("sp", lambda e: e.nop(), [("out", q) for q in range(nqb)] + [("dbgout", n) for n in dbg], [])

        T.finalize()
        nc._tracker = T

        @block.sync
        def _(e):
            T.emit("sp", e, esems, dsems)

        @block.tensor
        def _(e):
            T.emit("pe", e, esems, dsems)

        @block.scalar
        def _(e):
            T.emit("act", e, esems, dsems)

        @block.vector
        def _(e):
            T.emit("dve", e, esems, dsems)

        @block.gpsimd
        def _(e):
            T.emit("pool", e, esems, dsems)

    return nc


_CACHE = {}


def _rope_table(tok_idx):
    n = tok_idx.shape[0]
    t = np.maximum(tok_idx, 0)
    row = (t // 64).astype(np.float32)
    colp = (t % 64).astype(np.float32)
    inv = (np.float32(10000.0) ** (-np.arange(0, 32, 2, dtype=np.float32) / np.float32(32))).astype(np.float32)
    ang = np.concatenate([row[:, None] * inv[None, :], colp[:, None] * inv[None, :]], axis=-1).astype(np.float32)
    cos = np.cos(ang).astype(np.float32)
    sin = np.sin(ang).astype(np.float32)
    ident = tok_idx < 0
    cos[ident] = 1.0
    sin[ident] = 0.0
    return np.concatenate([cos, cos, -sin, sin], axis=1).astype(np.float32)


def kernel(x, c, ctx, c_ctx, w_mod, b_mod, norm1_g, norm2_g, w_in, q_norm_g, k_norm_g,
           gm_norm_g, gm_ws, gm_bs, w_br_attn, w_br_gm, w_out, w_ff1, w_ff2):
    f = lambda a: np.ascontiguousarray(np.asarray(a, dtype=np.float32))
    x, c, ctx, c_ctx = f(x), f(c), f(ctx), f(c_ctx)
    w_mod, b_mod, w_in = f(w_mod)[0], f(b_mod)[0], f(w_in)[0]
    n1, n2 = f(norm1_g)[0], f(norm2_g)[0]
    qg, kg, gmg = f(q_norm_g)[0], f(k_norm_g)[0], f(gm_norm_g)[0]
    ws, bs = f(gm_ws)[0], f(gm_bs)[0]
    bra, brg, wo, w1, w2 = f(w_br_attn)[0], f(w_br_gm)[0], f(w_out)[0], f(w_ff1)[0], f(w_ff2)[0]

    if "nc" not in _CACHE:
        _CACHE["nc"] = build_program()
    nc = _CACHE["nc"]

    qcols = 256 + np.array([kv * 256 + g * 64 + d for g in range(4) for kv in range(2) for d in range(64)])
    order = np.concatenate([np.arange(0, 256), qcols, np.arange(1280, 1792), np.arange(768, 1280), np.arange(1792, 3840)])
    w_in_p = np.ascontiguousarray(w_in[:, order])
    rows = np.array([kv * 256 + g * 64 + d for g in range(4) for kv in range(2) for d in range(64)])
    w_bra_p = np.ascontiguousarray(bra[rows, :])
    b_modT = np.ascontiguousarray(b_mod.reshape(48, 128).T)
    b_modg = np.ascontiguousarray(np.concatenate([b_mod[2048:3072], b_mod[5120:6144]])[None, :])
    n1g = np.ascontiguousarray(n1.reshape(8, 128).T)
    n2g = np.ascontiguousarray(n2.reshape(8, 128).T)
    gq_bc = np.ascontiguousarray(np.broadcast_to(np.tile(qg, 8)[None, :], (128, 512)))
    gk_bc = np.ascontiguousarray(np.broadcast_to(np.tile(kg, 2)[None, :], (128, 128)))
    gmg_bc = np.ascontiguousarray(np.broadcast_to(gmg.reshape(512)[None, :], (128, 512)))
    wsT = np.ascontiguousarray(ws.transpose(2, 0, 1))
    bsT = np.ascontiguousarray(np.broadcast_to(bs.reshape(4, 2, 1, 128), (4, 2, 64, 128)).transpose(1, 2, 0, 3).reshape(128, 4, 128))
    ident = np.eye(128, dtype=np.float32)

    in_maps = []
    for core in range(8):
        b, hf = core // 2, core % 2
        own = np.arange(hf * 4096, (hf + 1) * 4096)
        oth = np.arange((1 - hf) * 4096, (2 - hf) * 4096)
        xin = np.concatenate([ctx[b], x[b, own], x[b, oth]], axis=0)
        tok = np.concatenate([-np.ones(256, dtype=np.int64), own, oth])
        cT = np.ascontiguousarray(np.stack([c[b], c_ctx], axis=1).reshape(8, 128, 2).transpose(1, 0, 2))
        in_maps.append({
            "xin": np.ascontiguousarray(xin), "rope": _rope_table(tok), "cT": cT, "w_mod": w_mod, "b_modT": b_modT,
            "b_modg": b_modg, "n1g": n1g, "n2g": n2g, "w_in_p": w_in_p, "gq_bc": gq_bc, "gk_bc": gk_bc, "gmg_bc": gmg_bc,
            "wsT": wsT, "bsT": bsT, "w_bra_p": w_bra_p, "w_brg": brg, "w_out": wo, "w_ff1": w1, "w_ff2": w2, "ident": ident,
        })
    res = run_bass_kernel_spmd(nc, in_maps, core_ids=list(range(8)))
    out = np.empty((4, 8192, D), dtype=np.float32)
    for core in range(8):
        b, hf = core // 2, core % 2
        out[b, hf * 4096:(hf + 1) * 4096] = res.results[core]["out"]
    return out
```

```python
import numpy as np
import concourse.bass as bass
import concourse.mybir as mybir
from concourse.bass_utils import run_bass_kernel_spmd

F32 = mybir.dt.float32
BF16 = mybir.dt.bfloat16
AF = mybir.ActivationFunctionType
ALU = mybir.AluOpType
AX = mybir.AxisListType

D = 1024
NCTX_T = 2
NOWN_T = 32
NT = 66
NQB = 8
EPS = 1e-6
NB = 5
N_UNITS_QB = 27


class Tracker:
    def __init__(self):
        self.ops = []
        self.last_w = {}
        self.readers = {}
        self.dcount = {}

    def add(self, eng, fn, reads=(), writes=(), dsem=None):
        idx = len(self.ops)
        deps = set()
        if eng in ("act", "dve"):
            writes = list(writes) + [("pslk", r[1]) for r in reads if isinstance(r, tuple) and r[0] == "ps"]
        for r in reads:
            if r in self.last_w:
                deps.add(self.last_w[r])
        for w in writes:
            if w in self.last_w:
                deps.add(self.last_w[w])
            for rd in self.readers.get(w, ()):
                deps.add(rd)
        op = dict(eng=eng, fn=fn, deps=deps, dsem=dsem, marked=False, idx=idx, val=None, desc=(tuple(reads), tuple(writes)))
        if dsem is not None:
            self.dcount[dsem] = self.dcount.get(dsem, 0) + 16
            op["dval"] = self.dcount[dsem]
        self.ops.append(op)
        for r in reads:
            self.readers.setdefault(r, []).append(idx)
        for w in writes:
            self.last_w[w] = idx
            self.readers[w] = []
        return idx

    def finalize(self):
        ops = self.ops
        for op in ops:
            red = {}
            for d in op["deps"]:
                dop = ops[d]
                if dop["dsem"] is not None:
                    key = ("d", dop["dsem"])
                    if key not in red or ops[red[key]]["dval"] < dop["dval"]:
                        red[key] = d
                else:
                    if dop["eng"] == "pe" and op["eng"] == "pe" and op["dsem"] is None:
                        continue
                    key = ("e", dop["eng"])
                    if key not in red or red[key] < d:
                        red[key] = d
            op["rdeps"] = list(red.values())
            for d in op["rdeps"]:
                if ops[d]["dsem"] is None:
                    ops[d]["marked"] = True
        cnt = {}
        for op in ops:
            if op["dsem"] is None and op["marked"]:
                cnt[op["eng"]] = cnt.get(op["eng"], 0) + 1
                op["val"] = cnt[op["eng"]]

    def trace(self, engname):
        waited = {}
        out = []
        for op in self.ops:
            if op["eng"] != engname:
                continue
            ws = []
            for d in op["rdeps"]:
                dop = self.ops[d]
                if dop["dsem"] is not None:
                    val = self.dcount[dop["dsem"]] if dop["dsem"] in ("const", "cast", "gb") else dop["dval"]
                    key = ("d", dop["dsem"])
                else:
                    val, key = dop["val"], ("e", dop["eng"])
                if waited.get(key, 0) < val:
                    ws.append((key[1], val))
                    waited[key] = val
            inc = (op["dsem"], op.get("dval")) if op["dsem"] else ((engname, op["val"]) if op["marked"] else None)
            out.append((op["idx"], ws, op["desc"], inc))
        return out

    def emit(self, engname, engobj, esems, dsems):
        waited = {}
        for op in self.ops:
            if op["eng"] != engname:
                continue
            for d in op["rdeps"]:
                dop = self.ops[d]
                if dop["dsem"] is not None:
                    val = self.dcount[dop["dsem"]] if dop["dsem"] in ("const", "cast", "gb") else dop["dval"]
                    sem, key = dsems[dop["dsem"]], ("d", dop["dsem"])
                else:
                    sem, val, key = esems[dop["eng"]], dop["val"], ("e", dop["eng"])
                if waited.get(key, 0) < val:
                    engobj.wait_ge(sem, val)
                    waited[key] = val
            ins = op["fn"](engobj)
            if op["dsem"] is not None:
                ins.then_inc(dsems[op["dsem"]], 16)
            elif op["marked"]:
                ins.then_inc(esems[engname], 1)


def build_program(stage=3, nqb=NQB, skip=()):
    nc = bass.Bass("TRN2", target_bir_lowering=False)
    T = Tracker()
    dbg = {}

    def din(name, shape, dt=F32):
        return nc.dram_tensor(name, list(shape), dt, kind="ExternalInput").ap()

    xin = din("xin", [NT * 128, D])
    rope = din("rope", [NT * 128, 128])
    cT_d = din("cT", [128, 8, 2])
    wmod_d = din("w_mod", [D, 6 * D])
    bmodT_d = din("b_modT", [128, 48])
    bmodg_d = din("b_modg", [1, 2048])
    n1g_d = din("n1g", [128, 8])
    n2g_d = din("n2g", [128, 8])
    win_d = din("w_in_p", [D, 3840])
    gq_d = din("gq_bc", [128, 512])
    gk_d = din("gk_bc", [128, 128])
    gmg_d = din("gmg_bc", [128, 512])
    wsT_d = din("wsT", [128, 8, 128])
    bsT_d = din("bsT", [128, 4, 128])
    bra_d = din("w_bra_p", [512, D])
    brg_d = din("w_brg", [512, D])
    wout_d = din("w_out", [D, D])
    ff1_d = din("w_ff1", [D, 4 * D])
    ff2_d = din("w_ff2", [4 * D, D])
    ident_d = din("ident", [128, 128])
    out_d = nc.dram_tensor("out", [NOWN_T * 128, D], F32, kind="ExternalOutput").ap()
    gscr = nc.dram_tensor("gscr", [1, 2048], F32, kind="Internal").ap()
    wsc = nc.dram_tensor("wscratch", [N_UNITS_QB, 128, 4096], BF16, kind="Internal").ap()

    import contextlib
    es = contextlib.ExitStack()

    def sb(name, shape, dt):
        return es.enter_context(nc.sbuf_tensor(name, list(shape), dt))

    with es:
        KT = sb("KT", [128, NT * 128], BF16)
        Vaug = sb("Vaug", [128, NT, 192], BF16)
        xbuf = [sb(f"xbuf{i}", [128, 4, D], F32) for i in range(2)]
        ropeb = [sb(f"ropeb{i}", [128, 4, 128], F32) for i in range(2)]
        xn = [sb(f"xn{i}", [128, D], BF16) for i in range(2)]
        hT = sb("hT", [128, 8, 512], BF16)
        big = sb("big", [128, 32, 512], BF16)
        ringT = sb("ring", [128, NB * 4096], BF16)
        ring = [ringT[:, i * 4096:(i + 1) * 4096] for i in range(NB)]
        gbc = sb("gbc", [128, 2048], F32)
        scr = [sb(f"scr{i}", [128, 512], F32) for i in range(6)]
        rc = sb("rc", [128, 1024], F32)
        QTb = sb("QTb", [128, 4, 512], BF16)
        Wkv = sb("Wkv", [128, 8, 256], BF16)
        identb = sb("identb", [128, 128], BF16)
        gq = sb("gq", [128, 512], F32)
        gk = sb("gk", [128, 128], F32)
        gmg = sb("gmg", [128, 512], F32)
        wsT = sb("wsTb", [128, 8, 128], BF16)
        bsT = sb("bsTs", [128, 4, 128], F32)
        cT = sb("cTs", [128, 8, 2], F32)
        scT = sb("scT", [128, 8, 2], F32)
        bmodT = sb("bmodTs", [128, 48], F32)
        n1g = sb("n1gs", [128, 8], F32)
        n2g = sb("n2gs", [128, 8], F32)
        modT = sb("modT", [128, 48, 2], F32)
        a1 = sb("a1", [128, 8, 2], F32)
        a2 = sb("a2", [128, 8, 2], F32)
        ones1 = sb("ones1", [1, 128], F32)
        ss = sb("ss", [128, 8], F32)
        rs = sb("rs", [128, 8], F32)
        rstd = sb("rstd", [128, 8], F32)
        hs = sb("hs", [128, 8], F32)
        hl = sb("hl", [128, 8], F32)
        hr = sb("hr", [128, 8], F32)
        hs32 = sb("hs32", [128, 32], F32)
        hl32 = sb("hl32", [128, 32], F32)
        hr32 = sb("hr32", [128, 32], F32)
        ps = es.enter_context(nc.psum_tensor("ps", [128, 4096], F32))
        sqj = [rc[:, 0:512].bitcast(BF16), rc[:, 512:1024].bitcast(BF16)]
        grow = big[0:1, 0:8, :].rearrange("p a b -> p (a b)").bitcast(F32)
        bmodg = big[0:1, 8:16, :].rearrange("p a b -> p (a b)").bitcast(F32)

        esems = {e: es.enter_context(nc.semaphore("sem_" + e)) for e in ["pe", "act", "dve", "pool", "sp"]}
        dnames = [f"c{i}" for i in range(6)] + ["dbg", "gb", "gb2", "const", "cast", "x0", "x1", "r0", "r1", "o0", "o1", "wm0", "wm1", "wm2"] + [f"w{i}" for i in range(NB)]
        dsems = {d: es.enter_context(nc.semaphore("ds_" + d)) for d in dnames}
        block = es.enter_context(nc.Block())

        def bank(b):
            return ps[:, b * 512:(b + 1) * 512]

        def mm(out, lhsT, rhs, start, stop, reads, writes, sgc=False):
            T.add("pe", lambda e: e.matmul(out, lhsT=lhsT, rhs=rhs, start=start, stop=stop, skip_group_check=sgc), reads, writes)

        def tr(out, in_, reads, writes):
            T.add("pe", lambda e: e.transpose(out, in_, identb[:, :]), list(reads) + ["identb"], writes)

        def act(out, in_, func, reads, writes, scale=None, bias=None, accum=None):
            kw = {}
            if scale is not None:
                kw["scale"] = scale
            if bias is not None:
                kw["bias"] = bias
            if accum is not None:
                kw["accum_out"] = accum
            T.add("act", lambda e: e.activation(out=out, in_=in_, func=func, **kw), reads, writes)

        def tt(out, in0, in1, op, reads, writes, eng="dve"):
            T.add(eng, lambda e: e.tensor_tensor(out=out, in0=in0, in1=in1, op=op), reads, writes)

        def ts(out, in0, s1, s2, op0, op1, reads, writes, eng="dve"):
            if op1 is None:
                T.add(eng, lambda e: e.tensor_scalar(out=out, in0=in0, scalar1=s1, scalar2=None, op0=op0), reads, writes)
            else:
                T.add(eng, lambda e: e.tensor_scalar(out=out, in0=in0, scalar1=s1, scalar2=s2, op0=op0, op1=op1), reads, writes)

        def stt(out, in0, scalar, in1, op0, op1, reads, writes):
            T.add("dve", lambda e: e.scalar_tensor_tensor(out=out, in0=in0, scalar=scalar, in1=in1, op0=op0, op1=op1), reads, writes)

        def recip(out, in_, reads, writes):
            T.add("dve", lambda e: e.reciprocal(out=out, in_=in_), reads, writes)

        def cp(out, in_, reads, writes, eng="dve"):
            T.add(eng, lambda e: e.tensor_copy(out=out, in_=in_), reads, writes)

        def dma(q, out, in_, reads, writes, dsem):
            T.add(q, lambda e: e.dma_start(out=out, in_=in_), reads, writes, dsem=dsem)

        def memset(ap, val, writes, eng="pool"):
            T.add(eng, lambda e: e.memset(ap, val), (), writes)

        for (dst, src, nm) in [(cT[:], cT_d, "cT"), (bmodT[:], bmodT_d, "bmodT"), (bmodg[:], bmodg_d, "bmodg"),
                               (n1g[:], n1g_d, "n1g"), (n2g[:], n2g_d, "n2g"), (gq[:], gq_d, "gq"), (gk[:], gk_d, "gk"),
                               (gmg[:], gmg_d, "gmg"), (bsT[:], bsT_d, "bsT")]:
            dma("sp", dst, src, (), [nm], "const")
        dma("pool", identb[:], ident_d, (), ["identb"], "cast")
        dma("pool", Wkv[:], win_d[:, 0:256].rearrange("(c p) n -> p c n", p=128), (), ["Wkv"], "cast")
        dma("pool", wsT[:], wsT_d, (), ["wsT"], "cast")
        memset(Vaug[:, :, 64:128], 1.0, [("Vaug", t) for t in range(NT)])
        memset(ones1[:], 1.0, ["ones1"])

        def wsrc_kn(w, c0, ncols):
            return w[:, c0:c0 + ncols].rearrange("(c p) n -> p c n", p=128)

        unit_src = []
        unit_src.append((wsrc_kn(win_d, 256, 512), 8))
        unit_src.append((wsrc_kn(win_d, 768, 512), 8))
        unit_src.append((wsrc_kn(win_d, 1280, 512), 8))
        unit_src.append((wsrc_kn(win_d, 1792, 512), 8))
        unit_src.append((wsrc_kn(win_d, 2816, 512), 8))
        unit_src.append((bra_d.rearrange("(c p) n -> p c n", p=128), 4))
        unit_src.append((brg_d.rearrange("(c p) n -> p c n", p=128), 4))
        unit_src.append((wsrc_kn(win_d, 2304, 512), 8))
        unit_src.append((wsrc_kn(win_d, 3328, 512), 8))
        unit_src.append((wsrc_kn(wout_d, 0, 512), 8))
        unit_src.append((wsrc_kn(wout_d, 512, 512), 8))
        for j in range(8):
            unit_src.append((wsrc_kn(ff1_d, j * 512, 512), 8))
        for nh in range(2):
            for kg in range(4):
                src = ff2_d[kg * 1024:(kg + 1) * 1024, nh * 512:(nh + 1) * 512].rearrange("(c p) n -> p c n", p=128)
                unit_src.append((src, 8))
        assert len(unit_src) == N_UNITS_QB
        def cast_unit(u, extra_reads=()):
            src, nch = unit_src[u]
            dst = wsc[u].rearrange("p (c n) -> p c n", c=nch)
            dma("pool", dst, src, [("cslot", u % 6)] + list(extra_reads), [("wsc", u), ("cslot", u % 6)], f"c{u % 6}")

        N_EARLY = 11 if nqb else N_UNITS_QB
        for u in range(N_EARLY):
            if "cast" in skip:
                break
            cast_unit(u)

        act(scT[:], cT[:], AF.Silu, ["cT"], ["scT"])
        for c in range(8):
            s_ = c % NB
            pa = ring[s_].bitcast(F32)
            dma("sp" if c % 2 == 0 else "act", pa, wmod_d[c * 128:(c + 1) * 128, 0:2048], (), [("ring", s_)], f"w{s_}")
            for jj in range(16):
                mm(ps[:, 2 * jj:2 * jj + 2], pa[:, jj * 128:(jj + 1) * 128], scT[:, c, :], c == 0 and jj == 0, c == 7 and jj == 15,
                   [("ring", s_), "scT"], [("ps", 0)], sgc=True)
        tt(modT[:, 0:16, :], ps[:, 0:32].rearrange("p (j k) -> p j k", k=2),
           bmodT[:, 0:16].unsqueeze(2).broadcast_to([128, 16, 2]), ALU.add, [("ps", 0), "bmodT"], ["modTa"])
        stt(a1[:], modT[:, 8:16, :], 1.0, n1g[:, :].unsqueeze(2).broadcast_to([128, 8, 2]), ALU.add, ALU.mult, ["modTa", "n1g"], ["a1"])

        def mod_b_buf(c):
            p_ = c % 2
            return p_, ringT[:, (2 * p_) * 4096:(2 * p_ + 2) * 4096].bitcast(F32)

        def mod_b_dma(c):
            p_, pb = mod_b_buf(c)
            dma("sp", pb, wmod_d[c * 128:(c + 1) * 128, 2048:6144], (), [("ring", 2 * p_), ("ring", 2 * p_ + 1)], f"wm{p_}")

        def mod_b_mm(c, j0, j1):
            p_, pb = mod_b_buf(c)
            for jj in range(j0, j1):
                mm(ps[:, 7 * 512 + 2 * jj:7 * 512 + 2 * jj + 2], pb[:, jj * 128:(jj + 1) * 128], scT[:, c, :], c == 0 and jj == 0, c == 7 and jj == 31,
                   [("ring", 2 * p_), ("ring", 2 * p_ + 1), "scT"], [("ps", 7)], sgc=True)

        accB = big[:, 8:24, :].rearrange("p a b -> p (a b)").bitcast(F32)
        accK = [("big", r) for r in range(8, 24)]
        ones2 = sb("ones2", [128, 2], F32)
        memset(ones2[:], 1.0, ["ones2"])

        def mod_b_pool(c):
            p_, pb = mod_b_buf(c)
            rk = [("ring", 2 * p_), ("ring", 2 * p_ + 1)]
            if c == 0:
                ts(accB, pb, scT[:, c, 0:1], 0.0, ALU.mult, ALU.add, rk + ["scT"], accK, eng="pool")
            else:
                ts(pb, pb, scT[:, c, 0:1], 0.0, ALU.mult, ALU.add, rk + ["scT"], rk, eng="pool")
                tt(accB, accB, pb, ALU.add, rk + accK, accK, eng="pool")

        def mod_b_reduce():
            for jj in range(32):
                mm(ps[:, 7 * 512 + 2 * jj:7 * 512 + 2 * jj + 2], accB[:, jj * 128:(jj + 1) * 128], ones2[:, :], True, True,
                   accK + ["ones2"], [("ps", 7)])

        def mod_b_piece(c):
            mod_b_dma(c)
            mod_b_pool(c)

        def mod_b_finish():
            mod_b_reduce()
            tt(modT[:, 16:48, :], ps[:, 7 * 512:7 * 512 + 64].rearrange("p (j k) -> p j k", k=2),
               bmodT[:, 16:48].unsqueeze(2).broadcast_to([128, 32, 2]), ALU.add, [("ps", 7), "bmodT"], ["modTb"])
            stt(a2[:], modT[:, 32:40, :], 1.0, n2g[:, :].unsqueeze(2).broadcast_to([128, 8, 2]), ALU.add, ALU.mult, ["modTb", "n2g"], ["a2"])
            for k_, j0 in enumerate((16, 40)):
                T.add("pool", lambda e, k_=k_, j0=j0: e.dma_start(out=gscr[0, k_ * 1024:(k_ + 1) * 1024].rearrange("(c p) -> p c", p=128),
                                                             in_=modT[:, j0:j0 + 8, 0], allow_slow_non_contiguous=True),
                      ["modTb"], [("gscr", k_)], dsem="gb")
            dma("pool", gbc[:], gscr[0, :].partition_broadcast(128), [("gscr", 0), ("gscr", 1)], ["gbc"], "gb2")

        cnt = {"xn": 0, "ev": 0, "sq": 0, "par": 0}

        def nm_stats(xb, xres, nt):
            par = cnt["par"] % 2
            cnt["par"] += 1
            po = par * 4
            for t in range(nt):
                kq = cnt["sq"] % 2
                cnt["sq"] += 1
                act(sqj[kq], xb[:, t, :], AF.Square, [xres], [("ss", par, t), ("rcj", kq)], accum=ss[:, po + t:po + t + 1])
            act(rs[:, po:po + nt], ss[:, po:po + nt], AF.Ln, [("ss", par, t) for t in range(nt)], [("rs", par)], scale=1.0 / D, bias=EPS)
            act(rstd[:, po:po + nt], rs[:, po:po + nt], AF.Exp, [("rs", par)], [("rstd", par)], scale=-0.5)
            return dict(xb=xb, xres=xres, nt=nt, par=par, po=po, use_act=not cnt.get("phaseA", False))

        def xn_scale(xb, xres, t, k, rs_ap, rs_key, use_act=True):
            if t % 2 == 0 or not use_act:
                ts(xn[k][:], xb[:, t, :], rs_ap, None, ALU.mult, None, [xres, rs_key], [("xn", k)])
            else:
                act(xn[k][:], xb[:, t, :], AF.Copy, [xres, rs_key], [("xn", k)], scale=rs_ap)

        def xn_tr(t, k, pb=0):
            for c in range(8):
                bk = pb + c // 2
                o = bank(bk).bitcast(BF16)[:, (c % 2) * 512 + t * 128:(c % 2) * 512 + (t + 1) * 128]
                tr(o, xn[k][:, c * 128:(c + 1) * 128], [("xn", k)], [("ps", bk)])

        def xn_tile(xb, xres, t, k, rs_ap, rs_key, pb=0, use_act=True):
            xn_scale(xb, xres, t, k, rs_ap, rs_key, use_act)
            xn_tr(t, k, pb)

        def _unused_xn_tile(xb, xres, t, k, rs_ap, rs_key, pb=0, use_act=True):
            for c in range(8):
                bk = pb + c // 2
                o = bank(bk).bitcast(BF16)[:, (c % 2) * 512 + t * 128:(c % 2) * 512 + (t + 1) * 128]
                tr(o, xn[k][:, c * 128:(c + 1) * 128], [("xn", k)], [("ps", bk)])

        def nm_xn_tr(cx):
            xb, xres, nt, par, po = cx["xb"], cx["xres"], cx["nt"], cx["par"], cx["po"]
            for t in range(nt):
                k = cnt["xn"] % 2
                cnt["xn"] += 1
                xn_tile(xb, xres, t, k, rstd[:, po + t:po + t + 1], ("rstd", par), use_act=cx.get("use_act", True))

        def nm_evac(cx, a_t, sh_off, col, hd=None, pb=0):
            nt = cx["nt"]
            hdst, hkey = hd if hd is not None else (hT, "hT")
            for c in range(8):
                bk = pb + c // 2
                src = bank(bk).bitcast(BF16)[:, (c % 2) * 512:(c % 2) * 512 + nt * 128]
                if c % 2 == 0:
                    act(hdst[:, c, 0:nt * 128], src, AF.Identity, [("ps", bk), a_t[1], a_t[2]], [(hkey, c)],
                        scale=a_t[0][:, c, col:col + 1], bias=modT[:, sh_off + c, col:col + 1])
                else:
                    ts(hdst[:, c, 0:nt * 128], src, a_t[0][:, c, col:col + 1], modT[:, sh_off + c, col:col + 1], ALU.mult, ALU.add,
                       [("ps", bk), a_t[1], a_t[2]], [(hkey, c)])

        def norm_mod_T(xb, xres, nt, a_t, sh_off, col, hd=None):
            cx = nm_stats(xb, xres, nt)
            nm_xn_tr(cx)
            nm_evac(cx, a_t, sh_off, col, hd)

        def head_rstd(src_sq, nh, res_in):
            T.add("dve", lambda e: e.tensor_reduce(out=hs[:, 0:nh], in_=src_sq.rearrange("p (h d) -> p h d", d=64), axis=AX.X, op=ALU.add),
                  [res_in], ["hs"])
            act(hl[:, 0:nh], hs[:, 0:nh], AF.Ln, ["hs"], ["hl"], scale=1.0 / 64, bias=EPS)
            act(hr[:, 0:nh], hl[:, 0:nh], AF.Exp, ["hl"], ["hr"], scale=-0.5)

        def norm_rope(psrc, psres, nh, gain, gres, rp, rpres, outb, outres):
            W = nh * 64
            s0, s1, s2, s3 = scr[0][:, 0:W], scr[1][:, 0:W], scr[2][:, 0:W], scr[3][:, 0:W]
            act(s0, psrc, AF.Square, [psres], [("scr", 0)])
            head_rstd(s0, nh, ("scr", 0))
            tt(s1, psrc, gain, ALU.mult, [psres, gres], [("scr", 1)])
            tt(s2.rearrange("p (h d) -> p h d", d=64), s1.rearrange("p (h d) -> p h d", d=64),
               hr[:, 0:nh].unsqueeze(2).broadcast_to([128, nh, 64]), ALU.mult, [("scr", 1), "hr"], [("scr", 2)])
            v2 = s2.rearrange("p (h d) -> p h d", d=64)
            tt(s3.rearrange("p (h d) -> p h d", d=64), v2, rp[:, 0:64].unsqueeze(1).broadcast_to([128, nh, 64]), ALU.mult,
               [("scr", 2), rpres], [("scr", 3)])
            v0 = s0.rearrange("p (h d) -> p h d", d=64)
            tt(v0[:, :, 0:32], v2[:, :, 32:64], rp[:, 64:96].unsqueeze(1).broadcast_to([128, nh, 32]), ALU.mult,
               [("scr", 2), rpres], [("scr", 0)])
            tt(v0[:, :, 32:64], v2[:, :, 0:32], rp[:, 96:128].unsqueeze(1).broadcast_to([128, nh, 32]), ALU.mult,
               [("scr", 2), rpres], [("scr", 0)])
            tt(outb, s3, s0, ALU.add, [("scr", 3), ("scr", 0)], [outres])

        def dump(name, ap, res, dt=F32):
            if "nodump" in skip:
                return
            d = nc.dram_tensor("dbg_" + name, list(ap.shape), F32, kind="ExternalOutput").ap()
            dbg[name] = d
            dma("pool", d, ap, res, [("dbgout", name)], "dbg")

        ld = {"n": 0}

        def load_xo(row0, nt, s):
            dma("sp", xbuf[s][:, 0:nt, :], xin[row0:row0 + nt * 128, :].rearrange("(t p) d -> p t d", p=128), (), [("xbuf", s)], f"x{s}")

        def load_rope(row0, nt, s):
            dma("sp", ropeb[s][:, 0:nt, :], rope[row0:row0 + nt * 128, :].rearrange("(t p) d -> p t d", p=128), (), [("ropeb", s)], f"r{s}")

        def load_x(row0, nt):
            s = ld["n"] % 2
            ld["n"] += 1
            load_xo(row0, nt, s)
            load_rope(row0, nt, s)
            return s

        supers = [(0, 2, 1)] + [(256 + i * 512, 4, 0) for i in range(16)]
        if stage == 0:
            supers = []
            for c in range(8):
                mod_b_piece(c)
            mod_b_finish()
            dump("modT", modT[:], ["modTa", "modTb"])
            dump("gbc", gbc[:], ["gbc"])
            dump("a1", a1[:], ["a1"])
        if stage == 1:
            import os
            supers = supers[:int(os.environ.get("NSUP", "3"))]
        krb = big[:, 24:26, :].rearrange("p a b -> p (a b)")
        nsup = len(supers)
        hbufs = [(hT, "hT"), (big[:, 0:8, :], "big")]
        a1t = (a1, "a1", "modTa")

        def a_kv(si):
            row0, nt, col = supers[si]
            hA, hAk = hbufs[si % 2]
            for t in range(nt):
                bk = 4 + t // 2
                for c in range(8):
                    mm(ps[:, bk * 512 + (t % 2) * 256: bk * 512 + (t % 2) * 256 + 256], hA[:, c, t * 128:(t + 1) * 128], Wkv[:, c, :],
                       c == 0, c == 7, [(hAk, c), "Wkv"], [("ps", bk)])

        def a_post(si):
            row0, nt, col = supers[si]
            slot = si % 2
            tile0 = row0 // 128
            W = nt * 128
            kvv = ps[:, 4 * 512:4 * 512 + nt * 256].rearrange("p (t n) -> p t n", n=256)
            kview = kvv[:, :, 0:128]
            kvb = [("ps", 4)] + ([("ps", 5)] if nt > 2 else [])
            s0, s1, s2, s3 = scr[0][:, 0:W], scr[1][:, 0:W], scr[2][:, 0:W], scr[3][:, 0:W]
            tv = lambda a: a.rearrange("p (t n) -> p t n", n=128)
            hv = lambda a: a.rearrange("p (h d) -> p h d", d=64)
            qv = lambda a: a.rearrange("p (t h d) -> p t h d", h=2, d=64)
            rp = ropeb[slot]
            rpk = ("ropeb", slot)
            act(tv(s0), kview, AF.Square, kvb, [("scr", 0)])
            head_rstd(s0, 2 * nt, ("scr", 0))
            tt(tv(s1), kview, gk[:, :].unsqueeze(1).broadcast_to([128, nt, 128]), ALU.mult, kvb + ["gk"], [("scr", 1)])
            tt(hv(s2), hv(s1), hr[:, 0:2 * nt].unsqueeze(2).broadcast_to([128, 2 * nt, 64]), ALU.mult, [("scr", 1), "hr"], [("scr", 2)])
            tt(qv(s3), qv(s2), rp[:, 0:nt, 0:64].unsqueeze(2).broadcast_to([128, nt, 2, 64]), ALU.mult, [("scr", 2), rpk], [("scr", 3)])
            tt(qv(s0)[:, :, :, 0:32], qv(s2)[:, :, :, 32:64], rp[:, 0:nt, 64:96].unsqueeze(2).broadcast_to([128, nt, 2, 32]), ALU.mult,
               [("scr", 2), rpk], [("scr", 0)])
            tt(qv(s0)[:, :, :, 32:64], qv(s2)[:, :, :, 0:32], rp[:, 0:nt, 96:128].unsqueeze(2).broadcast_to([128, nt, 2, 32]), ALU.mult,
               [("scr", 2), rpk], [("scr", 0)])
            tt(krb[:, 0:W], s3, s0, ALU.add, [("scr", 3), ("scr", 0)], [("big", 24)])
            for t in range(nt):
                tr(bank(6).bitcast(BF16)[:, t * 128:(t + 1) * 128], krb[:, t * 128:(t + 1) * 128], [("big", 24)], [("ps", 6)])
            act(Vaug[:, tile0:tile0 + nt, 0:64], kvv[:, :, 128:192], AF.Copy, kvb, [("Vaug", tile0 + t) for t in range(nt)])
            act(Vaug[:, tile0:tile0 + nt, 128:192], kvv[:, :, 192:256], AF.Copy, kvb, [("Vaug", tile0 + t) for t in range(nt)])
            cp(KT[:, tile0 * 128:(tile0 + nt) * 128], bank(6).bitcast(BF16)[:, 0:nt * 128], [("ps", 6)], [("KT", si)])

        cnt["phaseA"] = True
        actx = {}
        if nsup:
            for k_ in range(min(2, nsup)):
                load_xo(supers[k_][0], supers[k_][1], k_ % 2)
                load_rope(supers[k_][0], supers[k_][1], k_ % 2)
            ld["n"] = 0
            actx[0] = nm_stats(xbuf[0], ("xbuf", 0), supers[0][1])
            nm_xn_tr(actx[0])
            if nsup > 2:
                load_xo(supers[2][0], supers[2][1], 0)
            nm_evac(actx[0], a1t, 0, supers[0][2], hbufs[0])
        for k_ in range(nsup):
            if k_ % 2 == 0 and k_ // 2 < 8:
                mod_b_dma(k_ // 2)
            if k_ % 2 == 1 and k_ // 2 < 8:
                mod_b_pool(k_ // 2)
            n_ = k_ + 1
            if n_ < nsup:
                actx[n_] = nm_stats(xbuf[n_ % 2], ("xbuf", n_ % 2), supers[n_][1])
            a_kv(k_)
            if n_ < nsup:
                nm_xn_tr(actx[n_])
                if k_ + 3 < nsup:
                    load_xo(supers[k_ + 3][0], supers[k_ + 3][1], (k_ + 3) % 2)
            a_post(k_)
            if k_ + 2 < nsup:
                load_rope(supers[k_ + 2][0], supers[k_ + 2][1], k_ % 2)
            if n_ < nsup:
                nm_evac(actx[n_], a1t, 0, supers[n_][2], hbufs[n_ % 2])

        cnt["phaseA"] = False
        ld["n"] = 1
        slot = load_x(256, 4) if (nqb and stage > 1) else None
        if stage >= 1:
            if len(supers) < 17:
                raise NotImplementedError("debug stage with truncated phase A not supported any more")
            mod_b_finish()
        if stage == 1:
            dump("KT", KT[:, 0:1280], [("KT", i) for i in range(3)], BF16)
            dump("Vaug", Vaug[:, 0:10, :], [("Vaug", i) for i in range(10)], BF16)
            dump("hT", hT[:], [("hT", c) for c in range(8)], BF16)
        if stage <= 1:
            nqb = 0
        wctr = {"n": 0}

        def load_unit(u):
            s = wctr["n"] % NB
            wctr["n"] += 1
            dma("sp", ring[s][:, :], wsc[u], [("wsc", u)], [("ring", s)], f"w{s}")
            return s

        def R(s):
            return ("ring", s)

        def unit8(s):
            return ring[s][:, :].rearrange("p (c n) -> p c n", c=8)

        def unit4(s):
            return ring[s][:, :].rearrange("p (c n) -> p c n", c=4)

        yT = [big[:, c, :] for c in range(8)]
        uT = [big[:, 8 + j, :] for j in range(4)]
        gmT = [big[:, 12 + j, :] for j in range(4)]
        attnT = [big[:, 16 + j, :] for j in range(4)]
        QT = [QTb[:, j, :] for j in range(4)]
        vnb = [big[:, 16 + t, :] for t in range(4)]
        PT = [big[:, 28:30, :].rearrange("p a b -> p (a b)"), big[:, 30:32, :].rearrange("p a b -> p (a b)"),
              big[:, 26:28, :].rearrange("p a b -> p (a b)")]
        PTK = [[("big", 28), ("big", 29)], [("big", 30), ("big", 31)], [("big", 26), ("big", 27)]]

        def pre_q_pieces(slot_):
            xb_, xres_, rpb_ = xbuf[slot_], ("xbuf", slot_), ropeb[slot_]
            st = {}
            qrb = scr[4][:, :].bitcast(BF16)[:, 0:512]

            def p_stats():
                st["cx"] = nm_stats(xb_, xres_, 4)

            def p_xs(t):
                cx = st["cx"]
                k = cnt["xn"] % 2
                cnt["xn"] += 1
                st[("k", t)] = k
                xn_scale(xb_, xres_, t, k, rstd[:, cx["po"] + t:cx["po"] + t + 1], ("rstd", cx["par"]))

            def p_xt(t):
                xn_tr(t, st[("k", t)], pb=4)

            def p_xn(t):
                p_xs(t)
                p_xt(t)

            def p_evac():
                nm_evac(st["cx"], (a1, "a1", "modTa"), 0, 0, None, pb=4)

            def p_qproj():
                su = load_unit(0)
                for t in range(4):
                    for c in range(8):
                        mm(bank(4 + t), hT[:, c, t * 128:(t + 1) * 128], unit8(su)[:, c, :], c == 0, c == 7, [("hT", c), R(su)], [("ps", 4 + t)])

            def p_qdve(t):
                norm_rope(bank(4 + t), ("ps", 4 + t), 8, gq[:, :], "gq", rpb_[:, t, :], ("ropeb", slot_), qrb, ("scr", 4))

            def p_qpe(t):
                for g in range(4):
                    tr(bank(4 + t).bitcast(BF16)[:, g * 128:(g + 1) * 128], qrb[:, g * 128:(g + 1) * 128], [("scr", 4)], [("ps", 4 + t)])
                cp(QTb[:, :, t * 128:(t + 1) * 128], bank(4 + t).bitcast(BF16)[:, 0:512].rearrange("p (g q) -> p g q", q=128),
                   [("ps", 4 + t)], [("QT", g) for g in range(4)])

            return dict(stats=p_stats, xn=p_xn, xs=p_xs, xt=p_xt, evac=p_evac, qproj=p_qproj, qdve=p_qdve, qpe=p_qpe)

        def run_pre_q_all(pq):
            pq["stats"]()
            for t in range(4):
                pq["xn"](t)
            pq["evac"]()
            pq["qproj"]()
            for t in range(4):
                pq["qdve"](t)
                pq["qpe"](t)

        if nqb:
            run_pre_q_all(pre_q_pieces(slot))
            pre_units = (load_unit(1), load_unit(2))

        for qb in range(nqb):
            row0 = 256 + qb * 512
            xb = xbuf[slot]
            xres = ("xbuf", slot)
            rpb = ropeb[slot]
            nslot = None
            sv, suu = pre_units
            for t in range(4):
                for c in range(8):
                    mm(bank(t), hT[:, c, t * 128:(t + 1) * 128], unit8(sv)[:, c, :], c == 0, c == 7, [("hT", c), R(sv)], [("ps", t)])
            for j in range(4):
                for c in range(8):
                    mm(bank(4 + j), unit8(suu)[:, c, j * 128:(j + 1) * 128], hT[:, c, :], c == 0, c == 7, [("hT", c), R(suu)], [("ps", 4 + j)])
            for t in range(4):
                act(scr[t][:, :], bank(t), AF.Gelu_apprx_tanh, [("ps", t)], [("scr", t)])
            for j in range(4):
                act(uT[j], bank(4 + j), AF.Gelu_apprx_tanh, [("ps", 4 + j)], [("big", 8 + j)])
            for t in range(4):
                sq_ = scr[4 + t % 2]
                act(sq_[:, :], scr[t][:, :], AF.Square, [("scr", t)], [("scr", 4 + t % 2)])
                T.add("dve", lambda e, t=t, sq_=sq_: e.tensor_reduce(out=hs32[:, 8 * t:8 * t + 8], in_=sq_[:, :].rearrange("p (h d) -> p h d", d=64),
                                                                     axis=AX.X, op=ALU.add), [("scr", 4 + t % 2)], [("hs32", t)])
            act(hl32[:, :], hs32[:, :], AF.Ln, [("hs32", t) for t in range(4)], ["hl32"], scale=1.0 / 64, bias=EPS)
            act(hr32[:, :], hl32[:, :], AF.Exp, ["hl32"], ["hr32"], scale=-0.5)
            for t in range(4):
                tm_ = scr[4 + t % 2]
                tt(tm_[:, :], scr[t][:, :], gmg[:, :], ALU.mult, [("scr", t), "gmg"], [("scr", 4 + t % 2)])
                tt(vnb[t].rearrange("p (h d) -> p h d", d=64), tm_[:, :].rearrange("p (h d) -> p h d", d=64),
                   hr32[:, 8 * t:8 * t + 8].unsqueeze(2).broadcast_to([128, 8, 64]), ALU.mult, [("scr", 4 + t % 2), "hr32"], [("big", 16 + t)])

            def spatial_gm(t):
                bk = 6 + t % 2
                for j in range(4):
                    for gg in range(2):
                        g = 2 * j + gg
                        mm(ps[gg * 64:(gg + 1) * 64, bk * 512 + j * 128: bk * 512 + (j + 1) * 128], vnb[t][:, g * 64:(g + 1) * 64], wsT[:, g, :],
                           True, True, [("big", 16 + t), "wsT"], [("ps", bk)])
                tmp = scr[t % 2]
                tt(tmp[:, :].rearrange("p (j q) -> p j q", q=128), bank(bk).rearrange("p (j q) -> p j q", q=128), bsT[:, :, :], ALU.add,
                   [("ps", bk), "bsT"], [("scr", t % 2)])
                tt(big[:, 12:16, t * 128:(t + 1) * 128], tmp[:, :].rearrange("p (j q) -> p j q", q=128), big[:, 8:12, t * 128:(t + 1) * 128], ALU.mult,
                   [("scr", t % 2)] + [("big", 8 + j) for j in range(4)], [("big", 12 + j) for j in range(4)])

            steps = [(g, kb) for g in range(4) for kb in range(NT)]

            def ksup(kb):
                return 0 if kb < 2 else 1 + (kb - 2) // 4

            def qk(i):
                g, kb = steps[i]
                sbk = (i % 2) * 2
                mm(bank(sbk), KT[0:64, kb * 128:(kb + 1) * 128], QT[g][0:64, :], True, True, [("KT", ksup(kb)), ("QT", g)], [("ps", sbk)])
                mm(bank(sbk + 1), KT[64:128, kb * 128:(kb + 1) * 128], QT[g][64:128, :], True, True, [("KT", ksup(kb)), ("QT", g)], [("ps", sbk + 1)])
                pace = [("pace", i)] if (qb == 0 and i % 12 == 0) else []
                act(PT[i % 3], ps[:, sbk * 512:sbk * 512 + 1024], AF.Exp, [("ps", sbk), ("ps", sbk + 1)], PTK[i % 3] + pace, scale=0.125)
                if pace and N_EARLY + i // 12 < N_UNITS_QB:
                    cast_unit(N_EARLY + i // 12, pace)

            def pv(i):
                g, kb = steps[i]
                oa = 4 + 2 * (g % 2)
                mm(bank(oa), Vaug[:, kb, 0:128], PT[i % 3][:, 0:512], kb == 0, kb == NT - 1, [("Vaug", kb)] + PTK[i % 3], [("ps", oa)])
                mm(bank(oa + 1), Vaug[:, kb, 64:192], PT[i % 3][:, 512:1024], kb == 0, kb == NT - 1, [("Vaug", kb)] + PTK[i % 3], [("ps", oa + 1)])
                if kb == NT - 1:
                    if g == 3:
                        act(rc[64:128, 0:512], bank(oa)[64:128, :], AF.Ln, [("ps", oa)], ["rc", ("rcj", 0), ("rcj", 1)])
                        act(rc[64:128, 0:512], rc[64:128, 0:512], AF.Exp, ["rc"], ["rc"], scale=-1.0)
                        act(rc[0:64, 512:1024], bank(oa + 1)[0:64, :], AF.Ln, [("ps", oa + 1)], ["rc", ("rcj", 0), ("rcj", 1)])
                        act(rc[0:64, 512:1024], rc[0:64, 512:1024], AF.Exp, ["rc"], ["rc"], scale=-1.0)
                    else:
                        recip(rc[64:128, 0:512], bank(oa)[64:128, :], [("ps", oa)], ["rc", ("rcj", 0), ("rcj", 1)])
                        recip(rc[0:64, 512:1024], bank(oa + 1)[0:64, :], [("ps", oa + 1)], ["rc", ("rcj", 0), ("rcj", 1)])
                    tt(attnT[g][0:64, :], bank(oa)[0:64, :], rc[64:128, 0:512], ALU.mult, [("ps", oa), "rc", ("rcj", 0), ("rcj", 1)], [("big", 16 + g)])
                    tt(attnT[g][64:128, :], bank(oa + 1)[64:128, :], rc[0:64, 512:1024], ALU.mult, [("ps", oa + 1), "rc", ("rcj", 0), ("rcj", 1)], [("big", 16 + g)])

            qk(0)
            qk(1)
            for i in range(len(steps)):
                if i + 2 < len(steps):
                    qk(i + 2)
                pv(i)
                if i in (1, 3, 5, 7):
                    spatial_gm((i - 1) // 2)

            if qb + 1 < nqb:
                nslot = load_x(row0 + 512, 4)

            sga = [load_unit(3), None]
            sgb = [load_unit(4), None]
            sbra = load_unit(5)
            sbrg = load_unit(6)
            for m in range(8):
                if m == 4:
                    sga[1] = load_unit(7)
                    sgb[1] = load_unit(8)
                h = m // 4
                b0 = (m % 2) * 4
                for c in range(8):
                    mm(bank(b0), unit8(sga[h])[:, c, (m % 4) * 128:(m % 4 + 1) * 128], hT[:, c, :], c == 0, c == 7, [("hT", c), R(sga[h])], [("ps", b0)])
                for c in range(8):
                    mm(bank(b0 + 1), unit8(sgb[h])[:, c, (m % 4) * 128:(m % 4 + 1) * 128], hT[:, c, :], c == 0, c == 7, [("hT", c), R(sgb[h])], [("ps", b0 + 1)])
                for c in range(4):
                    mm(bank(b0 + 2), unit4(sbra)[:, c, m * 128:(m + 1) * 128], attnT[c], c == 0, c == 3, [("big", 16 + c), R(sbra)], [("ps", b0 + 2)])
                for c in range(4):
                    mm(bank(b0 + 3), unit4(sbrg)[:, c, m * 128:(m + 1) * 128], gmT[c], c == 0, c == 3, [("big", 12 + c), R(sbrg)], [("ps", b0 + 3)])
                act(scr[0][:, :], bank(b0), AF.Sigmoid, [("ps", b0)], [("scr", 0)])
                act(scr[1][:, :], bank(b0 + 1), AF.Sigmoid, [("ps", b0 + 1)], [("scr", 1)])
                tt(scr[2][:, :], bank(b0 + 2), scr[0][:, :], ALU.mult, [("ps", b0 + 2), ("scr", 0)], [("scr", 2)])
                tt(scr[3][:, :], bank(b0 + 3), scr[1][:, :], ALU.mult, [("ps", b0 + 3), ("scr", 1)], [("scr", 3)])
                tt(yT[m], scr[2][:, :], scr[3][:, :], ALU.add, [("scr", 2), ("scr", 3)], [("big", m)], eng="pool")

            so = [load_unit(9), load_unit(10)]
            par = cnt["par"] % 2
            cnt["par"] += 1
            po = par * 4
            k8c = {"n": 0}

            def b8_mm(t):
                for nh in range(2):
                    bk = 4 + k8c["n"] % 4
                    k8c["n"] += 1
                    for c in range(8):
                        mm(bank(bk), yT[c][:, t * 128:(t + 1) * 128], unit8(so[nh])[:, c, :], c == 0, c == 7, [("big", c), R(so[nh])], [("ps", bk)])
                    sk = 4 + (k8c["n"] % 2)
                    tt(scr[sk][:, :], bank(bk), gbc[:, nh * 512:(nh + 1) * 512], ALU.mult, [("ps", bk), "gbc"], [("scr", sk)])
                    tt(xb[:, t, nh * 512:(nh + 1) * 512], scr[sk][:, :], xb[:, t, nh * 512:(nh + 1) * 512], ALU.add, [("scr", sk), xres], [xres, ("x1t", t)], eng="pool")

            def b9_tile(t):
                kq = cnt["sq"] % 2
                cnt["sq"] += 1
                cs = slice(po + t, po + t + 1)
                act(sqj[kq], xb[:, t, :], AF.Square, [("x1t", t)], [("ss", par, t), ("rcj", kq)], accum=ss[:, cs])
                act(rs[:, cs], ss[:, cs], AF.Ln, [("ss", par, t)], [("rs", par), ("rs", par, t)], scale=1.0 / D, bias=EPS)
                act(rstd[:, cs], rs[:, cs], AF.Exp, [("rs", par, t)], [("rstd", par), ("rstd", par, t)], scale=-0.5)
                k = cnt["xn"] % 2
                cnt["xn"] += 1
                xn_tile(xb, ("x1t", t), t, k, rstd[:, cs], ("rstd", par, t))

            for t in range(4):
                b8_mm(t)
                if t >= 1:
                    b9_tile(t - 1)
            b9_tile(3)
            nm_evac(dict(nt=4), (a2, "a2", "modTb"), 24, 0)

            pq = pre_q_pieces(nslot) if nslot is not None else None
            if pq:
                pq["stats"]()
            for j in range(32):
                if j % 4 == 0:
                    sf = load_unit(11 + j // 4)
                bk = j % 8
                for c in range(8):
                    mm(bank(bk), unit8(sf)[:, c, (j % 4) * 128:(j % 4 + 1) * 128], hT[:, c, :], c == 0, c == 7, [("hT", c), R(sf)], [("ps", bk)])
                sk = j % 4
                act(scr[sk][:, :], bank(bk), AF.Relu, [("ps", bk)], [("scr", sk)])
                tt(big[:, j, :], scr[sk][:, :], scr[sk][:, :], ALU.mult, [("scr", sk)], [("big", j)])
                if pq and j == 27:
                    pq["xs"](0)
                    pq["xs"](1)

            if pq:
                pq["xt"](0)
                pq["xt"](1)
                pq["xs"](2)
                pq["xs"](3)
            blk = 0
            for nh in range(2):
                for kg in range(4):
                    s2 = load_unit(19 + nh * 4 + kg)
                    for t in range(4):
                        bk = t
                        for c in range(8):
                            mm(bank(bk), big[:, kg * 8 + c, t * 128:(t + 1) * 128], unit8(s2)[:, c, :], kg == 0 and c == 0, kg == 3 and c == 7,
                               [("big", kg * 8 + c), R(s2)], [("ps", bk)])
                    if pq:
                        if blk == 0:
                            pq["xt"](2)
                            pq["xt"](3)
                            pq["evac"]()
                        elif blk == 1:
                            pq["qproj"]()
                        elif blk == 2:
                            pq["qdve"](0)
                        elif blk in (3, 4, 5):
                            pq["qpe"](blk - 3)
                            pq["qdve"](blk - 2)
                        elif blk == 6:
                            pq["qpe"](3)
                    blk += 1
                for t in range(4):
                    bk = t
                    sk = 5
                    tt(scr[sk][:, :], bank(bk), gbc[:, 1024 + nh * 512:1024 + (nh + 1) * 512], ALU.mult, [("ps", bk), "gbc"], [("scr", sk)])
                    tt(xb[:, t, nh * 512:(nh + 1) * 512], scr[sk][:, :], xb[:, t, nh * 512:(nh + 1) * 512], ALU.add, [("scr", sk), xres], [xres], eng="pool")
            if qb + 1 < nqb:
                pre_units = (load_unit(1), load_unit(2))
            dma("sp", out_d[qb * 512:(qb + 1) * 512, :].rearrange("(t p) d -> p t d", p=128), xb[:, :, :], [xres], [("out", qb)], f"o{slot}")
            slot = nslot

        T.add("sp", lambda e: e.nop(), [("out", q) for q in range(nqb)] + [("dbgout", n) for n in dbg], [])

        T.finalize()
        nc._tracker = T

        @block.sync
        def _(e):
            T.emit("sp", e, esems, dsems)

        @block.tensor
        def _(e):
            T.emit("pe", e, esems, dsems)

        @block.scalar
        def _(e):
            T.emit("act", e, esems, dsems)

        @block.vector
        def _(e):
            T.emit("dve", e, esems, dsems)

        @block.gpsimd
        def _(e):
            T.emit("pool", e, esems, dsems)

    return nc


_CACHE = {}


def _rope_table(tok_idx):
    n = tok_idx.shape[0]
    t = np.maximum(tok_idx, 0)
    row = (t // 64).astype(np.float32)
    colp = (t % 64).astype(np.float32)
    inv = (np.float32(10000.0) ** (-np.arange(0, 32, 2, dtype=np.float32) / np.float32(32))).astype(np.float32)
    ang = np.concatenate([row[:, None] * inv[None, :], colp[:, None] * inv[None, :]], axis=-1).astype(np.float32)
    cos = np.cos(ang).astype(np.float32)
    sin = np.sin(ang).astype(np.float32)
    ident = tok_idx < 0
    cos[ident] = 1.0
    sin[ident] = 0.0
    return np.concatenate([cos, cos, -sin, sin], axis=1).astype(np.float32)


def kernel(x, c, ctx, c_ctx, w_mod, b_mod, norm1_g, norm2_g, w_in, q_norm_g, k_norm_g,
           gm_norm_g, gm_ws, gm_bs, w_br_attn, w_br_gm, w_out, w_ff1, w_ff2):
    f = lambda a: np.ascontiguousarray(np.asarray(a, dtype=np.float32))
    x, c, ctx, c_ctx = f(x), f(c), f(ctx), f(c_ctx)
    w_mod, b_mod, w_in = f(w_mod)[0], f(b_mod)[0], f(w_in)[0]
    n1, n2 = f(norm1_g)[0], f(norm2_g)[0]
    qg, kg, gmg = f(q_norm_g)[0], f(k_norm_g)[0], f(gm_norm_g)[0]
    ws, bs = f(gm_ws)[0], f(gm_bs)[0]
    bra, brg, wo, w1, w2 = f(w_br_attn)[0], f(w_br_gm)[0], f(w_out)[0], f(w_ff1)[0], f(w_ff2)[0]

    if "nc" not in _CACHE:
        _CACHE["nc"] = build_program()
    nc = _CACHE["nc"]

    qcols = 256 + np.array([kv * 256 + g * 64 + d for g in range(4) for kv in range(2) for d in range(64)])
    order = np.concatenate([np.arange(0, 256), qcols, np.arange(1280, 1792), np.arange(768, 1280), np.arange(1792, 3840)])
    w_in_p = np.ascontiguousarray(w_in[:, order])
    rows = np.array([kv * 256 + g * 64 + d for g in range(4) for kv in range(2) for d in range(64)])
    w_bra_p = np.ascontiguousarray(bra[rows, :])
    b_modT = np.ascontiguousarray(b_mod.reshape(48, 128).T)
    b_modg = np.ascontiguousarray(np.concatenate([b_mod[2048:3072], b_mod[5120:6144]])[None, :])
    n1g = np.ascontiguousarray(n1.reshape(8, 128).T)
    n2g = np.ascontiguousarray(n2.reshape(8, 128).T)
    gq_bc = np.ascontiguousarray(np.broadcast_to(np.tile(qg, 8)[None, :], (128, 512)))
    gk_bc = np.ascontiguousarray(np.broadcast_to(np.tile(kg, 2)[None, :], (128, 128)))
    gmg_bc = np.ascontiguousarray(np.broadcast_to(gmg.reshape(512)[None, :], (128, 512)))
    wsT = np.ascontiguousarray(ws.transpose(2, 0, 1))
    bsT = np.ascontiguousarray(np.broadcast_to(bs.reshape(4, 2, 1, 128), (4, 2, 64, 128)).transpose(1, 2, 0, 3).reshape(128, 4, 128))
    ident = np.eye(128, dtype=np.float32)

    in_maps = []
    for core in range(8):
        b, hf = core // 2, core % 2
        own = np.arange(hf * 4096, (hf + 1) * 4096)
        oth = np.arange((1 - hf) * 4096, (2 - hf) * 4096)
        xin = np.concatenate([ctx[b], x[b, own], x[b, oth]], axis=0)
        tok = np.concatenate([-np.ones(256, dtype=np.int64), own, oth])
        cT = np.ascontiguousarray(np.stack([c[b], c_ctx], axis=1).reshape(8, 128, 2).transpose(1, 0, 2))
        in_maps.append({
            "xin": np.ascontiguousarray(xin), "rope": _rope_table(tok), "cT": cT, "w_mod": w_mod, "b_modT": b_modT,
            "b_modg": b_modg, "n1g": n1g, "n2g": n2g, "w_in_p": w_in_p, "gq_bc": gq_bc, "gk_bc": gk_bc, "gmg_bc": gmg_bc,
            "wsT": wsT, "bsT": bsT, "w_bra_p": w_bra_p, "w_brg": brg, "w_out": wo, "w_ff1": w1, "w_ff2": w2, "ident": ident,
        })
    res = run_bass_kernel_spmd(nc, in_maps, core_ids=list(range(8)))
    out = np.empty((4, 8192, D), dtype=np.float32)
    for core in range(8):
        b, hf = core // 2, core % 2
        out[b, hf * 4096:(hf + 1) * 4096] = res.results[core]["out"]
    return out
```

```python
import numpy as np
import concourse.bass as bass
import concourse.mybir as mybir
from concourse.bass_utils import run_bass_kernel_spmd

F32 = mybir.dt.float32
BF16 = mybir.dt.bfloat16
AF = mybir.ActivationFunctionType
ALU = mybir.AluOpType
AX = mybir.AxisListType

D = 1024
NCTX_T = 2
NOWN_T = 32
NT = 66
NQB = 8
EPS = 1e-6
NB = 5
N_UNITS_QB = 27


class Tracker:
    def __init__(self):
        self.ops = []
        self.last_w = {}
        self.readers = {}
        self.dcount = {}

    def add(self, eng, fn, reads=(), writes=(), dsem=None):
        idx = len(self.ops)
        deps = set()
        if eng in ("act", "dve"):
            writes = list(writes) + [("pslk", r[1]) for r in reads if isinstance(r, tuple) and r[0] == "ps"]
        for r in reads:
            if r in self.last_w:
                deps.add(self.last_w[r])
        for w in writes:
            if w in self.last_w:
                deps.add(self.last_w[w])
            for rd in self.readers.get(w, ()):
                deps.add(rd)
        op = dict(eng=eng, fn=fn, deps=deps, dsem=dsem, marked=False, idx=idx, val=None, desc=(tuple(reads), tuple(writes)))
        if dsem is not None:
            self.dcount[dsem] = self.dcount.get(dsem, 0) + 16
            op["dval"] = self.dcount[dsem]
        self.ops.append(op)
        for r in reads:
            self.readers.setdefault(r, []).append(idx)
        for w in writes:
            self.last_w[w] = idx
            self.readers[w] = []
        return idx

    def finalize(self):
        ops = self.ops
        for op in ops:
            red = {}
            for d in op["deps"]:
                dop = ops[d]
                if dop["dsem"] is not None:
                    key = ("d", dop["dsem"])
                    if key not in red or ops[red[key]]["dval"] < dop["dval"]:
                        red[key] = d
                else:
                    if dop["eng"] == "pe" and op["eng"] == "pe" and op["dsem"] is None:
                        continue
                    key = ("e", dop["eng"])
                    if key not in red or red[key] < d:
                        red[key] = d
            op["rdeps"] = list(red.values())
            for d in op["rdeps"]:
                if ops[d]["dsem"] is None:
                    ops[d]["marked"] = True
        cnt = {}
        for op in ops:
            if op["dsem"] is None and op["marked"]:
                cnt[op["eng"]] = cnt.get(op["eng"], 0) + 1
                op["val"] = cnt[op["eng"]]

    def trace(self, engname):
        waited = {}
        out = []
        for op in self.ops:
            if op["eng"] != engname:
                continue
            ws = []
            for d in op["rdeps"]:
                dop = self.ops[d]
                if dop["dsem"] is not None:
                    val = self.dcount[dop["dsem"]] if dop["dsem"] in ("const", "cast", "gb") else dop["dval"]
                    key = ("d", dop["dsem"])
                else:
                    val, key = dop["val"], ("e", dop["eng"])
                if waited.get(key, 0) < val:
                    ws.append((key[1], val))
                    waited[key] = val
            inc = (op["dsem"], op.get("dval")) if op["dsem"] else ((engname, op["val"]) if op["marked"] else None)
            out.append((op["idx"], ws, op["desc"], inc))
        return out

    def emit(self, engname, engobj, esems, dsems):
        waited = {}
        for op in self.ops:
            if op["eng"] != engname:
                continue
            for d in op["rdeps"]:
                dop = self.ops[d]
                if dop["dsem"] is not None:
                    val = self.dcount[dop["dsem"]] if dop["dsem"] in ("const", "cast", "gb") else dop["dval"]
                    sem, key = dsems[dop["dsem"]], ("d", dop["dsem"])
                else:
                    sem, val, key = esems[dop["eng"]], dop["val"], ("e", dop["eng"])
                if waited.get(key, 0) < val:
                    engobj.wait_ge(sem, val)
                    waited[key] = val
            ins = op["fn"](engobj)
            if op["dsem"] is not None:
                ins.then_inc(dsems[op["dsem"]], 16)
            elif op["marked"]:
                ins.then_inc(esems[engname], 1)


def build_program(stage=3, nqb=NQB, skip=()):
    nc = bass.Bass("TRN2", target_bir_lowering=False)
    T = Tracker()
    dbg = {}

    def din(name, shape, dt=F32):
        return nc.dram_tensor(name, list(shape), dt, kind="ExternalInput").ap()

    xin = din("xin", [NT * 128, D])
    rope = din("rope", [NT * 128, 128])
    cT_d = din("cT", [128, 8, 2])
    wmod_d = din("w_mod", [D, 6 * D])
    bmodT_d = din("b_modT", [128, 48])
    bmodg_d = din("b_modg", [1, 2048])
    n1g_d = din("n1g", [128, 8])
    n2g_d = din("n2g", [128, 8])
    win_d = din("w_in_p", [D, 3840])
    gq_d = din("gq_bc", [128, 512])
    gk_d = din("gk_bc", [128, 128])
    gmg_d = din("gmg_bc", [128, 512])
    wsT_d = din("wsT", [128, 8, 128])
    bsT_d = din("bsT", [128, 4, 128])
    bra_d = din("w_bra_p", [512, D])
    brg_d = din("w_brg", [512, D])
    wout_d = din("w_out", [D, D])
    ff1_d = din("w_ff1", [D, 4 * D])
    ff2_d = din("w_ff2", [4 * D, D])
    ident_d = din("ident", [128, 128])
    out_d = nc.dram_tensor("out", [NOWN_T * 128, D], F32, kind="ExternalOutput").ap()
    gscr = nc.dram_tensor("gscr", [1, 2048], F32, kind="Internal").ap()
    wsc = nc.dram_tensor("wscratch", [N_UNITS_QB, 128, 4096], BF16, kind="Internal").ap()

    import contextlib
    es = contextlib.ExitStack()

    def sb(name, shape, dt):
        return es.enter_context(nc.sbuf_tensor(name, list(shape), dt))

    with es:
        KT = sb("KT", [128, NT * 128], BF16)
        Vaug = sb("Vaug", [128, NT, 192], BF16)
        xbuf = [sb(f"xbuf{i}", [128, 4, D], F32) for i in range(2)]
        ropeb = [sb(f"ropeb{i}", [128, 4, 128], F32) for i in range(2)]
        xn = [sb(f"xn{i}", [128, D], BF16) for i in range(2)]
        hT = sb("hT", [128, 8, 512], BF16)
        big = sb("big", [128, 32, 512], BF16)
        ringT = sb("ring", [128, NB * 4096], BF16)
        ring = [ringT[:, i * 4096:(i + 1) * 4096] for i in range(NB)]
        gbc = sb("gbc", [128, 2048], F32)
        scr = [sb(f"scr{i}", [128, 512], F32) for i in range(6)]
        rc = sb("rc", [128, 1024], F32)
        QTb = sb("QTb", [128, 4, 512], BF16)
        Wkv = sb("Wkv", [128, 8, 256], BF16)
        identb = sb("identb", [128, 128], BF16)
        gq = sb("gq", [128, 512], F32)
        gk = sb("gk", [128, 128], F32)
        gmg = sb("gmg", [128, 512], F32)
        wsT = sb("wsTb", [128, 8, 128], BF16)
        bsT = sb("bsTs", [128, 4, 128], F32)
        cT = sb("cTs", [128, 8, 2], F32)
        scT = sb("scT", [128, 8, 2], F32)
        bmodT = sb("bmodTs", [128, 48], F32)
        n1g = sb("n1gs", [128, 8], F32)
        n2g = sb("n2gs", [128, 8], F32)
        modT = sb("modT", [128, 48, 2], F32)
        a1 = sb("a1", [128, 8, 2], F32)
        a2 = sb("a2", [128, 8, 2], F32)
        ones1 = sb("ones1", [1, 128], F32)
        ss = sb("ss", [128, 8], F32)
        rs = sb("rs", [128, 8], F32)
        rstd = sb("rstd", [128, 8], F32)
        hs = sb("hs", [128, 8], F32)
        hl = sb("hl", [128, 8], F32)
        hr = sb("hr", [128, 8], F32)
        hs32 = sb("hs32", [128, 32], F32)
        hl32 = sb("hl32", [128, 32], F32)
        hr32 = sb("hr32", [128, 32], F32)
        ps = es.enter_context(nc.psum_tensor("ps", [128, 4096], F32))
        sqj = [rc[:, 0:512].bitcast(BF16), rc[:, 512:1024].bitcast(BF16)]
        grow = big[0:1, 0:8, :].rearrange("p a b -> p (a b)").bitcast(F32)
        bmodg = big[0:1, 8:16, :].rearrange("p a b -> p (a b)").bitcast(F32)

        esems = {e: es.enter_context(nc.semaphore("sem_" + e)) for e in ["pe", "act", "dve", "pool", "sp"]}
        dnames = [f"c{i}" for i in range(6)] + ["dbg", "gb", "gb2", "const", "cast", "x0", "x1", "r0", "r1", "o0", "o1", "wm0", "wm1", "wm2"] + [f"w{i}" for i in range(NB)]
        dsems = {d: es.enter_context(nc.semaphore("ds_" + d)) for d in dnames}
        block = es.enter_context(nc.Block())

        def bank(b):
            return ps[:, b * 512:(b + 1) * 512]

        def mm(out, lhsT, rhs, start, stop, reads, writes, sgc=False):
            T.add("pe", lambda e: e.matmul(out, lhsT=lhsT, rhs=rhs, start=start, stop=stop, skip_group_check=sgc), reads, writes)

        def tr(out, in_, reads, writes):
            T.add("pe", lambda e: e.transpose(out, in_, identb[:, :]), list(reads) + ["identb"], writes)

        def act(out, in_, func, reads, writes, scale=None, bias=None, accum=None):
            kw = {}
            if scale is not None:
                kw["scale"] = scale
            if bias is not None:
                kw["bias"] = bias
            if accum is not None:
                kw["accum_out"] = accum
            T.add("act", lambda e: e.activation(out=out, in_=in_, func=func, **kw), reads, writes)

        def tt(out, in0, in1, op, reads, writes, eng="dve"):
            T.add(eng, lambda e: e.tensor_tensor(out=out, in0=in0, in1=in1, op=op), reads, writes)

        def ts(out, in0, s1, s2, op0, op1, reads, writes, eng="dve"):
            if op1 is None:
                T.add(eng, lambda e: e.tensor_scalar(out=out, in0=in0, scalar1=s1, scalar2=None, op0=op0), reads, writes)
            else:
                T.add(eng, lambda e: e.tensor_scalar(out=out, in0=in0, scalar1=s1, scalar2=s2, op0=op0, op1=op1), reads, writes)

        def stt(out, in0, scalar, in1, op0, op1, reads, writes):
            T.add("dve", lambda e: e.scalar_tensor_tensor(out=out, in0=in0, scalar=scalar, in1=in1, op0=op0, op1=op1), reads, writes)

        def recip(out, in_, reads, writes):
            T.add("dve", lambda e: e.reciprocal(out=out, in_=in_), reads, writes)

        def cp(out, in_, reads, writes, eng="dve"):
            T.add(eng, lambda e: e.tensor_copy(out=out, in_=in_), reads, writes)

        def dma(q, out, in_, reads, writes, dsem):
            T.add(q, lambda e: e.dma_start(out=out, in_=in_), reads, writes, dsem=dsem)

        def memset(ap, val, writes, eng="pool"):
            T.add(eng, lambda e: e.memset(ap, val), (), writes)

        for (dst, src, nm) in [(cT[:], cT_d, "cT"), (bmodT[:], bmodT_d, "bmodT"), (bmodg[:], bmodg_d, "bmodg"),
                               (n1g[:], n1g_d, "n1g"), (n2g[:], n2g_d, "n2g"), (gq[:], gq_d, "gq"), (gk[:], gk_d, "gk"),
                               (gmg[:], gmg_d, "gmg"), (bsT[:], bsT_d, "bsT")]:
            dma("sp", dst, src, (), [nm], "const")
        dma("pool", identb[:], ident_d, (), ["identb"], "cast")
        dma("pool", Wkv[:], win_d[:, 0:256].rearrange("(c p) n -> p c n", p=128), (), ["Wkv"], "cast")
        dma("pool", wsT[:], wsT_d, (), ["wsT"], "cast")
        memset(Vaug[:, :, 64:128], 1.0, [("Vaug", t) for t in range(NT)])
        memset(ones1[:], 1.0, ["ones1"])

        def wsrc_kn(w, c0, ncols):
            return w[:, c0:c0 + ncols].rearrange("(c p) n -> p c n", p=128)

        unit_src = []
        unit_src.append((wsrc_kn(win_d, 256, 512), 8))
        unit_src.append((wsrc_kn(win_d, 768, 512), 8))
        unit_src.append((wsrc_kn(win_d, 1280, 512), 8))
        unit_src.append((wsrc_kn(win_d, 1792, 512), 8))
        unit_src.append((wsrc_kn(win_d, 2816, 512), 8))
        unit_src.append((bra_d.rearrange("(c p) n -> p c n", p=128), 4))
        unit_src.append((brg_d.rearrange("(c p) n -> p c n", p=128), 4))
        unit_src.append((wsrc_kn(win_d, 2304, 512), 8))
        unit_src.append((wsrc_kn(win_d, 3328, 512), 8))
        unit_src.append((wsrc_kn(wout_d, 0, 512), 8))
        unit_src.append((wsrc_kn(wout_d, 512, 512), 8))
        for j in range(8):
            unit_src.append((wsrc_kn(ff1_d, j * 512, 512), 8))
        for nh in range(2):
            for kg in range(4):
                src = ff2_d[kg * 1024:(kg + 1) * 1024, nh * 512:(nh + 1) * 512].rearrange("(c p) n -> p c n", p=128)
                unit_src.append((src, 8))
        assert len(unit_src) == N_UNITS_QB
        def cast_unit(u, extra_reads=()):
            src, nch = unit_src[u]
            dst = wsc[u].rearrange("p (c n) -> p c n", c=nch)
            dma("pool", dst, src, [("cslot", u % 6)] + list(extra_reads), [("wsc", u), ("cslot", u % 6)], f"c{u % 6}")

        N_EARLY = 11 if nqb else N_UNITS_QB
        for u in range(N_EARLY):
            if "cast" in skip:
                break
            cast_unit(u)

        act(scT[:], cT[:], AF.Silu, ["cT"], ["scT"])
        for c in range(8):
            s_ = c % NB
            pa = ring[s_].bitcast(F32)
            dma("sp" if c % 2 == 0 else "act", pa, wmod_d[c * 128:(c + 1) * 128, 0:2048], (), [("ring", s_)], f"w{s_}")
            for jj in range(16):
                mm(ps[:, 2 * jj:2 * jj + 2], pa[:, jj * 128:(jj + 1) * 128], scT[:, c, :], c == 0 and jj == 0, c == 7 and jj == 15,
                   [("ring", s_), "scT"], [("ps", 0)], sgc=True)
        tt(modT[:, 0:16, :], ps[:, 0:32].rearrange("p (j k) -> p j k", k=2),
           bmodT[:, 0:16].unsqueeze(2).broadcast_to([128, 16, 2]), ALU.add, [("ps", 0), "bmodT"], ["modTa"])
        stt(a1[:], modT[:, 8:16, :], 1.0, n1g[:, :].unsqueeze(2).broadcast_to([128, 8, 2]), ALU.add, ALU.mult, ["modTa", "n1g"], ["a1"])

        def mod_b_buf(c):
            p_ = c % 2
            return p_, ringT[:, (2 * p_) * 4096:(2 * p_ + 2) * 4096].bitcast(F32)

        def mod_b_dma(c):
            p_, pb = mod_b_buf(c)
            dma("sp", pb, wmod_d[c * 128:(c + 1) * 128, 2048:6144], (), [("ring", 2 * p_), ("ring", 2 * p_ + 1)], f"wm{p_}")

        def mod_b_mm(c, j0, j1):
            p_, pb = mod_b_buf(c)
            for jj in range(j0, j1):
                mm(ps[:, 7 * 512 + 2 * jj:7 * 512 + 2 * jj + 2], pb[:, jj * 128:(jj + 1) * 128], scT[:, c, :], c == 0 and jj == 0, c == 7 and jj == 31,
                   [("ring", 2 * p_), ("ring", 2 * p_ + 1), "scT"], [("ps", 7)], sgc=True)

        def mod_b_piece(c):
            mod_b_dma(c)
            mod_b_mm(c, 0, 32)

        def mod_b_finish():
            tt(modT[:, 16:48, :], ps[:, 7 * 512:7 * 512 + 64].rearrange("p (j k) -> p j k", k=2),
               bmodT[:, 16:48].unsqueeze(2).broadcast_to([128, 32, 2]), ALU.add, [("ps", 7), "bmodT"], ["modTb"])
            stt(a2[:], modT[:, 32:40, :], 1.0, n2g[:, :].unsqueeze(2).broadcast_to([128, 8, 2]), ALU.add, ALU.mult, ["modTb", "n2g"], ["a2"])
            for k_, j0 in enumerate((16, 40)):
                T.add("pool", lambda e, k_=k_, j0=j0: e.dma_start(out=gscr[0, k_ * 1024:(k_ + 1) * 1024].rearrange("(c p) -> p c", p=128),
                                                             in_=modT[:, j0:j0 + 8, 0], allow_slow_non_contiguous=True),
                      ["modTb"], [("gscr", k_)], dsem="gb")
            dma("pool", gbc[:], gscr[0, :].partition_broadcast(128), [("gscr", 0), ("gscr", 1)], ["gbc"], "gb2")

        cnt = {"xn": 0, "ev": 0, "sq": 0, "par": 0}

        def nm_stats(xb, xres, nt):
            par = cnt["par"] % 2
            cnt["par"] += 1
            po = par * 4
            for t in range(nt):
                kq = cnt["sq"] % 2
                cnt["sq"] += 1
                act(sqj[kq], xb[:, t, :], AF.Square, [xres], [("ss", par, t), ("rcj", kq)], accum=ss[:, po + t:po + t + 1])
            act(rs[:, po:po + nt], ss[:, po:po + nt], AF.Ln, [("ss", par, t) for t in range(nt)], [("rs", par)], scale=1.0 / D, bias=EPS)
            act(rstd[:, po:po + nt], rs[:, po:po + nt], AF.Exp, [("rs", par)], [("rstd", par)], scale=-0.5)
            return dict(xb=xb, xres=xres, nt=nt, par=par, po=po, use_act=not cnt.get("phaseA", False))

        def xn_scale(xb, xres, t, k, rs_ap, rs_key, use_act=True):
            if t % 2 == 0 or not use_act:
                ts(xn[k][:], xb[:, t, :], rs_ap, None, ALU.mult, None, [xres, rs_key], [("xn", k)])
            else:
                act(xn[k][:], xb[:, t, :], AF.Copy, [xres, rs_key], [("xn", k)], scale=rs_ap)

        def xn_tr(t, k, pb=0):
            for c in range(8):
                bk = pb + c // 2
                o = bank(bk).bitcast(BF16)[:, (c % 2) * 512 + t * 128:(c % 2) * 512 + (t + 1) * 128]
                tr(o, xn[k][:, c * 128:(c + 1) * 128], [("xn", k)], [("ps", bk)])

        def xn_tile(xb, xres, t, k, rs_ap, rs_key, pb=0, use_act=True):
            xn_scale(xb, xres, t, k, rs_ap, rs_key, use_act)
            xn_tr(t, k, pb)

        def _unused_xn_tile(xb, xres, t, k, rs_ap, rs_key, pb=0, use_act=True):
            for c in range(8):
                bk = pb + c // 2
                o = bank(bk).bitcast(BF16)[:, (c % 2) * 512 + t * 128:(c % 2) * 512 + (t + 1) * 128]
                tr(o, xn[k][:, c * 128:(c + 1) * 128], [("xn", k)], [("ps", bk)])

        def nm_xn_tr(cx):
            xb, xres, nt, par, po = cx["xb"], cx["xres"], cx["nt"], cx["par"], cx["po"]
            for t in range(nt):
                k = cnt["xn"] % 2
                cnt["xn"] += 1
                xn_tile(xb, xres, t, k, rstd[:, po + t:po + t + 1], ("rstd", par), use_act=cx.get("use_act", True))

        def nm_evac(cx, a_t, sh_off, col, hd=None, pb=0):
            nt = cx["nt"]
            hdst, hkey = hd if hd is not None else (hT, "hT")
            for c in range(8):
                bk = pb + c // 2
                src = bank(bk).bitcast(BF16)[:, (c % 2) * 512:(c % 2) * 512 + nt * 128]
                if c % 2 == 0:
                    act(hdst[:, c, 0:nt * 128], src, AF.Identity, [("ps", bk), a_t[1], a_t[2]], [(hkey, c)],
                        scale=a_t[0][:, c, col:col + 1], bias=modT[:, sh_off + c, col:col + 1])
                else:
                    ts(hdst[:, c, 0:nt * 128], src, a_t[0][:, c, col:col + 1], modT[:, sh_off + c, col:col + 1], ALU.mult, ALU.add,
                       [("ps", bk), a_t[1], a_t[2]], [(hkey, c)])

        def norm_mod_T(xb, xres, nt, a_t, sh_off, col, hd=None):
            cx = nm_stats(xb, xres, nt)
            nm_xn_tr(cx)
            nm_evac(cx, a_t, sh_off, col, hd)

        def head_rstd(src_sq, nh, res_in):
            T.add("dve", lambda e: e.tensor_reduce(out=hs[:, 0:nh], in_=src_sq.rearrange("p (h d) -> p h d", d=64), axis=AX.X, op=ALU.add),
                  [res_in], ["hs"])
            act(hl[:, 0:nh], hs[:, 0:nh], AF.Ln, ["hs"], ["hl"], scale=1.0 / 64, bias=EPS)
            act(hr[:, 0:nh], hl[:, 0:nh], AF.Exp, ["hl"], ["hr"], scale=-0.5)

        def norm_rope(psrc, psres, nh, gain, gres, rp, rpres, outb, outres):
            W = nh * 64
            s0, s1, s2, s3 = scr[0][:, 0:W], scr[1][:, 0:W], scr[2][:, 0:W], scr[3][:, 0:W]
            act(s0, psrc, AF.Square, [psres], [("scr", 0)])
            head_rstd(s0, nh, ("scr", 0))
            tt(s1, psrc, gain, ALU.mult, [psres, gres], [("scr", 1)])
            tt(s2.rearrange("p (h d) -> p h d", d=64), s1.rearrange("p (h d) -> p h d", d=64),
               hr[:, 0:nh].unsqueeze(2).broadcast_to([128, nh, 64]), ALU.mult, [("scr", 1), "hr"], [("scr", 2)])
            v2 = s2.rearrange("p (h d) -> p h d", d=64)
            tt(s3.rearrange("p (h d) -> p h d", d=64), v2, rp[:, 0:64].unsqueeze(1).broadcast_to([128, nh, 64]), ALU.mult,
               [("scr", 2), rpres], [("scr", 3)])
            v0 = s0.rearrange("p (h d) -> p h d", d=64)
            tt(v0[:, :, 0:32], v2[:, :, 32:64], rp[:, 64:96].unsqueeze(1).broadcast_to([128, nh, 32]), ALU.mult,
               [("scr", 2), rpres], [("scr", 0)])
            tt(v0[:, :, 32:64], v2[:, :, 0:32], rp[:, 96:128].unsqueeze(1).broadcast_to([128, nh, 32]), ALU.mult,
               [("scr", 2), rpres], [("scr", 0)])
            tt(outb, s3, s0, ALU.add, [("scr", 3), ("scr", 0)], [outres])

        def dump(name, ap, res, dt=F32):
            if "nodump" in skip:
                return
            d = nc.dram_tensor("dbg_" + name, list(ap.shape), F32, kind="ExternalOutput").ap()
            dbg[name] = d
            dma("pool", d, ap, res, [("dbgout", name)], "dbg")

        ld = {"n": 0}

        def load_xo(row0, nt, s):
            dma("sp", xbuf[s][:, 0:nt, :], xin[row0:row0 + nt * 128, :].rearrange("(t p) d -> p t d", p=128), (), [("xbuf", s)], f"x{s}")

        def load_rope(row0, nt, s):
            dma("sp", ropeb[s][:, 0:nt, :], rope[row0:row0 + nt * 128, :].rearrange("(t p) d -> p t d", p=128), (), [("ropeb", s)], f"r{s}")

        def load_x(row0, nt):
            s = ld["n"] % 2
            ld["n"] += 1
            load_xo(row0, nt, s)
            load_rope(row0, nt, s)
            return s

        supers = [(0, 2, 1)] + [(256 + i * 512, 4, 0) for i in range(16)]
        if stage == 0:
            supers = []
            for c in range(8):
                mod_b_piece(c)
            mod_b_finish()
            dump("modT", modT[:], ["modTa", "modTb"])
            dump("gbc", gbc[:], ["gbc"])
            dump("a1", a1[:], ["a1"])
        if stage == 1:
            import os
            supers = supers[:int(os.environ.get("NSUP", "3"))]
        krb = big[:, 24:26, :].rearrange("p a b -> p (a b)")
        nsup = len(supers)
        hbufs = [(hT, "hT"), (big[:, 0:8, :], "big")]
        a1t = (a1, "a1", "modTa")

        def a_kv(si):
            row0, nt, col = supers[si]
            hA, hAk = hbufs[si % 2]
            for t in range(nt):
                bk = 4 + t // 2
                for c in range(8):
                    mm(ps[:, bk * 512 + (t % 2) * 256: bk * 512 + (t % 2) * 256 + 256], hA[:, c, t * 128:(t + 1) * 128], Wkv[:, c, :],
                       c == 0, c == 7, [(hAk, c), "Wkv"], [("ps", bk)])
                if 1 <= si <= 16:
                    h_ = (si - 1) % 2
                    mod_b_mm((si - 1) // 2, 16 * h_ + 4 * t, 16 * h_ + 4 * t + 4)

        def a_post(si):
            row0, nt, col = supers[si]
            slot = si % 2
            tile0 = row0 // 128
            W = nt * 128
            kvv = ps[:, 4 * 512:4 * 512 + nt * 256].rearrange("p (t n) -> p t n", n=256)
            kview = kvv[:, :, 0:128]
            kvb = [("ps", 4)] + ([("ps", 5)] if nt > 2 else [])
            s0, s1, s2, s3 = scr[0][:, 0:W], scr[1][:, 0:W], scr[2][:, 0:W], scr[3][:, 0:W]
            tv = lambda a: a.rearrange("p (t n) -> p t n", n=128)
            hv = lambda a: a.rearrange("p (h d) -> p h d", d=64)
            qv = lambda a: a.rearrange("p (t h d) -> p t h d", h=2, d=64)
            rp = ropeb[slot]
            rpk = ("ropeb", slot)
            act(tv(s0), kview, AF.Square, kvb, [("scr", 0)])
            head_rstd(s0, 2 * nt, ("scr", 0))
            tt(tv(s1), kview, gk[:, :].unsqueeze(1).broadcast_to([128, nt, 128]), ALU.mult, kvb + ["gk"], [("scr", 1)])
            tt(hv(s2), hv(s1), hr[:, 0:2 * nt].unsqueeze(2).broadcast_to([128, 2 * nt, 64]), ALU.mult, [("scr", 1), "hr"], [("scr", 2)])
            tt(qv(s3), qv(s2), rp[:, 0:nt, 0:64].unsqueeze(2).broadcast_to([128, nt, 2, 64]), ALU.mult, [("scr", 2), rpk], [("scr", 3)])
            tt(qv(s0)[:, :, :, 0:32], qv(s2)[:, :, :, 32:64], rp[:, 0:nt, 64:96].unsqueeze(2).broadcast_to([128, nt, 2, 32]), ALU.mult,
               [("scr", 2), rpk], [("scr", 0)])
            tt(qv(s0)[:, :, :, 32:64], qv(s2)[:, :, :, 0:32], rp[:, 0:nt, 96:128].unsqueeze(2).broadcast_to([128, nt, 2, 32]), ALU.mult,
               [("scr", 2), rpk], [("scr", 0)])
            tt(krb[:, 0:W], s3, s0, ALU.add, [("scr", 3), ("scr", 0)], [("big", 24)])
            for t in range(nt):
                tr(bank(6).bitcast(BF16)[:, t * 128:(t + 1) * 128], krb[:, t * 128:(t + 1) * 128], [("big", 24)], [("ps", 6)])
            act(Vaug[:, tile0:tile0 + nt, 0:64], kvv[:, :, 128:192], AF.Copy, kvb, [("Vaug", tile0 + t) for t in range(nt)])
            act(Vaug[:, tile0:tile0 + nt, 128:192], kvv[:, :, 192:256], AF.Copy, kvb, [("Vaug", tile0 + t) for t in range(nt)])
            cp(KT[:, tile0 * 128:(tile0 + nt) * 128], bank(6).bitcast(BF16)[:, 0:nt * 128], [("ps", 6)], [("KT", si)])

        cnt["phaseA"] = True
        actx = {}
        if nsup:
            for k_ in range(min(2, nsup)):
                load_xo(supers[k_][0], supers[k_][1], k_ % 2)
                load_rope(supers[k_][0], supers[k_][1], k_ % 2)
            ld["n"] = 0
            actx[0] = nm_stats(xbuf[0], ("xbuf", 0), supers[0][1])
            nm_xn_tr(actx[0])
            if nsup > 2:
                load_xo(supers[2][0], supers[2][1], 0)
            nm_evac(actx[0], a1t, 0, supers[0][2], hbufs[0])
        for k_ in range(nsup):
            if k_ % 2 == 0 and k_ // 2 < 8:
                mod_b_dma(k_ // 2)
            n_ = k_ + 1
            if n_ < nsup:
                actx[n_] = nm_stats(xbuf[n_ % 2], ("xbuf", n_ % 2), supers[n_][1])
            a_kv(k_)
            if n_ < nsup:
                nm_xn_tr(actx[n_])
                if k_ + 3 < nsup:
                    load_xo(supers[k_ + 3][0], supers[k_ + 3][1], (k_ + 3) % 2)
            a_post(k_)
            if k_ + 2 < nsup:
                load_rope(supers[k_ + 2][0], supers[k_ + 2][1], k_ % 2)
            if n_ < nsup:
                nm_evac(actx[n_], a1t, 0, supers[n_][2], hbufs[n_ % 2])

        cnt["phaseA"] = False
        ld["n"] = 1
        slot = load_x(256, 4) if (nqb and stage > 1) else None
        if stage >= 1:
            if len(supers) < 17:
                raise NotImplementedError("debug stage with truncated phase A not supported any more")
            mod_b_finish()
        if stage == 1:
            dump("KT", KT[:, 0:1280], [("KT", i) for i in range(3)], BF16)
            dump("Vaug", Vaug[:, 0:10, :], [("Vaug", i) for i in range(10)], BF16)
            dump("hT", hT[:], [("hT", c) for c in range(8)], BF16)
        if stage <= 1:
            nqb = 0
        wctr = {"n": 0}

        def load_unit(u):
            s = wctr["n"] % NB
            wctr["n"] += 1
            dma("sp", ring[s][:, :], wsc[u], [("wsc", u)], [("ring", s)], f"w{s}")
            return s

        def R(s):
            return ("ring", s)

        def unit8(s):
            return ring[s][:, :].rearrange("p (c n) -> p c n", c=8)

        def unit4(s):
            return ring[s][:, :].rearrange("p (c n) -> p c n", c=4)

        yT = [big[:, c, :] for c in range(8)]
        uT = [big[:, 8 + j, :] for j in range(4)]
        gmT = [big[:, 12 + j, :] for j in range(4)]
        attnT = [big[:, 16 + j, :] for j in range(4)]
        QT = [QTb[:, j, :] for j in range(4)]
        vnb = [big[:, 16 + t, :] for t in range(4)]
        PT = [big[:, 28:30, :].rearrange("p a b -> p (a b)"), big[:, 30:32, :].rearrange("p a b -> p (a b)"),
              big[:, 26:28, :].rearrange("p a b -> p (a b)")]
        PTK = [[("big", 28), ("big", 29)], [("big", 30), ("big", 31)], [("big", 26), ("big", 27)]]

        def pre_q_pieces(slot_):
            xb_, xres_, rpb_ = xbuf[slot_], ("xbuf", slot_), ropeb[slot_]
            st = {}
            qrb = scr[4][:, :].bitcast(BF16)[:, 0:512]

            def p_stats():
                st["cx"] = nm_stats(xb_, xres_, 4)

            def p_xs(t):
                cx = st["cx"]
                k = cnt["xn"] % 2
                cnt["xn"] += 1
                st[("k", t)] = k
                xn_scale(xb_, xres_, t, k, rstd[:, cx["po"] + t:cx["po"] + t + 1], ("rstd", cx["par"]))

            def p_xt(t):
                xn_tr(t, st[("k", t)], pb=4)

            def p_xn(t):
                p_xs(t)
                p_xt(t)

            def p_evac():
                nm_evac(st["cx"], (a1, "a1", "modTa"), 0, 0, None, pb=4)

            def p_qproj():
                su = load_unit(0)
                for t in range(4):
                    for c in range(8):
                        mm(bank(4 + t), hT[:, c, t * 128:(t + 1) * 128], unit8(su)[:, c, :], c == 0, c == 7, [("hT", c), R(su)], [("ps", 4 + t)])

            def p_qdve(t):
                norm_rope(bank(4 + t), ("ps", 4 + t), 8, gq[:, :], "gq", rpb_[:, t, :], ("ropeb", slot_), qrb, ("scr", 4))

            def p_qpe(t):
                for g in range(4):
                    tr(bank(4 + t).bitcast(BF16)[:, g * 128:(g + 1) * 128], qrb[:, g * 128:(g + 1) * 128], [("scr", 4)], [("ps", 4 + t)])
                cp(QTb[:, :, t * 128:(t + 1) * 128], bank(4 + t).bitcast(BF16)[:, 0:512].rearrange("p (g q) -> p g q", q=128),
                   [("ps", 4 + t)], [("QT", g) for g in range(4)])

            return dict(stats=p_stats, xn=p_xn, xs=p_xs, xt=p_xt, evac=p_evac, qproj=p_qproj, qdve=p_qdve, qpe=p_qpe)

        def run_pre_q_all(pq):
            pq["stats"]()
            for t in range(4):
                pq["xn"](t)
            pq["evac"]()
            pq["qproj"]()
            for t in range(4):
                pq["qdve"](t)
                pq["qpe"](t)

        if nqb:
            run_pre_q_all(pre_q_pieces(slot))
            pre_units = (load_unit(1), load_unit(2))

        for qb in range(nqb):
            row0 = 256 + qb * 512
            xb = xbuf[slot]
            xres = ("xbuf", slot)
            rpb = ropeb[slot]
            nslot = None
            sv, suu = pre_units
            for t in range(4):
                for c in range(8):
                    mm(bank(t), hT[:, c, t * 128:(t + 1) * 128], unit8(sv)[:, c, :], c == 0, c == 7, [("hT", c), R(sv)], [("ps", t)])
            for j in range(4):
                for c in range(8):
                    mm(bank(4 + j), unit8(suu)[:, c, j * 128:(j + 1) * 128], hT[:, c, :], c == 0, c == 7, [("hT", c), R(suu)], [("ps", 4 + j)])
            for t in range(4):
                act(scr[t][:, :], bank(t), AF.Gelu_apprx_tanh, [("ps", t)], [("scr", t)])
            for j in range(4):
                act(uT[j], bank(4 + j), AF.Gelu_apprx_tanh, [("ps", 4 + j)], [("big", 8 + j)])
            for t in range(4):
                sq_ = scr[4 + t % 2]
                act(sq_[:, :], scr[t][:, :], AF.Square, [("scr", t)], [("scr", 4 + t % 2)])
                T.add("dve", lambda e, t=t, sq_=sq_: e.tensor_reduce(out=hs32[:, 8 * t:8 * t + 8], in_=sq_[:, :].rearrange("p (h d) -> p h d", d=64),
                                                                     axis=AX.X, op=ALU.add), [("scr", 4 + t % 2)], [("hs32", t)])
            act(hl32[:, :], hs32[:, :], AF.Ln, [("hs32", t) for t in range(4)], ["hl32"], scale=1.0 / 64, bias=EPS)
            act(hr32[:, :], hl32[:, :], AF.Exp, ["hl32"], ["hr32"], scale=-0.5)
            for t in range(4):
                tm_ = scr[4 + t % 2]
                tt(tm_[:, :], scr[t][:, :], gmg[:, :], ALU.mult, [("scr", t), "gmg"], [("scr", 4 + t % 2)])
                tt(vnb[t].rearrange("p (h d) -> p h d", d=64), tm_[:, :].rearrange("p (h d) -> p h d", d=64),
                   hr32[:, 8 * t:8 * t + 8].unsqueeze(2).broadcast_to([128, 8, 64]), ALU.mult, [("scr", 4 + t % 2), "hr32"], [("big", 16 + t)])

            def spatial_gm(t):
                bk = 6 + t % 2
                for j in range(4):
                    for gg in range(2):
                        g = 2 * j + gg
                        mm(ps[gg * 64:(gg + 1) * 64, bk * 512 + j * 128: bk * 512 + (j + 1) * 128], vnb[t][:, g * 64:(g + 1) * 64], wsT[:, g, :],
                           True, True, [("big", 16 + t), "wsT"], [("ps", bk)])
                tmp = scr[t % 2]
                tt(tmp[:, :].rearrange("p (j q) -> p j q", q=128), bank(bk).rearrange("p (j q) -> p j q", q=128), bsT[:, :, :], ALU.add,
                   [("ps", bk), "bsT"], [("scr", t % 2)])
                tt(big[:, 12:16, t * 128:(t + 1) * 128], tmp[:, :].rearrange("p (j q) -> p j q", q=128), big[:, 8:12, t * 128:(t + 1) * 128], ALU.mult,
                   [("scr", t % 2)] + [("big", 8 + j) for j in range(4)], [("big", 12 + j) for j in range(4)])

            steps = [(g, kb) for g in range(4) for kb in range(NT)]

            def ksup(kb):
                return 0 if kb < 2 else 1 + (kb - 2) // 4

            def qk(i):
                g, kb = steps[i]
                sbk = (i % 2) * 2
                mm(bank(sbk), KT[0:64, kb * 128:(kb + 1) * 128], QT[g][0:64, :], True, True, [("KT", ksup(kb)), ("QT", g)], [("ps", sbk)])
                mm(bank(sbk + 1), KT[64:128, kb * 128:(kb + 1) * 128], QT[g][64:128, :], True, True, [("KT", ksup(kb)), ("QT", g)], [("ps", sbk + 1)])
                pace = [("pace", i)] if (qb == 0 and i % 12 == 0) else []
                act(PT[i % 3], ps[:, sbk * 512:sbk * 512 + 1024], AF.Exp, [("ps", sbk), ("ps", sbk + 1)], PTK[i % 3] + pace, scale=0.125)
                if pace and N_EARLY + i // 12 < N_UNITS_QB:
                    cast_unit(N_EARLY + i // 12, pace)

            def pv(i):
                g, kb = steps[i]
                oa = 4 + 2 * (g % 2)
                mm(bank(oa), Vaug[:, kb, 0:128], PT[i % 3][:, 0:512], kb == 0, kb == NT - 1, [("Vaug", kb)] + PTK[i % 3], [("ps", oa)])
                mm(bank(oa + 1), Vaug[:, kb, 64:192], PT[i % 3][:, 512:1024], kb == 0, kb == NT - 1, [("Vaug", kb)] + PTK[i % 3], [("ps", oa + 1)])
                if kb == NT - 1:
                    if g == 3:
                        act(rc[64:128, 0:512], bank(oa)[64:128, :], AF.Ln, [("ps", oa)], ["rc", ("rcj", 0), ("rcj", 1)])
                        act(rc[64:128, 0:512], rc[64:128, 0:512], AF.Exp, ["rc"], ["rc"], scale=-1.0)
                        act(rc[0:64, 512:1024], bank(oa + 1)[0:64, :], AF.Ln, [("ps", oa + 1)], ["rc", ("rcj", 0), ("rcj", 1)])
                        act(rc[0:64, 512:1024], rc[0:64, 512:1024], AF.Exp, ["rc"], ["rc"], scale=-1.0)
                    else:
                        recip(rc[64:128, 0:512], bank(oa)[64:128, :], [("ps", oa)], ["rc", ("rcj", 0), ("rcj", 1)])
                        recip(rc[0:64, 512:1024], bank(oa + 1)[0:64, :], [("ps", oa + 1)], ["rc", ("rcj", 0), ("rcj", 1)])
                    tt(attnT[g][0:64, :], bank(oa)[0:64, :], rc[64:128, 0:512], ALU.mult, [("ps", oa), "rc", ("rcj", 0), ("rcj", 1)], [("big", 16 + g)])
                    tt(attnT[g][64:128, :], bank(oa + 1)[64:128, :], rc[0:64, 512:1024], ALU.mult, [("ps", oa + 1), "rc", ("rcj", 0), ("rcj", 1)], [("big", 16 + g)])

            qk(0)
            qk(1)
            for i in range(len(steps)):
                if i + 2 < len(steps):
                    qk(i + 2)
                pv(i)
                if i in (1, 3, 5, 7):
                    spatial_gm((i - 1) // 2)

            if qb + 1 < nqb:
                nslot = load_x(row0 + 512, 4)

            sga = [load_unit(3), None]
            sgb = [load_unit(4), None]
            sbra = load_unit(5)
            sbrg = load_unit(6)
            for m in range(8):
                if m == 4:
                    sga[1] = load_unit(7)
                    sgb[1] = load_unit(8)
                h = m // 4
                b0 = (m % 2) * 4
                for c in range(8):
                    mm(bank(b0), unit8(sga[h])[:, c, (m % 4) * 128:(m % 4 + 1) * 128], hT[:, c, :], c == 0, c == 7, [("hT", c), R(sga[h])], [("ps", b0)])
                for c in range(8):
                    mm(bank(b0 + 1), unit8(sgb[h])[:, c, (m % 4) * 128:(m % 4 + 1) * 128], hT[:, c, :], c == 0, c == 7, [("hT", c), R(sgb[h])], [("ps", b0 + 1)])
                for c in range(4):
                    mm(bank(b0 + 2), unit4(sbra)[:, c, m * 128:(m + 1) * 128], attnT[c], c == 0, c == 3, [("big", 16 + c), R(sbra)], [("ps", b0 + 2)])
                for c in range(4):
                    mm(bank(b0 + 3), unit4(sbrg)[:, c, m * 128:(m + 1) * 128], gmT[c], c == 0, c == 3, [("big", 12 + c), R(sbrg)], [("ps", b0 + 3)])
                act(scr[0][:, :], bank(b0), AF.Sigmoid, [("ps", b0)], [("scr", 0)])
                act(scr[1][:, :], bank(b0 + 1), AF.Sigmoid, [("ps", b0 + 1)], [("scr", 1)])
                tt(scr[2][:, :], bank(b0 + 2), scr[0][:, :], ALU.mult, [("ps", b0 + 2), ("scr", 0)], [("scr", 2)])
                tt(scr[3][:, :], bank(b0 + 3), scr[1][:, :], ALU.mult, [("ps", b0 + 3), ("scr", 1)], [("scr", 3)])
                tt(yT[m], scr[2][:, :], scr[3][:, :], ALU.add, [("scr", 2), ("scr", 3)], [("big", m)], eng="pool")

            so = [load_unit(9), load_unit(10)]
            par = cnt["par"] % 2
            cnt["par"] += 1
            po = par * 4
            k8c = {"n": 0}

            def b8_mm(t):
                for nh in range(2):
                    bk = 4 + k8c["n"] % 4
                    k8c["n"] += 1
                    for c in range(8):
                        mm(bank(bk), yT[c][:, t * 128:(t + 1) * 128], unit8(so[nh])[:, c, :], c == 0, c == 7, [("big", c), R(so[nh])], [("ps", bk)])
                    sk = 4 + (k8c["n"] % 2)
                    tt(scr[sk][:, :], bank(bk), gbc[:, nh * 512:(nh + 1) * 512], ALU.mult, [("ps", bk), "gbc"], [("scr", sk)])
                    tt(xb[:, t, nh * 512:(nh + 1) * 512], scr[sk][:, :], xb[:, t, nh * 512:(nh + 1) * 512], ALU.add, [("scr", sk), xres], [xres, ("x1t", t)],
                       eng=("pool" if nh == 0 else "dve"))

            def b9_tile(t):
                kq = cnt["sq"] % 2
                cnt["sq"] += 1
                cs = slice(po + t, po + t + 1)
                act(sqj[kq], xb[:, t, :], AF.Square, [("x1t", t)], [("ss", par, t), ("rcj", kq)], accum=ss[:, cs])
                act(rs[:, cs], ss[:, cs], AF.Ln, [("ss", par, t)], [("rs", par), ("rs", par, t)], scale=1.0 / D, bias=EPS)
                act(rstd[:, cs], rs[:, cs], AF.Exp, [("rs", par, t)], [("rstd", par), ("rstd", par, t)], scale=-0.5)
                k = cnt["xn"] % 2
                cnt["xn"] += 1
                xn_tile(xb, ("x1t", t), t, k, rstd[:, cs], ("rstd", par, t))

            for t in range(4):
                b8_mm(t)
                if t >= 1:
                    b9_tile(t - 1)
            b9_tile(3)
            nm_evac(dict(nt=4), (a2, "a2", "modTb"), 24, 0)

            pq = pre_q_pieces(nslot) if nslot is not None else None
            if pq:
                pq["stats"]()
            for j in range(32):
                if j % 4 == 0:
                    sf = load_unit(11 + j // 4)
                bk = j % 8
                for c in range(8):
                    mm(bank(bk), unit8(sf)[:, c, (j % 4) * 128:(j % 4 + 1) * 128], hT[:, c, :], c == 0, c == 7, [("hT", c), R(sf)], [("ps", bk)])
                sk = j % 4
                act(scr[sk][:, :], bank(bk), AF.Relu, [("ps", bk)], [("scr", sk)])
                tt(big[:, j, :], scr[sk][:, :], scr[sk][:, :], ALU.mult, [("scr", sk)], [("big", j)])
                if pq and j == 27:
                    pq["xs"](0)
                    pq["xs"](1)

            if pq:
                pq["xt"](0)
                pq["xt"](1)
                pq["xs"](2)
                pq["xs"](3)
            blk = 0
            for nh in range(2):
                for kg in range(4):
                    s2 = load_unit(19 + nh * 4 + kg)
                    for t in range(4):
                        bk = t
                        for c in range(8):
                            mm(bank(bk), big[:, kg * 8 + c, t * 128:(t + 1) * 128], unit8(s2)[:, c, :], kg == 0 and c == 0, kg == 3 and c == 7,
                               [("big", kg * 8 + c), R(s2)], [("ps", bk)])
                    if pq:
                        if blk == 0:
                            pq["xt"](2)
                            pq["xt"](3)
                            pq["evac"]()
                        elif blk == 1:
                            pq["qproj"]()
                        elif blk == 2:
                            pq["qdve"](0)
                        elif blk in (3, 4, 5):
                            pq["qpe"](blk - 3)
                            pq["qdve"](blk - 2)
                        elif blk == 6:
                            pq["qpe"](3)
                    blk += 1
                for t in range(4):
                    bk = t
                    sk = 5
                    tt(scr[sk][:, :], bank(bk), gbc[:, 1024 + nh * 512:1024 + (nh + 1) * 512], ALU.mult, [("ps", bk), "gbc"], [("scr", sk)])
                    tt(xb[:, t, nh * 512:(nh + 1) * 512], scr[sk][:, :], xb[:, t, nh * 512:(nh + 1) * 512], ALU.add, [("scr", sk), xres], [xres], eng="pool")
            if qb + 1 < nqb:
                pre_units = (load_unit(1), load_unit(2))
            dma("sp", out_d[qb * 512:(qb + 1) * 512, :].rearrange("(t p) d -> p t d", p=128), xb[:, :, :], [xres], [("out", qb)], f"o{slot}")
            slot = nslot

        T.add("sp", lambda e: e.nop(), [("out", q) for q in range(nqb)] + [("dbgout", n) for n in dbg], [])

        T.finalize()
        nc._tracker = T

        @block.sync
        def _(e):
            T.emit("sp", e, esems, dsems)

        @block.tensor
        def _(e):
            T.emit("pe", e, esems, dsems)

        @block.scalar
        def _(e):
            T.emit("act", e, esems, dsems)

        @block.vector
        def _(e):
            T.emit("dve", e, esems, dsems)

        @block.gpsimd
        def _(e):
            T.emit("pool", e, esems, dsems)

    return nc


_CACHE = {}


def _rope_table(tok_idx):
    n = tok_idx.shape[0]
    t = np.maximum(tok_idx, 0)
    row = (t // 64).astype(np.float32)
    colp = (t % 64).astype(np.float32)
    inv = (np.float32(10000.0) ** (-np.arange(0, 32, 2, dtype=np.float32) / np.float32(32))).astype(np.float32)
    ang = np.concatenate([row[:, None] * inv[None, :], colp[:, None] * inv[None, :]], axis=-1).astype(np.float32)
    cos = np.cos(ang).astype(np.float32)
    sin = np.sin(ang).astype(np.float32)
    ident = tok_idx < 0
    cos[ident] = 1.0
    sin[ident] = 0.0
    return np.concatenate([cos, cos, -sin, sin], axis=1).astype(np.float32)


def kernel(x, c, ctx, c_ctx, w_mod, b_mod, norm1_g, norm2_g, w_in, q_norm_g, k_norm_g,
           gm_norm_g, gm_ws, gm_bs, w_br_attn, w_br_gm, w_out, w_ff1, w_ff2):
    f = lambda a: np.ascontiguousarray(np.asarray(a, dtype=np.float32))
    x, c, ctx, c_ctx = f(x), f(c), f(ctx), f(c_ctx)
    w_mod, b_mod, w_in = f(w_mod)[0], f(b_mod)[0], f(w_in)[0]
    n1, n2 = f(norm1_g)[0], f(norm2_g)[0]
    qg, kg, gmg = f(q_norm_g)[0], f(k_norm_g)[0], f(gm_norm_g)[0]
    ws, bs = f(gm_ws)[0], f(gm_bs)[0]
    bra, brg, wo, w1, w2 = f(w_br_attn)[0], f(w_br_gm)[0], f(w_out)[0], f(w_ff1)[0], f(w_ff2)[0]

    if "nc" not in _CACHE:
        _CACHE["nc"] = build_program()
    nc = _CACHE["nc"]

    qcols = 256 + np.array([kv * 256 + g * 64 + d for g in range(4) for kv in range(2) for d in range(64)])
    order = np.concatenate([np.arange(0, 256), qcols, np.arange(1280, 1792), np.arange(768, 1280), np.arange(1792, 3840)])
    w_in_p = np.ascontiguousarray(w_in[:, order])
    rows = np.array([kv * 256 + g * 64 + d for g in range(4) for kv in range(2) for d in range(64)])
    w_bra_p = np.ascontiguousarray(bra[rows, :])
    b_modT = np.ascontiguousarray(b_mod.reshape(48, 128).T)
    b_modg = np.ascontiguousarray(np.concatenate([b_mod[2048:3072], b_mod[5120:6144]])[None, :])
    n1g = np.ascontiguousarray(n1.reshape(8, 128).T)
    n2g = np.ascontiguousarray(n2.reshape(8, 128).T)
    gq_bc = np.ascontiguousarray(np.broadcast_to(np.tile(qg, 8)[None, :], (128, 512)))
    gk_bc = np.ascontiguousarray(np.broadcast_to(np.tile(kg, 2)[None, :], (128, 128)))
    gmg_bc = np.ascontiguousarray(np.broadcast_to(gmg.reshape(512)[None, :], (128, 512)))
    wsT = np.ascontiguousarray(ws.transpose(2, 0, 1))
    bsT = np.ascontiguousarray(np.broadcast_to(bs.reshape(4, 2, 1, 128), (4, 2, 64, 128)).transpose(1, 2, 0, 3).reshape(128, 4, 128))
    ident = np.eye(128, dtype=np.float32)

    in_maps = []
    for core in range(8):
        b, hf = core // 2, core % 2
        own = np.arange(hf * 4096, (hf + 1) * 4096)
        oth = np.arange((1 - hf) * 4096, (2 - hf) * 4096)
        xin = np.concatenate([ctx[b], x[b, own], x[b, oth]], axis=0)
        tok = np.concatenate([-np.ones(256, dtype=np.int64), own, oth])
        cT = np.ascontiguousarray(np.stack([c[b], c_ctx], axis=1).reshape(8, 128, 2).transpose(1, 0, 2))
        in_maps.append({
            "xin": np.ascontiguousarray(xin), "rope": _rope_table(tok), "cT": cT, "w_mod": w_mod, "b_modT": b_modT,
            "b_modg": b_modg, "n1g": n1g, "n2g": n2g, "w_in_p": w_in_p, "gq_bc": gq_bc, "gk_bc": gk_bc, "gmg_bc": gmg_bc,
            "wsT": wsT, "bsT": bsT, "w_bra_p": w_bra_p, "w_brg": brg, "w_out": wo, "w_ff1": w1, "w_ff2": w2, "ident": ident,
        })
    res = run_bass_kernel_spmd(nc, in_maps, core_ids=list(range(8)))
    out = np.empty((4, 8192, D), dtype=np.float32)
    for core in range(8):
        b, hf = core // 2, core % 2
        out[b, hf * 4096:(hf + 1) * 4096] = res.results[core]["out"]
    return out
```
